# Optimizing a Trainium2 kernel written in Bass

```python
import math
import jax
import jax.numpy as jnp
from jax import lax
import numpy as np

D_MODEL = 1024
BATCH = 4
SEQ = 4096
DEPTH = 4
DEC_BATCH = 128
DEC_SEQ = 4
PAST_LEN = 8192
PAGE_SIZE = 128

N_MIXERS = 3
N_A = (DEPTH + 2) // N_MIXERS
N_B = (DEPTH + 1) // N_MIXERS
N_C = DEPTH // N_MIXERS

HA = 8
DKA = D_MODEL // 2 // HA
DVA = D_MODEL // HA
GATE_SOFTCAP = 15.0
HB = 4
DKB = D_MODEL // 2 // HB
DVB = D_MODEL // HB
GLA_RANK = 16
GLA_TAU = 16.0
HC = 16
HKV = 4
HDC = D_MODEL // HC
GROUP = HC // HKV
WINDOW = 128
D_FF = 2816
CONV_W = 3

CHUNK = 64
EPS = 1e-6
NEG = -1e30

A_IN = 2 * HA * DKA + 2 * HA * DVA + 2 * HA
B_IN = 2 * HB * DKB + 2 * HB * DVB + GLA_RANK
C_IN = (HC + 2 * HKV) * HDC

kernel_name = "hybrid_mlstm_gla_swa_convffn_step"


def _rmsnorm(x, g):
    xf = x.astype(jnp.float32)
    y = xf * lax.rsqrt(jnp.mean(xf * xf, axis=-1, keepdims=True) + EPS)
    return (y * g.astype(jnp.float32)).astype(x.dtype)


def _softcap(z):
    return GATE_SOFTCAP * jnp.tanh(z / GATE_SOFTCAP)


def _to_chunks(t, L):
    b, T = t.shape[:2]
    return jnp.moveaxis(t.reshape((b, T // L, L) + t.shape[2:]), 1, 0)


def _from_chunks(t):
    nc, b, L = t.shape[:3]
    return jnp.moveaxis(t, 0, 1).reshape((b, nc * L) + t.shape[3:])


def _mlstm_chunk(carry, inp):
    c, n, m = carry
    q, k, v, ig, lf = inp
    L = q.shape[1]
    f_cum = jnp.cumsum(lf, axis=1)
    a = ig - f_cum
    m_t = f_cum + jnp.maximum(m[:, None], lax.cummax(a, axis=1))
    causal = jnp.tril(jnp.ones((L, L), bool))[None, :, :, None]
    logw = a[:, None, :, :] + (f_cum - m_t)[:, :, None, :]
    w = jnp.where(causal, jnp.exp(jnp.minimum(logw, 0.0)), 0.0)
    qk = jnp.einsum('bthd,bshd->btsh', q, k) * w
    g = jnp.exp(jnp.minimum(f_cum + m[:, None] - m_t, 0.0))
    num = jnp.einsum('btsh,bshv->bthv', qk, v) + g[..., None] * jnp.einsum('bthd,bhdv->bthv', q, c)
    den = jnp.sum(qk, axis=2) + g * jnp.einsum('bthd,bhd->bth', q, n)
    h = num / jnp.maximum(jnp.abs(den), jnp.exp(-m_t))[..., None]
    m_last = m_t[:, -1]
    f_last = f_cum[:, -1]
    w_last = jnp.exp(jnp.minimum(a + (f_last - m_last)[:, None], 0.0))
    g_last = jnp.exp(jnp.minimum(f_last + m - m_last, 0.0))
    c_new = g_last[..., None, None] * c + jnp.einsum('bsh,bshd,bshv->bhdv', w_last, k, v)
    n_new = g_last[..., None] * n + jnp.einsum('bsh,bshd->bhd', w_last, k)
    return (c_new, n_new, m_last), h


def _mlstm_mixer(h, c0, n0, m0, w_in, b_i, b_f, norm_g, w_out):
    b, T, _ = h.shape
    f32 = jnp.float32
    proj = jnp.einsum('btd,de->bte', h, w_in).astype(f32)
    s1 = HA * DKA
    s2 = 2 * HA * DKA
    s3 = s2 + HA * DVA
    s4 = s3 + HA * DVA
    s5 = s4 + HA
    q, k, v, o, ig, fg = jnp.split(proj, [s1, s2, s3, s4, s5], axis=-1)
    q = q.reshape(b, T, HA, DKA)
    k = k.reshape(b, T, HA, DKA) * (DKA ** -0.5)
    v = v.reshape(b, T, HA, DVA)
    ig = _softcap(ig + b_i.astype(f32))
    lf = jax.nn.log_sigmoid(_softcap(fg + b_f.astype(f32)))
    L = math.gcd(CHUNK, T)
    xs = (_to_chunks(q, L), _to_chunks(k, L), _to_chunks(v, L), _to_chunks(ig, L), _to_chunks(lf, L))
    carry0 = (c0.astype(f32), n0.astype(f32), m0.astype(f32))
    (c1, n1, m1), hs = lax.scan(_mlstm_chunk, carry0, xs)
    hs = _from_chunks(hs)
    hs = _rmsnorm(hs, norm_g.reshape(HA, DVA)) * jax.nn.sigmoid(o).reshape(b, T, HA, DVA)
    y = jnp.einsum('bte,ed->btd', hs.reshape(b, T, HA * DVA).astype(h.dtype), w_out)
    return y, c1, n1, m1


def _gla_chunk(s, inp):
    q, k, v, lg = inp
    L = q.shape[1]
    bc = jnp.cumsum(lg, axis=1)
    causal = jnp.tril(jnp.ones((L, L), bool))[None, :, :, None, None]
    diff = bc[:, :, None] - bc[:, None, :]
    decay = jnp.where(causal, jnp.exp(jnp.minimum(diff, 0.0)), 0.0)
    att = jnp.einsum('bthd,bshd,btshd->btsh', q, k, decay)
    o = jnp.einsum('btsh,bshv->bthv', att, v) + jnp.einsum('bthd,bhdv->bthv', q * jnp.exp(bc), s)
    b_last = bc[:, -1]
    s_new = jnp.exp(b_last)[..., None] * s + jnp.einsum('bshd,bshv->bhdv', k * jnp.exp(b_last[:, None] - bc), v)
    return s_new, o


def _gla_mixer(h, s0, w_in, w_gate_up, b_gate, norm_g, w_out):
    b, T, _ = h.shape
    f32 = jnp.float32
    proj = jnp.einsum('btd,de->bte', h, w_in).astype(f32)
    s1 = HB * DKB
    s2 = 2 * HB * DKB
    s3 = s2 + HB * DVB
    s4 = s3 + HB * DVB
    q, k, v, r, z = jnp.split(proj, [s1, s2, s3, s4], axis=-1)
    q = q.reshape(b, T, HB, DKB) * (DKB ** -0.5)
    k = k.reshape(b, T, HB, DKB)
    v = v.reshape(b, T, HB, DVB)
    lg = jax.nn.log_sigmoid(jnp.einsum('btr,re->bte', z, w_gate_up.astype(f32)) + b_gate.astype(f32)) / GLA_TAU
    lg = lg.reshape(b, T, HB, DKB)
    L = math.gcd(CHUNK, T)
    xs = (_to_chunks(q, L), _to_chunks(k, L), _to_chunks(v, L), _to_chunks(lg, L))
    s1_state, o = lax.scan(_gla_chunk, s0.astype(f32), xs)
    o = _from_chunks(o)
    o = _rmsnorm(o, norm_g.reshape(HB, DVB)) * jax.nn.silu(r).reshape(b, T, HB, DVB)
    y = jnp.einsum('bte,ed->btd', o.reshape(b, T, HB * DVB).astype(h.dtype), w_out)
    return y, s1_state


def _sink_attention(q, k, v, mask, sinks):
    s = jnp.einsum('bnqkgd,bnrkd->bnkgqr', q, k) * (HDC ** -0.5)
    s = jnp.where(mask[None, :, None, None], s, NEG)
    sink = jnp.broadcast_to(sinks[None, None, :, :, None, None], s.shape[:-1] + (1,))
    p = jax.nn.softmax(jnp.concatenate([s, sink], axis=-1), axis=-1)[..., :-1]
    return jnp.einsum('bnkgqr,bnrkd->bnqkgd', p, v)


def _swa_mixer(h, k_buf, v_buf, w_in, b_in, sinks, w_out):
    b, T, _ = h.shape
    f32 = jnp.float32
    proj = (jnp.einsum('btd,de->bte', h, w_in) + b_in).astype(f32)
    q, k, v = jnp.split(proj, [HC * HDC, HC * HDC + HKV * HDC], axis=-1)
    q = q.reshape(b, T, HKV, GROUP, HDC)
    k = k.reshape(b, T, HKV, HDC)
    v = v.reshape(b, T, HKV, HDC)
    sinks = sinks.astype(f32).reshape(HKV, GROUP)
    if k_buf is None:
        nb = T // WINDOW
        pad = jnp.zeros((b, WINDOW, HKV, HDC), f32)
        kb = jnp.concatenate([pad, k], axis=1).reshape(b, nb + 1, WINDOW, HKV, HDC)
        vb = jnp.concatenate([pad, v], axis=1).reshape(b, nb + 1, WINDOW, HKV, HDC)
        kb = jnp.concatenate([kb[:, :-1], kb[:, 1:]], axis=2)
        vb = jnp.concatenate([vb[:, :-1], vb[:, 1:]], axis=2)
        qi = jnp.arange(WINDOW)
        ri = jnp.arange(2 * WINDOW)
        rel = WINDOW + qi[:, None] - ri[None, :]
        key_pos = (jnp.arange(nb)[:, None] - 1) * WINDOW + ri[None, :]
        mask = ((rel >= 0) & (rel <= WINDOW))[None] & (key_pos >= 0)[:, None, :]
        out = _sink_attention(q.reshape(b, nb, WINDOW, HKV, GROUP, HDC), kb, vb, mask, sinks)
        k_keep = k[:, -WINDOW:]
        v_keep = v[:, -WINDOW:]
    else:
        wb = k_buf.shape[1]
        k_all = jnp.concatenate([k_buf.astype(f32), k], axis=1)
        v_all = jnp.concatenate([v_buf.astype(f32), v], axis=1)
        q_pos = PAST_LEN + jnp.arange(T)
        k_pos = PAST_LEN - wb + jnp.arange(wb + T)
        rel = q_pos[:, None] - k_pos[None, :]
        mask = ((rel >= 0) & (rel <= WINDOW))[None]
        out = _sink_attention(q[:, None], k_all[:, None], v_all[:, None], mask, sinks)
        k_keep = k_all[:, -wb:]
        v_keep = v_all[:, -wb:]
    y = jnp.einsum('bte,ed->btd', out.reshape(b, T, HC * HDC).astype(h.dtype), w_out)
    return y, k_keep, v_keep


def _conv_ffn(h, conv_state, w_up, conv_w, conv_b, w_down):
    b, T, _ = h.shape
    f32 = jnp.float32
    u = jnp.einsum('btd,df->btf', h, w_up).astype(f32)
    gate, up = jnp.split(u, 2, axis=-1)
    g_ext = jnp.concatenate([conv_state.astype(f32), gate], axis=1)
    cw = conv_w.astype(f32)
    conv = conv_b.astype(f32) + sum(cw[j] * g_ext[:, j:j + T] for j in range(CONV_W))
    y = jnp.einsum('btf,fd->btd', (jax.nn.silu(conv) * up).astype(h.dtype), w_down)
    return y, g_ext[:, -(CONV_W - 1):]


def _trunk(x, st, w):
    new = {'a_c': [], 'a_n': [], 'a_m': [], 'b_s': [], 'c_k': [], 'c_v': [], 'f': []}
    for i in range(DEPTH):
        kind, j = i % N_MIXERS, i // N_MIXERS
        hn = _rmsnorm(x, w['norm_mix'][i])
        if kind == 0:
            y, c1, n1, m1 = _mlstm_mixer(hn, st['a_c'][j], st['a_n'][j], st['a_m'][j], w['a_w_in'][j],
                                         w['a_b_i'][j], w['a_b_f'][j], w['a_norm'][j], w['a_w_out'][j])
            new['a_c'].append(c1)
            new['a_n'].append(n1)
            new['a_m'].append(m1)
        elif kind == 1:
            y, s1 = _gla_mixer(hn, st['b_s'][j], w['b_w_in'][j], w['b_w_gate_up'][j], w['b_b_gate'][j],
                               w['b_norm'][j], w['b_w_out'][j])
            new['b_s'].append(s1)
        else:
            y, k1, v1 = _swa_mixer(hn, st['c_k'][j], st['c_v'][j], w['c_w_in'][j], w['c_b_in'][j],
                                   w['c_sinks'][j], w['c_w_out'][j])
            new['c_k'].append(k1)
            new['c_v'].append(v1)
        x = x + y
        y, f1 = _conv_ffn(_rmsnorm(x, w['norm_ffn'][i]), st['f'][i], w['f_w_up'][i], w['f_conv_w'][i],
                          w['f_conv_b'][i], w['f_w_down'][i])
        x = x + y
        new['f'].append(f1)
    out = {name: jnp.stack(vals).astype(x.dtype) for name, vals in new.items()}
    return _rmsnorm(x, w['norm_final']), out


def setup_inputs(seed: int = 0) -> dict:
    key = jax.random.key(seed)
    ks = iter(jax.random.split(key, 32))

    def nrm(shape, scale=1.0):
        return jax.random.normal(next(ks), shape, jnp.float32) * scale

    w_buf = min(WINDOW, PAST_LEN)
    return {
        'x_prompt': nrm((BATCH, SEQ, D_MODEL)),
        'x_sample': nrm((DEC_BATCH, DEC_SEQ, D_MODEL)),
        'state_mlstm_c': nrm((N_A, DEC_BATCH, HA, DKA, DVA), 0.5),
        'state_mlstm_n': nrm((N_A, DEC_BATCH, HA, DKA), 0.5),
        'state_mlstm_m': nrm((N_A, DEC_BATCH, HA), 1.0),
        'state_gla': nrm((N_B, DEC_BATCH, HB, DKB, DVB), 0.5),
        'cache_swa_k': nrm((N_C, DEC_BATCH, w_buf, HKV, HDC)),
        'cache_swa_v': nrm((N_C, DEC_BATCH, w_buf, HKV, HDC)),
        'state_ffn_conv': nrm((DEPTH, DEC_BATCH, CONV_W - 1, D_FF)),
        'norm_mix_g': 1.0 + nrm((DEPTH, D_MODEL), 0.1),
        'norm_ffn_g': 1.0 + nrm((DEPTH, D_MODEL), 0.1),
        'norm_final_g': 1.0 + nrm((D_MODEL,), 0.1),
        'a_w_in': nrm((N_A, D_MODEL, A_IN), D_MODEL ** -0.5),
        'a_b_i': nrm((N_A, HA), 0.1),
        'a_b_f': 3.0 + nrm((N_A, HA), 0.5),
        'a_norm_g': 1.0 + nrm((N_A, HA * DVA), 0.1),
        'a_w_out': nrm((N_A, HA * DVA, D_MODEL), (HA * DVA) ** -0.5),
        'b_w_in': nrm((N_B, D_MODEL, B_IN), D_MODEL ** -0.5),
        'b_w_gate_up': nrm((N_B, GLA_RANK, HB * DKB), GLA_RANK ** -0.5),
        'b_b_gate': nrm((N_B, HB * DKB), 0.1),
        'b_norm_g': 1.0 + nrm((N_B, HB * DVB), 0.1),
        'b_w_out': nrm((N_B, HB * DVB, D_MODEL), (HB * DVB) ** -0.5),
        'c_w_in': nrm((N_C, D_MODEL, C_IN), D_MODEL ** -0.5),
        'c_b_in': nrm((N_C, C_IN), 0.02),
        'c_sinks': nrm((N_C, HC), 1.0),
        'c_w_out': nrm((N_C, HC * HDC, D_MODEL), (HC * HDC) ** -0.5),
        'f_w_up': nrm((DEPTH, D_MODEL, 2 * D_FF), D_MODEL ** -0.5),
        'f_conv_w': nrm((DEPTH, CONV_W, D_FF), CONV_W ** -0.5),
        'f_conv_b': nrm((DEPTH, D_FF), 0.02),
        'f_w_down': nrm((DEPTH, D_FF, D_MODEL), D_FF ** -0.5),
    }


def reference(x_prompt, x_sample, state_mlstm_c, state_mlstm_n, state_mlstm_m, state_gla,
              cache_swa_k, cache_swa_v, state_ffn_conv, norm_mix_g, norm_ffn_g, norm_final_g,
              a_w_in, a_b_i, a_b_f, a_norm_g, a_w_out, b_w_in, b_w_gate_up, b_b_gate, b_norm_g, b_w_out,
              c_w_in, c_b_in, c_sinks, c_w_out, f_w_up, f_conv_w, f_conv_b, f_w_down):
    w = {'norm_mix': norm_mix_g, 'norm_ffn': norm_ffn_g, 'norm_final': norm_final_g,
         'a_w_in': a_w_in, 'a_b_i': a_b_i, 'a_b_f': a_b_f, 'a_norm': a_norm_g, 'a_w_out': a_w_out,
         'b_w_in': b_w_in, 'b_w_gate_up': b_w_gate_up, 'b_b_gate': b_b_gate, 'b_norm': b_norm_g,
         'b_w_out': b_w_out, 'c_w_in': c_w_in, 'c_b_in': c_b_in, 'c_sinks': c_sinks, 'c_w_out': c_w_out,
         'f_w_up': f_w_up, 'f_conv_w': f_conv_w, 'f_conv_b': f_conv_b, 'f_w_down': f_w_down}
    f32 = jnp.float32
    bp = x_prompt.shape[0]
    st_p = {'a_c': jnp.zeros((N_A, bp, HA, DKA, DVA), f32),
            'a_n': jnp.zeros((N_A, bp, HA, DKA), f32),
            'a_m': jnp.zeros((N_A, bp, HA), f32),
            'b_s': jnp.zeros((N_B, bp, HB, DKB, DVB), f32),
            'c_k': [None] * N_C, 'c_v': [None] * N_C,
            'f': jnp.zeros((DEPTH, bp, CONV_W - 1, D_FF), x_prompt.dtype)}
    st_s = {'a_c': state_mlstm_c, 'a_n': state_mlstm_n, 'a_m': state_mlstm_m, 'b_s': state_gla,
            'c_k': cache_swa_k, 'c_v': cache_swa_v, 'f': state_ffn_conv}
    y_prompt, np_ = _trunk(x_prompt, st_p, w)
    y_sample, ns_ = _trunk(x_sample, st_s, w)
    return (y_prompt, y_sample,
            np_['a_c'], np_['a_n'], np_['a_m'], np_['b_s'], np_['c_k'], np_['c_v'], np_['f'],
            ns_['a_c'], ns_['a_n'], ns_['a_m'], ns_['b_s'], ns_['c_k'], ns_['c_v'], ns_['f'])
```

```python
import numpy as np
from contextlib import ExitStack
import concourse.bass as bass
import concourse.mybir as mybir

F32 = mybir.dt.float32
BF16 = mybir.dt.bfloat16
ACT = mybir.ActivationFunctionType
ALU = mybir.AluOpType
AX = mybir.AxisListType

SEM_LIMIT = 30000


class Sem:
    __slots__ = ("h", "issued", "is_dma")

    def __init__(self, h, is_dma):
        self.h = h
        self.issued = 0
        self.is_dma = is_dma


class Buf:
    def __init__(self, t, name, space="sb"):
        self.t = t
        self.name = name
        self.space = space
        self.w = None
        self.r = {}
        self.chan = None
        self.schan = None

    def __getitem__(self, k):
        return self.t[k]


class Eng:
    def __init__(self, name, obj):
        self.name = name
        self.obj = obj
        self.sem = None
        self.prog = []
        self.seen = {}


class FW:
    def __init__(self, nc):
        self.nc = nc
        self.es = ExitStack()
        self.eng = {}
        for n in ("tensor", "vector", "scalar", "gpsimd", "sync"):
            self.eng[n] = Eng(n, getattr(nc, n))
        self.nsem = 0
        self.nbuf = 0
        self.out_tokens = []

    def new_sem(self, is_dma):
        self.nsem += 1
        h = self.es.enter_context(self.nc.semaphore("s%d" % self.nsem))
        return Sem(h, is_dma)

    def sbuf(self, shape, dtype=F32, name=None):
        self.nbuf += 1
        name = "%s_%d" % (name or "sb", self.nbuf)
        t = self.es.enter_context(self.nc.sbuf_tensor(name, list(shape), dtype))
        return Buf(t, name)

    def psum(self, shape, dtype=F32, name=None):
        self.nbuf += 1
        name = "%s_%d" % (name or "ps", self.nbuf)
        t = self.es.enter_context(self.nc.psum_tensor(name, list(shape), dtype))
        return Buf(t, name, "ps")

    def dram(self, name, shape, dtype=F32, kind="Internal"):
        t = self.nc.dram_tensor(name, list(shape), dtype, kind=kind)
        return Buf(t.ap(), name, "dram")

    def region(self, name):
        return Buf(None, name, "dram")

    def _needs(self, E, reads, writes):
        needs = {}

        def need(tok):
            if tok is None:
                return
            s, v = tok
            if s.is_dma:
                v = s.issued
            if needs.get(s, 0) < v:
                needs[s] = v

        for b in reads:
            need(b.w)
        for b in writes:
            need(b.w)
            for s, v in b.r.items():
                need((s, v))
        waits = []
        for s, v in needs.items():
            if E.name == "tensor" and s is E.sem:
                continue
            if E.seen.get(s, 0) >= v:
                continue
            E.seen[s] = v
            waits.append((s.h, v))
        return waits

    def op(self, eng, fn, reads=(), writes=()):
        E = self.eng[eng]
        if E.sem is None or E.sem.issued >= SEM_LIMIT:
            E.sem = self.new_sem(False)
        waits = self._needs(E, reads, writes)
        E.sem.issued += 1
        tok = (E.sem, E.sem.issued)
        E.prog.append((waits, fn, (E.sem.h, 1)))
        for b in reads:
            b.r[tok[0]] = tok[1]
        for b in writes:
            b.w = tok
            b.r = {}
        return tok

    def dma(self, out_ap, in_ap, reads=(), writes=(), q="sync", chan=None, **kw):
        E = self.eng[q]
        waits = self._needs(E, reads, writes)
        if chan is None:
            b = writes[0] if (writes and writes[0].space != "dram") else None
            if b is not None:
                if b.chan is None:
                    b.chan = self.new_sem(True)
                chan = b.chan
            else:
                b = reads[0]
                if b.schan is None:
                    b.schan = self.new_sem(True)
                chan = b.schan
        chan.issued += 16
        tok = (chan, chan.issued)

        def fn(e, out_ap=out_ap, in_ap=in_ap, kw=kw):
            kw2 = dict(kw); kw2.setdefault("allow_slow_non_contiguous", True); return e.dma_start(out=out_ap, in_=in_ap, **kw2)

        E.prog.append((waits, fn, (chan.h, 16)))
        for b in reads:
            b.r[tok[0]] = tok[1]
        for b in writes:
            b.w = tok
            b.r = {}
        return tok

    def final_wait(self, toks, eng="sync"):
        E = self.eng[eng]
        waits = []
        seen = {}
        for s, v in toks:
            if s.is_dma:
                v = s.issued
            if seen.get(s, 0) < v:
                seen[s] = v
        for s, v in seen.items():
            waits.append((s.h, v))
        E.prog.append((waits, None, None))

    def emit(self):
        nc = self.nc
        with nc.Block() as block:
            def run(E):
                def body(e):
                    for waits, fn, inc in E.prog:
                        for h, v in waits:
                            e.wait_ge(h, v)
                        if fn is not None:
                            ins = fn(e)
                            ins.then_inc(inc[0], inc[1])
                return body
            block.tensor(run(self.eng["tensor"]))
            block.vector(run(self.eng["vector"]))
            block.scalar(run(self.eng["scalar"]))
            block.gpsimd(run(self.eng["gpsimd"]))
            block.sync(run(self.eng["sync"]))

    def close(self):
        self.es.close()

    def stats(self):
        return {n: len(E.prog) for n, E in self.eng.items()}, self.nsem

from concourse.bass_utils import run_bass_kernel_spmd

D = 1024
TP = 4096
NS = 64
NSEQ = 16
DFF = 2816
NFC = 22
BLK = 256
EPS = 1e-6


def v3(ap, a):
    return ap.rearrange("p (a b) -> p a b", a=a)


def v4(ap, a, b):
    return ap.rearrange("p (a b c) -> p a b c", a=a, b=b)


class Arena:
    def __init__(self, fw, nbytes, name):
        self.fw = fw
        self.raw = fw.sbuf([128, nbytes // 4], F32, name)
        self.off = 0
        self.n = nbytes // 4

    def take(self, nfree, dtype=F32, name="ar"):
        words = nfree if dtype == F32 else (nfree + 1) // 2
        assert self.off + words <= self.n, (name, self.off, words, self.n)
        ap = self.raw.t[:, self.off:self.off + words]
        self.off += words
        if dtype != F32:
            ap = ap.bitcast(dtype)
        return Buf(ap, name, "sb")


def handoff(frm, to):
    toks = {}
    for b in frm:
        if b.w is not None:
            s, v = b.w
            toks[s] = max(toks.get(s, 0), v)
        for s, v in b.r.items():
            toks[s] = max(toks.get(s, 0), v)
    for b in to:
        for s, v in toks.items():
            b.r[s] = max(b.r.get(s, 0), v)


import os
CFG = {}

def build():
    CFG['L'] = int(os.environ.get('KN_LAYERS', '4')); CFG['parts'] = os.environ.get('KN_PARTS', 'ps'); CFG['ffn'] = int(os.environ.get('KN_FFN', '1')); CFG['nb'] = int(os.environ.get('KN_NB', '16')); CFG['mix'] = int(os.environ.get('KN_MIX', '1')); CFG['stop'] = float(os.environ.get('KN_STOP', '99')); CFG['cores'] = int(os.environ.get('KN_CORES', '8'))
    nc = bass.Bass("TRN2", target_bir_lowering=False)
    fw = FW(nc)

    def din(name, shape):
        return nc.dram_tensor(name, list(shape), F32, kind="ExternalInput").ap()

    def dout(name, shape):
        return nc.dram_tensor(name, list(shape), F32, kind="ExternalOutput").ap()

    xp = din("xp", [TP, D]); xsm = din("xsm", [NS, D])
    a_w_in = din("a_w_in", [2, D, 3088]); a_w_out = din("a_w_out", [2, D, D])
    b_w_in = din("b_w_in", [1, D, 3088]); b_w_out = din("b_w_out", [1, D, D])
    c_w_in = din("c_w_in", [1, D, 1536]); c_w_out = din("c_w_out", [1, D, D])
    f_w_up = din("f_w_up", [4, D, 2 * DFF]); f_w_down = din("f_w_down", [4, DFF, D])
    b_wgu = din("b_wgu", [1, 16, 512])
    norm_mix = din("norm_mix", [4, D]); norm_ffn = din("norm_ffn", [4, D]); norm_fin = din("norm_fin", [1, D])
    a_norm = din("a_norm", [2, D]); b_norm = din("b_norm", [1, D])
    a_bi = din("a_bi", [2, 8, 1]); a_bf = din("a_bf", [2, 8, 1])
    b_bg = din("b_bg", [1, 128, 4])
    c_bq = din("c_bq", [1, 64, 16]); c_bk = din("c_bk", [1, 64, 4]); c_bkv = din("c_bkv", [1, 512])
    c_snk = din("c_snk", [1, 16])
    f_cw = din("f_cw", [4, 128, NFC * 3]); f_cb = din("f_cb", [4, 128, NFC])
    k_ident = din("k_ident", [128, 128]); k_maskc = din("k_maskc", [128, 128]); k_maskp = din("k_maskp", [128, 128])
    k_ones8 = din("k_ones8", [8, 128]); k_negI8 = din("k_negI8", [8, 8]); k_sel8 = din("k_sel8", [8, 128]); k_selc = din("k_selc", [8, 4])
    k_rm_p = din("k_rm_p", [128, BLK]); k_rm_s = din("k_rm_s", [128, NS])
    si_c = din("si_c", [2, NSEQ, 64, 8 * 128]); si_n = din("si_n", [2, NSEQ, 64, 8]); si_m = din("si_m", [2, 8, NSEQ])
    si_g = din("si_g", [1, NSEQ, 128, 4 * 256]); si_k = din("si_k", [1, NSEQ, 128, 256]); si_v = din("si_v", [1, NSEQ, 128, 256])
    si_f = din("si_f", [4, 128, NFC * NSEQ * 2])
    yp = dout("yp", [TP, D]); ys = dout("ys", [NS, D])
    po_c = dout("po_c", [2, 64, 8 * 128]); po_n = dout("po_n", [2, 64, 8]); po_m = dout("po_m", [2, 8, 1])
    po_g = dout("po_g", [1, 128, 4 * 256]); po_k = dout("po_k", [1, 128, 256]); po_v = dout("po_v", [1, 128, 256])
    po_f = dout("po_f", [4, 128, NFC * 2])
    so_c = dout("so_c", [2, NSEQ, 64, 8 * 128]); so_n = dout("so_n", [2, NSEQ, 64, 8]); so_m = dout("so_m", [2, 8, NSEQ])
    so_g = dout("so_g", [1, NSEQ, 128, 4 * 256]); so_k = dout("so_k", [1, NSEQ, 128, 256]); so_v = dout("so_v", [1, NSEQ, 128, 256])
    so_f = dout("so_f", [4, 128, NFC * NSEQ * 2])
    res_p = nc.dram_tensor("res_p", [TP, D], F32, kind="Internal").ap()
    res_s = nc.dram_tensor("res_s", [NS, D], F32, kind="Internal").ap()

    OUT = fw.region("outputs")
    out_toks = []

    def dma_out(dst, src_ap, srcbuf, q="sync"):
        tok = fw.dma(dst, src_ap, reads=[srcbuf], writes=[], q=q)
        out_toks.append(tok)

    WA = fw.sbuf([128, 24704], BF16, "WA")
    WBa = Arena(fw, 22528 * 2, "WB")
    WB = Buf(WBa.raw.t[:, :].bitcast(BF16), "WBw", "sb")
    WBa.off = 4096
    WCa = Arena(fw, 22528 * 2, "WC")
    WC = Buf(WCa.raw.t[:, :].bitcast(BF16), "WCw", "sb")
    ident = fw.sbuf([128, 128], F32, "ident"); identb = fw.sbuf([128, 128], BF16, "identb")
    maskc = fw.sbuf([128, 128], F32, "maskc"); maskp = fw.sbuf([128, 128], F32, "maskp")
    ones8 = fw.sbuf([8, 128], F32, "ones8"); negI8 = fw.sbuf([8, 8], F32, "negI8")
    sel8 = fw.sbuf([8, 128], F32, "sel8"); selc = fw.sbuf([8, 4], F32, "selc")
    rm_p = fw.sbuf([128, BLK], F32, "rm_p"); rm_s = fw.sbuf([128, NS], F32, "rm_s")
    gb = fw.sbuf([128, D], F32, "gb")
    sp = fw.sbuf([128, 576], F32, "sp")
    XB = [fw.sbuf([128, 2, D], F32, "xb%d" % i) for i in range(2)]
    gfin = fw.sbuf([128, D], F32, "gfin")
    xn = fw.sbuf([128, 2, D], BF16, "xn")
    XST = [fw.sbuf([128, 8, BLK], BF16, "xsT%d" % i) for i in range(2)]
    stat = fw.sbuf([128, 64], F32, "stat")
    mhalf = fw.sbuf([128, 2], F32, "mhalf")
    hT = fw.sbuf([128, NFC, BLK], BF16, "hT")
    hsT = Buf(v3(hT.t[:, 0:8, :].rearrange("p a b -> p (a b)"), 8), "hsT", "sb")
    gext = [fw.sbuf([128, BLK + 2 * NSEQ], F32, "gext%d" % i) for i in range(2)]
    cacc = [fw.sbuf([128, BLK], F32, "cacc%d" % i) for i in range(2)]
    halo_p = fw.sbuf([128, NFC * 2], F32, "halo_p")
    halo_s = fw.sbuf([128, NFC * NSEQ * 2], F32, "halo_s")
    cwb = fw.sbuf([128, NFC * 3], F32, "cwb"); cbb = fw.sbuf([128, NFC], F32, "cbb")
    PST = fw.sbuf([128, 1032], F32, "PST")
    Cst = [Buf(v3(PST.t[:64, :], 8), "Cst0", "sb"), None, None]
    Sst = [Buf(v3(PST.t[:, 0:1024], 4), "Sst0", "sb"), None, None]
    kTprev = [Buf(v3(PST.t[:64, 0:256].bitcast(BF16), 4), "kTprev0", "sb"), None, None]
    vprev = [Buf(v3(PST.t[:, 512:642].bitcast(BF16), 4), "vprev0", "sb"), None, None]
    pst_bufs = [Cst[0], Sst[0], kTprev[0], vprev[0]]
    kvraw = [None, None]
    MS = {}
    def ms(ar, name, nfree, dtype=F32):
        MS[name] = ar.take(nfree, dtype, name)
        return MS[name]
    qT = ms(WCa, "qT", 8 * BLK); kT = ms(WCa, "kT", 8 * BLK)
    ktm = ms(WCa, "ktm", 512); vext = ms(WCa, "vext", 8 * 129 + 8)
    Wt = ms(WCa, "Wt", 512); numS = ms(WCa, "numS", 512); hs = ms(WCa, "hs", 1024)
    kw = ms(WCa, "kw", 512); sqs = ms(WCa, "sqs", 256)
    Wt2 = ms(WCa, "Wt2", 512)
    grow = ms(WCa, "grow", 5 * (BLK + NSEQ)); trow = ms(WCa, "trow", 3 * 128 + 16)
    cols = ms(WCa, "cols", 64); glb = ms(WCa, "glb", 16)
    print("WC scratch words", WCa.off, "of", WCa.n)
    gnb = ms(WBa, "gnb", 1024)
    so = ms(WBa, "so", 1024)
    rbd = ms(WBa, "rbd", 1024)
    o_save = WBa.off; WBa.off -= 1024
    lgT = ms(WBa, "lgT", 4 * BLK)
    WBa.off = o_save
    u0 = WBa.off
    for i in (1, 2):
        b_ = ms(WBa, "Cst%d" % i, 1032); Cst[i] = Buf(v3(b_.t[:64, :], 8), b_.name, "sb"); MS[b_.name] = Cst[i]
    u1 = WBa.off
    WBa.off = u0
    for i in (1, 2):
        b_ = ms(WBa, "Sst%d" % i, 1024); Sst[i] = Buf(v3(b_.t, 4), b_.name, "sb"); MS[b_.name] = Sst[i]
    u1 = max(u1, WBa.off)
    WBa.off = u0
    for i in (1, 2):
        b_ = ms(WBa, "kTprev%d" % i, 512); kTprev[i] = Buf(v3(b_.t[:64, 0:256].bitcast(BF16), 4), b_.name, "sb"); MS[b_.name] = kTprev[i]
        b_ = ms(WBa, "vprev%d" % i, 260); vprev[i] = Buf(v3(b_.t[:, 0:130].bitcast(BF16), 4), b_.name, "sb"); MS[b_.name] = vprev[i]
        kvraw[i - 1] = ms(WBa, "kvraw%d" % i, 512)
    WBa.off = max(WBa.off, u1)
    print("WB scratch words", WBa.off, "of", WBa.n)
    ve2 = ms(WBa, "ve2", 516); stbuf = ms(WBa, "stbuf", 516)
    print("WB scratch words (after filler bufs)", WBa.off, "of", WBa.n)
    VE = [Buf(vext.t[:, 0:516], "ve0", "sb"), ve2]
    SOB = [Buf(so.t[:, 0:512], "so0", "sb"), Buf(so.t[:, 512:1024], "so1", "sb")]
    QTB = [Buf(qT.t[:, 0:1024], "qT0", "sb"), Buf(qT.t[:, 1024:2048], "qT1", "sb")]
    KTB = [Buf(kT.t[:, 0:1024], "kT0", "sb"), Buf(kT.t[:, 1024:2048], "kT1", "sb")]
    for b_ in VE[:1] + SOB + QTB + KTB:
        MS[b_.name] = b_
    msb = list(MS.values())

    P = [fw.psum([128, 512], F32, "P%d" % i) for i in range(8)]

    for (b, src) in ((ident, k_ident), (maskc, k_maskc), (maskp, k_maskp), (ones8, k_ones8), (negI8, k_negI8),
                     (sel8, k_sel8), (selc, k_selc), (rm_p, k_rm_p), (rm_s, k_rm_s)):
        fw.dma(b[:, :], src, writes=[b])
    fw.op("vector", lambda e: e.tensor_copy(out=identb[:, :], in_=ident[:, :]), [ident], [identb])
    fw.op("vector", lambda e: e.memset(mhalf[:, :], -0.5), [], [mhalf])

    V = lambda fn, r, w: fw.op("vector", fn, r, w)
    A = lambda fn, r, w: fw.op("scalar", fn, r, w)
    G = lambda fn, r, w: fw.op("gpsimd", fn, r, w)
    T = lambda fn, r, w: fw.op("tensor", fn, r, w)

    def load_w(dst, ncol, src2d, nk, q="gpsimd"):
        view = v3(dst[:, 0:nk * ncol], nk)
        src = src2d.rearrange("(k p) e -> p k e", p=128)
        step = max(1, nk // 8) if nk > 8 else 1
        for k0 in range(0, nk, 2 if nk <= 8 else 4):
            k1 = min(nk, k0 + (2 if nk <= 8 else 4))
            fw.dma(view[:, k0:k1, :], src[:, k0:k1, :], writes=[dst], q=q)
        return view

    class Cx:
        def __init__(self, **kw):
            self.__dict__.update(kw)

    def front_load(cx):
        xb = XB[cx.par]
        col = 0
        for i, R in enumerate(cx.tiles):
            fw.dma(xb[:R, i, :], cx.src[cx.r0 + col:cx.r0 + col + R, :], reads=[cx.reg], writes=[xb])
            col += R

    def front_norm(cx, only=None):
        xb = XB[cx.par]
        for i, R in enumerate(cx.tiles):
            if only is not None and i != only:
                continue
            A(lambda e, i=i, R=R: e.activation(out=xn[:R, i, :], in_=xb[:R, i, :], func=ACT.Square, accum_out=stat[:R, 3 * i:3 * i + 1]), [xb], [xn, stat])
            V(lambda e, i=i, R=R: e.tensor_scalar(out=stat[:R, 3 * i + 1:3 * i + 2], in0=stat[:R, 3 * i:3 * i + 1], scalar1=1.0 / D, scalar2=EPS, op0=ALU.mult, op1=ALU.add), [stat], [stat])
            G(lambda e, i=i, R=R: e.tensor_tensor(out=stat[:R, 3 * i + 2:3 * i + 3], in0=stat[:R, 3 * i + 1:3 * i + 2], in1=mhalf[:R, 0:1], op=ALU.pow), [stat, mhalf], [stat])
            V(lambda e, i=i, R=R: e.scalar_tensor_tensor(out=xn[:R, i, :], in0=xb[:R, i, :], scalar=stat[:R, 3 * i + 2:3 * i + 3], in1=gb[:R, :], op0=ALU.mult, op1=ALU.mult), [xb, stat, gb], [xn])

    def front_T(cx, only=None):
        xsT = XST[cx.par]
        col = 0
        for i, R in enumerate(cx.tiles):
            if only is None or i == only:
                pst = P[7][:, :].bitcast(BF16)
                pst3 = v3(pst, 8)
                for kc in range(8):
                    T(lambda e, kc=kc, R=R, i=i, pst3=pst3: e.transpose(out=pst3[:, kc, :R], in_=xn[:R, i, kc * 128:(kc + 1) * 128], identity=identb[:R, :R]), [xn, identb], [P[7]])
                A(lambda e, R=R, col=col, pst3=pst3: e.activation(out=xsT[:, :, col:col + R], in_=pst3[:, :, :R], func=ACT.Copy), [P[7]], [xsT])
            col += R

    def front_compute(cx):
        front_norm(cx); front_T(cx)

    def proj_fm(cx, ps, M, W, c0, ntok):
        xsT = XST[cx.par]
        for kc in range(8):
            T(lambda e, kc=kc: e.matmul(ps[:M, :ntok], lhsT=W[:, kc, c0:c0 + M], rhs=xsT[:, kc, :ntok], start=(kc == 0), stop=(kc == 7)), [xsT, Wcur[0]], [ps])

    def proj_tm(cx, ps, Tn, col, W, c0, ncol):
        xsT = XST[cx.par]
        for kc in range(8):
            T(lambda e, kc=kc: e.matmul(ps[:Tn, :ncol], lhsT=xsT[:, kc, col:col + Tn], rhs=W[:, kc, c0:c0 + ncol], start=(kc == 0), stop=(kc == 7)), [xsT, Wcur[0]], [ps])

    Wcur = [WA]

    def epilogue_tile(cx, xb, i, R, col):
        if cx.final:
            A(lambda e: e.activation(out=xn[:R, i, :], in_=xb[:R, i, :], func=ACT.Square, accum_out=stat[:R, 56:57]), [xb], [xn, stat])
            A(lambda e: e.activation(out=stat[:R, 57:58], in_=stat[:R, 56:57], func=ACT.Ln, scale=1.0 / D, bias=EPS), [stat], [stat])
            A(lambda e: e.activation(out=stat[:R, 58:59], in_=stat[:R, 57:58], func=ACT.Exp, scale=-0.5), [stat], [stat])
            V(lambda e: e.scalar_tensor_tensor(out=xb[:R, i, :], in0=xb[:R, i, :], scalar=stat[:R, 58:59], in1=gfin[:R, :], op0=ALU.mult, op1=ALU.mult), [xb, stat, gfin], [xb])
            dma_out(cx.fdst[cx.r0 + col:cx.r0 + col + R, :], xb[:R, i, :], xb)
        else:
            fw.dma(cx.dst[cx.r0 + col:cx.r0 + col + R, :], xb[:R, i, :], reads=[xb], writes=[cx.reg])

    def out_proj_store(cx, Wo):
        xb = XB[cx.par]
        col = 0
        for i, R in enumerate(cx.tiles):
            for half in range(2):
                ps = P[half]
                for ec in range(8):
                    T(lambda e, ec=ec, R=R, col=col, half=half, ps=ps: e.matmul(ps[:R, :], lhsT=hsT[:, ec, col:col + R], rhs=Wo[:, ec, half * 512:(half + 1) * 512], start=(ec == 0), stop=(ec == 7)), [hT, WB], [ps])
                V(lambda e, i=i, R=R, half=half, ps=ps: e.tensor_tensor(out=xb[:R, i, half * 512:(half + 1) * 512], in0=ps[:R, :], in1=xb[:R, i, half * 512:(half + 1) * 512], op=ALU.add), [ps, xb], [xb])
            fw.dma(cx.dst[cx.r0 + col:cx.r0 + col + R, :], xb[:R, i, :], reads=[xb], writes=[cx.reg])
            col += R

    def hs_to_hsT(Tn, col):
        for g in range(2):
            ps = P[4 + g]
            ps3 = v3(ps[:, :], 4)
            for e4 in range(4):
                ec = g * 4 + e4
                T(lambda e, ec=ec, e4=e4, ps3=ps3: e.transpose(out=ps3[:, e4, :Tn], in_=hs[:Tn, ec * 128:(ec + 1) * 128], identity=ident[:Tn, :Tn]), [hs, ident], [ps])
            A(lambda e, g=g, ps3=ps3: e.activation(out=hsT[:, g * 4:(g + 1) * 4, col:col + Tn], in_=ps3[:, :, :Tn], func=ACT.Copy), [ps], [hT])

    def head_rmsnorm_gate(Tn, nh, dv, rden_ap, so_ap=None, so_buf=None):
        so_ap = so[:Tn, :] if so_ap is None else so_ap
        so_buf = so if so_buf is None else so_buf
        h3 = v3(hs[:Tn, :], nh)
        for h_ in range(nh):
            A(lambda e, h_=h_: e.activation(out=sqs[:Tn, 0:dv], in_=h3[:, h_, :], func=ACT.Square, accum_out=stat[:Tn, 8 + h_:9 + h_]), [hs], [sqs, stat])
        if rden_ap is not None:
            V(lambda e: e.tensor_tensor(out=stat[:Tn, 16:16 + nh], in0=rden_ap, in1=rden_ap, op=ALU.mult), [stat], [stat])
            V(lambda e: e.tensor_tensor(out=stat[:Tn, 8:8 + nh], in0=stat[:Tn, 8:8 + nh], in1=stat[:Tn, 16:16 + nh], op=ALU.mult), [stat], [stat])
        A(lambda e: e.activation(out=stat[:Tn, 8:8 + nh], in_=stat[:Tn, 8:8 + nh], func=ACT.Ln, scale=1.0 / dv, bias=EPS), [stat], [stat])
        A(lambda e: e.activation(out=stat[:Tn, 8:8 + nh], in_=stat[:Tn, 8:8 + nh], func=ACT.Exp, scale=-0.5), [stat], [stat])
        if rden_ap is not None:
            V(lambda e: e.tensor_tensor(out=stat[:Tn, 8:8 + nh], in0=stat[:Tn, 8:8 + nh], in1=rden_ap, op=ALU.mult), [stat], [stat])
        V(lambda e: e.tensor_tensor(out=h3, in0=h3, in1=stat[:Tn, 8:8 + nh].unsqueeze(2).to_broadcast([Tn, nh, dv]), op=ALU.mult), [hs, stat], [hs])
        V(lambda e: e.tensor_tensor(out=hs[:Tn, :], in0=hs[:Tn, :], in1=so_ap, op=ALU.mult), [hs, so_buf], [hs])

    def flush(lst):
        while lst:
            lst.pop(0)()

    def ensure_items(cx):
        if getattr(cx, 'pend_fm', None) is not None:
            return
        W = v3(WA[:, 0:8 * 3088], 8)
        ntok, Tn = cx.ntok, cx.Tn
        q3 = v3(QTB[cx.par].t[:64, :].bitcast(BF16), 8); k3 = v3(KTB[cx.par].t[:64, :].bitcast(BF16), 8)
        qTc = QTB[cx.par]; kTc = KTB[cx.par]
        fm = []
        for h in range(16):
            def it(h=h):
                ps = P[h % 2]
                proj_fm(cx, ps, 64, W, h * 64, ntok)
                if h < 8:
                    A(lambda e: e.activation(out=q3[:, h, :ntok], in_=ps[:64, :ntok], func=ACT.Copy), [ps], [qTc])
                else:
                    V(lambda e: e.tensor_scalar(out=k3[:, h - 8, :ntok], in0=ps[:64, :ntok], scalar1=0.125, scalar2=None, op0=ALU.mult), [ps], [kTc])
            fm.append(it)
        cx.pend_fm = fm
        cx.pend_tm = []
        for ti in range(cx.ntile):
            tp = (cx.tbase + ti) % 2
            c0 = ti * Tn
            ktm_v = ktm.t[:Tn, tp * 256:(tp + 1) * 256].bitcast(BF16)
            veb = VE[tp]; sob = SOB[tp]
            ve = v3(veb.t[:Tn, 0:516].bitcast(BF16), 8)
            so_v = sob.t[:Tn, :].bitcast(BF16)
            lst = []

            def it_k(c0=c0, ktm_v=ktm_v):
                proj_tm(cx, P[0], Tn, c0, W, 512, 512)
                V(lambda e: e.tensor_scalar(out=ktm_v, in0=P[0][:Tn, :], scalar1=0.125, scalar2=None, op0=ALU.mult), [P[0]], [ktm])
            lst.append(it_k)
            for hf in range(2):
                def it_v(hf=hf, c0=c0, ve=ve, veb=veb):
                    if hf == 0:
                        G(lambda e: e.memset(ve[:, :, 128:129], 1.0), [], [veb])
                    proj_tm(cx, P[1], Tn, c0, W, 1024 + hf * 512, 512)
                    A(lambda e: e.activation(out=ve[:, hf * 4:(hf + 1) * 4, 0:128], in_=v3(P[1][:Tn, :], 4), func=ACT.Copy), [P[1]], [veb])
                lst.append(it_v)
            for hf in range(2):
                def it_o(hf=hf, c0=c0, so_v=so_v, sob=sob):
                    proj_tm(cx, P[hf], Tn, c0, W, 2048 + hf * 512, 512)
                    A(lambda e: e.activation(out=so_v[:, hf * 512:(hf + 1) * 512], in_=P[hf][:Tn, :], func=ACT.Sigmoid), [P[hf]], [sob])
                    G(lambda e: e.tensor_tensor(out=so_v[:, hf * 512:(hf + 1) * 512], in0=so_v[:, hf * 512:(hf + 1) * 512], in1=gnb[:Tn, hf * 512:(hf + 1) * 512], op=ALU.mult), [sob, gnb], [sob])
                lst.append(it_o)
            cx.pend_tm.append(lst)

    def mlstm_block(cx):
        j, r0, tiles, reg, ntok, Tn, ntile, sample = cx.j, cx.r0, cx.tiles, cx.reg, cx.ntok, cx.Tn, cx.ntile, cx.sample
        W = v3(WA[:, 0:8 * 3088], 8)
        Wo = v3(WB[:, 0:8 * 1024], 8)
        if CFG['stop'] <= 1: return
        ensure_items(cx)
        flush(cx.pend_fm)
        q3 = v3(QTB[cx.par].t[:64, :].bitcast(BF16), 8); k3 = v3(KTB[cx.par].t[:64, :].bitcast(BF16), 8)
        qTc = QTB[cx.par]; kTc = KTB[cx.par]
        if CFG['stop'] <= 2: return
        GW = BLK + NSEQ
        igc = grow[:8, 0:ntok]; lf = grow[:8, GW:GW + ntok]; Fc = grow[:8, 2 * GW:2 * GW + ntok]; Mt = grow[:8, 3 * GW:3 * GW + ntok]
        nseg = ntile if sample else 1
        seglen = ntok // nseg
        mext = v3(grow[:8, 4 * GW:4 * GW + nseg * (seglen + 1)], nseg)
        proj_fm(cx, P[0], 8, W, 3072, ntok)
        proj_fm(cx, P[1], 8, W, 3080, ntok)
        A(lambda e: e.activation(out=igc, in_=P[0][:8, :ntok], func=ACT.Tanh, scale=1.0 / 15, bias=sp[:8, 0:1]), [P[0], sp], [grow])
        A(lambda e: e.activation(out=lf, in_=P[1][:8, :ntok], func=ACT.Tanh, scale=1.0 / 15, bias=sp[:8, 1:2]), [P[1], sp], [grow])
        V(lambda e: e.tensor_scalar(out=igc, in0=igc, scalar1=15.0, scalar2=None, op0=ALU.mult), [grow], [grow])
        xg = Fc; ug = Mt
        V(lambda e: e.tensor_scalar(out=xg, in0=lf, scalar1=15.0, scalar2=None, op0=ALU.mult), [grow], [grow])
        V(lambda e: e.scalar_tensor_tensor(out=ug, in0=xg, scalar=-1.0, in1=xg, op0=ALU.mult, op1=ALU.max), [grow], [grow])
        A(lambda e: e.activation(out=ug, in_=ug, func=ACT.Exp, scale=-1.0), [grow], [grow])
        V(lambda e: e.tensor_scalar(out=lf, in0=ug, scalar1=2.0, scalar2=None, op0=ALU.add), [grow], [grow])
        V(lambda e: e.reciprocal(out=lf, in_=lf), [grow], [grow])
        V(lambda e: e.tensor_tensor(out=ug, in0=ug, in1=lf, op=ALU.mult), [grow], [grow])
        V(lambda e: e.tensor_tensor(out=lf, in0=ug, in1=ug, op=ALU.mult), [grow], [grow])
        zp = trow[:8, 0:ntok]
        V(lambda e: e.tensor_scalar(out=zp, in0=lf, scalar1=1.0 / 9, scalar2=None, op0=ALU.mult), [grow], [trow])
        for cc in (1.0 / 7, 1.0 / 5, 1.0 / 3):
            V(lambda e, cc=cc: e.scalar_tensor_tensor(out=zp, in0=zp, scalar=cc, in1=lf, op0=ALU.add, op1=ALU.mult), [trow, grow], [trow])
        V(lambda e: e.scalar_tensor_tensor(out=zp, in0=zp, scalar=1.0, in1=ug, op0=ALU.add, op1=ALU.mult), [trow, grow], [trow])
        V(lambda e: e.tensor_scalar(out=xg, in0=xg, scalar1=0.0, scalar2=None, op0=ALU.min), [grow], [grow])
        V(lambda e: e.scalar_tensor_tensor(out=lf, in0=zp, scalar=-2.0, in1=xg, op0=ALU.mult, op1=ALU.add), [trow, grow], [grow])
        if sample:
            fw.dma(mext[:, :, 0:1], si_m[j].unsqueeze(2), writes=[grow])
        for s in range(nseg):
            V(lambda e, s=s: e.tensor_tensor_scan(out=mext[:, s, 1:1 + seglen], data0=lf[:, s * seglen:(s + 1) * seglen], data1=igc[:, s * seglen:(s + 1) * seglen], initial=mext[:, s, 0:1], op0=ALU.add, op1=ALU.max), [grow], [grow])
        rm = rm_s if sample else rm_p
        V(lambda e: e.tensor_tensor_scan(out=Fc, data0=rm[:8, :ntok], data1=lf, initial=0.0, op0=ALU.mult, op1=ALU.add), [grow, rm], [grow])
        V(lambda e: e.tensor_tensor(out=igc, in0=igc, in1=Fc, op=ALU.subtract), [grow], [grow])
        for s in range(nseg):
            V(lambda e, s=s: e.tensor_tensor(out=Mt[:, s * seglen:(s + 1) * seglen], in0=mext[:, s, 1:1 + seglen], in1=Fc[:, s * seglen:(s + 1) * seglen], op=ALU.subtract), [grow], [grow])
        a_r = igc
        cx.mid1()
        if CFG['stop'] <= 3: return
        def tile(ti):
            c0 = ti * Tn
            if sample:
                st = Cst[1 + ti % 2]
                fw.dma(st[:, :, 0:128], v3(si_c[j, ti], 8), writes=[st])
                fw.dma(st[:, :, 128:129], si_n[j, ti].unsqueeze(2), writes=[st])
                car = mext[:, ti, 0:1]; mt = mext[:, ti, 1:1 + Tn]
            else:
                st = Cst[0]
                car = mext[:, 0, c0:c0 + 1]; mt = mext[:, 0, 1 + c0:1 + c0 + Tn]
            flush(cx.pend_tm[ti])
            tp = (cx.tbase + ti) % 2
            ktm_v = ktm.t[:Tn, tp * 256:(tp + 1) * 256].bitcast(BF16)
            veb = VE[tp]; sob = SOB[tp]
            ve = v3(veb.t[:Tn, 0:516].bitcast(BF16), 8)
            so_v = sob.t[:Tn, :].bitcast(BF16)
            stb = v3(stbuf.t[:64, 0:516].bitcast(BF16), 8)
            A(lambda e: e.activation(out=stb, in_=st[:, :, :], func=ACT.Copy), [st], [stbuf])
            if ti + 1 < ntile:
                srcs = [cx.pend_tm[ti + 1]]
            elif cx.next is not None:
                cx.mid()
                ensure_items(cx.next)
                srcs = [cx.next.pend_fm, cx.next.pend_tm[0]]
            else:
                srcs = []
            npts = [7]

            def fillpt():
                rem = sum(len(l_) for l_ in srcs)
                k = 1 if rem > 0 else 0
                for l_ in srcs:
                    while k > 0 and l_:
                        l_.pop(0)()
                        k -= 1
            if CFG['stop'] <= 4: return
            g_r = trow[:8, 0:Tn]; enm_r = trow[:8, 128:128 + Tn]; wl_r = trow[:8, 256:256 + Tn]; nml = trow[:8, 384:385]
            V(lambda e: e.tensor_scalar(out=nml, in0=Mt[:, c0 + Tn - 1:c0 + Tn], scalar1=-1.0, scalar2=None, op0=ALU.mult), [grow], [trow])
            A(lambda e: e.activation(out=g_r, in_=Mt[:, c0:c0 + Tn], func=ACT.Exp, scale=-1.0, bias=car), [grow], [trow])
            A(lambda e: e.activation(out=enm_r, in_=mt, func=ACT.Exp, scale=-1.0), [grow], [trow])
            A(lambda e: e.activation(out=wl_r, in_=a_r[:, c0:c0 + Tn], func=ACT.Exp, scale=1.0, bias=nml), [grow, trow], [trow])
            px = P[7]
            for qi, row in enumerate((a_r[:, c0:c0 + Tn], g_r, enm_r, wl_r)):
                T(lambda e, qi=qi, row=row: e.transpose(out=px[:Tn, qi * 8:(qi + 1) * 8], in_=row, identity=ident[:8, :8]), [grow, trow, ident], [px])
            V(lambda e: e.tensor_copy(out=cols[:Tn, 0:32], in_=px[:Tn, 0:32]), [px], [cols])
            a_c = cols[:Tn, 0:8]; g_c = cols[:Tn, 8:16]; enm_c = cols[:Tn, 16:24]; wl_c = cols[:Tn, 24:32]
            rb3 = v3(rbd[:8, 0:8 * Tn], 8)
            V(lambda e: e.tensor_tensor(out=rb3, in0=Mt[:, c0:c0 + Tn].unsqueeze(1).to_broadcast([8, 8, Tn]), in1=negI8[:, :].unsqueeze(2).to_broadcast([8, 8, Tn]), op=ALU.mult), [grow, negI8], [rbd])
            V(lambda e: e.tensor_scalar(out=trow[:8, 388:396], in0=negI8[:, :], scalar1=g_r[:, Tn - 1:Tn], scalar2=-1.0, op0=ALU.mult, op1=ALU.mult), [negI8, trow], [trow])
            T(lambda e: e.matmul(px[:64, 40:48], lhsT=ones8[:, 0:64], rhs=trow[:8, 388:396], start=True, stop=True), [ones8, trow], [px])
            V(lambda e: e.tensor_copy(out=glb[:64, 0:8], in_=px[:64, 40:48]), [px], [glb])
            if CFG['stop'] <= 5: return
            kw3 = v3(kw.t[:Tn, 0:256].bitcast(BF16), 8)
            V(lambda e: e.tensor_tensor(out=kw3, in0=v3(ktm_v, 8), in1=wl_c.unsqueeze(2).to_broadcast([Tn, 8, 64]), op=ALU.mult), [ktm, cols], [kw])
            fillpt()
            pD = P[7]
            def bufs(hh):
                if hh == 0:
                    return P[2], P[3], Wt
                return P[6], P[5], Wt2

            def ptbuf(hh):
                if hh == 0:
                    return kw, v3(kw.t[:Tn, 256:512].bitcast(BF16)[:, 0:4 * Tn], 4)
                return sqs, v3(sqs.t[:Tn, 0:256].bitcast(BF16)[:, 0:4 * Tn], 4)

            def halfA(hh):
                psB, psS, Wtb = bufs(hh)
                T(lambda e: e.matmul(psB[:Tn, 0:4 * Tn], lhsT=ones8[:, :Tn], rhs=rbd[:8, hh * 4 * Tn:(hh + 1) * 4 * Tn], start=True, stop=True), [ones8, rbd], [psB])
                for h4 in range(4):
                    h = hh * 4 + h4
                    T(lambda e, h4=h4, h=h: e.matmul(psS[:Tn, h4 * Tn:(h4 + 1) * Tn], lhsT=k3[:, h, c0:c0 + Tn], rhs=q3[:, h, c0:c0 + Tn], start=True, stop=True), [kTc, qTc], [psS])
                W3 = v3(Wtb[:Tn, 0:4 * Tn], 4)
                for h4 in range(4):
                    h = hh * 4 + h4
                    A(lambda e, h=h, h4=h4: e.activation(out=W3[:, h4, :], in_=psB[:Tn, h4 * Tn:(h4 + 1) * Tn], func=ACT.Exp, bias=a_c[:, h:h + 1], scale=1.0), [psB, cols], [Wtb])
                V(lambda e: e.tensor_tensor(out=W3, in0=W3, in1=maskc[:Tn, :Tn].unsqueeze(1).to_broadcast([Tn, 4, Tn]), op=ALU.mult), [Wtb, maskc], [Wtb])
                ptB, PT3 = ptbuf(hh)
                V(lambda e: e.tensor_tensor(out=PT3, in0=v3(psS[:Tn, 0:4 * Tn], 4), in1=W3, op=ALU.mult), [psS, Wtb], [ptB])

            def halfB(hh):
                psN = P[4]; psI = P[5]
                Wtb, W3 = ptbuf(hh)
                for h4 in range(4):
                    h = hh * 4 + h4
                    T(lambda e, h=h, h4=h4: e.matmul(psN[:Tn, h4 * 128:(h4 + 1) * 128], lhsT=W3[:, h4, :], rhs=ve[:, h, 0:128], start=True, stop=True), [Wtb, veb], [psN])
                    T(lambda e, h=h, h4=h4: e.matmul(pD[:Tn, 64 + h:65 + h], lhsT=W3[:, h4, :], rhs=ve[:, h, 128:129], start=True, stop=True), [Wtb, veb], [pD])
                    T(lambda e, h4=h4, h=h: e.matmul(psI[:Tn, h4 * 128:(h4 + 1) * 128], lhsT=q3[:, h, c0:c0 + Tn], rhs=stb[:, h, 0:128], start=True, stop=True), [qTc, stbuf], [psI])
                    T(lambda e, h=h: e.matmul(pD[:Tn, 80 + h:81 + h], lhsT=q3[:, h, c0:c0 + Tn], rhs=stb[:, h, 128:129], start=True, stop=True), [qTc, stbuf], [pD])
                A(lambda e: e.activation(out=numS[:Tn, :], in_=psN[:Tn, :], func=ACT.Copy), [psN], [numS])
                hsl = v3(hs[:Tn, hh * 512:(hh + 1) * 512], 4)
                V(lambda e: e.tensor_tensor(out=hsl, in0=v3(psI[:Tn, :], 4), in1=g_c[:, hh * 4:(hh + 1) * 4].unsqueeze(2).to_broadcast([Tn, 4, 128]), op=ALU.mult), [psI, cols], [hs])
                V(lambda e: e.tensor_tensor(out=hs[:Tn, hh * 512:(hh + 1) * 512], in0=hs[:Tn, hh * 512:(hh + 1) * 512], in1=numS[:Tn, :], op=ALU.add), [hs, numS], [hs])

            halfA(0); fillpt(); halfA(1); fillpt(); halfB(0); fillpt(); halfB(1); fillpt()
            if CFG['stop'] <= 6: return
            V(lambda e: e.tensor_copy(out=stat[:Tn, 24:32], in_=pD[:Tn, 64:72]), [pD], [stat])
            V(lambda e: e.tensor_tensor(out=stat[:Tn, 32:40], in0=pD[:Tn, 80:88], in1=g_c, op=ALU.mult), [pD, cols], [stat])
            V(lambda e: e.tensor_tensor(out=stat[:Tn, 24:32], in0=stat[:Tn, 24:32], in1=stat[:Tn, 32:40], op=ALU.add), [stat], [stat])
            V(lambda e: e.tensor_scalar(out=stat[:Tn, 32:40], in0=stat[:Tn, 24:32], scalar1=-1.0, scalar2=None, op0=ALU.mult), [stat], [stat])
            V(lambda e: e.tensor_tensor(out=stat[:Tn, 24:32], in0=stat[:Tn, 24:32], in1=stat[:Tn, 32:40], op=ALU.max), [stat], [stat])
            V(lambda e: e.tensor_tensor(out=stat[:Tn, 24:32], in0=stat[:Tn, 24:32], in1=enm_c, op=ALU.max), [stat, cols], [stat])
            V(lambda e: e.reciprocal(out=stat[:Tn, 24:32], in_=stat[:Tn, 24:32]), [stat], [stat])
            head_rmsnorm_gate(Tn, 8, 128, stat[:Tn, 24:32], so_v, sob)
            fillpt()
            hs_to_hsT(Tn, c0)
            if CFG['stop'] <= 7: return
            pUn = P[7]
            for h in range(8):
                psU = P[2 + h // 4]
                T(lambda e, h=h, psU=psU: e.matmul(psU[:64, (h % 4) * 128:(h % 4 + 1) * 128], lhsT=kw3[:, h, :], rhs=ve[:, h, 0:128], start=True, stop=True), [kw, veb], [psU])
                T(lambda e, h=h: e.matmul(pUn[:64, 48 + h:49 + h], lhsT=kw3[:, h, :], rhs=ve[:, h, 128:129], start=True, stop=True), [kw, veb], [pUn])
            fillpt()
            for l_ in srcs:
                flush(l_)
            for h in range(8):
                psU = P[2 + h // 4]
                V(lambda e, h=h, psU=psU: e.scalar_tensor_tensor(out=st[:, h, 0:128], in0=st[:, h, 0:128], scalar=glb[:64, h:h + 1], in1=psU[:64, (h % 4) * 128:(h % 4 + 1) * 128], op0=ALU.mult, op1=ALU.add), [st, glb, psU], [st])
            V(lambda e: e.tensor_tensor(out=st[:, :, 128:129], in0=st[:, :, 128:129], in1=glb[:64, 0:8].unsqueeze(2), op=ALU.mult), [st, glb], [st])
            V(lambda e: e.tensor_tensor(out=st[:, :, 128:129], in0=st[:, :, 128:129], in1=pUn[:64, 48:56].unsqueeze(2), op=ALU.add), [st, pUn], [st])
            if sample:
                dma_out(v3(so_c[j, ti], 8), st[:, :, 0:128], st)
                dma_out(so_n[j, ti].unsqueeze(2), st[:, :, 128:129], st)
        for ti in range(ntile):
            tile(ti)
        if sample:
            dma_out(so_m[j].unsqueeze(2), mext[:, :, Tn:Tn + 1], grow)
        else:
            V(lambda e: e.tensor_copy(out=mext[:, 0, 0:1], in_=mext[:, 0, ntok:ntok + 1]), [grow], [grow])
            if cx.last_block:
                dma_out(po_m[j], mext[:, 0, 0:1], grow)
        out_proj_store(cx, Wo)

    def gla_block(cx):
        j, r0, tiles, reg, ntok, Tn, ntile, sample = cx.j, cx.r0, cx.tiles, cx.reg, cx.ntok, cx.Tn, cx.ntile, cx.sample
        W = v3(WA[:, 0:8 * 3088], 8)
        Wo = v3(WB[:, 0:8 * 1024], 8)
        q3 = v3(qT.t[:, 0:2 * BLK].bitcast(BF16), 4); k3 = v3(kT.t[:, 0:2 * BLK].bitcast(BF16), 4); kl3 = v3(qT[:, 4 * BLK:8 * BLK], 4); lg3 = v3(lgT[:, :], 4)
        for c in range(8):
            ps = P[c % 2]
            proj_fm(cx, ps, 128, W, c * 128, ntok)
            if c < 4:
                V(lambda e, c=c, ps=ps: e.tensor_scalar(out=q3[:, c, :ntok], in0=ps[:, :ntok], scalar1=128.0 ** -0.5, scalar2=None, op0=ALU.mult), [ps], [qT])
            else:
                A(lambda e, c=c, ps=ps: e.activation(out=k3[:, c - 4, :ntok], in_=ps[:, :ntok], func=ACT.Copy), [ps], [kT])
        proj_fm(cx, P[0], 16, W, 3072, ntok)
        zT = grow[:16, 0:ntok]
        V(lambda e: e.tensor_copy(out=zT, in_=P[0][:16, :ntok]), [P[0]], [grow])
        rm = rm_s if sample else rm_p
        for h in range(4):
            ps = P[h % 2]
            T(lambda e, h=h, ps=ps: e.matmul(ps[:, :ntok], lhsT=sp[:16, 16 + h * 128:16 + (h + 1) * 128], rhs=zT, start=True, stop=True), [sp, grow], [ps])
            A(lambda e, h=h, ps=ps: e.activation(out=lg3[:, h, :ntok], in_=ps[:, :ntok], func=ACT.Exp, scale=-1.0, bias=sp[:, 8 + h:9 + h]), [ps, sp], [lgT])
            A(lambda e, h=h: e.activation(out=lg3[:, h, :ntok], in_=lg3[:, h, :ntok], func=ACT.Ln, bias=1.0, scale=1.0), [lgT], [lgT])
            V(lambda e, h=h: e.tensor_scalar(out=lg3[:, h, :ntok], in0=lg3[:, h, :ntok], scalar1=-1.0 / 16, scalar2=None, op0=ALU.mult), [lgT], [lgT])
            V(lambda e, h=h: e.tensor_copy(out=hs[:, h * BLK:h * BLK + ntok], in_=lg3[:, h, :ntok]), [lgT], [hs])
            V(lambda e, h=h: e.tensor_tensor_scan(out=lg3[:, h, :ntok], data0=rm[:, :ntok], data1=hs[:, h * BLK:h * BLK + ntok], initial=0.0, op0=ALU.mult, op1=ALU.add), [hs, rm], [lgT])
        def tile(ti):
            c0 = ti * Tn
            if sample:
                st = Sst[1 + ti % 2]
                fw.dma(st[:, :, :], v3(si_g[j, ti], 4), writes=[st])
            else:
                st = Sst[0]
            bl = cols[:, 32:36]; ebl = cols[:, 36:40]
            V(lambda e: e.tensor_copy(out=bl.unsqueeze(2), in_=lg3[:, :, c0 + Tn - 1:c0 + Tn]), [lgT], [cols])
            A(lambda e: e.activation(out=ebl, in_=bl, func=ACT.Exp), [cols], [cols])
            for h in range(4):
                A(lambda e, h=h: e.activation(out=kl3[:, h, c0:c0 + Tn], in_=lg3[:, h, c0:c0 + Tn], func=ACT.Exp, scale=-1.0, bias=bl[:, h:h + 1]), [lgT, cols], [qT])
            G(lambda e: e.tensor_tensor(out=kl3[:, :, c0:c0 + Tn], in0=kl3[:, :, c0:c0 + Tn], in1=k3[:, :, c0:c0 + Tn], op=ALU.mult), [qT, kT], [qT])
            pk = P[6]
            for h in range(4):
                T(lambda e, h=h: e.transpose(out=pk[:Tn, h * 128:(h + 1) * 128], in_=kl3[:, h, c0:c0 + Tn], identity=ident[:, :]), [qT, ident], [pk])
            A(lambda e: e.activation(out=kw.t[:Tn, 0:256].bitcast(BF16), in_=pk[:Tn, :], func=ACT.Copy), [pk], [kw])
            kl_tm = v3(kw.t[:Tn, 0:256].bitcast(BF16), 4)
            A(lambda e: e.activation(out=v3(Wt[:, 0:4 * Tn], 4), in_=lg3[:, :, c0:c0 + Tn], func=ACT.Exp), [lgT], [Wt])
            V(lambda e: e.tensor_tensor(out=q3[:, :, c0:c0 + Tn], in0=q3[:, :, c0:c0 + Tn], in1=v3(Wt[:, 0:4 * Tn], 4), op=ALU.mult), [qT, Wt], [qT])
            A(lambda e: e.activation(out=v3(Wt[:, 0:4 * Tn], 4), in_=lg3[:, :, c0:c0 + Tn], func=ACT.Exp, scale=-1.0), [lgT], [Wt])
            V(lambda e: e.tensor_tensor(out=k3[:, :, c0:c0 + Tn], in0=k3[:, :, c0:c0 + Tn], in1=v3(Wt[:, 0:4 * Tn], 4), op=ALU.mult), [kT, Wt], [kT])
            vt = v3(vext.t[:Tn, 0:512].bitcast(BF16), 4)
            vtf = vext.t[:Tn, 0:512].bitcast(BF16)
            stb = v3(vext.t[:, 520:1032].bitcast(BF16), 4)
            G(lambda e: e.tensor_copy(out=stb, in_=st[:, :, :]), [st], [vext])
            for hf in range(2):
                proj_tm(cx, P[hf], Tn, c0, W, 1024 + hf * 512, 512)
                A(lambda e, hf=hf: e.activation(out=vtf[:, hf * 512:(hf + 1) * 512], in_=P[hf][:Tn, :], func=ACT.Copy), [P[hf]], [vext])
            for hf in range(2):
                proj_tm(cx, P[hf], Tn, c0, W, 2048 + hf * 512, 512)
                A(lambda e, hf=hf: e.activation(out=so[:Tn, hf * 512:(hf + 1) * 512], in_=P[hf][:Tn, :], func=ACT.Silu), [P[hf]], [so])
            G(lambda e: e.tensor_tensor(out=so[:Tn, :], in0=so[:Tn, :], in1=gnb[:Tn, :], op=ALU.mult), [so, gnb], [so])
            psS = P[3]
            for h in range(4):
                T(lambda e, h=h: e.matmul(psS[:Tn, h * Tn:(h + 1) * Tn], lhsT=k3[:, h, c0:c0 + Tn], rhs=q3[:, h, c0:c0 + Tn], start=True, stop=True), [kT, qT], [psS])
            PT3 = v3(numS.t[:Tn, 0:256].bitcast(BF16)[:, 0:4 * Tn], 4)
            V(lambda e: e.tensor_tensor(out=PT3, in0=v3(psS[:Tn, 0:4 * Tn], 4), in1=maskc[:Tn, :Tn].unsqueeze(1).to_broadcast([Tn, 4, Tn]), op=ALU.mult), [psS, maskc], [numS])
            for h in range(4):
                ps = P[4 + h // 2]
                o0 = (h % 2) * 256
                T(lambda e, h=h, ps=ps, o0=o0: e.matmul(ps[:Tn, o0:o0 + 256], lhsT=PT3[:, h, :], rhs=vt[:, h, :], start=True, stop=False), [numS, vext], [ps])
                T(lambda e, h=h, ps=ps, o0=o0: e.matmul(ps[:Tn, o0:o0 + 256], lhsT=q3[:, h, c0:c0 + Tn], rhs=stb[:, h, :], start=False, stop=True), [qT, vext], [ps])
            A(lambda e: e.activation(out=hs[:Tn, 0:512], in_=P[4][:Tn, :], func=ACT.Copy), [P[4]], [hs])
            V(lambda e: e.tensor_copy(out=hs[:Tn, 512:1024], in_=P[5][:Tn, :]), [P[5]], [hs])
            for h in range(4):
                ps = P[2]
                T(lambda e, h=h, ps=ps: e.matmul(ps[:, 0:256], lhsT=kl_tm[:, h, :], rhs=vt[:, h, :], start=True, stop=True), [kw, vext], [ps])
                V(lambda e, h=h, ps=ps: e.scalar_tensor_tensor(out=st[:, h, :], in0=st[:, h, :], scalar=ebl[:, h:h + 1], in1=ps[:, 0:256], op0=ALU.mult, op1=ALU.add), [st, cols, ps], [st])
            if sample:
                dma_out(v3(so_g[j, ti], 4), st[:, :, :], st)
            head_rmsnorm_gate(Tn, 4, 256, None)
            hs_to_hsT(Tn, c0)
        for ti in range(ntile):
            tile(ti)
            if ti == 0:
                cx.mid1()
        cx.mid()
        out_proj_store(cx, Wo)

    def swa_block(cx):
        j, r0, tiles, reg, ntok, Tn, ntile, sample, last_block = cx.j, cx.r0, cx.tiles, cx.reg, cx.ntok, cx.Tn, cx.ntile, cx.sample, cx.last_block
        W = v3(WA[:, 0:8 * 1536], 8)
        Wo = v3(WB[:, 0:8 * 1024], 8)
        q8 = v3(qT.t[:64, 0:1024].bitcast(BF16), 16); k2 = v3(kT.t[:64, 0:256].bitcast(BF16), 4)
        for h in range(20):
            ps = P[h % 2]
            proj_fm(cx, ps, 64, W, h * 64, ntok)
            if h < 16:
                A(lambda e, h=h, ps=ps: e.activation(out=q8[:, h, :ntok], in_=ps[:64, :ntok], func=ACT.Identity, bias=sp[:64, 528 + h:529 + h], scale=1.0), [ps, sp], [qT])
            else:
                A(lambda e, h=h, ps=ps: e.activation(out=k2[:, h - 16, :ntok], in_=ps[:64, :ntok], func=ACT.Identity, bias=sp[:64, 544 + h - 16:545 + h - 16], scale=1.0), [ps, sp], [kT])
        def tile(ti):
            c0 = ti * Tn
            proj_tm(cx, P[0], Tn, c0, W, 1024, 512)
            V(lambda e: e.tensor_tensor(out=ktm[:Tn, :], in0=P[0][:Tn, :], in1=gnb[:Tn, 0:512], op=ALU.add), [P[0], gnb], [ktm])
            ve = v3(vext.t[:Tn, 0:130].bitcast(BF16), 4)
            if sample:
                V(lambda e: e.memset(vext[:, 0:4 * 65], 0.0), [], [vext])
                V(lambda e: e.memset(numS[:, 0:256], 0.0), [], [numS])
            G(lambda e: e.memset(ve[:, :, 64:65], 1.0), [], [vext])
            G(lambda e: e.tensor_copy(out=ve[:, :, 0:64], in_=v3(ktm[:Tn, 256:512], 4)), [ktm], [vext])
            vefull = v3(vext.t[:, 0:130].bitcast(BF16), 4)
            if sample:
                kp = kTprev[1 + ti % 2]; vp = vprev[1 + ti % 2]; raw = kvraw[ti % 2]
                fw.dma(raw[:, 0:256], si_k[j, ti], writes=[raw])
                fw.dma(raw[:, 256:512], si_v[j, ti], writes=[raw])
                pk = P[6]
                for c in range(4):
                    T(lambda e, c=c: e.transpose(out=pk[:64, c * 128:(c + 1) * 128], in_=raw[:, c * 64:(c + 1) * 64], identity=ident[:, :]), [raw, ident], [pk])
                A(lambda e: e.activation(out=kp[:, :, :], in_=v3(pk[:64, 0:512], 4), func=ACT.Copy), [pk], [kp])
                G(lambda e: e.memset(vp[:, :, 64:65], 1.0), [], [vp])
                G(lambda e: e.tensor_copy(out=vp[:, :, 0:64], in_=v3(raw[:, 256:512], 4)), [raw], [vp])
                has_prev = True
                dma_out(so_k[j, ti, 0:124, :], raw[4:128, 0:256], raw)
                dma_out(so_v[j, ti, 0:124, :], raw[4:128, 256:512], raw)
                dma_out(so_k[j, ti, 124:128, :], ktm[:Tn, 0:256], ktm)
                dma_out(so_v[j, ti, 124:128, :], ktm[:Tn, 256:512], ktm)
            else:
                kp = kTprev[0]; vp = vprev[0]
                has_prev = not (r0 == 0 and ti == 0)
                if last_block and ti == ntile - 1:
                    dma_out(po_k[j], ktm[:Tn, 0:256], ktm)
                    dma_out(po_v[j], ktm[:Tn, 256:512], ktm)
            blocks = ([(kp, vp, 128, maskp)] if has_prev else []) + [(None, None, Tn, maskc)]
            pD = P[7]
            def kvhead(kh):
                PTs = []
                for bi, (kpb, vpb, nk, msk) in enumerate(blocks):
                    psS = P[2 + bi]
                    for g in range(4):
                        if kpb is None:
                            T(lambda e, g=g, psS=psS, kh=kh, nk=nk: e.matmul(psS[:nk, g * Tn:(g + 1) * Tn], lhsT=k2[:, kh, c0:c0 + nk], rhs=q8[:, kh * 4 + g, c0:c0 + Tn], start=True, stop=True), [kT, qT], [psS])
                        else:
                            T(lambda e, g=g, psS=psS, kpb=kpb, kh=kh, nk=nk: e.matmul(psS[:nk, g * Tn:(g + 1) * Tn], lhsT=kpb[:, kh, 0:nk], rhs=q8[:, kh * 4 + g, c0:c0 + Tn], start=True, stop=True), [kpb, qT], [psS])
                    PTb = Wt if bi == 0 else numS
                    PTv = PTb.t[:, 0:256].bitcast(BF16)
                    A(lambda e, psS=psS, PTb=PTb, PTv=PTv, nk=nk: e.activation(out=PTv[:nk, 0:4 * Tn], in_=psS[:nk, 0:4 * Tn], func=ACT.Exp, scale=0.125), [psS], [PTb])
                    V(lambda e, PTb=PTb, PTv=PTv, nk=nk, msk=msk: e.tensor_tensor(out=v3(PTv[:nk, 0:4 * Tn], 4), in0=v3(PTv[:nk, 0:4 * Tn], 4), in1=msk[:nk, :Tn].unsqueeze(1).to_broadcast([nk, 4, Tn]), op=ALU.mult), [PTb, msk], [PTb])
                    PTs.append((PTb, PTv, (128 if sample else nk), (vefull if sample else ve) if kpb is None else vpb, vext if kpb is None else vpb))
                for g in range(4):
                    hq = kh * 4 + g
                    psO = P[4 + hq // 8]
                    o0 = (hq % 8) * 64
                    for bi, (PTb, PTv, nk, vv, vbuf) in enumerate(PTs):
                        T(lambda e, g=g, PTb=PTb, PTv=PTv, nk=nk, vv=vv, psO=psO, o0=o0, bi=bi, kh=kh: e.matmul(psO[:Tn, o0:o0 + 64], lhsT=PTv[:nk, g * Tn:(g + 1) * Tn], rhs=vv[:nk, kh, 0:64], start=(bi == 0), stop=(bi == len(PTs) - 1)), [PTb, vbuf], [psO])
                    for bi, (PTb, PTv, nk, vv, vbuf) in enumerate(PTs):
                        T(lambda e, g=g, PTb=PTb, PTv=PTv, nk=nk, vv=vv, hq=hq, bi=bi, kh=kh: e.matmul(pD[:Tn, 96 + hq:97 + hq], lhsT=PTv[:nk, g * Tn:(g + 1) * Tn], rhs=vv[:nk, kh, 64:65], start=(bi == 0), stop=(bi == len(PTs) - 1)), [PTb, vbuf], [pD])
            for kh in range(4):
                kvhead(kh)
            V(lambda e: e.tensor_tensor(out=stat[:Tn, 40:56], in0=pD[:Tn, 96:112], in1=sp[:Tn, 552:568], op=ALU.add), [pD, sp], [stat])
            V(lambda e: e.reciprocal(out=stat[:Tn, 40:56], in_=stat[:Tn, 40:56]), [stat], [stat])
            for hf in range(2):
                V(lambda e, hf=hf: e.tensor_tensor(out=v3(hs[:Tn, hf * 512:(hf + 1) * 512], 8), in0=v3(P[4 + hf][:Tn, :], 8), in1=stat[:Tn, 40 + hf * 8:48 + hf * 8].unsqueeze(2).to_broadcast([Tn, 8, 64]), op=ALU.mult), [P[4 + hf], stat], [hs])
            hs_to_hsT(Tn, c0)
            if not sample:
                G(lambda e: e.tensor_copy(out=kTprev[0][:, :, :], in_=k2[:, :, c0:c0 + Tn]), [kT], [kTprev[0]])
                G(lambda e: e.tensor_copy(out=vprev[0][:, :, :], in_=ve), [vext], [vprev[0]])
        for ti in range(ntile):
            tile(ti)
            if ti == 0:
                cx.mid1()
        cx.mid()
        out_proj_store(cx, Wo)

    def ffn_block(cx):
        layer, r0, tiles, reg, ntok, nseq, halo = cx.layer, cx.r0, cx.tiles, cx.reg, cx.ntok, cx.nseq, cx.halo
        xb = XB[cx.par]; xsT = XST[cx.par]
        Wg = v3(WA[:, 0:8 * DFF], 8); Wu = v3(WB[:, 0:8 * DFF], 8); Wd = v3(WC[:, 0:NFC * D], NFC)
        Tq = ntok // nseq
        h4 = v4(halo[:, :], NFC, nseq)
        def stageA(c):
            psG = P[2 + (c % 2) * 2]; psU = P[3 + (c % 2) * 2]
            for kc in range(8):
                T(lambda e, kc=kc: e.matmul(psG[:, :ntok], lhsT=Wg[:, kc, c * 128:(c + 1) * 128], rhs=xsT[:, kc, :ntok], start=(kc == 0), stop=(kc == 7)), [xsT, WA], [psG])
            for kc in range(8):
                T(lambda e, kc=kc: e.matmul(psU[:, :ntok], lhsT=Wu[:, kc, c * 128:(c + 1) * 128], rhs=xsT[:, kc, :ntok], start=(kc == 0), stop=(kc == 7)), [xsT, WB], [psU])
            ge = gext[c % 2]; ca = cacc[c % 2]
            ge3 = v3(ge[:, 0:nseq * (Tq + 2)], nseq)
            ca3 = v3(ca[:, 0:ntok], nseq)
            G(lambda e: e.tensor_copy(out=ge3[:, :, 0:2], in_=h4[:, c, :, :]), [halo], [ge])
            A(lambda e: e.activation(out=ge3[:, :, 2:2 + Tq], in_=v3(psG[:, :ntok], nseq), func=ACT.Copy), [psG], [ge])
            G(lambda e: e.tensor_copy(out=h4[:, c, :, :], in_=ge3[:, :, Tq:Tq + 2]), [ge], [halo])
            A(lambda e: e.activation(out=ca3, in_=ge3[:, :, 0:Tq], func=ACT.Identity, scale=cwb[:, c * 3:c * 3 + 1], bias=cbb[:, c:c + 1]), [ge, cwb, cbb], [ca])

        def stageB(c):
            psU = P[3 + (c % 2) * 2]
            ge = gext[c % 2]; ca = cacc[c % 2]
            ge3 = v3(ge[:, 0:nseq * (Tq + 2)], nseq)
            ca3 = v3(ca[:, 0:ntok], nseq)
            V(lambda e: e.scalar_tensor_tensor(out=ca3, in0=ge3[:, :, 1:1 + Tq], scalar=cwb[:, c * 3 + 1:c * 3 + 2], in1=ca3, op0=ALU.mult, op1=ALU.add), [ge, cwb, ca], [ca])
            V(lambda e: e.scalar_tensor_tensor(out=ca3, in0=ge3[:, :, 2:2 + Tq], scalar=cwb[:, c * 3 + 2:c * 3 + 3], in1=ca3, op0=ALU.mult, op1=ALU.add), [ge, cwb, ca], [ca])
            A(lambda e: e.activation(out=ca[:, 0:ntok], in_=ca[:, 0:ntok], func=ACT.Silu), [ca], [ca])
            V(lambda e: e.tensor_tensor(out=hT[:, c, :ntok], in0=psU[:, :ntok], in1=ca[:, 0:ntok], op=ALU.mult), [psU, ca], [hT])

        for c in range(NFC + 1):
            if c < NFC:
                stageA(c)
            if c >= 1:
                stageB(c - 1)
            if c == 4:
                cx.midn(0)
            if c == 8:
                cx.midn(1)
            if c == 13:
                cx.midt(0)
            if c == 17:
                cx.midt(1)
        col = 0
        for i, R in enumerate(tiles):
            for half in range(2):
                ps = P[half]
                for c in range(NFC):
                    T(lambda e, c=c, R=R, col=col, half=half, ps=ps: e.matmul(ps[:R, :], lhsT=hT[:, c, col:col + R], rhs=Wd[:, c, half * 512:(half + 1) * 512], start=(c == 0), stop=(c == NFC - 1)), [hT, WC], [ps])
                V(lambda e, i=i, R=R, half=half, ps=ps: e.tensor_tensor(out=xb[:R, i, half * 512:(half + 1) * 512], in0=ps[:R, :], in1=xb[:R, i, half * 512:(half + 1) * 512], op=ALU.add), [ps, xb], [xb])
            epilogue_tile(cx, xb, i, R, col)
            col += R

    NB = TP // BLK
    fw.dma(gfin[:, :], norm_fin[0].partition_broadcast(128), writes=[gfin])

    def run_sublayer(fn, blocks):
        tb = 0
        for i, cx in enumerate(blocks):
            cx.par = i % 2
            cx.tbase = tb
            tb += getattr(cx, 'ntile', 0)
            cx.next = blocks[i + 1] if i + 1 < len(blocks) else None
        if not blocks:
            return
        front_load(blocks[0]); front_compute(blocks[0])
        for i, cx in enumerate(blocks):
            nxt = blocks[i + 1] if i + 1 < len(blocks) else None
            if nxt is not None:
                front_load(nxt)
                cx.mid1 = (lambda nxt=nxt: front_norm(nxt))
                cx.mid = (lambda nxt=nxt: front_T(nxt))
                cx.midn = (lambda i, nxt=nxt: front_norm(nxt, i))
                cx.midt = (lambda i, nxt=nxt: front_T(nxt, i))
            else:
                cx.mid1 = (lambda: None)
                cx.mid = (lambda: None)
                cx.midn = (lambda i: None)
                cx.midt = (lambda i: None)
            fn(cx)

    regs_p = [fw.region("rp%d" % i) for i in range(NB)]
    reg_s = fw.region("rs")
    zero_done = False
    for layer in range(CFG['L']):
        kind = layer % 3; j = layer // 3
        handoff([WC, WB], msb)
        handoff(pst_bufs, pst_bufs)
        w_in = (a_w_in, b_w_in, c_w_in)[kind][j]; w_out = (a_w_out, b_w_out, c_w_out)[kind][j]
        ncol = 1536 if kind == 2 else 3088
        load_w(WA, ncol, w_in, 8)
        load_w(WB, 1024, w_out, 8)
        fw.dma(gb[:, :], norm_mix[layer].partition_broadcast(128), writes=[gb])
        if kind == 0:
            fw.dma(gnb[:, :], a_norm[j].partition_broadcast(128), writes=[gnb])
            fw.dma(sp[:8, 0:1], a_bi[j], writes=[sp]); fw.dma(sp[:8, 1:2], a_bf[j], writes=[sp])
            V(lambda e: e.tensor_scalar(out=sp[:8, 0:2], in0=sp[:8, 0:2], scalar1=1.0 / 15, scalar2=None, op0=ALU.mult), [sp], [sp])
            V(lambda e: e.memset(Cst[0][:, :, :], 0.0), [], [Cst[0]])
            V(lambda e: e.memset(grow[:8, 4 * (BLK + NSEQ):4 * (BLK + NSEQ) + 1], 0.0), [], [grow])
        elif kind == 1:
            fw.dma(gnb[:, :], b_norm[j].partition_broadcast(128), writes=[gnb])
            fw.dma(sp[:, 8:12], b_bg[j], writes=[sp])
            V(lambda e: e.tensor_scalar(out=sp[:, 8:12], in0=sp[:, 8:12], scalar1=-1.0, scalar2=None, op0=ALU.mult), [sp], [sp])
            fw.dma(sp[:16, 16:16 + 512], b_wgu[j], writes=[sp])
            V(lambda e: e.memset(Sst[0][:, :, :], 0.0), [], [Sst[0]])
        else:
            fw.dma(gnb[:, 0:512], c_bkv[j].partition_broadcast(128), writes=[gnb])
            fw.dma(sp[:64, 528:544], c_bq[j], writes=[sp]); fw.dma(sp[:64, 544:548], c_bk[j], writes=[sp])
            fw.dma(sp[:, 552:568], c_snk[j].partition_broadcast(128), writes=[sp])
            A(lambda e: e.activation(out=sp[:, 552:568], in_=sp[:, 552:568], func=ACT.Exp), [sp], [sp])
        Wcur[0] = WA
        blocks = []
        for which in (CFG['parts'] if CFG['mix'] else ''):
            if which == "p":
                src = xp if layer == 0 else res_p
                if kind == 2:
                    nbk = 2 * CFG['nb']
                    for b in range(nbk):
                        blocks.append(Cx(j=j, layer=layer, src=src, dst=res_p, r0=b * 128, tiles=[128], reg=regs_p[b // 2], ntok=128, Tn=128, ntile=1, sample=False, last_block=(b == nbk - 1), final=False))
                else:
                    for b in range(CFG['nb']):
                        blocks.append(Cx(j=j, layer=layer, src=src, dst=res_p, r0=b * BLK, tiles=[128, 128], reg=regs_p[b], ntok=BLK, Tn=128, ntile=2, sample=False, last_block=(b == CFG['nb'] - 1), final=False))
            else:
                src = xsm if layer == 0 else res_s
                blocks.append(Cx(j=j, layer=layer, src=src, dst=res_s, r0=0, tiles=[NS], reg=reg_s, ntok=NS, Tn=4, ntile=NSEQ, sample=True, last_block=True, final=False))
        run_sublayer((mlstm_block, gla_block, swa_block)[kind], blocks)
        if 'p' in CFG['parts'] and CFG['mix']:
            if kind == 0:
                dma_out(v3(po_c[j], 8), Cst[0][:, :, 0:128], Cst[0])
                dma_out(po_n[j].unsqueeze(2), Cst[0][:, :, 128:129], Cst[0])
            elif kind == 1:
                dma_out(v3(po_g[j], 4), Sst[0][:, :, :], Sst[0])
        if not CFG['ffn']:
            continue
        handoff(msb, [WC, WB])
        load_w(WA, DFF, f_w_up[layer][:, 0:DFF], 8)
        load_w(WB, DFF, f_w_up[layer][:, DFF:2 * DFF], 8)
        load_w(WC, D, f_w_down[layer], NFC)
        fw.dma(gb[:, :], norm_ffn[layer].partition_broadcast(128), writes=[gb])
        fw.dma(cwb[:, :], f_cw[layer], writes=[cwb]); fw.dma(cbb[:, :], f_cb[layer], writes=[cbb])
        V(lambda e: e.memset(halo_p[:, :], 0.0), [], [halo_p])
        fw.dma(halo_s[:, :], si_f[layer], writes=[halo_s])
        fin = (layer == CFG['L'] - 1)
        blocks = []
        for b in range(CFG['nb'] if 'p' in CFG['parts'] else 0):
            blocks.append(Cx(j=j, layer=layer, src=(xp if (layer == 0 and not CFG['mix']) else res_p), dst=res_p, fdst=yp, r0=b * BLK, tiles=[128, 128], reg=regs_p[b], ntok=BLK, nseq=1, halo=halo_p, final=fin, po=(b == CFG['nb'] - 1)))
        if 's' in CFG['parts']:
            blocks.append(Cx(j=j, layer=layer, src=(xsm if (layer == 0 and not CFG['mix']) else res_s), dst=res_s, fdst=ys, r0=0, tiles=[NS], reg=reg_s, ntok=NS, nseq=NSEQ, halo=halo_s, final=fin, po=False))
        run_sublayer(ffn_block, blocks)
        dma_out(po_f[layer], halo_p[:, :], halo_p)
        dma_out(so_f[layer], halo_s[:, :], halo_s)
    fw.final_wait(out_toks)
    fw.emit()
    print("instr counts", fw.stats())
    return nc


_NC = [None]


def _lay(x):
    return np.ascontiguousarray(x, dtype=np.float32)


def kernel(**inp):
    inp = {k: np.asarray(v) for k, v in inp.items()}
    if _NC[0] is None:
        _NC[0] = build()
    nc = _NC[0]
    f32 = np.float32
    ii = np.arange(128)
    consts = {
        "k_ident": np.eye(128, dtype=f32),
        "k_maskc": (ii[:, None] <= ii[None, :]).astype(f32),
        "k_maskp": (ii[:, None] >= ii[None, :]).astype(f32),
        "k_ones8": np.ones((8, 128), f32),
        "k_negI8": -np.eye(8, dtype=f32),
        "k_sel8": ((np.arange(8)[:, None] % 2) == (ii[None, :] // 64)).astype(f32),
        "k_selc": ((np.arange(8)[:, None] // 2) == np.arange(4)[None, :]).astype(f32),
        "k_rm_p": np.tile(((np.arange(BLK) % 128) != 0).astype(f32)[None], (128, 1)),
        "k_rm_s": np.tile(((np.arange(NS) % 4) != 0).astype(f32)[None], (128, 1)),
    }
    c_w_in = inp["c_w_in"]
    c_b = inp["c_b_in"]
    c_bq = c_b[:, 0:1024].reshape(1, 16, 64).transpose(0, 2, 1)
    c_bk = c_b[:, 1024:1280].reshape(1, 4, 64).transpose(0, 2, 1)
    shared = {
        "a_w_in": inp["a_w_in"], "a_w_out": inp["a_w_out"], "b_w_in": inp["b_w_in"], "b_w_out": inp["b_w_out"],
        "c_w_in": c_w_in, "c_w_out": inp["c_w_out"], "f_w_up": inp["f_w_up"], "f_w_down": inp["f_w_down"],
        "b_wgu": inp["b_w_gate_up"],
        "norm_mix": inp["norm_mix_g"], "norm_ffn": inp["norm_ffn_g"], "norm_fin": inp["norm_final_g"].reshape(1, D),
        "a_norm": inp["a_norm_g"], "b_norm": inp["b_norm_g"],
        "a_bi": inp["a_b_i"].reshape(2, 8, 1), "a_bf": inp["a_b_f"].reshape(2, 8, 1),
        "b_bg": inp["b_b_gate"].reshape(1, 4, 128).transpose(0, 2, 1),
        "c_bq": c_bq, "c_bk": c_bk, "c_bkv": c_b[:, 1024:1536], "c_snk": inp["c_sinks"],
        "f_cw": inp["f_conv_w"].reshape(4, 3, NFC, 128).transpose(0, 3, 2, 1).reshape(4, 128, NFC * 3),
        "f_cb": inp["f_conv_b"].reshape(4, NFC, 128).transpose(0, 2, 1),
    }
    shared.update(consts)
    shared = {k: _lay(v) for k, v in shared.items()}
    in_maps = []
    for c in range(8):
        sl = slice(c * NSEQ, (c + 1) * NSEQ)
        m = dict(shared)
        m["xp"] = _lay(inp["x_prompt"][c % 4])
        m["xsm"] = _lay(inp["x_sample"][sl].reshape(NS, D))
        C = inp["state_mlstm_c"][:, sl]
        m["si_c"] = _lay(C.transpose(0, 1, 3, 2, 4).reshape(2, NSEQ, 64, 1024))
        n = inp["state_mlstm_n"][:, sl]
        m["si_n"] = _lay(n.transpose(0, 1, 3, 2))
        m["si_m"] = _lay(inp["state_mlstm_m"][:, sl].transpose(0, 2, 1))
        m["si_g"] = _lay(inp["state_gla"][:, sl].transpose(0, 1, 3, 2, 4).reshape(1, NSEQ, 128, 1024))
        m["si_k"] = _lay(inp["cache_swa_k"][:, sl].reshape(1, NSEQ, 128, 256))
        m["si_v"] = _lay(inp["cache_swa_v"][:, sl].reshape(1, NSEQ, 128, 256))
        f = inp["state_ffn_conv"][:, sl]
        m["si_f"] = _lay(f.reshape(4, NSEQ, 2, NFC, 128).transpose(0, 4, 3, 1, 2).reshape(4, 128, NFC * NSEQ * 2))
        in_maps.append(m)
    ncr = CFG.get('cores', 8)
    if os.environ.get('KN_TRACE'):
        res = run_bass_kernel_spmd(nc, in_maps[:ncr], core_ids=list(range(ncr)), trace=True)
        print('EXEC_TIME_NS', res.exec_time_ns, flush=True)
    else:
        res = run_bass_kernel_spmd(nc, in_maps[:ncr], core_ids=list(range(ncr)))
    R = list(res.results)
    while len(R) < 8:
        R.append(R[0])
    B = 4
    y_prompt = np.stack([R[b]["yp"] for b in range(B)])
    y_sample = np.concatenate([R[c]["ys"].reshape(NSEQ, 4, D) for c in range(8)], 0)

    def unC(x):
        return x.reshape(2, 64, 8, 128).transpose(0, 2, 1, 3)

    def unN(x):
        return x.transpose(0, 2, 1)

    p_c = np.stack([unC(R[b]["po_c"]) for b in range(B)], 1)
    p_n = np.stack([unN(R[b]["po_n"]) for b in range(B)], 1)
    p_m = np.stack([R[b]["po_m"].reshape(2, 8) for b in range(B)], 1)
    p_g = np.stack([R[b]["po_g"].reshape(1, 128, 4, 256).transpose(0, 2, 1, 3) for b in range(B)], 1)
    p_k = np.stack([R[b]["po_k"].reshape(1, 128, 4, 64) for b in range(B)], 1)
    p_v = np.stack([R[b]["po_v"].reshape(1, 128, 4, 64) for b in range(B)], 1)
    p_f = np.stack([R[b]["po_f"].reshape(4, 128, NFC, 2).transpose(0, 3, 2, 1).reshape(4, 2, DFF) for b in range(B)], 1)
    s_c = np.concatenate([R[c]["so_c"].reshape(2, NSEQ, 64, 8, 128).transpose(0, 1, 3, 2, 4) for c in range(8)], 1)
    s_n = np.concatenate([R[c]["so_n"].transpose(0, 1, 3, 2) for c in range(8)], 1)
    s_m = np.concatenate([R[c]["so_m"].transpose(0, 2, 1) for c in range(8)], 1)
    s_g = np.concatenate([R[c]["so_g"].reshape(1, NSEQ, 128, 4, 256).transpose(0, 1, 3, 2, 4) for c in range(8)], 1)
    s_k = np.concatenate([R[c]["so_k"].reshape(1, NSEQ, 128, 4, 64) for c in range(8)], 1)
    s_v = np.concatenate([R[c]["so_v"].reshape(1, NSEQ, 128, 4, 64) for c in range(8)], 1)
    s_f = np.concatenate([R[c]["so_f"].reshape(4, 128, NFC, NSEQ, 2).transpose(0, 3, 4, 2, 1).reshape(4, NSEQ, 2, DFF) for c in range(8)], 1)
    outs = (y_prompt, y_sample, p_c, p_n, p_m, p_g, p_k, p_v, p_f, s_c, s_n, s_m, s_g, s_k, s_v, s_f)
    return tuple(np.ascontiguousarray(o, dtype=np.float32) for o in outs)
```

```python
import numpy as np
from contextlib import ExitStack
import concourse.bass as bass
import concourse.mybir as mybir

F32 = mybir.dt.float32
BF16 = mybir.dt.bfloat16
ACT = mybir.ActivationFunctionType
ALU = mybir.AluOpType
AX = mybir.AxisListType

SEM_LIMIT = 30000


class Sem:
    __slots__ = ("h", "issued", "is_dma")

    def __init__(self, h, is_dma):
        self.h = h
        self.issued = 0
        self.is_dma = is_dma


class Buf:
    def __init__(self, t, name, space="sb"):
        self.t = t
        self.name = name
        self.space = space
        self.w = None
        self.r = {}
        self.chan = None
        self.schan = None

    def __getitem__(self, k):
        return self.t[k]


class Eng:
    def __init__(self, name, obj):
        self.name = name
        self.obj = obj
        self.sem = None
        self.prog = []
        self.seen = {}


class FW:
    def __init__(self, nc):
        self.nc = nc
        self.es = ExitStack()
        self.eng = {}
        for n in ("tensor", "vector", "scalar", "gpsimd", "sync"):
            self.eng[n] = Eng(n, getattr(nc, n))
        self.nsem = 0
        self.nbuf = 0
        self.out_tokens = []

    def new_sem(self, is_dma):
        self.nsem += 1
        h = self.es.enter_context(self.nc.semaphore("s%d" % self.nsem))
        return Sem(h, is_dma)

    def sbuf(self, shape, dtype=F32, name=None):
        self.nbuf += 1
        name = "%s_%d" % (name or "sb", self.nbuf)
        t = self.es.enter_context(self.nc.sbuf_tensor(name, list(shape), dtype))
        return Buf(t, name)

    def psum(self, shape, dtype=F32, name=None):
        self.nbuf += 1
        name = "%s_%d" % (name or "ps", self.nbuf)
        t = self.es.enter_context(self.nc.psum_tensor(name, list(shape), dtype))
        return Buf(t, name, "ps")

    def dram(self, name, shape, dtype=F32, kind="Internal"):
        t = self.nc.dram_tensor(name, list(shape), dtype, kind=kind)
        return Buf(t.ap(), name, "dram")

    def region(self, name):
        return Buf(None, name, "dram")

    def _needs(self, E, reads, writes):
        needs = {}

        def need(tok):
            if tok is None:
                return
            s, v = tok
            if s.is_dma:
                v = s.issued
            if needs.get(s, 0) < v:
                needs[s] = v

        for b in reads:
            need(b.w)
        for b in writes:
            need(b.w)
            for s, v in b.r.items():
                need((s, v))
        waits = []
        for s, v in needs.items():
            if E.name == "tensor" and s is E.sem:
                continue
            if E.seen.get(s, 0) >= v:
                continue
            E.seen[s] = v
            waits.append((s.h, v))
        return waits

    def op(self, eng, fn, reads=(), writes=()):
        E = self.eng[eng]
        if E.sem is None or E.sem.issued >= SEM_LIMIT:
            E.sem = self.new_sem(False)
        waits = self._needs(E, reads, writes)
        E.sem.issued += 1
        tok = (E.sem, E.sem.issued)
        E.prog.append((waits, fn, (E.sem.h, 1)))
        for b in reads:
            b.r[tok[0]] = tok[1]
        for b in writes:
            b.w = tok
            b.r = {}
        return tok

    def dma(self, out_ap, in_ap, reads=(), writes=(), q="sync", chan=None, **kw):
        E = self.eng[q]
        waits = self._needs(E, reads, writes)
        if chan is None:
            b = writes[0] if (writes and writes[0].space != "dram") else None
            if b is not None:
                if b.chan is None:
                    b.chan = self.new_sem(True)
                chan = b.chan
            else:
                b = reads[0]
                if b.schan is None:
                    b.schan = self.new_sem(True)
                chan = b.schan
        chan.issued += 16
        tok = (chan, chan.issued)

        def fn(e, out_ap=out_ap, in_ap=in_ap, kw=kw):
            kw2 = dict(kw); kw2.setdefault("allow_slow_non_contiguous", True); return e.dma_start(out=out_ap, in_=in_ap, **kw2)

        E.prog.append((waits, fn, (chan.h, 16)))
        for b in reads:
            b.r[tok[0]] = tok[1]
        for b in writes:
            b.w = tok
            b.r = {}
        return tok

    def final_wait(self, toks, eng="sync"):
        E = self.eng[eng]
        waits = []
        seen = {}
        for s, v in toks:
            if s.is_dma:
                v = s.issued
            if seen.get(s, 0) < v:
                seen[s] = v
        for s, v in seen.items():
            waits.append((s.h, v))
        E.prog.append((waits, None, None))

    def emit(self):
        nc = self.nc
        with nc.Block() as block:
            def run(E):
                def body(e):
                    for waits, fn, inc in E.prog:
                        for h, v in waits:
                            e.wait_ge(h, v)
                        if fn is not None:
                            ins = fn(e)
                            ins.then_inc(inc[0], inc[1])
                return body
            block.tensor(run(self.eng["tensor"]))
            block.vector(run(self.eng["vector"]))
            block.scalar(run(self.eng["scalar"]))
            block.gpsimd(run(self.eng["gpsimd"]))
            block.sync(run(self.eng["sync"]))

    def close(self):
        self.es.close()

    def stats(self):
        return {n: len(E.prog) for n, E in self.eng.items()}, self.nsem

from concourse.bass_utils import run_bass_kernel_spmd

D = 1024
TP = 4096
NS = 64
NSEQ = 16
DFF = 2816
NFC = 22
BLK = 256
EPS = 1e-6


def v3(ap, a):
    return ap.rearrange("p (a b) -> p a b", a=a)


def v4(ap, a, b):
    return ap.rearrange("p (a b c) -> p a b c", a=a, b=b)


class Arena:
    def __init__(self, fw, nbytes, name):
        self.fw = fw
        self.raw = fw.sbuf([128, nbytes // 4], F32, name)
        self.off = 0
        self.n = nbytes // 4

    def take(self, nfree, dtype=F32, name="ar"):
        words = nfree if dtype == F32 else (nfree + 1) // 2
        assert self.off + words <= self.n, (name, self.off, words, self.n)
        ap = self.raw.t[:, self.off:self.off + words]
        self.off += words
        if dtype != F32:
            ap = ap.bitcast(dtype)
        return Buf(ap, name, "sb")


def handoff(frm, to):
    toks = {}
    for b in frm:
        if b.w is not None:
            s, v = b.w
            toks[s] = max(toks.get(s, 0), v)
        for s, v in b.r.items():
            toks[s] = max(toks.get(s, 0), v)
    for b in to:
        for s, v in toks.items():
            b.r[s] = max(b.r.get(s, 0), v)


import os
CFG = {}

def build():
    CFG['L'] = int(os.environ.get('KN_LAYERS', '4')); CFG['parts'] = os.environ.get('KN_PARTS', 'ps'); CFG['ffn'] = int(os.environ.get('KN_FFN', '1')); CFG['nb'] = int(os.environ.get('KN_NB', '16')); CFG['mix'] = int(os.environ.get('KN_MIX', '1')); CFG['stop'] = float(os.environ.get('KN_STOP', '99')); CFG['cores'] = int(os.environ.get('KN_CORES', '8'))
    nc = bass.Bass("TRN2", target_bir_lowering=False)
    fw = FW(nc)

    def din(name, shape):
        return nc.dram_tensor(name, list(shape), F32, kind="ExternalInput").ap()

    def dout(name, shape):
        return nc.dram_tensor(name, list(shape), F32, kind="ExternalOutput").ap()

    xp = din("xp", [TP, D]); xsm = din("xsm", [NS, D])
    a_w_in = din("a_w_in", [2, D, 3088]); a_w_out = din("a_w_out", [2, D, D])
    b_w_in = din("b_w_in", [1, D, 3088]); b_w_out = din("b_w_out", [1, D, D])
    c_w_in = din("c_w_in", [1, D, 1536]); c_w_out = din("c_w_out", [1, D, D])
    f_w_up = din("f_w_up", [4, D, 2 * DFF]); f_w_down = din("f_w_down", [4, DFF, D])
    b_wgu = din("b_wgu", [1, 16, 512])
    norm_mix = din("norm_mix", [4, D]); norm_ffn = din("norm_ffn", [4, D]); norm_fin = din("norm_fin", [1, D])
    a_norm = din("a_norm", [2, D]); b_norm = din("b_norm", [1, D])
    a_bi = din("a_bi", [2, 8, 1]); a_bf = din("a_bf", [2, 8, 1])
    b_bg = din("b_bg", [1, 128, 4])
    c_bq = din("c_bq", [1, 64, 16]); c_bk = din("c_bk", [1, 64, 4]); c_bkv = din("c_bkv", [1, 512])
    c_snk = din("c_snk", [1, 16])
    f_cw = din("f_cw", [4, 128, NFC * 3]); f_cb = din("f_cb", [4, 128, NFC])
    k_ident = din("k_ident", [128, 128]); k_maskc = din("k_maskc", [128, 128]); k_maskp = din("k_maskp", [128, 128])
    k_ones8 = din("k_ones8", [8, 128]); k_negI8 = din("k_negI8", [8, 8]); k_sel8 = din("k_sel8", [8, 128]); k_selc = din("k_selc", [8, 4])
    k_rm_p = din("k_rm_p", [128, BLK]); k_rm_s = din("k_rm_s", [128, NS])
    si_c = din("si_c", [2, NSEQ, 64, 8 * 128]); si_n = din("si_n", [2, NSEQ, 64, 8]); si_m = din("si_m", [2, 8, NSEQ])
    si_g = din("si_g", [1, NSEQ, 128, 4 * 256]); si_k = din("si_k", [1, NSEQ, 128, 256]); si_v = din("si_v", [1, NSEQ, 128, 256])
    si_f = din("si_f", [4, 128, NFC * NSEQ * 2])
    yp = dout("yp", [TP, D]); ys = dout("ys", [NS, D])
    po_c = dout("po_c", [2, 64, 8 * 128]); po_n = dout("po_n", [2, 64, 8]); po_m = dout("po_m", [2, 8, 1])
    po_g = dout("po_g", [1, 128, 4 * 256]); po_k = dout("po_k", [1, 128, 256]); po_v = dout("po_v", [1, 128, 256])
    po_f = dout("po_f", [4, 128, NFC * 2])
    so_c = dout("so_c", [2, NSEQ, 64, 8 * 128]); so_n = dout("so_n", [2, NSEQ, 64, 8]); so_m = dout("so_m", [2, 8, NSEQ])
    so_g = dout("so_g", [1, NSEQ, 128, 4 * 256]); so_k = dout("so_k", [1, NSEQ, 128, 256]); so_v = dout("so_v", [1, NSEQ, 128, 256])
    so_f = dout("so_f", [4, 128, NFC * NSEQ * 2])
    res_p = nc.dram_tensor("res_p", [TP, D], F32, kind="Internal").ap()
    res_s = nc.dram_tensor("res_s", [NS, D], F32, kind="Internal").ap()

    OUT = fw.region("outputs")
    out_toks = []

    def dma_out(dst, src_ap, srcbuf, q="sync"):
        tok = fw.dma(dst, src_ap, reads=[srcbuf], writes=[], q=q)
        out_toks.append(tok)

    WA = fw.sbuf([128, 24704], BF16, "WA")
    WBa = Arena(fw, 22528 * 2, "WB")
    WB = Buf(WBa.raw.t[:, :].bitcast(BF16), "WBw", "sb")
    WBa.off = 4096
    WCa = Arena(fw, 22528 * 2, "WC")
    WC = Buf(WCa.raw.t[:, :].bitcast(BF16), "WCw", "sb")
    ident = fw.sbuf([128, 128], F32, "ident"); identb = fw.sbuf([128, 128], BF16, "identb")
    maskc = fw.sbuf([128, 128], F32, "maskc"); maskp = fw.sbuf([128, 128], F32, "maskp")
    ones8 = fw.sbuf([8, 128], F32, "ones8"); negI8 = fw.sbuf([8, 8], F32, "negI8")
    sel8 = fw.sbuf([8, 128], F32, "sel8"); selc = fw.sbuf([8, 4], F32, "selc")
    rm_p = fw.sbuf([128, BLK], F32, "rm_p"); rm_s = fw.sbuf([128, NS], F32, "rm_s")
    gb = fw.sbuf([128, D], F32, "gb")
    sp = fw.sbuf([128, 576], F32, "sp")
    XB = [fw.sbuf([128, 2, D], F32, "xb%d" % i) for i in range(2)]
    gfin = fw.sbuf([128, D], F32, "gfin")
    xn = fw.sbuf([128, 2, D], BF16, "xn")
    XST = [fw.sbuf([128, 8, BLK], BF16, "xsT%d" % i) for i in range(2)]
    stat = fw.sbuf([128, 64], F32, "stat")
    mhalf = fw.sbuf([128, 2], F32, "mhalf")
    hT = fw.sbuf([128, NFC, BLK], BF16, "hT")
    hsT = Buf(v3(hT.t[:, 0:8, :].rearrange("p a b -> p (a b)"), 8), "hsT", "sb")
    gext = [fw.sbuf([128, BLK + 2 * NSEQ], F32, "gext%d" % i) for i in range(2)]
    cacc = [fw.sbuf([128, BLK], F32, "cacc%d" % i) for i in range(2)]
    halo_p = fw.sbuf([128, NFC * 2], F32, "halo_p")
    halo_s = fw.sbuf([128, NFC * NSEQ * 2], F32, "halo_s")
    cwb = fw.sbuf([128, NFC * 3], F32, "cwb"); cbb = fw.sbuf([128, NFC], F32, "cbb")
    PST = fw.sbuf([128, 1032], F32, "PST")
    Cst = [Buf(v3(PST.t[:64, :], 8), "Cst0", "sb"), None, None]
    Sst = [Buf(v3(PST.t[:, 0:1024], 4), "Sst0", "sb"), None, None]
    kTprev = [Buf(v3(PST.t[:64, 0:256].bitcast(BF16), 4), "kTprev0", "sb"), None, None]
    vprev = [Buf(v3(PST.t[:, 512:642].bitcast(BF16), 4), "vprev0", "sb"), None, None]
    pst_bufs = [Cst[0], Sst[0], kTprev[0], vprev[0]]
    kvraw = [None, None]
    MS = {}
    def ms(ar, name, nfree, dtype=F32):
        MS[name] = ar.take(nfree, dtype, name)
        return MS[name]
    qT = ms(WCa, "qT", 8 * BLK); kT = ms(WCa, "kT", 8 * BLK)
    ktm = ms(WCa, "ktm", 512); vext = ms(WCa, "vext", 8 * 129 + 8)
    Wt = ms(WCa, "Wt", 512); numS = ms(WCa, "numS", 512); hs = ms(WCa, "hs", 1024)
    kw = ms(WCa, "kw", 512); sqs = ms(WCa, "sqs", 256)
    Wt2 = ms(WCa, "Wt2", 512)
    grow = ms(WCa, "grow", 5 * (BLK + NSEQ)); trow = ms(WCa, "trow", 3 * 128 + 16)
    cols = ms(WCa, "cols", 64); glb = ms(WCa, "glb", 16)
    print("WC scratch words", WCa.off, "of", WCa.n)
    gnb = ms(WBa, "gnb", 1024)
    so = ms(WBa, "so", 1024)
    rbd = ms(WBa, "rbd", 1024)
    o_save = WBa.off; WBa.off -= 1024
    lgT = ms(WBa, "lgT", 4 * BLK)
    WBa.off = o_save
    u0 = WBa.off
    for i in (1, 2):
        b_ = ms(WBa, "Cst%d" % i, 1032); Cst[i] = Buf(v3(b_.t[:64, :], 8), b_.name, "sb"); MS[b_.name] = Cst[i]
    u1 = WBa.off
    WBa.off = u0
    for i in (1, 2):
        b_ = ms(WBa, "Sst%d" % i, 1024); Sst[i] = Buf(v3(b_.t, 4), b_.name, "sb"); MS[b_.name] = Sst[i]
    u1 = max(u1, WBa.off)
    WBa.off = u0
    for i in (1, 2):
        b_ = ms(WBa, "kTprev%d" % i, 512); kTprev[i] = Buf(v3(b_.t[:64, 0:256].bitcast(BF16), 4), b_.name, "sb"); MS[b_.name] = kTprev[i]
        b_ = ms(WBa, "vprev%d" % i, 260); vprev[i] = Buf(v3(b_.t[:, 0:130].bitcast(BF16), 4), b_.name, "sb"); MS[b_.name] = vprev[i]
        kvraw[i - 1] = ms(WBa, "kvraw%d" % i, 512)
    WBa.off = max(WBa.off, u1)
    print("WB scratch words", WBa.off, "of", WBa.n)
    ve2 = ms(WBa, "ve2", 516); stbuf = ms(WBa, "stbuf", 516)
    print("WB scratch words (after filler bufs)", WBa.off, "of", WBa.n)
    VE = [Buf(vext.t[:, 0:516], "ve0", "sb"), ve2]
    SOB = [Buf(so.t[:, 0:512], "so0", "sb"), Buf(so.t[:, 512:1024], "so1", "sb")]
    QTB = [Buf(qT.t[:, 0:1024], "qT0", "sb"), Buf(qT.t[:, 1024:2048], "qT1", "sb")]
    KTB = [Buf(kT.t[:, 0:1024], "kT0", "sb"), Buf(kT.t[:, 1024:2048], "kT1", "sb")]
    for b_ in VE[:1] + SOB + QTB + KTB:
        MS[b_.name] = b_
    msb = list(MS.values())

    P = [fw.psum([128, 512], F32, "P%d" % i) for i in range(8)]

    for (b, src) in ((ident, k_ident), (maskc, k_maskc), (maskp, k_maskp), (ones8, k_ones8), (negI8, k_negI8),
                     (sel8, k_sel8), (selc, k_selc), (rm_p, k_rm_p), (rm_s, k_rm_s)):
        fw.dma(b[:, :], src, writes=[b])
    fw.op("vector", lambda e: e.tensor_copy(out=identb[:, :], in_=ident[:, :]), [ident], [identb])
    fw.op("vector", lambda e: e.memset(mhalf[:, :], -0.5), [], [mhalf])

    V = lambda fn, r, w: fw.op("vector", fn, r, w)
    A = lambda fn, r, w: fw.op("scalar", fn, r, w)
    G = lambda fn, r, w: fw.op("gpsimd", fn, r, w)
    T = lambda fn, r, w: fw.op("tensor", fn, r, w)

    def load_w(dst, ncol, src2d, nk, q="gpsimd"):
        view = v3(dst[:, 0:nk * ncol], nk)
        src = src2d.rearrange("(k p) e -> p k e", p=128)
        step = max(1, nk // 8) if nk > 8 else 1
        for k0 in range(0, nk, 2 if nk <= 8 else 4):
            k1 = min(nk, k0 + (2 if nk <= 8 else 4))
            fw.dma(view[:, k0:k1, :], src[:, k0:k1, :], writes=[dst], q=q)
        return view

    class Cx:
        def __init__(self, **kw):
            self.__dict__.update(kw)

    def front_load(cx):
        xb = XB[cx.par]
        col = 0
        for i, R in enumerate(cx.tiles):
            fw.dma(xb[:R, i, :], cx.src[cx.r0 + col:cx.r0 + col + R, :], reads=[cx.reg], writes=[xb])
            col += R

    def front_norm(cx, only=None):
        xb = XB[cx.par]
        for i, R in enumerate(cx.tiles):
            if only is not None and i != only:
                continue
            A(lambda e, i=i, R=R: e.activation(out=xn[:R, i, :], in_=xb[:R, i, :], func=ACT.Square, accum_out=stat[:R, 3 * i:3 * i + 1]), [xb], [xn, stat])
            V(lambda e, i=i, R=R: e.tensor_scalar(out=stat[:R, 3 * i + 1:3 * i + 2], in0=stat[:R, 3 * i:3 * i + 1], scalar1=1.0 / D, scalar2=EPS, op0=ALU.mult, op1=ALU.add), [stat], [stat])
            G(lambda e, i=i, R=R: e.tensor_tensor(out=stat[:R, 3 * i + 2:3 * i + 3], in0=stat[:R, 3 * i + 1:3 * i + 2], in1=mhalf[:R, 0:1], op=ALU.pow), [stat, mhalf], [stat])
            V(lambda e, i=i, R=R: e.scalar_tensor_tensor(out=xn[:R, i, :], in0=xb[:R, i, :], scalar=stat[:R, 3 * i + 2:3 * i + 3], in1=gb[:R, :], op0=ALU.mult, op1=ALU.mult), [xb, stat, gb], [xn])

    def front_T(cx, only=None):
        xsT = XST[cx.par]
        col = 0
        for i, R in enumerate(cx.tiles):
            if only is None or i == only:
                pst = P[7][:, :].bitcast(BF16)
                pst3 = v3(pst, 8)
                for kc in range(8):
                    T(lambda e, kc=kc, R=R, i=i, pst3=pst3: e.transpose(out=pst3[:, kc, :R], in_=xn[:R, i, kc * 128:(kc + 1) * 128], identity=identb[:R, :R]), [xn, identb], [P[7]])
                A(lambda e, R=R, col=col, pst3=pst3: e.activation(out=xsT[:, :, col:col + R], in_=pst3[:, :, :R], func=ACT.Copy), [P[7]], [xsT])
            col += R

    def front_compute(cx):
        front_norm(cx); front_T(cx)

    def proj_fm(cx, ps, M, W, c0, ntok):
        xsT = XST[cx.par]
        for kc in range(8):
            T(lambda e, kc=kc: e.matmul(ps[:M, :ntok], lhsT=W[:, kc, c0:c0 + M], rhs=xsT[:, kc, :ntok], start=(kc == 0), stop=(kc == 7)), [xsT, Wcur[0]], [ps])

    def proj_tm(cx, ps, Tn, col, W, c0, ncol):
        xsT = XST[cx.par]
        for kc in range(8):
            T(lambda e, kc=kc: e.matmul(ps[:Tn, :ncol], lhsT=xsT[:, kc, col:col + Tn], rhs=W[:, kc, c0:c0 + ncol], start=(kc == 0), stop=(kc == 7)), [xsT, Wcur[0]], [ps])

    Wcur = [WA]

    def epilogue_tile(cx, xb, i, R, col):
        if cx.final:
            A(lambda e: e.activation(out=xn[:R, i, :], in_=xb[:R, i, :], func=ACT.Square, accum_out=stat[:R, 56:57]), [xb], [xn, stat])
            A(lambda e: e.activation(out=stat[:R, 57:58], in_=stat[:R, 56:57], func=ACT.Ln, scale=1.0 / D, bias=EPS), [stat], [stat])
            A(lambda e: e.activation(out=stat[:R, 58:59], in_=stat[:R, 57:58], func=ACT.Exp, scale=-0.5), [stat], [stat])
            V(lambda e: e.scalar_tensor_tensor(out=xb[:R, i, :], in0=xb[:R, i, :], scalar=stat[:R, 58:59], in1=gfin[:R, :], op0=ALU.mult, op1=ALU.mult), [xb, stat, gfin], [xb])
            dma_out(cx.fdst[cx.r0 + col:cx.r0 + col + R, :], xb[:R, i, :], xb)
        else:
            fw.dma(cx.dst[cx.r0 + col:cx.r0 + col + R, :], xb[:R, i, :], reads=[xb], writes=[cx.reg])

    def out_proj_store(cx, Wo):
        xb = XB[cx.par]
        col = 0
        for i, R in enumerate(cx.tiles):
            for half in range(2):
                ps = P[half]
                for ec in range(8):
                    T(lambda e, ec=ec, R=R, col=col, half=half, ps=ps: e.matmul(ps[:R, :], lhsT=hsT[:, ec, col:col + R], rhs=Wo[:, ec, half * 512:(half + 1) * 512], start=(ec == 0), stop=(ec == 7)), [hT, WB], [ps])
                V(lambda e, i=i, R=R, half=half, ps=ps: e.tensor_tensor(out=xb[:R, i, half * 512:(half + 1) * 512], in0=ps[:R, :], in1=xb[:R, i, half * 512:(half + 1) * 512], op=ALU.add), [ps, xb], [xb])
            fw.dma(cx.dst[cx.r0 + col:cx.r0 + col + R, :], xb[:R, i, :], reads=[xb], writes=[cx.reg])
            col += R

    def hs_to_hsT(Tn, col):
        for g in range(2):
            ps = P[4 + g]
            ps3 = v3(ps[:, :], 4)
            for e4 in range(4):
                ec = g * 4 + e4
                T(lambda e, ec=ec, e4=e4, ps3=ps3: e.transpose(out=ps3[:, e4, :Tn], in_=hs[:Tn, ec * 128:(ec + 1) * 128], identity=ident[:Tn, :Tn]), [hs, ident], [ps])
            A(lambda e, g=g, ps3=ps3: e.activation(out=hsT[:, g * 4:(g + 1) * 4, col:col + Tn], in_=ps3[:, :, :Tn], func=ACT.Copy), [ps], [hT])

    def head_rmsnorm_gate(Tn, nh, dv, rden_ap, so_ap=None, so_buf=None):
        so_ap = so[:Tn, :] if so_ap is None else so_ap
        so_buf = so if so_buf is None else so_buf
        h3 = v3(hs[:Tn, :], nh)
        sqv = numS.t[:Tn, 0:512].bitcast(BF16)
        V(lambda e: e.tensor_tensor(out=sqv, in0=hs[:Tn, :], in1=hs[:Tn, :], op=ALU.mult), [hs], [numS])
        V(lambda e: e.tensor_reduce(out=stat[:Tn, 8:8 + nh], in_=v3(sqv, nh), axis=AX.X, op=ALU.add), [numS], [stat])
        if rden_ap is not None:
            V(lambda e: e.tensor_tensor(out=stat[:Tn, 16:16 + nh], in0=rden_ap, in1=rden_ap, op=ALU.mult), [stat], [stat])
            V(lambda e: e.tensor_tensor(out=stat[:Tn, 8:8 + nh], in0=stat[:Tn, 8:8 + nh], in1=stat[:Tn, 16:16 + nh], op=ALU.mult), [stat], [stat])
        A(lambda e: e.activation(out=stat[:Tn, 8:8 + nh], in_=stat[:Tn, 8:8 + nh], func=ACT.Ln, scale=1.0 / dv, bias=EPS), [stat], [stat])
        A(lambda e: e.activation(out=stat[:Tn, 8:8 + nh], in_=stat[:Tn, 8:8 + nh], func=ACT.Exp, scale=-0.5), [stat], [stat])
        if rden_ap is not None:
            V(lambda e: e.tensor_tensor(out=stat[:Tn, 8:8 + nh], in0=stat[:Tn, 8:8 + nh], in1=rden_ap, op=ALU.mult), [stat], [stat])
        V(lambda e: e.tensor_tensor(out=h3, in0=h3, in1=stat[:Tn, 8:8 + nh].unsqueeze(2).to_broadcast([Tn, nh, dv]), op=ALU.mult), [hs, stat], [hs])
        V(lambda e: e.tensor_tensor(out=hs[:Tn, :], in0=hs[:Tn, :], in1=so_ap, op=ALU.mult), [hs, so_buf], [hs])

    def flush(lst):
        while lst:
            lst.pop(0)()

    def ensure_items(cx):
        if getattr(cx, 'pend_fm', None) is not None:
            return
        W = v3(WA[:, 0:8 * 3088], 8)
        ntok, Tn = cx.ntok, cx.Tn
        q3 = v3(QTB[cx.par].t[:64, :].bitcast(BF16), 8); k3 = v3(KTB[cx.par].t[:64, :].bitcast(BF16), 8)
        qTc = QTB[cx.par]; kTc = KTB[cx.par]
        fm = []
        for h in range(16):
            def it(h=h):
                ps = P[h % 2]
                proj_fm(cx, ps, 64, W, h * 64, ntok)
                if h < 8:
                    A(lambda e: e.activation(out=q3[:, h, :ntok], in_=ps[:64, :ntok], func=ACT.Copy), [ps], [qTc])
                else:
                    V(lambda e: e.tensor_scalar(out=k3[:, h - 8, :ntok], in0=ps[:64, :ntok], scalar1=0.125, scalar2=None, op0=ALU.mult), [ps], [kTc])
            fm.append(it)
        cx.pend_fm = fm
        cx.pend_tm = []
        for ti in range(cx.ntile):
            tp = (cx.tbase + ti) % 2
            c0 = ti * Tn
            ktm_v = ktm.t[:Tn, tp * 256:(tp + 1) * 256].bitcast(BF16)
            veb = VE[tp]; sob = SOB[tp]
            ve = v3(veb.t[:Tn, 0:516].bitcast(BF16), 8)
            so_v = sob.t[:Tn, :].bitcast(BF16)
            lst = []

            def it_k(c0=c0, ktm_v=ktm_v):
                proj_tm(cx, P[0], Tn, c0, W, 512, 512)
                V(lambda e: e.tensor_scalar(out=ktm_v, in0=P[0][:Tn, :], scalar1=0.125, scalar2=None, op0=ALU.mult), [P[0]], [ktm])
            lst.append(it_k)
            for hf in range(2):
                def it_v(hf=hf, c0=c0, ve=ve, veb=veb):
                    if hf == 0:
                        G(lambda e: e.memset(ve[:, :, 128:129], 1.0), [], [veb])
                    proj_tm(cx, P[1], Tn, c0, W, 1024 + hf * 512, 512)
                    A(lambda e: e.activation(out=ve[:, hf * 4:(hf + 1) * 4, 0:128], in_=v3(P[1][:Tn, :], 4), func=ACT.Copy), [P[1]], [veb])
                lst.append(it_v)
            for hf in range(2):
                def it_o(hf=hf, c0=c0, so_v=so_v, sob=sob):
                    proj_tm(cx, P[hf], Tn, c0, W, 2048 + hf * 512, 512)
                    A(lambda e: e.activation(out=so_v[:, hf * 512:(hf + 1) * 512], in_=P[hf][:Tn, :], func=ACT.Sigmoid), [P[hf]], [sob])
                    G(lambda e: e.tensor_tensor(out=so_v[:, hf * 512:(hf + 1) * 512], in0=so_v[:, hf * 512:(hf + 1) * 512], in1=gnb[:Tn, hf * 512:(hf + 1) * 512], op=ALU.mult), [sob, gnb], [sob])
                lst.append(it_o)
            cx.pend_tm.append(lst)

    def mlstm_block(cx):
        j, r0, tiles, reg, ntok, Tn, ntile, sample = cx.j, cx.r0, cx.tiles, cx.reg, cx.ntok, cx.Tn, cx.ntile, cx.sample
        W = v3(WA[:, 0:8 * 3088], 8)
        Wo = v3(WB[:, 0:8 * 1024], 8)
        if CFG['stop'] <= 1: return
        ensure_items(cx)
        flush(cx.pend_fm)
        q3 = v3(QTB[cx.par].t[:64, :].bitcast(BF16), 8); k3 = v3(KTB[cx.par].t[:64, :].bitcast(BF16), 8)
        qTc = QTB[cx.par]; kTc = KTB[cx.par]
        if CFG['stop'] <= 2: return
        GW = BLK + NSEQ
        igc = grow[:8, 0:ntok]; lf = grow[:8, GW:GW + ntok]; Fc = grow[:8, 2 * GW:2 * GW + ntok]; Mt = grow[:8, 3 * GW:3 * GW + ntok]
        nseg = ntile if sample else 1
        seglen = ntok // nseg
        mext = v3(grow[:8, 4 * GW:4 * GW + nseg * (seglen + 1)], nseg)
        proj_fm(cx, P[0], 8, W, 3072, ntok)
        proj_fm(cx, P[1], 8, W, 3080, ntok)
        A(lambda e: e.activation(out=igc, in_=P[0][:8, :ntok], func=ACT.Tanh, scale=1.0 / 15, bias=sp[:8, 0:1]), [P[0], sp], [grow])
        A(lambda e: e.activation(out=lf, in_=P[1][:8, :ntok], func=ACT.Tanh, scale=1.0 / 15, bias=sp[:8, 1:2]), [P[1], sp], [grow])
        V(lambda e: e.tensor_scalar(out=igc, in0=igc, scalar1=15.0, scalar2=None, op0=ALU.mult), [grow], [grow])
        xg = Fc; ug = Mt
        V(lambda e: e.tensor_scalar(out=xg, in0=lf, scalar1=15.0, scalar2=None, op0=ALU.mult), [grow], [grow])
        V(lambda e: e.scalar_tensor_tensor(out=ug, in0=xg, scalar=-1.0, in1=xg, op0=ALU.mult, op1=ALU.max), [grow], [grow])
        A(lambda e: e.activation(out=ug, in_=ug, func=ACT.Exp, scale=-1.0), [grow], [grow])
        V(lambda e: e.tensor_scalar(out=lf, in0=ug, scalar1=2.0, scalar2=None, op0=ALU.add), [grow], [grow])
        V(lambda e: e.reciprocal(out=lf, in_=lf), [grow], [grow])
        V(lambda e: e.tensor_tensor(out=ug, in0=ug, in1=lf, op=ALU.mult), [grow], [grow])
        V(lambda e: e.tensor_tensor(out=lf, in0=ug, in1=ug, op=ALU.mult), [grow], [grow])
        zp = trow[:8, 0:ntok]
        V(lambda e: e.tensor_scalar(out=zp, in0=lf, scalar1=1.0 / 9, scalar2=None, op0=ALU.mult), [grow], [trow])
        for cc in (1.0 / 7, 1.0 / 5, 1.0 / 3):
            V(lambda e, cc=cc: e.scalar_tensor_tensor(out=zp, in0=zp, scalar=cc, in1=lf, op0=ALU.add, op1=ALU.mult), [trow, grow], [trow])
        V(lambda e: e.scalar_tensor_tensor(out=zp, in0=zp, scalar=1.0, in1=ug, op0=ALU.add, op1=ALU.mult), [trow, grow], [trow])
        V(lambda e: e.tensor_scalar(out=xg, in0=xg, scalar1=0.0, scalar2=None, op0=ALU.min), [grow], [grow])
        V(lambda e: e.scalar_tensor_tensor(out=lf, in0=zp, scalar=-2.0, in1=xg, op0=ALU.mult, op1=ALU.add), [trow, grow], [grow])
        if sample:
            fw.dma(mext[:, :, 0:1], si_m[j].unsqueeze(2), writes=[grow])
        for s in range(nseg):
            V(lambda e, s=s: e.tensor_tensor_scan(out=mext[:, s, 1:1 + seglen], data0=lf[:, s * seglen:(s + 1) * seglen], data1=igc[:, s * seglen:(s + 1) * seglen], initial=mext[:, s, 0:1], op0=ALU.add, op1=ALU.max), [grow], [grow])
        rm = rm_s if sample else rm_p
        V(lambda e: e.tensor_tensor_scan(out=Fc, data0=rm[:8, :ntok], data1=lf, initial=0.0, op0=ALU.mult, op1=ALU.add), [grow, rm], [grow])
        V(lambda e: e.tensor_tensor(out=igc, in0=igc, in1=Fc, op=ALU.subtract), [grow], [grow])
        for s in range(nseg):
            V(lambda e, s=s: e.tensor_tensor(out=Mt[:, s * seglen:(s + 1) * seglen], in0=mext[:, s, 1:1 + seglen], in1=Fc[:, s * seglen:(s + 1) * seglen], op=ALU.subtract), [grow], [grow])
        a_r = igc
        cx.mid1()
        if CFG['stop'] <= 3: return
        def tile(ti):
            c0 = ti * Tn
            if sample:
                st = Cst[1 + ti % 2]
                fw.dma(st[:, :, 0:128], v3(si_c[j, ti], 8), writes=[st])
                fw.dma(st[:, :, 128:129], si_n[j, ti].unsqueeze(2), writes=[st])
                car = mext[:, ti, 0:1]; mt = mext[:, ti, 1:1 + Tn]
            else:
                st = Cst[0]
                car = mext[:, 0, c0:c0 + 1]; mt = mext[:, 0, 1 + c0:1 + c0 + Tn]
            flush(cx.pend_tm[ti])
            tp = (cx.tbase + ti) % 2
            ktm_v = ktm.t[:Tn, tp * 256:(tp + 1) * 256].bitcast(BF16)
            veb = VE[tp]; sob = SOB[tp]
            ve = v3(veb.t[:Tn, 0:516].bitcast(BF16), 8)
            so_v = sob.t[:Tn, :].bitcast(BF16)
            stb = v3(stbuf.t[:64, 0:516].bitcast(BF16), 8)
            A(lambda e: e.activation(out=stb, in_=st[:, :, :], func=ACT.Copy), [st], [stbuf])
            if ti + 1 < ntile:
                srcs = [cx.pend_tm[ti + 1]]
            elif cx.next is not None:
                cx.mid()
                ensure_items(cx.next)
                srcs = [cx.next.pend_fm, cx.next.pend_tm[0]]
            else:
                srcs = []
            npts = [7]

            def fillpt():
                rem = sum(len(l_) for l_ in srcs)
                k = 1 if rem > 0 else 0
                for l_ in srcs:
                    while k > 0 and l_:
                        l_.pop(0)()
                        k -= 1
            if CFG['stop'] <= 4: return
            g_r = trow[:8, 0:Tn]; enm_r = trow[:8, 128:128 + Tn]; wl_r = trow[:8, 256:256 + Tn]; nml = trow[:8, 384:385]
            V(lambda e: e.tensor_scalar(out=nml, in0=Mt[:, c0 + Tn - 1:c0 + Tn], scalar1=-1.0, scalar2=None, op0=ALU.mult), [grow], [trow])
            A(lambda e: e.activation(out=g_r, in_=Mt[:, c0:c0 + Tn], func=ACT.Exp, scale=-1.0, bias=car), [grow], [trow])
            A(lambda e: e.activation(out=enm_r, in_=mt, func=ACT.Exp, scale=-1.0), [grow], [trow])
            A(lambda e: e.activation(out=wl_r, in_=a_r[:, c0:c0 + Tn], func=ACT.Exp, scale=1.0, bias=nml), [grow, trow], [trow])
            px = P[7]
            for qi, row in enumerate((a_r[:, c0:c0 + Tn], g_r, enm_r, wl_r)):
                T(lambda e, qi=qi, row=row: e.transpose(out=px[:Tn, qi * 8:(qi + 1) * 8], in_=row, identity=ident[:8, :8]), [grow, trow, ident], [px])
            V(lambda e: e.tensor_copy(out=cols[:Tn, 0:32], in_=px[:Tn, 0:32]), [px], [cols])
            a_c = cols[:Tn, 0:8]; g_c = cols[:Tn, 8:16]; enm_c = cols[:Tn, 16:24]; wl_c = cols[:Tn, 24:32]
            rb3 = v3(rbd[:8, 0:8 * Tn], 8)
            V(lambda e: e.tensor_tensor(out=rb3, in0=Mt[:, c0:c0 + Tn].unsqueeze(1).to_broadcast([8, 8, Tn]), in1=negI8[:, :].unsqueeze(2).to_broadcast([8, 8, Tn]), op=ALU.mult), [grow, negI8], [rbd])
            V(lambda e: e.tensor_scalar(out=trow[:8, 388:396], in0=negI8[:, :], scalar1=g_r[:, Tn - 1:Tn], scalar2=-1.0, op0=ALU.mult, op1=ALU.mult), [negI8, trow], [trow])
            T(lambda e: e.matmul(px[:64, 40:48], lhsT=ones8[:, 0:64], rhs=trow[:8, 388:396], start=True, stop=True), [ones8, trow], [px])
            V(lambda e: e.tensor_copy(out=glb[:64, 0:8], in_=px[:64, 40:48]), [px], [glb])
            if CFG['stop'] <= 5: return
            kw3 = v3(kw.t[:Tn, 0:256].bitcast(BF16), 8)
            V(lambda e: e.tensor_tensor(out=kw3, in0=v3(ktm_v, 8), in1=wl_c.unsqueeze(2).to_broadcast([Tn, 8, 64]), op=ALU.mult), [ktm, cols], [kw])
            fillpt()
            pD = P[7]
            def bufs(hh):
                if hh == 0:
                    return P[2], P[3], Wt
                return P[6], P[5], Wt2

            def ptbuf(hh):
                if hh == 0:
                    return kw, v3(kw.t[:Tn, 256:512].bitcast(BF16)[:, 0:4 * Tn], 4)
                return sqs, v3(sqs.t[:Tn, 0:256].bitcast(BF16)[:, 0:4 * Tn], 4)

            def halfA(hh):
                psB, psS, Wtb = bufs(hh)
                T(lambda e: e.matmul(psB[:Tn, 0:4 * Tn], lhsT=ones8[:, :Tn], rhs=rbd[:8, hh * 4 * Tn:(hh + 1) * 4 * Tn], start=True, stop=True), [ones8, rbd], [psB])
                for h4 in range(4):
                    h = hh * 4 + h4
                    T(lambda e, h4=h4, h=h: e.matmul(psS[:Tn, h4 * Tn:(h4 + 1) * Tn], lhsT=k3[:, h, c0:c0 + Tn], rhs=q3[:, h, c0:c0 + Tn], start=True, stop=True), [kTc, qTc], [psS])
                W3 = v3(Wtb[:Tn, 0:4 * Tn], 4)
                for h4 in range(4):
                    h = hh * 4 + h4
                    A(lambda e, h=h, h4=h4: e.activation(out=W3[:, h4, :], in_=psB[:Tn, h4 * Tn:(h4 + 1) * Tn], func=ACT.Exp, bias=a_c[:, h:h + 1], scale=1.0), [psB, cols], [Wtb])
                V(lambda e: e.tensor_tensor(out=W3, in0=W3, in1=maskc[:Tn, :Tn].unsqueeze(1).to_broadcast([Tn, 4, Tn]), op=ALU.mult), [Wtb, maskc], [Wtb])
                ptB, PT3 = ptbuf(hh)
                V(lambda e: e.tensor_tensor(out=PT3, in0=v3(psS[:Tn, 0:4 * Tn], 4), in1=W3, op=ALU.mult), [psS, Wtb], [ptB])

            def halfB(hh):
                psN = P[4]; psI = P[5]
                Wtb, W3 = ptbuf(hh)
                for h4 in range(4):
                    h = hh * 4 + h4
                    T(lambda e, h=h, h4=h4: e.matmul(psN[:Tn, h4 * 128:(h4 + 1) * 128], lhsT=W3[:, h4, :], rhs=ve[:, h, 0:128], start=True, stop=True), [Wtb, veb], [psN])
                    T(lambda e, h=h, h4=h4: e.matmul(pD[:Tn, 64 + h:65 + h], lhsT=W3[:, h4, :], rhs=ve[:, h, 128:129], start=True, stop=True), [Wtb, veb], [pD])
                    T(lambda e, h4=h4, h=h: e.matmul(psI[:Tn, h4 * 128:(h4 + 1) * 128], lhsT=q3[:, h, c0:c0 + Tn], rhs=stb[:, h, 0:128], start=True, stop=True), [qTc, stbuf], [psI])
                    T(lambda e, h=h: e.matmul(pD[:Tn, 80 + h:81 + h], lhsT=q3[:, h, c0:c0 + Tn], rhs=stb[:, h, 128:129], start=True, stop=True), [qTc, stbuf], [pD])
                A(lambda e: e.activation(out=numS[:Tn, :], in_=psN[:Tn, :], func=ACT.Copy), [psN], [numS])
                hsl = v3(hs[:Tn, hh * 512:(hh + 1) * 512], 4)
                V(lambda e: e.tensor_tensor(out=hsl, in0=v3(psI[:Tn, :], 4), in1=g_c[:, hh * 4:(hh + 1) * 4].unsqueeze(2).to_broadcast([Tn, 4, 128]), op=ALU.mult), [psI, cols], [hs])
                V(lambda e: e.tensor_tensor(out=hs[:Tn, hh * 512:(hh + 1) * 512], in0=hs[:Tn, hh * 512:(hh + 1) * 512], in1=numS[:Tn, :], op=ALU.add), [hs, numS], [hs])

            halfA(0); fillpt(); halfA(1); fillpt(); halfB(0); fillpt(); halfB(1); fillpt()
            if CFG['stop'] <= 6: return
            V(lambda e: e.tensor_tensor(out=stat[:Tn, 32:40], in0=pD[:Tn, 80:88], in1=g_c, op=ALU.mult), [pD, cols], [stat])
            V(lambda e: e.tensor_tensor(out=stat[:Tn, 24:32], in0=pD[:Tn, 64:72], in1=stat[:Tn, 32:40], op=ALU.add), [pD, stat], [stat])
            V(lambda e: e.scalar_tensor_tensor(out=stat[:Tn, 24:32], in0=stat[:Tn, 24:32], scalar=-1.0, in1=stat[:Tn, 24:32], op0=ALU.mult, op1=ALU.max), [stat], [stat])
            V(lambda e: e.tensor_tensor(out=stat[:Tn, 24:32], in0=stat[:Tn, 24:32], in1=enm_c, op=ALU.max), [stat, cols], [stat])
            V(lambda e: e.reciprocal(out=stat[:Tn, 24:32], in_=stat[:Tn, 24:32]), [stat], [stat])
            head_rmsnorm_gate(Tn, 8, 128, stat[:Tn, 24:32], so_v, sob)
            fillpt()
            hs_to_hsT(Tn, c0)
            if CFG['stop'] <= 7: return
            pUn = P[7]
            for h in range(8):
                psU = P[2 + h // 4]
                T(lambda e, h=h, psU=psU: e.matmul(psU[:64, (h % 4) * 128:(h % 4 + 1) * 128], lhsT=kw3[:, h, :], rhs=ve[:, h, 0:128], start=True, stop=True), [kw, veb], [psU])
                T(lambda e, h=h: e.matmul(pUn[:64, 48 + h:49 + h], lhsT=kw3[:, h, :], rhs=ve[:, h, 128:129], start=True, stop=True), [kw, veb], [pUn])
            fillpt()
            for l_ in srcs:
                flush(l_)
            for h in range(8):
                psU = P[2 + h // 4]
                V(lambda e, h=h, psU=psU: e.scalar_tensor_tensor(out=st[:, h, 0:128], in0=st[:, h, 0:128], scalar=glb[:64, h:h + 1], in1=psU[:64, (h % 4) * 128:(h % 4 + 1) * 128], op0=ALU.mult, op1=ALU.add), [st, glb, psU], [st])
            V(lambda e: e.tensor_tensor(out=st[:, :, 128:129], in0=st[:, :, 128:129], in1=glb[:64, 0:8].unsqueeze(2), op=ALU.mult), [st, glb], [st])
            V(lambda e: e.tensor_tensor(out=st[:, :, 128:129], in0=st[:, :, 128:129], in1=pUn[:64, 48:56].unsqueeze(2), op=ALU.add), [st, pUn], [st])
            if sample:
                dma_out(v3(so_c[j, ti], 8), st[:, :, 0:128], st)
                dma_out(so_n[j, ti].unsqueeze(2), st[:, :, 128:129], st)
        for ti in range(ntile):
            tile(ti)
        if sample:
            dma_out(so_m[j].unsqueeze(2), mext[:, :, Tn:Tn + 1], grow)
        else:
            V(lambda e: e.tensor_copy(out=mext[:, 0, 0:1], in_=mext[:, 0, ntok:ntok + 1]), [grow], [grow])
            if cx.last_block:
                dma_out(po_m[j], mext[:, 0, 0:1], grow)
        out_proj_store(cx, Wo)

    def gla_block(cx):
        j, r0, tiles, reg, ntok, Tn, ntile, sample = cx.j, cx.r0, cx.tiles, cx.reg, cx.ntok, cx.Tn, cx.ntile, cx.sample
        W = v3(WA[:, 0:8 * 3088], 8)
        Wo = v3(WB[:, 0:8 * 1024], 8)
        q3 = v3(qT.t[:, 0:2 * BLK].bitcast(BF16), 4); k3 = v3(kT.t[:, 0:2 * BLK].bitcast(BF16), 4); kl3 = v3(qT[:, 4 * BLK:8 * BLK], 4); lg3 = v3(lgT[:, :], 4)
        for c in range(8):
            ps = P[c % 2]
            proj_fm(cx, ps, 128, W, c * 128, ntok)
            if c < 4:
                V(lambda e, c=c, ps=ps: e.tensor_scalar(out=q3[:, c, :ntok], in0=ps[:, :ntok], scalar1=128.0 ** -0.5, scalar2=None, op0=ALU.mult), [ps], [qT])
            else:
                A(lambda e, c=c, ps=ps: e.activation(out=k3[:, c - 4, :ntok], in_=ps[:, :ntok], func=ACT.Copy), [ps], [kT])
        proj_fm(cx, P[0], 16, W, 3072, ntok)
        zT = grow[:16, 0:ntok]
        V(lambda e: e.tensor_copy(out=zT, in_=P[0][:16, :ntok]), [P[0]], [grow])
        rm = rm_s if sample else rm_p
        for h in range(4):
            ps = P[h % 2]
            T(lambda e, h=h, ps=ps: e.matmul(ps[:, :ntok], lhsT=sp[:16, 16 + h * 128:16 + (h + 1) * 128], rhs=zT, start=True, stop=True), [sp, grow], [ps])
            A(lambda e, h=h, ps=ps: e.activation(out=lg3[:, h, :ntok], in_=ps[:, :ntok], func=ACT.Exp, scale=-1.0, bias=sp[:, 8 + h:9 + h]), [ps, sp], [lgT])
            A(lambda e, h=h: e.activation(out=lg3[:, h, :ntok], in_=lg3[:, h, :ntok], func=ACT.Ln, bias=1.0, scale=1.0), [lgT], [lgT])
            V(lambda e, h=h: e.tensor_scalar(out=lg3[:, h, :ntok], in0=lg3[:, h, :ntok], scalar1=-1.0 / 16, scalar2=None, op0=ALU.mult), [lgT], [lgT])
            V(lambda e, h=h: e.tensor_copy(out=hs[:, h * BLK:h * BLK + ntok], in_=lg3[:, h, :ntok]), [lgT], [hs])
            V(lambda e, h=h: e.tensor_tensor_scan(out=lg3[:, h, :ntok], data0=rm[:, :ntok], data1=hs[:, h * BLK:h * BLK + ntok], initial=0.0, op0=ALU.mult, op1=ALU.add), [hs, rm], [lgT])
        def tile(ti):
            c0 = ti * Tn
            if sample:
                st = Sst[1 + ti % 2]
                fw.dma(st[:, :, :], v3(si_g[j, ti], 4), writes=[st])
            else:
                st = Sst[0]
            bl = cols[:, 32:36]; ebl = cols[:, 36:40]
            V(lambda e: e.tensor_copy(out=bl.unsqueeze(2), in_=lg3[:, :, c0 + Tn - 1:c0 + Tn]), [lgT], [cols])
            A(lambda e: e.activation(out=ebl, in_=bl, func=ACT.Exp), [cols], [cols])
            for h in range(4):
                A(lambda e, h=h: e.activation(out=kl3[:, h, c0:c0 + Tn], in_=lg3[:, h, c0:c0 + Tn], func=ACT.Exp, scale=-1.0, bias=bl[:, h:h + 1]), [lgT, cols], [qT])
            G(lambda e: e.tensor_tensor(out=kl3[:, :, c0:c0 + Tn], in0=kl3[:, :, c0:c0 + Tn], in1=k3[:, :, c0:c0 + Tn], op=ALU.mult), [qT, kT], [qT])
            pk = P[6]
            for h in range(4):
                T(lambda e, h=h: e.transpose(out=pk[:Tn, h * 128:(h + 1) * 128], in_=kl3[:, h, c0:c0 + Tn], identity=ident[:, :]), [qT, ident], [pk])
            A(lambda e: e.activation(out=kw.t[:Tn, 0:256].bitcast(BF16), in_=pk[:Tn, :], func=ACT.Copy), [pk], [kw])
            kl_tm = v3(kw.t[:Tn, 0:256].bitcast(BF16), 4)
            A(lambda e: e.activation(out=v3(Wt[:, 0:4 * Tn], 4), in_=lg3[:, :, c0:c0 + Tn], func=ACT.Exp), [lgT], [Wt])
            V(lambda e: e.tensor_tensor(out=q3[:, :, c0:c0 + Tn], in0=q3[:, :, c0:c0 + Tn], in1=v3(Wt[:, 0:4 * Tn], 4), op=ALU.mult), [qT, Wt], [qT])
            A(lambda e: e.activation(out=v3(Wt[:, 0:4 * Tn], 4), in_=lg3[:, :, c0:c0 + Tn], func=ACT.Exp, scale=-1.0), [lgT], [Wt])
            V(lambda e: e.tensor_tensor(out=k3[:, :, c0:c0 + Tn], in0=k3[:, :, c0:c0 + Tn], in1=v3(Wt[:, 0:4 * Tn], 4), op=ALU.mult), [kT, Wt], [kT])
            vt = v3(vext.t[:Tn, 0:512].bitcast(BF16), 4)
            vtf = vext.t[:Tn, 0:512].bitcast(BF16)
            stb = v3(vext.t[:, 520:1032].bitcast(BF16), 4)
            G(lambda e: e.tensor_copy(out=stb, in_=st[:, :, :]), [st], [vext])
            for hf in range(2):
                proj_tm(cx, P[hf], Tn, c0, W, 1024 + hf * 512, 512)
                A(lambda e, hf=hf: e.activation(out=vtf[:, hf * 512:(hf + 1) * 512], in_=P[hf][:Tn, :], func=ACT.Copy), [P[hf]], [vext])
            for hf in range(2):
                proj_tm(cx, P[hf], Tn, c0, W, 2048 + hf * 512, 512)
                A(lambda e, hf=hf: e.activation(out=so[:Tn, hf * 512:(hf + 1) * 512], in_=P[hf][:Tn, :], func=ACT.Silu), [P[hf]], [so])
            G(lambda e: e.tensor_tensor(out=so[:Tn, :], in0=so[:Tn, :], in1=gnb[:Tn, :], op=ALU.mult), [so, gnb], [so])
            psS = P[3]
            for h in range(4):
                T(lambda e, h=h: e.matmul(psS[:Tn, h * Tn:(h + 1) * Tn], lhsT=k3[:, h, c0:c0 + Tn], rhs=q3[:, h, c0:c0 + Tn], start=True, stop=True), [kT, qT], [psS])
            PT3 = v3(numS.t[:Tn, 0:256].bitcast(BF16)[:, 0:4 * Tn], 4)
            V(lambda e: e.tensor_tensor(out=PT3, in0=v3(psS[:Tn, 0:4 * Tn], 4), in1=maskc[:Tn, :Tn].unsqueeze(1).to_broadcast([Tn, 4, Tn]), op=ALU.mult), [psS, maskc], [numS])
            for h in range(4):
                ps = P[4 + h // 2]
                o0 = (h % 2) * 256
                T(lambda e, h=h, ps=ps, o0=o0: e.matmul(ps[:Tn, o0:o0 + 256], lhsT=PT3[:, h, :], rhs=vt[:, h, :], start=True, stop=False), [numS, vext], [ps])
                T(lambda e, h=h, ps=ps, o0=o0: e.matmul(ps[:Tn, o0:o0 + 256], lhsT=q3[:, h, c0:c0 + Tn], rhs=stb[:, h, :], start=False, stop=True), [qT, vext], [ps])
            A(lambda e: e.activation(out=hs[:Tn, 0:512], in_=P[4][:Tn, :], func=ACT.Copy), [P[4]], [hs])
            V(lambda e: e.tensor_copy(out=hs[:Tn, 512:1024], in_=P[5][:Tn, :]), [P[5]], [hs])
            for h in range(4):
                ps = P[2]
                T(lambda e, h=h, ps=ps: e.matmul(ps[:, 0:256], lhsT=kl_tm[:, h, :], rhs=vt[:, h, :], start=True, stop=True), [kw, vext], [ps])
                V(lambda e, h=h, ps=ps: e.scalar_tensor_tensor(out=st[:, h, :], in0=st[:, h, :], scalar=ebl[:, h:h + 1], in1=ps[:, 0:256], op0=ALU.mult, op1=ALU.add), [st, cols, ps], [st])
            if sample:
                dma_out(v3(so_g[j, ti], 4), st[:, :, :], st)
            head_rmsnorm_gate(Tn, 4, 256, None)
            hs_to_hsT(Tn, c0)
        for ti in range(ntile):
            tile(ti)
            if ti == 0:
                cx.mid1()
        cx.mid()
        out_proj_store(cx, Wo)

    def swa_block(cx):
        j, r0, tiles, reg, ntok, Tn, ntile, sample, last_block = cx.j, cx.r0, cx.tiles, cx.reg, cx.ntok, cx.Tn, cx.ntile, cx.sample, cx.last_block
        W = v3(WA[:, 0:8 * 1536], 8)
        Wo = v3(WB[:, 0:8 * 1024], 8)
        q8 = v3(qT.t[:64, 0:1024].bitcast(BF16), 16); k2 = v3(kT.t[:64, 0:256].bitcast(BF16), 4)
        for h in range(20):
            ps = P[h % 2]
            proj_fm(cx, ps, 64, W, h * 64, ntok)
            if h < 16:
                A(lambda e, h=h, ps=ps: e.activation(out=q8[:, h, :ntok], in_=ps[:64, :ntok], func=ACT.Identity, bias=sp[:64, 528 + h:529 + h], scale=1.0), [ps, sp], [qT])
            else:
                A(lambda e, h=h, ps=ps: e.activation(out=k2[:, h - 16, :ntok], in_=ps[:64, :ntok], func=ACT.Identity, bias=sp[:64, 544 + h - 16:545 + h - 16], scale=1.0), [ps, sp], [kT])
        def tile(ti):
            c0 = ti * Tn
            proj_tm(cx, P[0], Tn, c0, W, 1024, 512)
            V(lambda e: e.tensor_tensor(out=ktm[:Tn, :], in0=P[0][:Tn, :], in1=gnb[:Tn, 0:512], op=ALU.add), [P[0], gnb], [ktm])
            ve = v3(vext.t[:Tn, 0:130].bitcast(BF16), 4)
            if sample:
                V(lambda e: e.memset(vext[:, 0:4 * 65], 0.0), [], [vext])
                V(lambda e: e.memset(numS[:, 0:256], 0.0), [], [numS])
            G(lambda e: e.memset(ve[:, :, 64:65], 1.0), [], [vext])
            G(lambda e: e.tensor_copy(out=ve[:, :, 0:64], in_=v3(ktm[:Tn, 256:512], 4)), [ktm], [vext])
            vefull = v3(vext.t[:, 0:130].bitcast(BF16), 4)
            if sample:
                kp = kTprev[1 + ti % 2]; vp = vprev[1 + ti % 2]; raw = kvraw[ti % 2]
                fw.dma(raw[:, 0:256], si_k[j, ti], writes=[raw])
                fw.dma(raw[:, 256:512], si_v[j, ti], writes=[raw])
                pk = P[6]
                for c in range(4):
                    T(lambda e, c=c: e.transpose(out=pk[:64, c * 128:(c + 1) * 128], in_=raw[:, c * 64:(c + 1) * 64], identity=ident[:, :]), [raw, ident], [pk])
                A(lambda e: e.activation(out=kp[:, :, :], in_=v3(pk[:64, 0:512], 4), func=ACT.Copy), [pk], [kp])
                G(lambda e: e.memset(vp[:, :, 64:65], 1.0), [], [vp])
                G(lambda e: e.tensor_copy(out=vp[:, :, 0:64], in_=v3(raw[:, 256:512], 4)), [raw], [vp])
                has_prev = True
                dma_out(so_k[j, ti, 0:124, :], raw[4:128, 0:256], raw)
                dma_out(so_v[j, ti, 0:124, :], raw[4:128, 256:512], raw)
                dma_out(so_k[j, ti, 124:128, :], ktm[:Tn, 0:256], ktm)
                dma_out(so_v[j, ti, 124:128, :], ktm[:Tn, 256:512], ktm)
            else:
                kp = kTprev[0]; vp = vprev[0]
                has_prev = not (r0 == 0 and ti == 0)
                if last_block and ti == ntile - 1:
                    dma_out(po_k[j], ktm[:Tn, 0:256], ktm)
                    dma_out(po_v[j], ktm[:Tn, 256:512], ktm)
            blocks = ([(kp, vp, 128, maskp)] if has_prev else []) + [(None, None, Tn, maskc)]
            pD = P[7]
            def kvhead(kh):
                PTs = []
                for bi, (kpb, vpb, nk, msk) in enumerate(blocks):
                    psS = P[2 + bi]
                    for g in range(4):
                        if kpb is None:
                            T(lambda e, g=g, psS=psS, kh=kh, nk=nk: e.matmul(psS[:nk, g * Tn:(g + 1) * Tn], lhsT=k2[:, kh, c0:c0 + nk], rhs=q8[:, kh * 4 + g, c0:c0 + Tn], start=True, stop=True), [kT, qT], [psS])
                        else:
                            T(lambda e, g=g, psS=psS, kpb=kpb, kh=kh, nk=nk: e.matmul(psS[:nk, g * Tn:(g + 1) * Tn], lhsT=kpb[:, kh, 0:nk], rhs=q8[:, kh * 4 + g, c0:c0 + Tn], start=True, stop=True), [kpb, qT], [psS])
                    PTb = Wt if bi == 0 else numS
                    PTv = PTb.t[:, 0:256].bitcast(BF16)
                    A(lambda e, psS=psS, PTb=PTb, PTv=PTv, nk=nk: e.activation(out=PTv[:nk, 0:4 * Tn], in_=psS[:nk, 0:4 * Tn], func=ACT.Exp, scale=0.125), [psS], [PTb])
                    V(lambda e, PTb=PTb, PTv=PTv, nk=nk, msk=msk: e.tensor_tensor(out=v3(PTv[:nk, 0:4 * Tn], 4), in0=v3(PTv[:nk, 0:4 * Tn], 4), in1=msk[:nk, :Tn].unsqueeze(1).to_broadcast([nk, 4, Tn]), op=ALU.mult), [PTb, msk], [PTb])
                    PTs.append((PTb, PTv, (128 if sample else nk), (vefull if sample else ve) if kpb is None else vpb, vext if kpb is None else vpb))
                for g in range(4):
                    hq = kh * 4 + g
                    psO = P[4 + hq // 8]
                    o0 = (hq % 8) * 64
                    for bi, (PTb, PTv, nk, vv, vbuf) in enumerate(PTs):
                        T(lambda e, g=g, PTb=PTb, PTv=PTv, nk=nk, vv=vv, psO=psO, o0=o0, bi=bi, kh=kh: e.matmul(psO[:Tn, o0:o0 + 64], lhsT=PTv[:nk, g * Tn:(g + 1) * Tn], rhs=vv[:nk, kh, 0:64], start=(bi == 0), stop=(bi == len(PTs) - 1)), [PTb, vbuf], [psO])
                    for bi, (PTb, PTv, nk, vv, vbuf) in enumerate(PTs):
                        T(lambda e, g=g, PTb=PTb, PTv=PTv, nk=nk, vv=vv, hq=hq, bi=bi, kh=kh: e.matmul(pD[:Tn, 96 + hq:97 + hq], lhsT=PTv[:nk, g * Tn:(g + 1) * Tn], rhs=vv[:nk, kh, 64:65], start=(bi == 0), stop=(bi == len(PTs) - 1)), [PTb, vbuf], [pD])
            for kh in range(4):
                kvhead(kh)
            V(lambda e: e.tensor_tensor(out=stat[:Tn, 40:56], in0=pD[:Tn, 96:112], in1=sp[:Tn, 552:568], op=ALU.add), [pD, sp], [stat])
            V(lambda e: e.reciprocal(out=stat[:Tn, 40:56], in_=stat[:Tn, 40:56]), [stat], [stat])
            for hf in range(2):
                V(lambda e, hf=hf: e.tensor_tensor(out=v3(hs[:Tn, hf * 512:(hf + 1) * 512], 8), in0=v3(P[4 + hf][:Tn, :], 8), in1=stat[:Tn, 40 + hf * 8:48 + hf * 8].unsqueeze(2).to_broadcast([Tn, 8, 64]), op=ALU.mult), [P[4 + hf], stat], [hs])
            hs_to_hsT(Tn, c0)
            if not sample:
                G(lambda e: e.tensor_copy(out=kTprev[0][:, :, :], in_=k2[:, :, c0:c0 + Tn]), [kT], [kTprev[0]])
                G(lambda e: e.tensor_copy(out=vprev[0][:, :, :], in_=ve), [vext], [vprev[0]])
        for ti in range(ntile):
            tile(ti)
            if ti == 0:
                cx.mid1()
        cx.mid()
        out_proj_store(cx, Wo)

    def ffn_block(cx):
        layer, r0, tiles, reg, ntok, nseq, halo = cx.layer, cx.r0, cx.tiles, cx.reg, cx.ntok, cx.nseq, cx.halo
        xb = XB[cx.par]; xsT = XST[cx.par]
        Wg = v3(WA[:, 0:8 * DFF], 8); Wu = v3(WB[:, 0:8 * DFF], 8); Wd = v3(WC[:, 0:NFC * D], NFC)
        Tq = ntok // nseq
        h4 = v4(halo[:, :], NFC, nseq)
        def stageA(c):
            psG = P[2 + (c % 2) * 2]; psU = P[3 + (c % 2) * 2]
            for kc in range(8):
                T(lambda e, kc=kc: e.matmul(psG[:, :ntok], lhsT=Wg[:, kc, c * 128:(c + 1) * 128], rhs=xsT[:, kc, :ntok], start=(kc == 0), stop=(kc == 7)), [xsT, WA], [psG])
            for kc in range(8):
                T(lambda e, kc=kc: e.matmul(psU[:, :ntok], lhsT=Wu[:, kc, c * 128:(c + 1) * 128], rhs=xsT[:, kc, :ntok], start=(kc == 0), stop=(kc == 7)), [xsT, WB], [psU])
            ge = gext[c % 2]; ca = cacc[c % 2]
            ge3 = v3(ge[:, 0:nseq * (Tq + 2)], nseq)
            ca3 = v3(ca[:, 0:ntok], nseq)
            G(lambda e: e.tensor_copy(out=ge3[:, :, 0:2], in_=h4[:, c, :, :]), [halo], [ge])
            A(lambda e: e.activation(out=ge3[:, :, 2:2 + Tq], in_=v3(psG[:, :ntok], nseq), func=ACT.Copy), [psG], [ge])
            G(lambda e: e.tensor_copy(out=h4[:, c, :, :], in_=ge3[:, :, Tq:Tq + 2]), [ge], [halo])
            G(lambda e: e.tensor_scalar(out=ca3, in0=ge3[:, :, 0:Tq], scalar1=cwb[:, c * 3:c * 3 + 1], scalar2=cbb[:, c:c + 1], op0=ALU.mult, op1=ALU.add), [ge, cwb, cbb], [ca])

        def stageB(c):
            psU = P[3 + (c % 2) * 2]
            ge = gext[c % 2]; ca = cacc[c % 2]
            ge3 = v3(ge[:, 0:nseq * (Tq + 2)], nseq)
            ca3 = v3(ca[:, 0:ntok], nseq)
            V(lambda e: e.scalar_tensor_tensor(out=ca3, in0=ge3[:, :, 1:1 + Tq], scalar=cwb[:, c * 3 + 1:c * 3 + 2], in1=ca3, op0=ALU.mult, op1=ALU.add), [ge, cwb, ca], [ca])
            V(lambda e: e.scalar_tensor_tensor(out=ca3, in0=ge3[:, :, 2:2 + Tq], scalar=cwb[:, c * 3 + 2:c * 3 + 3], in1=ca3, op0=ALU.mult, op1=ALU.add), [ge, cwb, ca], [ca])
            A(lambda e: e.activation(out=ca[:, 0:ntok], in_=ca[:, 0:ntok], func=ACT.Silu), [ca], [ca])
            V(lambda e: e.tensor_tensor(out=hT[:, c, :ntok], in0=psU[:, :ntok], in1=ca[:, 0:ntok], op=ALU.mult), [psU, ca], [hT])

        for c in range(NFC + 1):
            if c < NFC:
                stageA(c)
            if c >= 1:
                stageB(c - 1)
            if c == 4:
                cx.midn(0)
            if c == 8:
                cx.midn(1)
            if c == 13:
                cx.midt(0)
            if c == 17:
                cx.midt(1)
        col = 0
        for i, R in enumerate(tiles):
            for half in range(2):
                ps = P[half]
                for c in range(NFC):
                    T(lambda e, c=c, R=R, col=col, half=half, ps=ps: e.matmul(ps[:R, :], lhsT=hT[:, c, col:col + R], rhs=Wd[:, c, half * 512:(half + 1) * 512], start=(c == 0), stop=(c == NFC - 1)), [hT, WC], [ps])
                V(lambda e, i=i, R=R, half=half, ps=ps: e.tensor_tensor(out=xb[:R, i, half * 512:(half + 1) * 512], in0=ps[:R, :], in1=xb[:R, i, half * 512:(half + 1) * 512], op=ALU.add), [ps, xb], [xb])
            epilogue_tile(cx, xb, i, R, col)
            col += R

    NB = TP // BLK
    fw.dma(gfin[:, :], norm_fin[0].partition_broadcast(128), writes=[gfin])

    def run_sublayer(fn, blocks):
        tb = 0
        for i, cx in enumerate(blocks):
            cx.par = i % 2
            cx.tbase = tb
            tb += getattr(cx, 'ntile', 0)
            cx.next = blocks[i + 1] if i + 1 < len(blocks) else None
        if not blocks:
            return
        front_load(blocks[0]); front_compute(blocks[0])
        for i, cx in enumerate(blocks):
            nxt = blocks[i + 1] if i + 1 < len(blocks) else None
            if nxt is not None:
                front_load(nxt)
                cx.mid1 = (lambda nxt=nxt: front_norm(nxt))
                cx.mid = (lambda nxt=nxt: front_T(nxt))
                cx.midn = (lambda i, nxt=nxt: front_norm(nxt, i))
                cx.midt = (lambda i, nxt=nxt: front_T(nxt, i))
            else:
                cx.mid1 = (lambda: None)
                cx.mid = (lambda: None)
                cx.midn = (lambda i: None)
                cx.midt = (lambda i: None)
            fn(cx)

    regs_p = [fw.region("rp%d" % i) for i in range(NB)]
    reg_s = fw.region("rs")
    zero_done = False
    for layer in range(CFG['L']):
        kind = layer % 3; j = layer // 3
        handoff([WC, WB], msb)
        handoff(pst_bufs, pst_bufs)
        w_in = (a_w_in, b_w_in, c_w_in)[kind][j]; w_out = (a_w_out, b_w_out, c_w_out)[kind][j]
        ncol = 1536 if kind == 2 else 3088
        load_w(WA, ncol, w_in, 8)
        load_w(WB, 1024, w_out, 8)
        fw.dma(gb[:, :], norm_mix[layer].partition_broadcast(128), writes=[gb])
        if kind == 0:
            fw.dma(gnb[:, :], a_norm[j].partition_broadcast(128), writes=[gnb])
            fw.dma(sp[:8, 0:1], a_bi[j], writes=[sp]); fw.dma(sp[:8, 1:2], a_bf[j], writes=[sp])
            V(lambda e: e.tensor_scalar(out=sp[:8, 0:2], in0=sp[:8, 0:2], scalar1=1.0 / 15, scalar2=None, op0=ALU.mult), [sp], [sp])
            V(lambda e: e.memset(Cst[0][:, :, :], 0.0), [], [Cst[0]])
            V(lambda e: e.memset(grow[:8, 4 * (BLK + NSEQ):4 * (BLK + NSEQ) + 1], 0.0), [], [grow])
        elif kind == 1:
            fw.dma(gnb[:, :], b_norm[j].partition_broadcast(128), writes=[gnb])
            fw.dma(sp[:, 8:12], b_bg[j], writes=[sp])
            V(lambda e: e.tensor_scalar(out=sp[:, 8:12], in0=sp[:, 8:12], scalar1=-1.0, scalar2=None, op0=ALU.mult), [sp], [sp])
            fw.dma(sp[:16, 16:16 + 512], b_wgu[j], writes=[sp])
            V(lambda e: e.memset(Sst[0][:, :, :], 0.0), [], [Sst[0]])
        else:
            fw.dma(gnb[:, 0:512], c_bkv[j].partition_broadcast(128), writes=[gnb])
            fw.dma(sp[:64, 528:544], c_bq[j], writes=[sp]); fw.dma(sp[:64, 544:548], c_bk[j], writes=[sp])
            fw.dma(sp[:, 552:568], c_snk[j].partition_broadcast(128), writes=[sp])
            A(lambda e: e.activation(out=sp[:, 552:568], in_=sp[:, 552:568], func=ACT.Exp), [sp], [sp])
        Wcur[0] = WA
        blocks = []
        for which in (CFG['parts'] if CFG['mix'] else ''):
            if which == "p":
                src = xp if layer == 0 else res_p
                if kind == 2:
                    nbk = 2 * CFG['nb']
                    for b in range(nbk):
                        blocks.append(Cx(j=j, layer=layer, src=src, dst=res_p, r0=b * 128, tiles=[128], reg=regs_p[b // 2], ntok=128, Tn=128, ntile=1, sample=False, last_block=(b == nbk - 1), final=False))
                else:
                    for b in range(CFG['nb']):
                        blocks.append(Cx(j=j, layer=layer, src=src, dst=res_p, r0=b * BLK, tiles=[128, 128], reg=regs_p[b], ntok=BLK, Tn=128, ntile=2, sample=False, last_block=(b == CFG['nb'] - 1), final=False))
            else:
                src = xsm if layer == 0 else res_s
                blocks.append(Cx(j=j, layer=layer, src=src, dst=res_s, r0=0, tiles=[NS], reg=reg_s, ntok=NS, Tn=4, ntile=NSEQ, sample=True, last_block=True, final=False))
        run_sublayer((mlstm_block, gla_block, swa_block)[kind], blocks)
        if 'p' in CFG['parts'] and CFG['mix']:
            if kind == 0:
                dma_out(v3(po_c[j], 8), Cst[0][:, :, 0:128], Cst[0])
                dma_out(po_n[j].unsqueeze(2), Cst[0][:, :, 128:129], Cst[0])
            elif kind == 1:
                dma_out(v3(po_g[j], 4), Sst[0][:, :, :], Sst[0])
        if not CFG['ffn']:
            continue
        handoff(msb, [WC, WB])
        load_w(WA, DFF, f_w_up[layer][:, 0:DFF], 8)
        load_w(WB, DFF, f_w_up[layer][:, DFF:2 * DFF], 8)
        load_w(WC, D, f_w_down[layer], NFC)
        fw.dma(gb[:, :], norm_ffn[layer].partition_broadcast(128), writes=[gb])
        fw.dma(cwb[:, :], f_cw[layer], writes=[cwb]); fw.dma(cbb[:, :], f_cb[layer], writes=[cbb])
        V(lambda e: e.memset(halo_p[:, :], 0.0), [], [halo_p])
        fw.dma(halo_s[:, :], si_f[layer], writes=[halo_s])
        fin = (layer == CFG['L'] - 1)
        blocks = []
        for b in range(CFG['nb'] if 'p' in CFG['parts'] else 0):
            blocks.append(Cx(j=j, layer=layer, src=(xp if (layer == 0 and not CFG['mix']) else res_p), dst=res_p, fdst=yp, r0=b * BLK, tiles=[128, 128], reg=regs_p[b], ntok=BLK, nseq=1, halo=halo_p, final=fin, po=(b == CFG['nb'] - 1)))
        if 's' in CFG['parts']:
            blocks.append(Cx(j=j, layer=layer, src=(xsm if (layer == 0 and not CFG['mix']) else res_s), dst=res_s, fdst=ys, r0=0, tiles=[NS], reg=reg_s, ntok=NS, nseq=NSEQ, halo=halo_s, final=fin, po=False))
        run_sublayer(ffn_block, blocks)
        dma_out(po_f[layer], halo_p[:, :], halo_p)
        dma_out(so_f[layer], halo_s[:, :], halo_s)
    fw.final_wait(out_toks)
    fw.emit()
    print("instr counts", fw.stats())
    return nc


_NC = [None]


def _lay(x):
    return np.ascontiguousarray(x, dtype=np.float32)


def kernel(**inp):
    inp = {k: np.asarray(v) for k, v in inp.items()}
    if _NC[0] is None:
        _NC[0] = build()
    nc = _NC[0]
    f32 = np.float32
    ii = np.arange(128)
    consts = {
        "k_ident": np.eye(128, dtype=f32),
        "k_maskc": (ii[:, None] <= ii[None, :]).astype(f32),
        "k_maskp": (ii[:, None] >= ii[None, :]).astype(f32),
        "k_ones8": np.ones((8, 128), f32),
        "k_negI8": -np.eye(8, dtype=f32),
        "k_sel8": ((np.arange(8)[:, None] % 2) == (ii[None, :] // 64)).astype(f32),
        "k_selc": ((np.arange(8)[:, None] // 2) == np.arange(4)[None, :]).astype(f32),
        "k_rm_p": np.tile(((np.arange(BLK) % 128) != 0).astype(f32)[None], (128, 1)),
        "k_rm_s": np.tile(((np.arange(NS) % 4) != 0).astype(f32)[None], (128, 1)),
    }
    c_w_in = inp["c_w_in"]
    c_b = inp["c_b_in"]
    c_bq = c_b[:, 0:1024].reshape(1, 16, 64).transpose(0, 2, 1)
    c_bk = c_b[:, 1024:1280].reshape(1, 4, 64).transpose(0, 2, 1)
    shared = {
        "a_w_in": inp["a_w_in"], "a_w_out": inp["a_w_out"], "b_w_in": inp["b_w_in"], "b_w_out": inp["b_w_out"],
        "c_w_in": c_w_in, "c_w_out": inp["c_w_out"], "f_w_up": inp["f_w_up"], "f_w_down": inp["f_w_down"],
        "b_wgu": inp["b_w_gate_up"],
        "norm_mix": inp["norm_mix_g"], "norm_ffn": inp["norm_ffn_g"], "norm_fin": inp["norm_final_g"].reshape(1, D),
        "a_norm": inp["a_norm_g"], "b_norm": inp["b_norm_g"],
        "a_bi": inp["a_b_i"].reshape(2, 8, 1), "a_bf": inp["a_b_f"].reshape(2, 8, 1),
        "b_bg": inp["b_b_gate"].reshape(1, 4, 128).transpose(0, 2, 1),
        "c_bq": c_bq, "c_bk": c_bk, "c_bkv": c_b[:, 1024:1536], "c_snk": inp["c_sinks"],
        "f_cw": inp["f_conv_w"].reshape(4, 3, NFC, 128).transpose(0, 3, 2, 1).reshape(4, 128, NFC * 3),
        "f_cb": inp["f_conv_b"].reshape(4, NFC, 128).transpose(0, 2, 1),
    }
    shared.update(consts)
    shared = {k: _lay(v) for k, v in shared.items()}
    in_maps = []
    for c in range(8):
        sl = slice(c * NSEQ, (c + 1) * NSEQ)
        m = dict(shared)
        m["xp"] = _lay(inp["x_prompt"][c % 4])
        m["xsm"] = _lay(inp["x_sample"][sl].reshape(NS, D))
        C = inp["state_mlstm_c"][:, sl]
        m["si_c"] = _lay(C.transpose(0, 1, 3, 2, 4).reshape(2, NSEQ, 64, 1024))
        n = inp["state_mlstm_n"][:, sl]
        m["si_n"] = _lay(n.transpose(0, 1, 3, 2))
        m["si_m"] = _lay(inp["state_mlstm_m"][:, sl].transpose(0, 2, 1))
        m["si_g"] = _lay(inp["state_gla"][:, sl].transpose(0, 1, 3, 2, 4).reshape(1, NSEQ, 128, 1024))
        m["si_k"] = _lay(inp["cache_swa_k"][:, sl].reshape(1, NSEQ, 128, 256))
        m["si_v"] = _lay(inp["cache_swa_v"][:, sl].reshape(1, NSEQ, 128, 256))
        f = inp["state_ffn_conv"][:, sl]
        m["si_f"] = _lay(f.reshape(4, NSEQ, 2, NFC, 128).transpose(0, 4, 3, 1, 2).reshape(4, 128, NFC * NSEQ * 2))
        in_maps.append(m)
    ncr = CFG.get('cores', 8)
    if os.environ.get('KN_TRACE'):
        res = run_bass_kernel_spmd(nc, in_maps[:ncr], core_ids=list(range(ncr)), trace=True)
        print('EXEC_TIME_NS', res.exec_time_ns, flush=True)
    else:
        res = run_bass_kernel_spmd(nc, in_maps[:ncr], core_ids=list(range(ncr)))
    R = list(res.results)
    while len(R) < 8:
        R.append(R[0])
    B = 4
    y_prompt = np.stack([R[b]["yp"] for b in range(B)])
    y_sample = np.concatenate([R[c]["ys"].reshape(NSEQ, 4, D) for c in range(8)], 0)

    def unC(x):
        return x.reshape(2, 64, 8, 128).transpose(0, 2, 1, 3)

    def unN(x):
        return x.transpose(0, 2, 1)

    p_c = np.stack([unC(R[b]["po_c"]) for b in range(B)], 1)
    p_n = np.stack([unN(R[b]["po_n"]) for b in range(B)], 1)
    p_m = np.stack([R[b]["po_m"].reshape(2, 8) for b in range(B)], 1)
    p_g = np.stack([R[b]["po_g"].reshape(1, 128, 4, 256).transpose(0, 2, 1, 3) for b in range(B)], 1)
    p_k = np.stack([R[b]["po_k"].reshape(1, 128, 4, 64) for b in range(B)], 1)
    p_v = np.stack([R[b]["po_v"].reshape(1, 128, 4, 64) for b in range(B)], 1)
    p_f = np.stack([R[b]["po_f"].reshape(4, 128, NFC, 2).transpose(0, 3, 2, 1).reshape(4, 2, DFF) for b in range(B)], 1)
    s_c = np.concatenate([R[c]["so_c"].reshape(2, NSEQ, 64, 8, 128).transpose(0, 1, 3, 2, 4) for c in range(8)], 1)
    s_n = np.concatenate([R[c]["so_n"].transpose(0, 1, 3, 2) for c in range(8)], 1)
    s_m = np.concatenate([R[c]["so_m"].transpose(0, 2, 1) for c in range(8)], 1)
    s_g = np.concatenate([R[c]["so_g"].reshape(1, NSEQ, 128, 4, 256).transpose(0, 1, 3, 2, 4) for c in range(8)], 1)
    s_k = np.concatenate([R[c]["so_k"].reshape(1, NSEQ, 128, 4, 64) for c in range(8)], 1)
    s_v = np.concatenate([R[c]["so_v"].reshape(1, NSEQ, 128, 4, 64) for c in range(8)], 1)
    s_f = np.concatenate([R[c]["so_f"].reshape(4, 128, NFC, NSEQ, 2).transpose(0, 3, 4, 2, 1).reshape(4, NSEQ, 2, DFF) for c in range(8)], 1)
    outs = (y_prompt, y_sample, p_c, p_n, p_m, p_g, p_k, p_v, p_f, s_c, s_n, s_m, s_g, s_k, s_v, s_f)
    return tuple(np.ascontiguousarray(o, dtype=np.float32) for o in outs)
```

```python
import numpy as np
from contextlib import ExitStack
import concourse.bass as bass
import concourse.mybir as mybir

F32 = mybir.dt.float32
BF16 = mybir.dt.bfloat16
ACT = mybir.ActivationFunctionType
ALU = mybir.AluOpType
AX = mybir.AxisListType

SEM_LIMIT = 30000


class Sem:
    __slots__ = ("h", "issued", "is_dma")

    def __init__(self, h, is_dma):
        self.h = h
        self.issued = 0
        self.is_dma = is_dma


class Buf:
    def __init__(self, t, name, space="sb"):
        self.t = t
        self.name = name
        self.space = space
        self.w = None
        self.r = {}
        self.chan = None
        self.schan = None

    def __getitem__(self, k):
        return self.t[k]


class Eng:
    def __init__(self, name, obj):
        self.name = name
        self.obj = obj
        self.sem = None
        self.prog = []
        self.seen = {}


class FW:
    def __init__(self, nc):
        self.nc = nc
        self.es = ExitStack()
        self.eng = {}
        for n in ("tensor", "vector", "scalar", "gpsimd", "sync"):
            self.eng[n] = Eng(n, getattr(nc, n))
        self.nsem = 0
        self.nbuf = 0
        self.out_tokens = []

    def new_sem(self, is_dma):
        self.nsem += 1
        h = self.es.enter_context(self.nc.semaphore("s%d" % self.nsem))
        return Sem(h, is_dma)

    def sbuf(self, shape, dtype=F32, name=None):
        self.nbuf += 1
        name = "%s_%d" % (name or "sb", self.nbuf)
        t = self.es.enter_context(self.nc.sbuf_tensor(name, list(shape), dtype))
        return Buf(t, name)

    def psum(self, shape, dtype=F32, name=None):
        self.nbuf += 1
        name = "%s_%d" % (name or "ps", self.nbuf)
        t = self.es.enter_context(self.nc.psum_tensor(name, list(shape), dtype))
        return Buf(t, name, "ps")

    def dram(self, name, shape, dtype=F32, kind="Internal"):
        t = self.nc.dram_tensor(name, list(shape), dtype, kind=kind)
        return Buf(t.ap(), name, "dram")

    def region(self, name):
        return Buf(None, name, "dram")

    def _needs(self, E, reads, writes):
        needs = {}

        def need(tok):
            if tok is None:
                return
            s, v = tok
            if s.is_dma:
                v = s.issued
            if needs.get(s, 0) < v:
                needs[s] = v

        for b in reads:
            need(b.w)
        for b in writes:
            need(b.w)
            for s, v in b.r.items():
                need((s, v))
        waits = []
        for s, v in needs.items():
            if E.name == "tensor" and s is E.sem:
                continue
            if E.seen.get(s, 0) >= v:
                continue
            E.seen[s] = v
            waits.append((s.h, v))
        return waits

    def op(self, eng, fn, reads=(), writes=()):
        E = self.eng[eng]
        if E.sem is None or E.sem.issued >= SEM_LIMIT:
            E.sem = self.new_sem(False)
        waits = self._needs(E, reads, writes)
        E.sem.issued += 1
        tok = (E.sem, E.sem.issued)
        E.prog.append((waits, fn, (E.sem.h, 1)))
        for b in reads:
            b.r[tok[0]] = tok[1]
        for b in writes:
            b.w = tok
            b.r = {}
        return tok

    def dma(self, out_ap, in_ap, reads=(), writes=(), q="sync", chan=None, **kw):
        E = self.eng[q]
        waits = self._needs(E, reads, writes)
        if chan is None:
            b = writes[0] if (writes and writes[0].space != "dram") else None
            if b is not None:
                if b.chan is None:
                    b.chan = self.new_sem(True)
                chan = b.chan
            else:
                b = reads[0]
                if b.schan is None:
                    b.schan = self.new_sem(True)
                chan = b.schan
        chan.issued += 16
        tok = (chan, chan.issued)

        def fn(e, out_ap=out_ap, in_ap=in_ap, kw=kw):
            kw2 = dict(kw); kw2.setdefault("allow_slow_non_contiguous", True); return e.dma_start(out=out_ap, in_=in_ap, **kw2)

        E.prog.append((waits, fn, (chan.h, 16)))
        for b in reads:
            b.r[tok[0]] = tok[1]
        for b in writes:
            b.w = tok
            b.r = {}
        return tok

    def final_wait(self, toks, eng="sync"):
        E = self.eng[eng]
        waits = []
        seen = {}
        for s, v in toks:
            if s.is_dma:
                v = s.issued
            if seen.get(s, 0) < v:
                seen[s] = v
        for s, v in seen.items():
            waits.append((s.h, v))
        E.prog.append((waits, None, None))

    def emit(self):
        nc = self.nc
        with nc.Block() as block:
            def run(E):
                def body(e):
                    for waits, fn, inc in E.prog:
                        for h, v in waits:
                            e.wait_ge(h, v)
                        if fn is not None:
                            ins = fn(e)
                            ins.then_inc(inc[0], inc[1])
                return body
            block.tensor(run(self.eng["tensor"]))
            block.vector(run(self.eng["vector"]))
            block.scalar(run(self.eng["scalar"]))
            block.gpsimd(run(self.eng["gpsimd"]))
            block.sync(run(self.eng["sync"]))

    def close(self):
        self.es.close()

    def stats(self):
        return {n: len(E.prog) for n, E in self.eng.items()}, self.nsem

from concourse.bass_utils import run_bass_kernel_spmd

D = 1024
TP = 4096
NS = 64
NSEQ = 16
DFF = 2816
NFC = 22
BLK = 256
EPS = 1e-6


def v3(ap, a):
    return ap.rearrange("p (a b) -> p a b", a=a)


def v4(ap, a, b):
    return ap.rearrange("p (a b c) -> p a b c", a=a, b=b)


class Arena:
    def __init__(self, fw, nbytes, name):
        self.fw = fw
        self.raw = fw.sbuf([128, nbytes // 4], F32, name)
        self.off = 0
        self.n = nbytes // 4

    def take(self, nfree, dtype=F32, name="ar"):
        words = nfree if dtype == F32 else (nfree + 1) // 2
        assert self.off + words <= self.n, (name, self.off, words, self.n)
        ap = self.raw.t[:, self.off:self.off + words]
        self.off += words
        if dtype != F32:
            ap = ap.bitcast(dtype)
        return Buf(ap, name, "sb")


def handoff(frm, to):
    toks = {}
    for b in frm:
        if b.w is not None:
            s, v = b.w
            toks[s] = max(toks.get(s, 0), v)
        for s, v in b.r.items():
            toks[s] = max(toks.get(s, 0), v)
    for b in to:
        for s, v in toks.items():
            b.r[s] = max(b.r.get(s, 0), v)


import os
CFG = {}

def build():
    CFG['L'] = int(os.environ.get('KN_LAYERS', '4')); CFG['parts'] = os.environ.get('KN_PARTS', 'ps'); CFG['ffn'] = int(os.environ.get('KN_FFN', '1')); CFG['nb'] = int(os.environ.get('KN_NB', '16')); CFG['mix'] = int(os.environ.get('KN_MIX', '1')); CFG['stop'] = float(os.environ.get('KN_STOP', '99')); CFG['cores'] = int(os.environ.get('KN_CORES', '8'))
    nc = bass.Bass("TRN2", target_bir_lowering=False)
    fw = FW(nc)

    def din(name, shape):
        return nc.dram_tensor(name, list(shape), F32, kind="ExternalInput").ap()

    def dout(name, shape):
        return nc.dram_tensor(name, list(shape), F32, kind="ExternalOutput").ap()

    xp = din("xp", [TP, D]); xsm = din("xsm", [NS, D])
    a_w_in = din("a_w_in", [2, D, 3088]); a_w_out = din("a_w_out", [2, D, D])
    b_w_in = din("b_w_in", [1, D, 3088]); b_w_out = din("b_w_out", [1, D, D])
    c_w_in = din("c_w_in", [1, D, 1536]); c_w_out = din("c_w_out", [1, D, D])
    f_w_up = din("f_w_up", [4, D, 2 * DFF]); f_w_down = din("f_w_down", [4, DFF, D])
    b_wgu = din("b_wgu", [1, 16, 512])
    norm_mix = din("norm_mix", [4, D]); norm_ffn = din("norm_ffn", [4, D]); norm_fin = din("norm_fin", [1, D])
    a_norm = din("a_norm", [2, D]); b_norm = din("b_norm", [1, D])
    a_bi = din("a_bi", [2, 8, 1]); a_bf = din("a_bf", [2, 8, 1])
    b_bg = din("b_bg", [1, 128, 4])
    c_bq = din("c_bq", [1, 64, 16]); c_bk = din("c_bk", [1, 64, 4]); c_bkv = din("c_bkv", [1, 512])
    c_snk = din("c_snk", [1, 16])
    f_cw = din("f_cw", [4, 128, NFC * 3]); f_cb = din("f_cb", [4, 128, NFC])
    k_ident = din("k_ident", [128, 128]); k_maskc = din("k_maskc", [128, 128]); k_maskp = din("k_maskp", [128, 128])
    k_ones8 = din("k_ones8", [8, 128]); k_negI8 = din("k_negI8", [8, 8]); k_sel8 = din("k_sel8", [8, 128]); k_selc = din("k_selc", [8, 4])
    k_rm_p = din("k_rm_p", [128, BLK]); k_rm_s = din("k_rm_s", [128, NS])
    si_c = din("si_c", [2, NSEQ, 64, 8 * 128]); si_n = din("si_n", [2, NSEQ, 64, 8]); si_m = din("si_m", [2, 8, NSEQ])
    si_g = din("si_g", [1, NSEQ, 128, 4 * 256]); si_k = din("si_k", [1, NSEQ, 128, 256]); si_v = din("si_v", [1, NSEQ, 128, 256])
    si_f = din("si_f", [4, 128, NFC * NSEQ * 2])
    yp = dout("yp", [TP, D]); ys = dout("ys", [NS, D])
    po_c = dout("po_c", [2, 64, 8 * 128]); po_n = dout("po_n", [2, 64, 8]); po_m = dout("po_m", [2, 8, 1])
    po_g = dout("po_g", [1, 128, 4 * 256]); po_k = dout("po_k", [1, 128, 256]); po_v = dout("po_v", [1, 128, 256])
    po_f = dout("po_f", [4, 128, NFC * 2])
    so_c = dout("so_c", [2, NSEQ, 64, 8 * 128]); so_n = dout("so_n", [2, NSEQ, 64, 8]); so_m = dout("so_m", [2, 8, NSEQ])
    so_g = dout("so_g", [1, NSEQ, 128, 4 * 256]); so_k = dout("so_k", [1, NSEQ, 128, 256]); so_v = dout("so_v", [1, NSEQ, 128, 256])
    so_f = dout("so_f", [4, 128, NFC * NSEQ * 2])
    res_p = nc.dram_tensor("res_p", [TP, D], F32, kind="Internal").ap()
    res_s = nc.dram_tensor("res_s", [NS, D], F32, kind="Internal").ap()

    OUT = fw.region("outputs")
    out_toks = []

    def dma_out(dst, src_ap, srcbuf, q="sync"):
        tok = fw.dma(dst, src_ap, reads=[srcbuf], writes=[], q=q)
        out_toks.append(tok)

    WA = fw.sbuf([128, 24704], BF16, "WA")
    WBa = Arena(fw, 22528 * 2, "WB")
    WB = Buf(WBa.raw.t[:, :].bitcast(BF16), "WBw", "sb")
    WBa.off = 4096
    WCa = Arena(fw, 22528 * 2, "WC")
    WC = Buf(WCa.raw.t[:, :].bitcast(BF16), "WCw", "sb")
    ident = fw.sbuf([128, 128], F32, "ident"); identb = fw.sbuf([128, 128], BF16, "identb")
    maskc = fw.sbuf([128, 128], F32, "maskc"); maskp = fw.sbuf([128, 128], F32, "maskp")
    ones8 = fw.sbuf([8, 128], F32, "ones8"); negI8 = fw.sbuf([8, 8], F32, "negI8")
    sel8 = fw.sbuf([8, 128], F32, "sel8"); selc = fw.sbuf([8, 4], F32, "selc")
    rm_p = fw.sbuf([128, BLK], F32, "rm_p"); rm_s = fw.sbuf([128, NS], F32, "rm_s")
    gb = fw.sbuf([128, D], F32, "gb")
    sp = fw.sbuf([128, 576], F32, "sp")
    XB = [fw.sbuf([128, 2, D], F32, "xb%d" % i) for i in range(2)]
    gfin = fw.sbuf([128, D], F32, "gfin")
    xn = fw.sbuf([128, 2, D], BF16, "xn")
    XST = [fw.sbuf([128, 8, BLK], BF16, "xsT%d" % i) for i in range(2)]
    stat = fw.sbuf([128, 64], F32, "stat")
    mhalf = fw.sbuf([128, 2], F32, "mhalf")
    hT = fw.sbuf([128, NFC, BLK], BF16, "hT")
    hsT = Buf(v3(hT.t[:, 0:8, :].rearrange("p a b -> p (a b)"), 8), "hsT", "sb")
    gext = [fw.sbuf([128, BLK + 2 * NSEQ], F32, "gext%d" % i) for i in range(2)]
    cacc = [fw.sbuf([128, BLK], F32, "cacc%d" % i) for i in range(2)]
    halo_p = fw.sbuf([128, NFC * 2], F32, "halo_p")
    halo_s = fw.sbuf([128, NFC * NSEQ * 2], F32, "halo_s")
    cwb = fw.sbuf([128, NFC * 3], F32, "cwb"); cbb = fw.sbuf([128, NFC], F32, "cbb")
    PST = fw.sbuf([128, 1032], F32, "PST")
    Cst = [Buf(v3(PST.t[:64, :], 8), "Cst0", "sb"), None, None]
    Sst = [Buf(v3(PST.t[:, 0:1024], 4), "Sst0", "sb"), None, None]
    kTprev = [Buf(v3(PST.t[:64, 0:256].bitcast(BF16), 4), "kTprev0", "sb"), None, None]
    vprev = [Buf(v3(PST.t[:, 512:642].bitcast(BF16), 4), "vprev0", "sb"), None, None]
    pst_bufs = [Cst[0], Sst[0], kTprev[0], vprev[0]]
    kvraw = [None, None]
    MS = {}
    def ms(ar, name, nfree, dtype=F32):
        MS[name] = ar.take(nfree, dtype, name)
        return MS[name]
    qT = ms(WCa, "qT", 8 * BLK); kT = ms(WCa, "kT", 8 * BLK)
    ktm = ms(WCa, "ktm", 512); vext = ms(WCa, "vext", 8 * 129 + 8)
    Wt = ms(WCa, "Wt", 512); numS = ms(WCa, "numS", 512); hs = ms(WCa, "hs", 1024)
    kw = ms(WCa, "kw", 512); sqs = ms(WCa, "sqs", 256)
    Wt2 = ms(WCa, "Wt2", 512)
    grow = ms(WCa, "grow", 5 * (BLK + NSEQ)); trow = ms(WCa, "trow", 3 * 128 + 16)
    cols = ms(WCa, "cols", 64); glb = ms(WCa, "glb", 16)
    print("WC scratch words", WCa.off, "of", WCa.n)
    gnb = ms(WBa, "gnb", 1024)
    so = ms(WBa, "so", 1024)
    rbd = ms(WBa, "rbd", 1024)
    o_save = WBa.off; WBa.off -= 1024
    lgT = ms(WBa, "lgT", 4 * BLK)
    WBa.off = o_save
    u0 = WBa.off
    for i in (1, 2):
        b_ = ms(WBa, "Cst%d" % i, 1032); Cst[i] = Buf(v3(b_.t[:64, :], 8), b_.name, "sb"); MS[b_.name] = Cst[i]
    u1 = WBa.off
    WBa.off = u0
    for i in (1, 2):
        b_ = ms(WBa, "Sst%d" % i, 1024); Sst[i] = Buf(v3(b_.t, 4), b_.name, "sb"); MS[b_.name] = Sst[i]
    u1 = max(u1, WBa.off)
    WBa.off = u0
    for i in (1, 2):
        b_ = ms(WBa, "kTprev%d" % i, 512); kTprev[i] = Buf(v3(b_.t[:64, 0:256].bitcast(BF16), 4), b_.name, "sb"); MS[b_.name] = kTprev[i]
        b_ = ms(WBa, "vprev%d" % i, 260); vprev[i] = Buf(v3(b_.t[:, 0:130].bitcast(BF16), 4), b_.name, "sb"); MS[b_.name] = vprev[i]
        kvraw[i - 1] = ms(WBa, "kvraw%d" % i, 512)
    WBa.off = max(WBa.off, u1)
    print("WB scratch words", WBa.off, "of", WBa.n)
    ve2 = ms(WBa, "ve2", 516); stbuf = ms(WBa, "stbuf", 516)
    print("WB scratch words (after filler bufs)", WBa.off, "of", WBa.n)
    VE = [Buf(vext.t[:, 0:516], "ve0", "sb"), ve2]
    SOB = [Buf(so.t[:, 0:512], "so0", "sb"), Buf(so.t[:, 512:1024], "so1", "sb")]
    QTB = [Buf(qT.t[:, 0:1024], "qT0", "sb"), Buf(qT.t[:, 1024:2048], "qT1", "sb")]
    KTB = [Buf(kT.t[:, 0:1024], "kT0", "sb"), Buf(kT.t[:, 1024:2048], "kT1", "sb")]
    for b_ in VE[:1] + SOB + QTB + KTB:
        MS[b_.name] = b_
    msb = list(MS.values())

    P = [fw.psum([128, 512], F32, "P%d" % i) for i in range(8)]

    for (b, src) in ((ident, k_ident), (maskc, k_maskc), (maskp, k_maskp), (ones8, k_ones8), (negI8, k_negI8),
                     (sel8, k_sel8), (selc, k_selc), (rm_p, k_rm_p), (rm_s, k_rm_s)):
        fw.dma(b[:, :], src, writes=[b])
    fw.op("vector", lambda e: e.tensor_copy(out=identb[:, :], in_=ident[:, :]), [ident], [identb])
    fw.op("vector", lambda e: e.memset(mhalf[:, :], -0.5), [], [mhalf])

    V = lambda fn, r, w: fw.op("vector", fn, r, w)
    A = lambda fn, r, w: fw.op("scalar", fn, r, w)
    G = lambda fn, r, w: fw.op("gpsimd", fn, r, w)
    T = lambda fn, r, w: fw.op("tensor", fn, r, w)

    def load_w(dst, ncol, src2d, nk, q="gpsimd"):
        view = v3(dst[:, 0:nk * ncol], nk)
        src = src2d.rearrange("(k p) e -> p k e", p=128)
        step = max(1, nk // 8) if nk > 8 else 1
        for k0 in range(0, nk, 2 if nk <= 8 else 4):
            k1 = min(nk, k0 + (2 if nk <= 8 else 4))
            fw.dma(view[:, k0:k1, :], src[:, k0:k1, :], writes=[dst], q=q)
        return view

    class Cx:
        def __init__(self, **kw):
            self.__dict__.update(kw)

    def front_load(cx):
        xb = XB[cx.par]
        col = 0
        for i, R in enumerate(cx.tiles):
            fw.dma(xb[:R, i, :], cx.src[cx.r0 + col:cx.r0 + col + R, :], reads=[cx.reg], writes=[xb])
            col += R

    def front_norm(cx, only=None):
        xb = XB[cx.par]
        for i, R in enumerate(cx.tiles):
            if only is not None and i != only:
                continue
            A(lambda e, i=i, R=R: e.activation(out=xn[:R, i, :], in_=xb[:R, i, :], func=ACT.Square, accum_out=stat[:R, 3 * i:3 * i + 1]), [xb], [xn, stat])
            V(lambda e, i=i, R=R: e.tensor_scalar(out=stat[:R, 3 * i + 1:3 * i + 2], in0=stat[:R, 3 * i:3 * i + 1], scalar1=1.0 / D, scalar2=EPS, op0=ALU.mult, op1=ALU.add), [stat], [stat])
            G(lambda e, i=i, R=R: e.tensor_tensor(out=stat[:R, 3 * i + 2:3 * i + 3], in0=stat[:R, 3 * i + 1:3 * i + 2], in1=mhalf[:R, 0:1], op=ALU.pow), [stat, mhalf], [stat])
            V(lambda e, i=i, R=R: e.scalar_tensor_tensor(out=xn[:R, i, :], in0=xb[:R, i, :], scalar=stat[:R, 3 * i + 2:3 * i + 3], in1=gb[:R, :], op0=ALU.mult, op1=ALU.mult), [xb, stat, gb], [xn])

    def front_T(cx, only=None):
        xsT = XST[cx.par]
        col = 0
        for i, R in enumerate(cx.tiles):
            if only is None or i == only:
                pst = P[7][:, :].bitcast(BF16)
                pst3 = v3(pst, 8)
                for kc in range(8):
                    T(lambda e, kc=kc, R=R, i=i, pst3=pst3: e.transpose(out=pst3[:, kc, :R], in_=xn[:R, i, kc * 128:(kc + 1) * 128], identity=identb[:R, :R]), [xn, identb], [P[7]])
                A(lambda e, R=R, col=col, pst3=pst3: e.activation(out=xsT[:, :, col:col + R], in_=pst3[:, :, :R], func=ACT.Copy), [P[7]], [xsT])
            col += R

    def front_compute(cx):
        front_norm(cx); front_T(cx)

    def proj_fm(cx, ps, M, W, c0, ntok):
        xsT = XST[cx.par]
        for kc in range(8):
            T(lambda e, kc=kc: e.matmul(ps[:M, :ntok], lhsT=W[:, kc, c0:c0 + M], rhs=xsT[:, kc, :ntok], start=(kc == 0), stop=(kc == 7)), [xsT, Wcur[0]], [ps])

    def proj_tm(cx, ps, Tn, col, W, c0, ncol):
        xsT = XST[cx.par]
        for kc in range(8):
            T(lambda e, kc=kc: e.matmul(ps[:Tn, :ncol], lhsT=xsT[:, kc, col:col + Tn], rhs=W[:, kc, c0:c0 + ncol], start=(kc == 0), stop=(kc == 7)), [xsT, Wcur[0]], [ps])

    Wcur = [WA]

    def epilogue_tile(cx, xb, i, R, col):
        if cx.final:
            A(lambda e: e.activation(out=xn[:R, i, :], in_=xb[:R, i, :], func=ACT.Square, accum_out=stat[:R, 56:57]), [xb], [xn, stat])
            A(lambda e: e.activation(out=stat[:R, 57:58], in_=stat[:R, 56:57], func=ACT.Ln, scale=1.0 / D, bias=EPS), [stat], [stat])
            A(lambda e: e.activation(out=stat[:R, 58:59], in_=stat[:R, 57:58], func=ACT.Exp, scale=-0.5), [stat], [stat])
            V(lambda e: e.scalar_tensor_tensor(out=xb[:R, i, :], in0=xb[:R, i, :], scalar=stat[:R, 58:59], in1=gfin[:R, :], op0=ALU.mult, op1=ALU.mult), [xb, stat, gfin], [xb])
            dma_out(cx.fdst[cx.r0 + col:cx.r0 + col + R, :], xb[:R, i, :], xb)
        else:
            fw.dma(cx.dst[cx.r0 + col:cx.r0 + col + R, :], xb[:R, i, :], reads=[xb], writes=[cx.reg])

    def out_proj_store(cx, Wo):
        xb = XB[cx.par]
        col = 0
        for i, R in enumerate(cx.tiles):
            for half in range(2):
                ps = P[half]
                for ec in range(8):
                    T(lambda e, ec=ec, R=R, col=col, half=half, ps=ps: e.matmul(ps[:R, :], lhsT=hsT[:, ec, col:col + R], rhs=Wo[:, ec, half * 512:(half + 1) * 512], start=(ec == 0), stop=(ec == 7)), [hT, WB], [ps])
                V(lambda e, i=i, R=R, half=half, ps=ps: e.tensor_tensor(out=xb[:R, i, half * 512:(half + 1) * 512], in0=ps[:R, :], in1=xb[:R, i, half * 512:(half + 1) * 512], op=ALU.add), [ps, xb], [xb])
            fw.dma(cx.dst[cx.r0 + col:cx.r0 + col + R, :], xb[:R, i, :], reads=[xb], writes=[cx.reg])
            col += R

    def hs_to_hsT(Tn, col):
        for g in range(2):
            ps = P[4 + g]
            ps3 = v3(ps[:, :], 4)
            for e4 in range(4):
                ec = g * 4 + e4
                T(lambda e, ec=ec, e4=e4, ps3=ps3: e.transpose(out=ps3[:, e4, :Tn], in_=hs[:Tn, ec * 128:(ec + 1) * 128], identity=ident[:Tn, :Tn]), [hs, ident], [ps])
            A(lambda e, g=g, ps3=ps3: e.activation(out=hsT[:, g * 4:(g + 1) * 4, col:col + Tn], in_=ps3[:, :, :Tn], func=ACT.Copy), [ps], [hT])

    def head_rmsnorm_gate(Tn, nh, dv, rden_ap, so_ap=None, so_buf=None):
        so_ap = so[:Tn, :] if so_ap is None else so_ap
        so_buf = so if so_buf is None else so_buf
        h3 = v3(hs[:Tn, :], nh)
        sqv = numS.t[:Tn, 0:512].bitcast(BF16)
        V(lambda e: e.tensor_tensor(out=sqv, in0=hs[:Tn, :], in1=hs[:Tn, :], op=ALU.mult), [hs], [numS])
        V(lambda e: e.tensor_reduce(out=stat[:Tn, 8:8 + nh], in_=v3(sqv, nh), axis=AX.X, op=ALU.add), [numS], [stat])
        if rden_ap is not None:
            V(lambda e: e.tensor_tensor(out=stat[:Tn, 16:16 + nh], in0=rden_ap, in1=rden_ap, op=ALU.mult), [stat], [stat])
            V(lambda e: e.tensor_tensor(out=stat[:Tn, 8:8 + nh], in0=stat[:Tn, 8:8 + nh], in1=stat[:Tn, 16:16 + nh], op=ALU.mult), [stat], [stat])
        A(lambda e: e.activation(out=stat[:Tn, 8:8 + nh], in_=stat[:Tn, 8:8 + nh], func=ACT.Ln, scale=1.0 / dv, bias=EPS), [stat], [stat])
        A(lambda e: e.activation(out=stat[:Tn, 8:8 + nh], in_=stat[:Tn, 8:8 + nh], func=ACT.Exp, scale=-0.5), [stat], [stat])
        if rden_ap is not None:
            V(lambda e: e.tensor_tensor(out=stat[:Tn, 8:8 + nh], in0=stat[:Tn, 8:8 + nh], in1=rden_ap, op=ALU.mult), [stat], [stat])
        V(lambda e: e.tensor_tensor(out=h3, in0=h3, in1=stat[:Tn, 8:8 + nh].unsqueeze(2).to_broadcast([Tn, nh, dv]), op=ALU.mult), [hs, stat], [hs])
        V(lambda e: e.tensor_tensor(out=hs[:Tn, :], in0=hs[:Tn, :], in1=so_ap, op=ALU.mult), [hs, so_buf], [hs])

    def flush(lst):
        while lst:
            lst.pop(0)()

    def ensure_items(cx):
        if getattr(cx, 'pend_fm', None) is not None:
            return
        W = v3(WA[:, 0:8 * 3088], 8)
        ntok, Tn = cx.ntok, cx.Tn
        q3 = v3(QTB[cx.par].t[:64, :].bitcast(BF16), 8); k3 = v3(KTB[cx.par].t[:64, :].bitcast(BF16), 8)
        qTc = QTB[cx.par]; kTc = KTB[cx.par]
        fm = []
        for h in range(16):
            def it(h=h):
                ps = P[h % 2]
                proj_fm(cx, ps, 64, W, h * 64, ntok)
                if h < 8:
                    A(lambda e: e.activation(out=q3[:, h, :ntok], in_=ps[:64, :ntok], func=ACT.Copy), [ps], [qTc])
                else:
                    V(lambda e: e.tensor_scalar(out=k3[:, h - 8, :ntok], in0=ps[:64, :ntok], scalar1=0.125, scalar2=None, op0=ALU.mult), [ps], [kTc])
            fm.append(it)
        cx.pend_fm = fm
        cx.pend_tm = []
        for ti in range(cx.ntile):
            tp = (cx.tbase + ti) % 2
            c0 = ti * Tn
            ktm_v = ktm.t[:Tn, tp * 256:(tp + 1) * 256].bitcast(BF16)
            veb = VE[tp]; sob = SOB[tp]
            ve = v3(veb.t[:Tn, 0:516].bitcast(BF16), 8)
            so_v = sob.t[:Tn, :].bitcast(BF16)
            lst = []

            def it_k(c0=c0, ktm_v=ktm_v):
                proj_tm(cx, P[0], Tn, c0, W, 512, 512)
                V(lambda e: e.tensor_scalar(out=ktm_v, in0=P[0][:Tn, :], scalar1=0.125, scalar2=None, op0=ALU.mult), [P[0]], [ktm])
            lst.append(it_k)
            for hf in range(2):
                def it_v(hf=hf, c0=c0, ve=ve, veb=veb):
                    if hf == 0:
                        G(lambda e: e.memset(ve[:, :, 128:129], 1.0), [], [veb])
                    proj_tm(cx, P[1], Tn, c0, W, 1024 + hf * 512, 512)
                    A(lambda e: e.activation(out=ve[:, hf * 4:(hf + 1) * 4, 0:128], in_=v3(P[1][:Tn, :], 4), func=ACT.Copy), [P[1]], [veb])
                lst.append(it_v)
            for hf in range(2):
                def it_o(hf=hf, c0=c0, so_v=so_v, sob=sob):
                    proj_tm(cx, P[hf], Tn, c0, W, 2048 + hf * 512, 512)
                    A(lambda e: e.activation(out=so_v[:, hf * 512:(hf + 1) * 512], in_=P[hf][:Tn, :], func=ACT.Sigmoid), [P[hf]], [sob])
                    G(lambda e: e.tensor_tensor(out=so_v[:, hf * 512:(hf + 1) * 512], in0=so_v[:, hf * 512:(hf + 1) * 512], in1=gnb[:Tn, hf * 512:(hf + 1) * 512], op=ALU.mult), [sob, gnb], [sob])
                lst.append(it_o)
            cx.pend_tm.append(lst)

    def mlstm_block(cx):
        j, r0, tiles, reg, ntok, Tn, ntile, sample = cx.j, cx.r0, cx.tiles, cx.reg, cx.ntok, cx.Tn, cx.ntile, cx.sample
        W = v3(WA[:, 0:8 * 3088], 8)
        Wo = v3(WB[:, 0:8 * 1024], 8)
        if CFG['stop'] <= 1: return
        ensure_items(cx)
        flush(cx.pend_fm)
        q3 = v3(QTB[cx.par].t[:64, :].bitcast(BF16), 8); k3 = v3(KTB[cx.par].t[:64, :].bitcast(BF16), 8)
        qTc = QTB[cx.par]; kTc = KTB[cx.par]
        if CFG['stop'] <= 2: return
        GW = BLK + NSEQ
        igc = grow[:8, 0:ntok]; lf = grow[:8, GW:GW + ntok]; Fc = grow[:8, 2 * GW:2 * GW + ntok]; Mt = grow[:8, 3 * GW:3 * GW + ntok]
        nseg = ntile if sample else 1
        seglen = ntok // nseg
        mext = v3(grow[:8, 4 * GW:4 * GW + nseg * (seglen + 1)], nseg)
        proj_fm(cx, P[0], 8, W, 3072, ntok)
        proj_fm(cx, P[1], 8, W, 3080, ntok)
        A(lambda e: e.activation(out=igc, in_=P[0][:8, :ntok], func=ACT.Tanh, scale=1.0 / 15, bias=sp[:8, 0:1]), [P[0], sp], [grow])
        A(lambda e: e.activation(out=lf, in_=P[1][:8, :ntok], func=ACT.Tanh, scale=1.0 / 15, bias=sp[:8, 1:2]), [P[1], sp], [grow])
        V(lambda e: e.tensor_scalar(out=igc, in0=igc, scalar1=15.0, scalar2=None, op0=ALU.mult), [grow], [grow])
        xg = Fc; ug = Mt
        V(lambda e: e.tensor_scalar(out=xg, in0=lf, scalar1=15.0, scalar2=None, op0=ALU.mult), [grow], [grow])
        V(lambda e: e.scalar_tensor_tensor(out=ug, in0=xg, scalar=-1.0, in1=xg, op0=ALU.mult, op1=ALU.max), [grow], [grow])
        A(lambda e: e.activation(out=ug, in_=ug, func=ACT.Exp, scale=-1.0), [grow], [grow])
        V(lambda e: e.tensor_scalar(out=lf, in0=ug, scalar1=2.0, scalar2=None, op0=ALU.add), [grow], [grow])
        V(lambda e: e.reciprocal(out=lf, in_=lf), [grow], [grow])
        V(lambda e: e.tensor_tensor(out=ug, in0=ug, in1=lf, op=ALU.mult), [grow], [grow])
        V(lambda e: e.tensor_tensor(out=lf, in0=ug, in1=ug, op=ALU.mult), [grow], [grow])
        zp = trow[:8, 0:ntok]
        V(lambda e: e.tensor_scalar(out=zp, in0=lf, scalar1=1.0 / 9, scalar2=None, op0=ALU.mult), [grow], [trow])
        for cc in (1.0 / 7, 1.0 / 5, 1.0 / 3):
            V(lambda e, cc=cc: e.scalar_tensor_tensor(out=zp, in0=zp, scalar=cc, in1=lf, op0=ALU.add, op1=ALU.mult), [trow, grow], [trow])
        V(lambda e: e.scalar_tensor_tensor(out=zp, in0=zp, scalar=1.0, in1=ug, op0=ALU.add, op1=ALU.mult), [trow, grow], [trow])
        V(lambda e: e.tensor_scalar(out=xg, in0=xg, scalar1=0.0, scalar2=None, op0=ALU.min), [grow], [grow])
        V(lambda e: e.scalar_tensor_tensor(out=lf, in0=zp, scalar=-2.0, in1=xg, op0=ALU.mult, op1=ALU.add), [trow, grow], [grow])
        if sample:
            fw.dma(mext[:, :, 0:1], si_m[j].unsqueeze(2), writes=[grow])
        for s in range(nseg):
            V(lambda e, s=s: e.tensor_tensor_scan(out=mext[:, s, 1:1 + seglen], data0=lf[:, s * seglen:(s + 1) * seglen], data1=igc[:, s * seglen:(s + 1) * seglen], initial=mext[:, s, 0:1], op0=ALU.add, op1=ALU.max), [grow], [grow])
        rm = rm_s if sample else rm_p
        V(lambda e: e.tensor_tensor_scan(out=Fc, data0=rm[:8, :ntok], data1=lf, initial=0.0, op0=ALU.mult, op1=ALU.add), [grow, rm], [grow])
        V(lambda e: e.tensor_tensor(out=igc, in0=igc, in1=Fc, op=ALU.subtract), [grow], [grow])
        for s in range(nseg):
            V(lambda e, s=s: e.tensor_tensor(out=Mt[:, s * seglen:(s + 1) * seglen], in0=mext[:, s, 1:1 + seglen], in1=Fc[:, s * seglen:(s + 1) * seglen], op=ALU.subtract), [grow], [grow])
        a_r = igc
        cx.mid1()
        if CFG['stop'] <= 3: return
        def tile(ti):
            c0 = ti * Tn
            if sample:
                st = Cst[1 + ti % 2]
                fw.dma(st[:, :, 0:128], v3(si_c[j, ti], 8), writes=[st])
                fw.dma(st[:, :, 128:129], si_n[j, ti].unsqueeze(2), writes=[st])
                car = mext[:, ti, 0:1]; mt = mext[:, ti, 1:1 + Tn]
            else:
                st = Cst[0]
                car = mext[:, 0, c0:c0 + 1]; mt = mext[:, 0, 1 + c0:1 + c0 + Tn]
            flush(cx.pend_tm[ti])
            tp = (cx.tbase + ti) % 2
            ktm_v = ktm.t[:Tn, tp * 256:(tp + 1) * 256].bitcast(BF16)
            veb = VE[tp]; sob = SOB[tp]
            ve = v3(veb.t[:Tn, 0:516].bitcast(BF16), 8)
            so_v = sob.t[:Tn, :].bitcast(BF16)
            stb = v3(stbuf.t[:64, 0:516].bitcast(BF16), 8)
            A(lambda e: e.activation(out=stb, in_=st[:, :, :], func=ACT.Copy), [st], [stbuf])
            if ti + 1 < ntile:
                srcs = [cx.pend_tm[ti + 1]]
            elif cx.next is not None:
                cx.mid()
                ensure_items(cx.next)
                srcs = [cx.next.pend_fm, cx.next.pend_tm[0]]
            else:
                srcs = []
            npts = [7]

            def fillpt():
                rem = sum(len(l_) for l_ in srcs)
                k = 1 if rem > 0 else 0
                for l_ in srcs:
                    while k > 0 and l_:
                        l_.pop(0)()
                        k -= 1
            if CFG['stop'] <= 4: return
            g_r = trow[:8, 0:Tn]; enm_r = trow[:8, 128:128 + Tn]; wl_r = trow[:8, 256:256 + Tn]; nml = trow[:8, 384:385]
            V(lambda e: e.tensor_scalar(out=nml, in0=Mt[:, c0 + Tn - 1:c0 + Tn], scalar1=-1.0, scalar2=None, op0=ALU.mult), [grow], [trow])
            A(lambda e: e.activation(out=g_r, in_=Mt[:, c0:c0 + Tn], func=ACT.Exp, scale=-1.0, bias=car), [grow], [trow])
            A(lambda e: e.activation(out=enm_r, in_=mt, func=ACT.Exp, scale=-1.0), [grow], [trow])
            A(lambda e: e.activation(out=wl_r, in_=a_r[:, c0:c0 + Tn], func=ACT.Exp, scale=1.0, bias=nml), [grow, trow], [trow])
            px = P[7]
            for qi, row in enumerate((a_r[:, c0:c0 + Tn], g_r, enm_r, wl_r)):
                T(lambda e, qi=qi, row=row: e.transpose(out=px[:Tn, qi * 8:(qi + 1) * 8], in_=row, identity=ident[:8, :8]), [grow, trow, ident], [px])
            V(lambda e: e.tensor_copy(out=cols[:Tn, 0:32], in_=px[:Tn, 0:32]), [px], [cols])
            a_c = cols[:Tn, 0:8]; g_c = cols[:Tn, 8:16]; enm_c = cols[:Tn, 16:24]; wl_c = cols[:Tn, 24:32]
            rb3 = v3(rbd[:8, 0:8 * Tn], 8)
            V(lambda e: e.tensor_tensor(out=rb3, in0=Mt[:, c0:c0 + Tn].unsqueeze(1).to_broadcast([8, 8, Tn]), in1=negI8[:, :].unsqueeze(2).to_broadcast([8, 8, Tn]), op=ALU.mult), [grow, negI8], [rbd])
            V(lambda e: e.tensor_scalar(out=trow[:8, 388:396], in0=negI8[:, :], scalar1=g_r[:, Tn - 1:Tn], scalar2=-1.0, op0=ALU.mult, op1=ALU.mult), [negI8, trow], [trow])
            T(lambda e: e.matmul(px[:64, 40:48], lhsT=ones8[:, 0:64], rhs=trow[:8, 388:396], start=True, stop=True), [ones8, trow], [px])
            V(lambda e: e.tensor_copy(out=glb[:64, 0:8], in_=px[:64, 40:48]), [px], [glb])
            if CFG['stop'] <= 5: return
            kw3 = v3(kw.t[:Tn, 0:256].bitcast(BF16), 8)
            V(lambda e: e.tensor_tensor(out=kw3, in0=v3(ktm_v, 8), in1=wl_c.unsqueeze(2).to_broadcast([Tn, 8, 64]), op=ALU.mult), [ktm, cols], [kw])
            fillpt()
            pD = P[7]
            def bufs(hh):
                if hh == 0:
                    return P[2], P[3], Wt
                return P[6], P[5], Wt2

            def ptbuf(hh):
                if hh == 0:
                    return kw, v3(kw.t[:Tn, 256:512].bitcast(BF16)[:, 0:4 * Tn], 4)
                return sqs, v3(sqs.t[:Tn, 0:256].bitcast(BF16)[:, 0:4 * Tn], 4)

            def halfA(hh):
                psB, psS, Wtb = bufs(hh)
                T(lambda e: e.matmul(psB[:Tn, 0:4 * Tn], lhsT=ones8[:, :Tn], rhs=rbd[:8, hh * 4 * Tn:(hh + 1) * 4 * Tn], start=True, stop=True), [ones8, rbd], [psB])
                for h4 in range(4):
                    h = hh * 4 + h4
                    T(lambda e, h4=h4, h=h: e.matmul(psS[:Tn, h4 * Tn:(h4 + 1) * Tn], lhsT=k3[:, h, c0:c0 + Tn], rhs=q3[:, h, c0:c0 + Tn], start=True, stop=True), [kTc, qTc], [psS])
                W3 = v3(Wtb[:Tn, 0:4 * Tn], 4)
                for h4 in range(4):
                    h = hh * 4 + h4
                    A(lambda e, h=h, h4=h4: e.activation(out=W3[:, h4, :], in_=psB[:Tn, h4 * Tn:(h4 + 1) * Tn], func=ACT.Exp, bias=a_c[:, h:h + 1], scale=1.0), [psB, cols], [Wtb])
                V(lambda e: e.tensor_tensor(out=W3, in0=W3, in1=maskc[:Tn, :Tn].unsqueeze(1).to_broadcast([Tn, 4, Tn]), op=ALU.mult), [Wtb, maskc], [Wtb])
                ptB, PT3 = ptbuf(hh)
                V(lambda e: e.tensor_tensor(out=PT3, in0=v3(psS[:Tn, 0:4 * Tn], 4), in1=W3, op=ALU.mult), [psS, Wtb], [ptB])

            def halfB(hh):
                psN = P[4]; psI = P[5]
                Wtb, W3 = ptbuf(hh)
                for h4 in range(4):
                    h = hh * 4 + h4
                    T(lambda e, h=h, h4=h4: e.matmul(psN[:Tn, h4 * 128:(h4 + 1) * 128], lhsT=W3[:, h4, :], rhs=ve[:, h, 0:128], start=True, stop=True), [Wtb, veb], [psN])
                    T(lambda e, h=h, h4=h4: e.matmul(pD[:Tn, 64 + h:65 + h], lhsT=W3[:, h4, :], rhs=ve[:, h, 128:129], start=True, stop=True), [Wtb, veb], [pD])
                    T(lambda e, h4=h4, h=h: e.matmul(psI[:Tn, h4 * 128:(h4 + 1) * 128], lhsT=q3[:, h, c0:c0 + Tn], rhs=stb[:, h, 0:128], start=True, stop=True), [qTc, stbuf], [psI])
                    T(lambda e, h=h: e.matmul(pD[:Tn, 80 + h:81 + h], lhsT=q3[:, h, c0:c0 + Tn], rhs=stb[:, h, 128:129], start=True, stop=True), [qTc, stbuf], [pD])
                A(lambda e: e.activation(out=numS[:Tn, :], in_=psN[:Tn, :], func=ACT.Copy), [psN], [numS])
                hsl = v3(hs[:Tn, hh * 512:(hh + 1) * 512], 4)
                V(lambda e: e.tensor_tensor(out=hsl, in0=v3(psI[:Tn, :], 4), in1=g_c[:, hh * 4:(hh + 1) * 4].unsqueeze(2).to_broadcast([Tn, 4, 128]), op=ALU.mult), [psI, cols], [hs])
                V(lambda e: e.tensor_tensor(out=hs[:Tn, hh * 512:(hh + 1) * 512], in0=hs[:Tn, hh * 512:(hh + 1) * 512], in1=numS[:Tn, :], op=ALU.add), [hs, numS], [hs])

            halfA(0); fillpt(); halfA(1); fillpt(); halfB(0); fillpt(); halfB(1); fillpt()
            if CFG['stop'] <= 6: return
            V(lambda e: e.tensor_tensor(out=stat[:Tn, 32:40], in0=pD[:Tn, 80:88], in1=g_c, op=ALU.mult), [pD, cols], [stat])
            V(lambda e: e.tensor_tensor(out=stat[:Tn, 24:32], in0=pD[:Tn, 64:72], in1=stat[:Tn, 32:40], op=ALU.add), [pD, stat], [stat])
            V(lambda e: e.scalar_tensor_tensor(out=stat[:Tn, 24:32], in0=stat[:Tn, 24:32], scalar=-1.0, in1=stat[:Tn, 24:32], op0=ALU.mult, op1=ALU.max), [stat], [stat])
            V(lambda e: e.tensor_tensor(out=stat[:Tn, 24:32], in0=stat[:Tn, 24:32], in1=enm_c, op=ALU.max), [stat, cols], [stat])
            V(lambda e: e.reciprocal(out=stat[:Tn, 24:32], in_=stat[:Tn, 24:32]), [stat], [stat])
            head_rmsnorm_gate(Tn, 8, 128, stat[:Tn, 24:32], so_v, sob)
            fillpt()
            hs_to_hsT(Tn, c0)
            if CFG['stop'] <= 7: return
            pUn = P[7]
            for h in range(8):
                psU = P[2 + h // 4]
                T(lambda e, h=h, psU=psU: e.matmul(psU[:64, (h % 4) * 128:(h % 4 + 1) * 128], lhsT=kw3[:, h, :], rhs=ve[:, h, 0:128], start=True, stop=True), [kw, veb], [psU])
                T(lambda e, h=h: e.matmul(pUn[:64, 48 + h:49 + h], lhsT=kw3[:, h, :], rhs=ve[:, h, 128:129], start=True, stop=True), [kw, veb], [pUn])
            fillpt()
            for l_ in srcs:
                flush(l_)
            for h in range(8):
                psU = P[2 + h // 4]
                V(lambda e, h=h, psU=psU: e.scalar_tensor_tensor(out=st[:, h, 0:128], in0=st[:, h, 0:128], scalar=glb[:64, h:h + 1], in1=psU[:64, (h % 4) * 128:(h % 4 + 1) * 128], op0=ALU.mult, op1=ALU.add), [st, glb, psU], [st])
            V(lambda e: e.tensor_tensor(out=st[:, :, 128:129], in0=st[:, :, 128:129], in1=glb[:64, 0:8].unsqueeze(2), op=ALU.mult), [st, glb], [st])
            V(lambda e: e.tensor_tensor(out=st[:, :, 128:129], in0=st[:, :, 128:129], in1=pUn[:64, 48:56].unsqueeze(2), op=ALU.add), [st, pUn], [st])
            if sample:
                dma_out(v3(so_c[j, ti], 8), st[:, :, 0:128], st)
                dma_out(so_n[j, ti].unsqueeze(2), st[:, :, 128:129], st)
        for ti in range(ntile):
            tile(ti)
        if sample:
            dma_out(so_m[j].unsqueeze(2), mext[:, :, Tn:Tn + 1], grow)
        else:
            V(lambda e: e.tensor_copy(out=mext[:, 0, 0:1], in_=mext[:, 0, ntok:ntok + 1]), [grow], [grow])
            if cx.last_block:
                dma_out(po_m[j], mext[:, 0, 0:1], grow)
        out_proj_store(cx, Wo)

    def gla_block(cx):
        j, r0, tiles, reg, ntok, Tn, ntile, sample = cx.j, cx.r0, cx.tiles, cx.reg, cx.ntok, cx.Tn, cx.ntile, cx.sample
        W = v3(WA[:, 0:8 * 3088], 8)
        Wo = v3(WB[:, 0:8 * 1024], 8)
        q3 = v3(qT.t[:, 0:2 * BLK].bitcast(BF16), 4); k3 = v3(kT.t[:, 0:2 * BLK].bitcast(BF16), 4); kl3 = v3(qT[:, 4 * BLK:8 * BLK], 4); lg3 = v3(lgT[:, :], 4)
        for c in range(8):
            ps = P[c % 2]
            proj_fm(cx, ps, 128, W, c * 128, ntok)
            if c < 4:
                V(lambda e, c=c, ps=ps: e.tensor_scalar(out=q3[:, c, :ntok], in0=ps[:, :ntok], scalar1=128.0 ** -0.5, scalar2=None, op0=ALU.mult), [ps], [qT])
            else:
                A(lambda e, c=c, ps=ps: e.activation(out=k3[:, c - 4, :ntok], in_=ps[:, :ntok], func=ACT.Copy), [ps], [kT])
        proj_fm(cx, P[0], 16, W, 3072, ntok)
        zT = grow[:16, 0:ntok]
        V(lambda e: e.tensor_copy(out=zT, in_=P[0][:16, :ntok]), [P[0]], [grow])
        rm = rm_s if sample else rm_p
        for h in range(4):
            ps = P[h % 2]
            T(lambda e, h=h, ps=ps: e.matmul(ps[:, :ntok], lhsT=sp[:16, 16 + h * 128:16 + (h + 1) * 128], rhs=zT, start=True, stop=True), [sp, grow], [ps])
            A(lambda e, h=h, ps=ps: e.activation(out=lg3[:, h, :ntok], in_=ps[:, :ntok], func=ACT.Exp, scale=-1.0, bias=sp[:, 8 + h:9 + h]), [ps, sp], [lgT])
            A(lambda e, h=h: e.activation(out=lg3[:, h, :ntok], in_=lg3[:, h, :ntok], func=ACT.Ln, bias=1.0, scale=1.0), [lgT], [lgT])
            V(lambda e, h=h: e.tensor_scalar(out=lg3[:, h, :ntok], in0=lg3[:, h, :ntok], scalar1=-1.0 / 16, scalar2=None, op0=ALU.mult), [lgT], [lgT])
            V(lambda e, h=h: e.tensor_copy(out=hs[:, h * BLK:h * BLK + ntok], in_=lg3[:, h, :ntok]), [lgT], [hs])
            V(lambda e, h=h: e.tensor_tensor_scan(out=lg3[:, h, :ntok], data0=rm[:, :ntok], data1=hs[:, h * BLK:h * BLK + ntok], initial=0.0, op0=ALU.mult, op1=ALU.add), [hs, rm], [lgT])
        def tile(ti):
            c0 = ti * Tn
            if sample:
                st = Sst[1 + ti % 2]
                fw.dma(st[:, :, :], v3(si_g[j, ti], 4), writes=[st])
            else:
                st = Sst[0]
            bl = cols[:, 32:36]; ebl = cols[:, 36:40]
            V(lambda e: e.tensor_copy(out=bl.unsqueeze(2), in_=lg3[:, :, c0 + Tn - 1:c0 + Tn]), [lgT], [cols])
            A(lambda e: e.activation(out=ebl, in_=bl, func=ACT.Exp), [cols], [cols])
            for h in range(4):
                A(lambda e, h=h: e.activation(out=kl3[:, h, c0:c0 + Tn], in_=lg3[:, h, c0:c0 + Tn], func=ACT.Exp, scale=-1.0, bias=bl[:, h:h + 1]), [lgT, cols], [qT])
            G(lambda e: e.tensor_tensor(out=kl3[:, :, c0:c0 + Tn], in0=kl3[:, :, c0:c0 + Tn], in1=k3[:, :, c0:c0 + Tn], op=ALU.mult), [qT, kT], [qT])
            pk = P[6]
            for h in range(4):
                T(lambda e, h=h: e.transpose(out=pk[:Tn, h * 128:(h + 1) * 128], in_=kl3[:, h, c0:c0 + Tn], identity=ident[:, :]), [qT, ident], [pk])
            A(lambda e: e.activation(out=kw.t[:Tn, 0:256].bitcast(BF16), in_=pk[:Tn, :], func=ACT.Copy), [pk], [kw])
            kl_tm = v3(kw.t[:Tn, 0:256].bitcast(BF16), 4)
            A(lambda e: e.activation(out=v3(Wt[:, 0:4 * Tn], 4), in_=lg3[:, :, c0:c0 + Tn], func=ACT.Exp), [lgT], [Wt])
            V(lambda e: e.tensor_tensor(out=q3[:, :, c0:c0 + Tn], in0=q3[:, :, c0:c0 + Tn], in1=v3(Wt[:, 0:4 * Tn], 4), op=ALU.mult), [qT, Wt], [qT])
            A(lambda e: e.activation(out=v3(Wt[:, 0:4 * Tn], 4), in_=lg3[:, :, c0:c0 + Tn], func=ACT.Exp, scale=-1.0), [lgT], [Wt])
            V(lambda e: e.tensor_tensor(out=k3[:, :, c0:c0 + Tn], in0=k3[:, :, c0:c0 + Tn], in1=v3(Wt[:, 0:4 * Tn], 4), op=ALU.mult), [kT, Wt], [kT])
            vt = v3(vext.t[:Tn, 0:512].bitcast(BF16), 4)
            vtf = vext.t[:Tn, 0:512].bitcast(BF16)
            stb = v3(vext.t[:, 520:1032].bitcast(BF16), 4)
            G(lambda e: e.tensor_copy(out=stb, in_=st[:, :, :]), [st], [vext])
            for hf in range(2):
                proj_tm(cx, P[hf], Tn, c0, W, 1024 + hf * 512, 512)
                A(lambda e, hf=hf: e.activation(out=vtf[:, hf * 512:(hf + 1) * 512], in_=P[hf][:Tn, :], func=ACT.Copy), [P[hf]], [vext])
            for hf in range(2):
                proj_tm(cx, P[hf], Tn, c0, W, 2048 + hf * 512, 512)
                A(lambda e, hf=hf: e.activation(out=so[:Tn, hf * 512:(hf + 1) * 512], in_=P[hf][:Tn, :], func=ACT.Silu), [P[hf]], [so])
            G(lambda e: e.tensor_tensor(out=so[:Tn, :], in0=so[:Tn, :], in1=gnb[:Tn, :], op=ALU.mult), [so, gnb], [so])
            psS = P[3]
            for h in range(4):
                T(lambda e, h=h: e.matmul(psS[:Tn, h * Tn:(h + 1) * Tn], lhsT=k3[:, h, c0:c0 + Tn], rhs=q3[:, h, c0:c0 + Tn], start=True, stop=True), [kT, qT], [psS])
            PT3 = v3(numS.t[:Tn, 0:256].bitcast(BF16)[:, 0:4 * Tn], 4)
            V(lambda e: e.tensor_tensor(out=PT3, in0=v3(psS[:Tn, 0:4 * Tn], 4), in1=maskc[:Tn, :Tn].unsqueeze(1).to_broadcast([Tn, 4, Tn]), op=ALU.mult), [psS, maskc], [numS])
            for h in range(4):
                ps = P[4 + h // 2]
                o0 = (h % 2) * 256
                T(lambda e, h=h, ps=ps, o0=o0: e.matmul(ps[:Tn, o0:o0 + 256], lhsT=PT3[:, h, :], rhs=vt[:, h, :], start=True, stop=False), [numS, vext], [ps])
                T(lambda e, h=h, ps=ps, o0=o0: e.matmul(ps[:Tn, o0:o0 + 256], lhsT=q3[:, h, c0:c0 + Tn], rhs=stb[:, h, :], start=False, stop=True), [qT, vext], [ps])
            A(lambda e: e.activation(out=hs[:Tn, 0:512], in_=P[4][:Tn, :], func=ACT.Copy), [P[4]], [hs])
            V(lambda e: e.tensor_copy(out=hs[:Tn, 512:1024], in_=P[5][:Tn, :]), [P[5]], [hs])
            for h in range(4):
                ps = P[2]
                T(lambda e, h=h, ps=ps: e.matmul(ps[:, 0:256], lhsT=kl_tm[:, h, :], rhs=vt[:, h, :], start=True, stop=True), [kw, vext], [ps])
                V(lambda e, h=h, ps=ps: e.scalar_tensor_tensor(out=st[:, h, :], in0=st[:, h, :], scalar=ebl[:, h:h + 1], in1=ps[:, 0:256], op0=ALU.mult, op1=ALU.add), [st, cols, ps], [st])
            if sample:
                dma_out(v3(so_g[j, ti], 4), st[:, :, :], st)
            head_rmsnorm_gate(Tn, 4, 256, None)
            hs_to_hsT(Tn, c0)
        for ti in range(ntile):
            tile(ti)
            if ti == 0:
                cx.mid1()
        cx.mid()
        out_proj_store(cx, Wo)

    def swa_block(cx):
        j, r0, tiles, reg, ntok, Tn, ntile, sample, last_block = cx.j, cx.r0, cx.tiles, cx.reg, cx.ntok, cx.Tn, cx.ntile, cx.sample, cx.last_block
        W = v3(WA[:, 0:8 * 1536], 8)
        Wo = v3(WB[:, 0:8 * 1024], 8)
        q8 = v3(qT.t[:64, 0:1024].bitcast(BF16), 16); k2 = v3(kT.t[:64, 0:256].bitcast(BF16), 4)
        for h in range(20):
            ps = P[h % 2]
            proj_fm(cx, ps, 64, W, h * 64, ntok)
            if h < 16:
                A(lambda e, h=h, ps=ps: e.activation(out=q8[:, h, :ntok], in_=ps[:64, :ntok], func=ACT.Identity, bias=sp[:64, 528 + h:529 + h], scale=1.0), [ps, sp], [qT])
            else:
                A(lambda e, h=h, ps=ps: e.activation(out=k2[:, h - 16, :ntok], in_=ps[:64, :ntok], func=ACT.Identity, bias=sp[:64, 544 + h - 16:545 + h - 16], scale=1.0), [ps, sp], [kT])
        def tile(ti):
            c0 = ti * Tn
            proj_tm(cx, P[0], Tn, c0, W, 1024, 512)
            V(lambda e: e.tensor_tensor(out=ktm[:Tn, :], in0=P[0][:Tn, :], in1=gnb[:Tn, 0:512], op=ALU.add), [P[0], gnb], [ktm])
            ve = v3(vext.t[:Tn, 0:130].bitcast(BF16), 4)
            if sample:
                V(lambda e: e.memset(vext[:, 0:4 * 65], 0.0), [], [vext])
                V(lambda e: e.memset(numS[:, 0:256], 0.0), [], [numS])
                V(lambda e: e.memset(Wt2[:, 256:512], 0.0), [], [Wt2])
            G(lambda e: e.memset(ve[:, :, 64:65], 1.0), [], [vext])
            G(lambda e: e.tensor_copy(out=ve[:, :, 0:64], in_=v3(ktm[:Tn, 256:512], 4)), [ktm], [vext])
            vefull = v3(vext.t[:, 0:130].bitcast(BF16), 4)
            if sample:
                kp = kTprev[1 + ti % 2]; vp = vprev[1 + ti % 2]; raw = kvraw[ti % 2]
                fw.dma(raw[:, 0:256], si_k[j, ti], writes=[raw])
                fw.dma(raw[:, 256:512], si_v[j, ti], writes=[raw])
                pk = P[6]
                for c in range(4):
                    T(lambda e, c=c: e.transpose(out=pk[:64, c * 128:(c + 1) * 128], in_=raw[:, c * 64:(c + 1) * 64], identity=ident[:, :]), [raw, ident], [pk])
                A(lambda e: e.activation(out=kp[:, :, :], in_=v3(pk[:64, 0:512], 4), func=ACT.Copy), [pk], [kp])
                G(lambda e: e.memset(vp[:, :, 64:65], 1.0), [], [vp])
                G(lambda e: e.tensor_copy(out=vp[:, :, 0:64], in_=v3(raw[:, 256:512], 4)), [raw], [vp])
                has_prev = True
                dma_out(so_k[j, ti, 0:124, :], raw[4:128, 0:256], raw)
                dma_out(so_v[j, ti, 0:124, :], raw[4:128, 256:512], raw)
                dma_out(so_k[j, ti, 124:128, :], ktm[:Tn, 0:256], ktm)
                dma_out(so_v[j, ti, 124:128, :], ktm[:Tn, 256:512], ktm)
            else:
                kp = kTprev[0]; vp = vprev[0]
                has_prev = not (r0 == 0 and ti == 0)
                if last_block and ti == ntile - 1:
                    dma_out(po_k[j], ktm[:Tn, 0:256], ktm)
                    dma_out(po_v[j], ktm[:Tn, 256:512], ktm)
            blocks = ([(kp, vp, 128, maskp)] if has_prev else []) + [(None, None, Tn, maskc)]
            pD = P[7]
            def kvhead(kh):
                PTs = []
                for bi, (kpb, vpb, nk, msk) in enumerate(blocks):
                    psS = P[2 + bi] if kh % 2 == 0 else P[bi]
                    for g in range(4):
                        if kpb is None:
                            T(lambda e, g=g, psS=psS, kh=kh, nk=nk: e.matmul(psS[:nk, g * Tn:(g + 1) * Tn], lhsT=k2[:, kh, c0:c0 + nk], rhs=q8[:, kh * 4 + g, c0:c0 + Tn], start=True, stop=True), [kT, qT], [psS])
                        else:
                            T(lambda e, g=g, psS=psS, kpb=kpb, kh=kh, nk=nk: e.matmul(psS[:nk, g * Tn:(g + 1) * Tn], lhsT=kpb[:, kh, 0:nk], rhs=q8[:, kh * 4 + g, c0:c0 + Tn], start=True, stop=True), [kpb, qT], [psS])
                    if kh % 2 == 0:
                        PTb = Wt if bi == 0 else numS
                        PTv = PTb.t[:, 0:256].bitcast(BF16)
                    else:
                        PTb = Wt2
                        PTv = Wt2.t[:, bi * 256:(bi + 1) * 256].bitcast(BF16)
                    A(lambda e, psS=psS, PTb=PTb, PTv=PTv, nk=nk: e.activation(out=PTv[:nk, 0:4 * Tn], in_=psS[:nk, 0:4 * Tn], func=ACT.Exp, scale=0.125), [psS], [PTb])
                    V(lambda e, PTb=PTb, PTv=PTv, nk=nk, msk=msk: e.tensor_tensor(out=v3(PTv[:nk, 0:4 * Tn], 4), in0=v3(PTv[:nk, 0:4 * Tn], 4), in1=msk[:nk, :Tn].unsqueeze(1).to_broadcast([nk, 4, Tn]), op=ALU.mult), [PTb, msk], [PTb])
                    PTs.append((PTb, PTv, (128 if sample else nk), (vefull if sample else ve) if kpb is None else vpb, vext if kpb is None else vpb))
                for g in range(4):
                    hq = kh * 4 + g
                    psO = P[4 + hq // 8]
                    o0 = (hq % 8) * 64
                    for bi, (PTb, PTv, nk, vv, vbuf) in enumerate(PTs):
                        T(lambda e, g=g, PTb=PTb, PTv=PTv, nk=nk, vv=vv, psO=psO, o0=o0, bi=bi, kh=kh: e.matmul(psO[:Tn, o0:o0 + 64], lhsT=PTv[:nk, g * Tn:(g + 1) * Tn], rhs=vv[:nk, kh, 0:64], start=(bi == 0), stop=(bi == len(PTs) - 1)), [PTb, vbuf], [psO])
                    for bi, (PTb, PTv, nk, vv, vbuf) in enumerate(PTs):
                        T(lambda e, g=g, PTb=PTb, PTv=PTv, nk=nk, vv=vv, hq=hq, bi=bi, kh=kh: e.matmul(pD[:Tn, 96 + hq:97 + hq], lhsT=PTv[:nk, g * Tn:(g + 1) * Tn], rhs=vv[:nk, kh, 64:65], start=(bi == 0), stop=(bi == len(PTs) - 1)), [PTb, vbuf], [pD])
            for kh in range(4):
                kvhead(kh)
            V(lambda e: e.tensor_tensor(out=stat[:Tn, 40:56], in0=pD[:Tn, 96:112], in1=sp[:Tn, 552:568], op=ALU.add), [pD, sp], [stat])
            V(lambda e: e.reciprocal(out=stat[:Tn, 40:56], in_=stat[:Tn, 40:56]), [stat], [stat])
            for hf in range(2):
                V(lambda e, hf=hf: e.tensor_tensor(out=v3(hs[:Tn, hf * 512:(hf + 1) * 512], 8), in0=v3(P[4 + hf][:Tn, :], 8), in1=stat[:Tn, 40 + hf * 8:48 + hf * 8].unsqueeze(2).to_broadcast([Tn, 8, 64]), op=ALU.mult), [P[4 + hf], stat], [hs])
            hs_to_hsT(Tn, c0)
            if not sample:
                G(lambda e: e.tensor_copy(out=kTprev[0][:, :, :], in_=k2[:, :, c0:c0 + Tn]), [kT], [kTprev[0]])
                G(lambda e: e.tensor_copy(out=vprev[0][:, :, :], in_=ve), [vext], [vprev[0]])
        for ti in range(ntile):
            tile(ti)
            if ti == 0:
                cx.mid1()
        cx.mid()
        out_proj_store(cx, Wo)

    def ffn_block(cx):
        layer, r0, tiles, reg, ntok, nseq, halo = cx.layer, cx.r0, cx.tiles, cx.reg, cx.ntok, cx.nseq, cx.halo
        xb = XB[cx.par]; xsT = XST[cx.par]
        Wg = v3(WA[:, 0:8 * DFF], 8); Wu = v3(WB[:, 0:8 * DFF], 8); Wd = v3(WC[:, 0:NFC * D], NFC)
        Tq = ntok // nseq
        h4 = v4(halo[:, :], NFC, nseq)
        def stageA(c):
            psG = P[2 + (c % 2) * 2]; psU = P[3 + (c % 2) * 2]
            for kc in range(8):
                T(lambda e, kc=kc: e.matmul(psG[:, :ntok], lhsT=Wg[:, kc, c * 128:(c + 1) * 128], rhs=xsT[:, kc, :ntok], start=(kc == 0), stop=(kc == 7)), [xsT, WA], [psG])
            for kc in range(8):
                T(lambda e, kc=kc: e.matmul(psU[:, :ntok], lhsT=Wu[:, kc, c * 128:(c + 1) * 128], rhs=xsT[:, kc, :ntok], start=(kc == 0), stop=(kc == 7)), [xsT, WB], [psU])
            ge = gext[c % 2]; ca = cacc[c % 2]
            ge3 = v3(ge[:, 0:nseq * (Tq + 2)], nseq)
            ca3 = v3(ca[:, 0:ntok], nseq)
            G(lambda e: e.tensor_copy(out=ge3[:, :, 0:2], in_=h4[:, c, :, :]), [halo], [ge])
            A(lambda e: e.activation(out=ge3[:, :, 2:2 + Tq], in_=v3(psG[:, :ntok], nseq), func=ACT.Copy), [psG], [ge])
            G(lambda e: e.tensor_copy(out=h4[:, c, :, :], in_=ge3[:, :, Tq:Tq + 2]), [ge], [halo])
            G(lambda e: e.tensor_scalar(out=ca3, in0=ge3[:, :, 0:Tq], scalar1=cwb[:, c * 3:c * 3 + 1], scalar2=cbb[:, c:c + 1], op0=ALU.mult, op1=ALU.add), [ge, cwb, cbb], [ca])

        def stageB(c):
            psU = P[3 + (c % 2) * 2]
            ge = gext[c % 2]; ca = cacc[c % 2]
            ge3 = v3(ge[:, 0:nseq * (Tq + 2)], nseq)
            ca3 = v3(ca[:, 0:ntok], nseq)
            V(lambda e: e.scalar_tensor_tensor(out=ca3, in0=ge3[:, :, 1:1 + Tq], scalar=cwb[:, c * 3 + 1:c * 3 + 2], in1=ca3, op0=ALU.mult, op1=ALU.add), [ge, cwb, ca], [ca])
            V(lambda e: e.scalar_tensor_tensor(out=ca3, in0=ge3[:, :, 2:2 + Tq], scalar=cwb[:, c * 3 + 2:c * 3 + 3], in1=ca3, op0=ALU.mult, op1=ALU.add), [ge, cwb, ca], [ca])
            A(lambda e: e.activation(out=ca[:, 0:ntok], in_=ca[:, 0:ntok], func=ACT.Silu), [ca], [ca])
            V(lambda e: e.tensor_tensor(out=hT[:, c, :ntok], in0=psU[:, :ntok], in1=ca[:, 0:ntok], op=ALU.mult), [psU, ca], [hT])

        for c in range(NFC + 1):
            if c < NFC:
                stageA(c)
            if c >= 1:
                stageB(c - 1)
        cx.midn(0); cx.midn(1)
        col = 0
        for i, R in enumerate(tiles):
            for half in range(2):
                ps = P[half]
                for c in range(NFC):
                    T(lambda e, c=c, R=R, col=col, half=half, ps=ps: e.matmul(ps[:R, :], lhsT=hT[:, c, col:col + R], rhs=Wd[:, c, half * 512:(half + 1) * 512], start=(c == 0), stop=(c == NFC - 1)), [hT, WC], [ps])
                V(lambda e, i=i, R=R, half=half, ps=ps: e.tensor_tensor(out=xb[:R, i, half * 512:(half + 1) * 512], in0=ps[:R, :], in1=xb[:R, i, half * 512:(half + 1) * 512], op=ALU.add), [ps, xb], [xb])
            cx.midt(i)
            if i == len(tiles) - 1 and len(tiles) == 1:
                cx.midt(1)
            epilogue_tile(cx, xb, i, R, col)
            col += R

    NB = TP // BLK
    fw.dma(gfin[:, :], norm_fin[0].partition_broadcast(128), writes=[gfin])

    def run_sublayer(fn, blocks):
        tb = 0
        for i, cx in enumerate(blocks):
            cx.par = i % 2
            cx.tbase = tb
            tb += getattr(cx, 'ntile', 0)
            cx.next = blocks[i + 1] if i + 1 < len(blocks) else None
        if not blocks:
            return
        front_load(blocks[0]); front_compute(blocks[0])
        for i, cx in enumerate(blocks):
            nxt = blocks[i + 1] if i + 1 < len(blocks) else None
            if nxt is not None:
                front_load(nxt)
                cx.mid1 = (lambda nxt=nxt: front_norm(nxt))
                cx.mid = (lambda nxt=nxt: front_T(nxt))
                cx.midn = (lambda i, nxt=nxt: front_norm(nxt, i))
                cx.midt = (lambda i, nxt=nxt: front_T(nxt, i))
            else:
                cx.mid1 = (lambda: None)
                cx.mid = (lambda: None)
                cx.midn = (lambda i: None)
                cx.midt = (lambda i: None)
            fn(cx)

    regs_p = [fw.region("rp%d" % i) for i in range(NB)]
    reg_s = fw.region("rs")
    zero_done = False
    for layer in range(CFG['L']):
        kind = layer % 3; j = layer // 3
        handoff([WC, WB], msb)
        handoff(pst_bufs, pst_bufs)
        w_in = (a_w_in, b_w_in, c_w_in)[kind][j]; w_out = (a_w_out, b_w_out, c_w_out)[kind][j]
        ncol = 1536 if kind == 2 else 3088
        load_w(WA, ncol, w_in, 8)
        load_w(WB, 1024, w_out, 8)
        fw.dma(gb[:, :], norm_mix[layer].partition_broadcast(128), writes=[gb])
        if kind == 0:
            fw.dma(gnb[:, :], a_norm[j].partition_broadcast(128), writes=[gnb])
            fw.dma(sp[:8, 0:1], a_bi[j], writes=[sp]); fw.dma(sp[:8, 1:2], a_bf[j], writes=[sp])
            V(lambda e: e.tensor_scalar(out=sp[:8, 0:2], in0=sp[:8, 0:2], scalar1=1.0 / 15, scalar2=None, op0=ALU.mult), [sp], [sp])
            V(lambda e: e.memset(Cst[0][:, :, :], 0.0), [], [Cst[0]])
            V(lambda e: e.memset(grow[:8, 4 * (BLK + NSEQ):4 * (BLK + NSEQ) + 1], 0.0), [], [grow])
        elif kind == 1:
            fw.dma(gnb[:, :], b_norm[j].partition_broadcast(128), writes=[gnb])
            fw.dma(sp[:, 8:12], b_bg[j], writes=[sp])
            V(lambda e: e.tensor_scalar(out=sp[:, 8:12], in0=sp[:, 8:12], scalar1=-1.0, scalar2=None, op0=ALU.mult), [sp], [sp])
            fw.dma(sp[:16, 16:16 + 512], b_wgu[j], writes=[sp])
            V(lambda e: e.memset(Sst[0][:, :, :], 0.0), [], [Sst[0]])
        else:
            fw.dma(gnb[:, 0:512], c_bkv[j].partition_broadcast(128), writes=[gnb])
            fw.dma(sp[:64, 528:544], c_bq[j], writes=[sp]); fw.dma(sp[:64, 544:548], c_bk[j], writes=[sp])
            fw.dma(sp[:, 552:568], c_snk[j].partition_broadcast(128), writes=[sp])
            A(lambda e: e.activation(out=sp[:, 552:568], in_=sp[:, 552:568], func=ACT.Exp), [sp], [sp])
        Wcur[0] = WA
        blocks = []
        for which in (CFG['parts'] if CFG['mix'] else ''):
            if which == "p":
                src = xp if layer == 0 else res_p
                if kind == 2:
                    nbk = 2 * CFG['nb']
                    for b in range(nbk):
                        blocks.append(Cx(j=j, layer=layer, src=src, dst=res_p, r0=b * 128, tiles=[128], reg=regs_p[b // 2], ntok=128, Tn=128, ntile=1, sample=False, last_block=(b == nbk - 1), final=False))
                else:
                    for b in range(CFG['nb']):
                        blocks.append(Cx(j=j, layer=layer, src=src, dst=res_p, r0=b * BLK, tiles=[128, 128], reg=regs_p[b], ntok=BLK, Tn=128, ntile=2, sample=False, last_block=(b == CFG['nb'] - 1), final=False))
            else:
                src = xsm if layer == 0 else res_s
                blocks.append(Cx(j=j, layer=layer, src=src, dst=res_s, r0=0, tiles=[NS], reg=reg_s, ntok=NS, Tn=4, ntile=NSEQ, sample=True, last_block=True, final=False))
        run_sublayer((mlstm_block, gla_block, swa_block)[kind], blocks)
        if 'p' in CFG['parts'] and CFG['mix']:
            if kind == 0:
                dma_out(v3(po_c[j], 8), Cst[0][:, :, 0:128], Cst[0])
                dma_out(po_n[j].unsqueeze(2), Cst[0][:, :, 128:129], Cst[0])
            elif kind == 1:
                dma_out(v3(po_g[j], 4), Sst[0][:, :, :], Sst[0])
        if not CFG['ffn']:
            continue
        handoff(msb, [WC, WB])
        load_w(WA, DFF, f_w_up[layer][:, 0:DFF], 8)
        load_w(WB, DFF, f_w_up[layer][:, DFF:2 * DFF], 8)
        load_w(WC, D, f_w_down[layer], NFC)
        fw.dma(gb[:, :], norm_ffn[layer].partition_broadcast(128), writes=[gb])
        fw.dma(cwb[:, :], f_cw[layer], writes=[cwb]); fw.dma(cbb[:, :], f_cb[layer], writes=[cbb])
        V(lambda e: e.memset(halo_p[:, :], 0.0), [], [halo_p])
        fw.dma(halo_s[:, :], si_f[layer], writes=[halo_s])
        fin = (layer == CFG['L'] - 1)
        blocks = []
        for b in range(CFG['nb'] if 'p' in CFG['parts'] else 0):
            blocks.append(Cx(j=j, layer=layer, src=(xp if (layer == 0 and not CFG['mix']) else res_p), dst=res_p, fdst=yp, r0=b * BLK, tiles=[128, 128], reg=regs_p[b], ntok=BLK, nseq=1, halo=halo_p, final=fin, po=(b == CFG['nb'] - 1)))
        if 's' in CFG['parts']:
            blocks.append(Cx(j=j, layer=layer, src=(xsm if (layer == 0 and not CFG['mix']) else res_s), dst=res_s, fdst=ys, r0=0, tiles=[NS], reg=reg_s, ntok=NS, nseq=NSEQ, halo=halo_s, final=fin, po=False))
        run_sublayer(ffn_block, blocks)
        dma_out(po_f[layer], halo_p[:, :], halo_p)
        dma_out(so_f[layer], halo_s[:, :], halo_s)
    fw.final_wait(out_toks)
    fw.emit()
    print("instr counts", fw.stats())
    return nc


_NC = [None]


def _lay(x):
    return np.ascontiguousarray(x, dtype=np.float32)


def kernel(**inp):
    inp = {k: np.asarray(v) for k, v in inp.items()}
    if _NC[0] is None:
        _NC[0] = build()
    nc = _NC[0]
    f32 = np.float32
    ii = np.arange(128)
    consts = {
        "k_ident": np.eye(128, dtype=f32),
        "k_maskc": (ii[:, None] <= ii[None, :]).astype(f32),
        "k_maskp": (ii[:, None] >= ii[None, :]).astype(f32),
        "k_ones8": np.ones((8, 128), f32),
        "k_negI8": -np.eye(8, dtype=f32),
        "k_sel8": ((np.arange(8)[:, None] % 2) == (ii[None, :] // 64)).astype(f32),
        "k_selc": ((np.arange(8)[:, None] // 2) == np.arange(4)[None, :]).astype(f32),
        "k_rm_p": np.tile(((np.arange(BLK) % 128) != 0).astype(f32)[None], (128, 1)),
        "k_rm_s": np.tile(((np.arange(NS) % 4) != 0).astype(f32)[None], (128, 1)),
    }
    c_w_in = inp["c_w_in"]
    c_b = inp["c_b_in"]
    c_bq = c_b[:, 0:1024].reshape(1, 16, 64).transpose(0, 2, 1)
    c_bk = c_b[:, 1024:1280].reshape(1, 4, 64).transpose(0, 2, 1)
    shared = {
        "a_w_in": inp["a_w_in"], "a_w_out": inp["a_w_out"], "b_w_in": inp["b_w_in"], "b_w_out": inp["b_w_out"],
        "c_w_in": c_w_in, "c_w_out": inp["c_w_out"], "f_w_up": inp["f_w_up"], "f_w_down": inp["f_w_down"],
        "b_wgu": inp["b_w_gate_up"],
        "norm_mix": inp["norm_mix_g"], "norm_ffn": inp["norm_ffn_g"], "norm_fin": inp["norm_final_g"].reshape(1, D),
        "a_norm": inp["a_norm_g"], "b_norm": inp["b_norm_g"],
        "a_bi": inp["a_b_i"].reshape(2, 8, 1), "a_bf": inp["a_b_f"].reshape(2, 8, 1),
        "b_bg": inp["b_b_gate"].reshape(1, 4, 128).transpose(0, 2, 1),
        "c_bq": c_bq, "c_bk": c_bk, "c_bkv": c_b[:, 1024:1536], "c_snk": inp["c_sinks"],
        "f_cw": inp["f_conv_w"].reshape(4, 3, NFC, 128).transpose(0, 3, 2, 1).reshape(4, 128, NFC * 3),
        "f_cb": inp["f_conv_b"].reshape(4, NFC, 128).transpose(0, 2, 1),
    }
    shared.update(consts)
    shared = {k: _lay(v) for k, v in shared.items()}
    in_maps = []
    for c in range(8):
        sl = slice(c * NSEQ, (c + 1) * NSEQ)
        m = dict(shared)
        m["xp"] = _lay(inp["x_prompt"][c % 4])
        m["xsm"] = _lay(inp["x_sample"][sl].reshape(NS, D))
        C = inp["state_mlstm_c"][:, sl]
        m["si_c"] = _lay(C.transpose(0, 1, 3, 2, 4).reshape(2, NSEQ, 64, 1024))
        n = inp["state_mlstm_n"][:, sl]
        m["si_n"] = _lay(n.transpose(0, 1, 3, 2))
        m["si_m"] = _lay(inp["state_mlstm_m"][:, sl].transpose(0, 2, 1))
        m["si_g"] = _lay(inp["state_gla"][:, sl].transpose(0, 1, 3, 2, 4).reshape(1, NSEQ, 128, 1024))
        m["si_k"] = _lay(inp["cache_swa_k"][:, sl].reshape(1, NSEQ, 128, 256))
        m["si_v"] = _lay(inp["cache_swa_v"][:, sl].reshape(1, NSEQ, 128, 256))
        f = inp["state_ffn_conv"][:, sl]
        m["si_f"] = _lay(f.reshape(4, NSEQ, 2, NFC, 128).transpose(0, 4, 3, 1, 2).reshape(4, 128, NFC * NSEQ * 2))
        in_maps.append(m)
    ncr = CFG.get('cores', 8)
    if os.environ.get('KN_TRACE'):
        res = run_bass_kernel_spmd(nc, in_maps[:ncr], core_ids=list(range(ncr)), trace=True)
        print('EXEC_TIME_NS', res.exec_time_ns, flush=True)
    else:
        res = run_bass_kernel_spmd(nc, in_maps[:ncr], core_ids=list(range(ncr)))
    R = list(res.results)
    while len(R) < 8:
        R.append(R[0])
    B = 4
    y_prompt = np.stack([R[b]["yp"] for b in range(B)])
    y_sample = np.concatenate([R[c]["ys"].reshape(NSEQ, 4, D) for c in range(8)], 0)

    def unC(x):
        return x.reshape(2, 64, 8, 128).transpose(0, 2, 1, 3)

    def unN(x):
        return x.transpose(0, 2, 1)

    p_c = np.stack([unC(R[b]["po_c"]) for b in range(B)], 1)
    p_n = np.stack([unN(R[b]["po_n"]) for b in range(B)], 1)
    p_m = np.stack([R[b]["po_m"].reshape(2, 8) for b in range(B)], 1)
    p_g = np.stack([R[b]["po_g"].reshape(1, 128, 4, 256).transpose(0, 2, 1, 3) for b in range(B)], 1)
    p_k = np.stack([R[b]["po_k"].reshape(1, 128, 4, 64) for b in range(B)], 1)
    p_v = np.stack([R[b]["po_v"].reshape(1, 128, 4, 64) for b in range(B)], 1)
    p_f = np.stack([R[b]["po_f"].reshape(4, 128, NFC, 2).transpose(0, 3, 2, 1).reshape(4, 2, DFF) for b in range(B)], 1)
    s_c = np.concatenate([R[c]["so_c"].reshape(2, NSEQ, 64, 8, 128).transpose(0, 1, 3, 2, 4) for c in range(8)], 1)
    s_n = np.concatenate([R[c]["so_n"].transpose(0, 1, 3, 2) for c in range(8)], 1)
    s_m = np.concatenate([R[c]["so_m"].transpose(0, 2, 1) for c in range(8)], 1)
    s_g = np.concatenate([R[c]["so_g"].reshape(1, NSEQ, 128, 4, 256).transpose(0, 1, 3, 2, 4) for c in range(8)], 1)
    s_k = np.concatenate([R[c]["so_k"].reshape(1, NSEQ, 128, 4, 64) for c in range(8)], 1)
    s_v = np.concatenate([R[c]["so_v"].reshape(1, NSEQ, 128, 4, 64) for c in range(8)], 1)
    s_f = np.concatenate([R[c]["so_f"].reshape(4, 128, NFC, NSEQ, 2).transpose(0, 3, 4, 2, 1).reshape(4, NSEQ, 2, DFF) for c in range(8)], 1)
    outs = (y_prompt, y_sample, p_c, p_n, p_m, p_g, p_k, p_v, p_f, s_c, s_n, s_m, s_g, s_k, s_v, s_f)
    return tuple(np.ascontiguousarray(o, dtype=np.float32) for o in outs)
```

```python
import numpy as np
from contextlib import ExitStack
import concourse.bass as bass
import concourse.mybir as mybir

F32 = mybir.dt.float32
BF16 = mybir.dt.bfloat16
ACT = mybir.ActivationFunctionType
ALU = mybir.AluOpType
AX = mybir.AxisListType

SEM_LIMIT = 30000


class Sem:
    __slots__ = ("h", "issued", "is_dma")

    def __init__(self, h, is_dma):
        self.h = h
        self.issued = 0
        self.is_dma = is_dma


class Buf:
    def __init__(self, t, name, space="sb"):
        self.t = t
        self.name = name
        self.space = space
        self.w = None
        self.r = {}
        self.chan = None
        self.schan = None

    def __getitem__(self, k):
        return self.t[k]


class Eng:
    def __init__(self, name, obj):
        self.name = name
        self.obj = obj
        self.sem = None
        self.prog = []
        self.seen = {}


class FW:
    def __init__(self, nc):
        self.nc = nc
        self.es = ExitStack()
        self.eng = {}
        for n in ("tensor", "vector", "scalar", "gpsimd", "sync"):
            self.eng[n] = Eng(n, getattr(nc, n))
        self.nsem = 0
        self.nbuf = 0
        self.out_tokens = []

    def new_sem(self, is_dma):
        self.nsem += 1
        h = self.es.enter_context(self.nc.semaphore("s%d" % self.nsem))
        return Sem(h, is_dma)

    def sbuf(self, shape, dtype=F32, name=None):
        self.nbuf += 1
        name = "%s_%d" % (name or "sb", self.nbuf)
        t = self.es.enter_context(self.nc.sbuf_tensor(name, list(shape), dtype))
        return Buf(t, name)

    def psum(self, shape, dtype=F32, name=None):
        self.nbuf += 1
        name = "%s_%d" % (name or "ps", self.nbuf)
        t = self.es.enter_context(self.nc.psum_tensor(name, list(shape), dtype))
        return Buf(t, name, "ps")

    def dram(self, name, shape, dtype=F32, kind="Internal"):
        t = self.nc.dram_tensor(name, list(shape), dtype, kind=kind)
        return Buf(t.ap(), name, "dram")

    def region(self, name):
        return Buf(None, name, "dram")

    def _needs(self, E, reads, writes):
        needs = {}

        def need(tok):
            if tok is None:
                return
            s, v = tok
            if s.is_dma:
                v = s.issued
            if needs.get(s, 0) < v:
                needs[s] = v

        for b in reads:
            need(b.w)
        for b in writes:
            need(b.w)
            for s, v in b.r.items():
                need((s, v))
        waits = []
        for s, v in needs.items():
            if E.name == "tensor" and s is E.sem:
                continue
            if E.seen.get(s, 0) >= v:
                continue
            E.seen[s] = v
            waits.append((s.h, v))
        return waits

    def op(self, eng, fn, reads=(), writes=()):
        E = self.eng[eng]
        if E.sem is None or E.sem.issued >= SEM_LIMIT:
            E.sem = self.new_sem(False)
        waits = self._needs(E, reads, writes)
        E.sem.issued += 1
        tok = (E.sem, E.sem.issued)
        E.prog.append((waits, fn, (E.sem.h, 1)))
        for b in reads:
            b.r[tok[0]] = tok[1]
        for b in writes:
            b.w = tok
            b.r = {}
        return tok

    def dma(self, out_ap, in_ap, reads=(), writes=(), q="sync", chan=None, **kw):
        E = self.eng[q]
        waits = self._needs(E, reads, writes)
        if chan is None:
            b = writes[0] if (writes and writes[0].space != "dram") else None
            if b is not None:
                if b.chan is None:
                    b.chan = self.new_sem(True)
                chan = b.chan
            else:
                b = reads[0]
                if b.schan is None:
                    b.schan = self.new_sem(True)
                chan = b.schan
        chan.issued += 16
        tok = (chan, chan.issued)

        def fn(e, out_ap=out_ap, in_ap=in_ap, kw=kw):
            kw2 = dict(kw); kw2.setdefault("allow_slow_non_contiguous", True); return e.dma_start(out=out_ap, in_=in_ap, **kw2)

        E.prog.append((waits, fn, (chan.h, 16)))
        for b in reads:
            b.r[tok[0]] = tok[1]
        for b in writes:
            b.w = tok
            b.r = {}
        return tok

    def final_wait(self, toks, eng="sync"):
        E = self.eng[eng]
        waits = []
        seen = {}
        for s, v in toks:
            if s.is_dma:
                v = s.issued
            if seen.get(s, 0) < v:
                seen[s] = v
        for s, v in seen.items():
            waits.append((s.h, v))
        E.prog.append((waits, None, None))

    def emit(self):
        nc = self.nc
        with nc.Block() as block:
            def run(E):
                def body(e):
                    for waits, fn, inc in E.prog:
                        for h, v in waits:
                            e.wait_ge(h, v)
                        if fn is not None:
                            ins = fn(e)
                            ins.then_inc(inc[0], inc[1])
                return body
            block.tensor(run(self.eng["tensor"]))
            block.vector(run(self.eng["vector"]))
            block.scalar(run(self.eng["scalar"]))
            block.gpsimd(run(self.eng["gpsimd"]))
            block.sync(run(self.eng["sync"]))

    def close(self):
        self.es.close()

    def stats(self):
        return {n: len(E.prog) for n, E in self.eng.items()}, self.nsem

from concourse.bass_utils import run_bass_kernel_spmd

D = 1024
TP = 4096
NS = 64
NSEQ = 16
DFF = 2816
NFC = 22
BLK = 256
EPS = 1e-6


def v3(ap, a):
    return ap.rearrange("p (a b) -> p a b", a=a)


def v4(ap, a, b):
    return ap.rearrange("p (a b c) -> p a b c", a=a, b=b)


class Arena:
    def __init__(self, fw, nbytes, name):
        self.fw = fw
        self.raw = fw.sbuf([128, nbytes // 4], F32, name)
        self.off = 0
        self.n = nbytes // 4

    def take(self, nfree, dtype=F32, name="ar"):
        words = nfree if dtype == F32 else (nfree + 1) // 2
        assert self.off + words <= self.n, (name, self.off, words, self.n)
        ap = self.raw.t[:, self.off:self.off + words]
        self.off += words
        if dtype != F32:
            ap = ap.bitcast(dtype)
        return Buf(ap, name, "sb")


def handoff(frm, to):
    toks = {}
    for b in frm:
        if b.w is not None:
            s, v = b.w
            toks[s] = max(toks.get(s, 0), v)
        for s, v in b.r.items():
            toks[s] = max(toks.get(s, 0), v)
    for b in to:
        for s, v in toks.items():
            b.r[s] = max(b.r.get(s, 0), v)


import os
CFG = {}

def build():
    CFG['L'] = int(os.environ.get('KN_LAYERS', '4')); CFG['parts'] = os.environ.get('KN_PARTS', 'ps'); CFG['ffn'] = int(os.environ.get('KN_FFN', '1')); CFG['nb'] = int(os.environ.get('KN_NB', '16')); CFG['mix'] = int(os.environ.get('KN_MIX', '1')); CFG['stop'] = float(os.environ.get('KN_STOP', '99')); CFG['cores'] = int(os.environ.get('KN_CORES', '8'))
    nc = bass.Bass("TRN2", target_bir_lowering=False)
    fw = FW(nc)

    def din(name, shape):
        return nc.dram_tensor(name, list(shape), F32, kind="ExternalInput").ap()

    def dout(name, shape):
        return nc.dram_tensor(name, list(shape), F32, kind="ExternalOutput").ap()

    xp = din("xp", [TP, D]); xsm = din("xsm", [NS, D])
    a_w_in = din("a_w_in", [2, D, 3088]); a_w_out = din("a_w_out", [2, D, D])
    b_w_in = din("b_w_in", [1, D, 3088]); b_w_out = din("b_w_out", [1, D, D])
    c_w_in = din("c_w_in", [1, D, 1536]); c_w_out = din("c_w_out", [1, D, D])
    f_w_up = din("f_w_up", [4, D, 2 * DFF]); f_w_down = din("f_w_down", [4, DFF, D])
    b_wgu = din("b_wgu", [1, 16, 512])
    norm_mix = din("norm_mix", [4, D]); norm_ffn = din("norm_ffn", [4, D]); norm_fin = din("norm_fin", [1, D])
    a_norm = din("a_norm", [2, D]); b_norm = din("b_norm", [1, D])
    a_bi = din("a_bi", [2, 8, 1]); a_bf = din("a_bf", [2, 8, 1])
    b_bg = din("b_bg", [1, 128, 4])
    c_bq = din("c_bq", [1, 64, 16]); c_bk = din("c_bk", [1, 64, 4]); c_bkv = din("c_bkv", [1, 512])
    c_snk = din("c_snk", [1, 16])
    f_cw = din("f_cw", [4, 128, NFC * 3]); f_cb = din("f_cb", [4, 128, NFC])
    k_ident = din("k_ident", [128, 128]); k_maskc = din("k_maskc", [128, 128]); k_maskp = din("k_maskp", [128, 128])
    k_ones8 = din("k_ones8", [8, 128]); k_negI8 = din("k_negI8", [8, 8]); k_sel8 = din("k_sel8", [8, 128]); k_selc = din("k_selc", [8, 4])
    k_rm_p = din("k_rm_p", [128, BLK]); k_rm_s = din("k_rm_s", [128, NS])
    si_c = din("si_c", [2, NSEQ, 64, 8 * 128]); si_n = din("si_n", [2, NSEQ, 64, 8]); si_m = din("si_m", [2, 8, NSEQ])
    si_g = din("si_g", [1, NSEQ, 128, 4 * 256]); si_k = din("si_k", [1, NSEQ, 128, 256]); si_v = din("si_v", [1, NSEQ, 128, 256])
    si_f = din("si_f", [4, 128, NFC * NSEQ * 2])
    yp = dout("yp", [TP, D]); ys = dout("ys", [NS, D])
    po_c = dout("po_c", [2, 64, 8 * 128]); po_n = dout("po_n", [2, 64, 8]); po_m = dout("po_m", [2, 8, 1])
    po_g = dout("po_g", [1, 128, 4 * 256]); po_k = dout("po_k", [1, 128, 256]); po_v = dout("po_v", [1, 128, 256])
    po_f = dout("po_f", [4, 128, NFC * 2])
    so_c = dout("so_c", [2, NSEQ, 64, 8 * 128]); so_n = dout("so_n", [2, NSEQ, 64, 8]); so_m = dout("so_m", [2, 8, NSEQ])
    so_g = dout("so_g", [1, NSEQ, 128, 4 * 256]); so_k = dout("so_k", [1, NSEQ, 128, 256]); so_v = dout("so_v", [1, NSEQ, 128, 256])
    so_f = dout("so_f", [4, 128, NFC * NSEQ * 2])
    res_p = nc.dram_tensor("res_p", [TP, D], F32, kind="Internal").ap()
    res_s = nc.dram_tensor("res_s", [NS, D], F32, kind="Internal").ap()

    OUT = fw.region("outputs")
    out_toks = []

    def dma_out(dst, src_ap, srcbuf, q="sync"):
        tok = fw.dma(dst, src_ap, reads=[srcbuf], writes=[], q=q)
        out_toks.append(tok)

    WA = fw.sbuf([128, 24704], BF16, "WA")
    WBa = Arena(fw, 22528 * 2, "WB")
    WB = Buf(WBa.raw.t[:, :].bitcast(BF16), "WBw", "sb")
    WBa.off = 4096
    WCa = Arena(fw, 22528 * 2, "WC")
    WC = Buf(WCa.raw.t[:, :].bitcast(BF16), "WCw", "sb")
    ident = fw.sbuf([128, 128], F32, "ident"); identb = fw.sbuf([128, 128], BF16, "identb")
    maskc = fw.sbuf([128, 128], F32, "maskc"); maskp = fw.sbuf([128, 128], F32, "maskp")
    ones8 = fw.sbuf([8, 128], F32, "ones8"); negI8 = fw.sbuf([8, 8], F32, "negI8")
    sel8 = fw.sbuf([8, 128], F32, "sel8"); selc = fw.sbuf([8, 4], F32, "selc")
    rm_p = fw.sbuf([128, BLK], F32, "rm_p"); rm_s = fw.sbuf([128, NS], F32, "rm_s")
    gb = fw.sbuf([128, D], F32, "gb")
    sp = fw.sbuf([128, 576], F32, "sp")
    XB = [fw.sbuf([128, 2, D], F32, "xb%d" % i) for i in range(2)]
    gfin = fw.sbuf([128, D], F32, "gfin")
    xn = fw.sbuf([128, 2, D], BF16, "xn")
    XST = [fw.sbuf([128, 8, BLK], BF16, "xsT%d" % i) for i in range(2)]
    stat = fw.sbuf([128, 64], F32, "stat")
    mhalf = fw.sbuf([128, 2], F32, "mhalf")
    hT = fw.sbuf([128, NFC, BLK], BF16, "hT")
    hsT = Buf(v3(hT.t[:, 0:8, :].rearrange("p a b -> p (a b)"), 8), "hsT", "sb")
    gext = [fw.sbuf([128, BLK + 2 * NSEQ], F32, "gext%d" % i) for i in range(2)]
    cacc = [fw.sbuf([128, BLK], F32, "cacc%d" % i) for i in range(2)]
    halo_p = fw.sbuf([128, NFC * 2], F32, "halo_p")
    halo_s = fw.sbuf([128, NFC * NSEQ * 2], F32, "halo_s")
    cwb = fw.sbuf([128, NFC * 3], F32, "cwb"); cbb = fw.sbuf([128, NFC], F32, "cbb")
    PST = fw.sbuf([128, 1032], F32, "PST")
    Cst = [Buf(v3(PST.t[:64, :], 8), "Cst0", "sb"), None, None]
    Sst = [Buf(v3(PST.t[:, 0:1024], 4), "Sst0", "sb"), None, None]
    kTprev = [Buf(v3(PST.t[:64, 0:256].bitcast(BF16), 4), "kTprev0", "sb"), None, None]
    vprev = [Buf(v3(PST.t[:, 512:642].bitcast(BF16), 4), "vprev0", "sb"), None, None]
    pst_bufs = [Cst[0], Sst[0], kTprev[0], vprev[0]]
    kvraw = [None, None]
    MS = {}
    def ms(ar, name, nfree, dtype=F32):
        MS[name] = ar.take(nfree, dtype, name)
        return MS[name]
    qT = ms(WCa, "qT", 8 * BLK); kT = ms(WCa, "kT", 8 * BLK)
    ktm = ms(WCa, "ktm", 512); vext = ms(WCa, "vext", 8 * 129 + 8)
    Wt = ms(WCa, "Wt", 512); numS = ms(WCa, "numS", 512); hs = ms(WCa, "hs", 1024)
    kw = ms(WCa, "kw", 512); sqs = ms(WCa, "sqs", 256)
    Wt2 = ms(WCa, "Wt2", 512)
    grow = ms(WCa, "grow", 5 * (BLK + NSEQ)); trow = ms(WCa, "trow", 3 * 128 + 16)
    cols = ms(WCa, "cols", 64); glb = ms(WCa, "glb", 16)
    print("WC scratch words", WCa.off, "of", WCa.n)
    gnb = ms(WBa, "gnb", 1024)
    so = ms(WBa, "so", 1024)
    rbd = ms(WBa, "rbd", 1024)
    o_save = WBa.off; WBa.off -= 1024
    lgT = ms(WBa, "lgT", 4 * BLK)
    WBa.off = o_save
    u0 = WBa.off
    for i in (1, 2):
        b_ = ms(WBa, "Cst%d" % i, 1032); Cst[i] = Buf(v3(b_.t[:64, :], 8), b_.name, "sb"); MS[b_.name] = Cst[i]
    u1 = WBa.off
    WBa.off = u0
    for i in (1, 2):
        b_ = ms(WBa, "Sst%d" % i, 1024); Sst[i] = Buf(v3(b_.t, 4), b_.name, "sb"); MS[b_.name] = Sst[i]
    u1 = max(u1, WBa.off)
    WBa.off = u0
    for i in (1, 2):
        b_ = ms(WBa, "kTprev%d" % i, 512); kTprev[i] = Buf(v3(b_.t[:64, 0:256].bitcast(BF16), 4), b_.name, "sb"); MS[b_.name] = kTprev[i]
        b_ = ms(WBa, "vprev%d" % i, 260); vprev[i] = Buf(v3(b_.t[:, 0:130].bitcast(BF16), 4), b_.name, "sb"); MS[b_.name] = vprev[i]
        kvraw[i - 1] = ms(WBa, "kvraw%d" % i, 512)
    WBa.off = max(WBa.off, u1)
    print("WB scratch words", WBa.off, "of", WBa.n)
    ve2 = ms(WBa, "ve2", 516); stbuf = ms(WBa, "stbuf", 516)
    print("WB scratch words (after filler bufs)", WBa.off, "of", WBa.n)
    VE = [Buf(vext.t[:, 0:516], "ve0", "sb"), ve2]
    SOB = [Buf(so.t[:, 0:512], "so0", "sb"), Buf(so.t[:, 512:1024], "so1", "sb")]
    QTB = [Buf(qT.t[:, 0:1024], "qT0", "sb"), Buf(qT.t[:, 1024:2048], "qT1", "sb")]
    KTB = [Buf(kT.t[:, 0:1024], "kT0", "sb"), Buf(kT.t[:, 1024:2048], "kT1", "sb")]
    for b_ in VE[:1] + SOB + QTB + KTB:
        MS[b_.name] = b_
    msb = list(MS.values())

    P = [fw.psum([128, 512], F32, "P%d" % i) for i in range(8)]

    for (b, src) in ((ident, k_ident), (maskc, k_maskc), (maskp, k_maskp), (ones8, k_ones8), (negI8, k_negI8),
                     (sel8, k_sel8), (selc, k_selc), (rm_p, k_rm_p), (rm_s, k_rm_s)):
        fw.dma(b[:, :], src, writes=[b])
    fw.op("vector", lambda e: e.tensor_copy(out=identb[:, :], in_=ident[:, :]), [ident], [identb])
    fw.op("vector", lambda e: e.memset(mhalf[:, :], -0.5), [], [mhalf])

    V = lambda fn, r, w: fw.op("vector", fn, r, w)
    A = lambda fn, r, w: fw.op("scalar", fn, r, w)
    G = lambda fn, r, w: fw.op("gpsimd", fn, r, w)
    T = lambda fn, r, w: fw.op("tensor", fn, r, w)

    def load_w(dst, ncol, src2d, nk, q="gpsimd"):
        view = v3(dst[:, 0:nk * ncol], nk)
        src = src2d.rearrange("(k p) e -> p k e", p=128)
        step = max(1, nk // 8) if nk > 8 else 1
        for k0 in range(0, nk, 2 if nk <= 8 else 4):
            k1 = min(nk, k0 + (2 if nk <= 8 else 4))
            fw.dma(view[:, k0:k1, :], src[:, k0:k1, :], writes=[dst], q=q)
        return view

    class Cx:
        def __init__(self, **kw):
            self.__dict__.update(kw)

    def front_load(cx):
        xb = XB[cx.par]
        col = 0
        for i, R in enumerate(cx.tiles):
            fw.dma(xb[:R, i, :], cx.src[cx.r0 + col:cx.r0 + col + R, :], reads=[cx.reg], writes=[xb])
            col += R

    def front_norm(cx, only=None):
        xb = XB[cx.par]
        for i, R in enumerate(cx.tiles):
            if only is not None and i != only:
                continue
            A(lambda e, i=i, R=R: e.activation(out=xn[:R, i, :], in_=xb[:R, i, :], func=ACT.Square, accum_out=stat[:R, 3 * i:3 * i + 1]), [xb], [xn, stat])
            V(lambda e, i=i, R=R: e.tensor_scalar(out=stat[:R, 3 * i + 1:3 * i + 2], in0=stat[:R, 3 * i:3 * i + 1], scalar1=1.0 / D, scalar2=EPS, op0=ALU.mult, op1=ALU.add), [stat], [stat])
            G(lambda e, i=i, R=R: e.tensor_tensor(out=stat[:R, 3 * i + 2:3 * i + 3], in0=stat[:R, 3 * i + 1:3 * i + 2], in1=mhalf[:R, 0:1], op=ALU.pow), [stat, mhalf], [stat])
            V(lambda e, i=i, R=R: e.scalar_tensor_tensor(out=xn[:R, i, :], in0=xb[:R, i, :], scalar=stat[:R, 3 * i + 2:3 * i + 3], in1=gb[:R, :], op0=ALU.mult, op1=ALU.mult), [xb, stat, gb], [xn])

    def front_T(cx, only=None):
        xsT = XST[cx.par]
        col = 0
        for i, R in enumerate(cx.tiles):
            if only is None or i == only:
                pst = P[7][:, :].bitcast(BF16)
                pst3 = v3(pst, 8)
                for kc in range(8):
                    T(lambda e, kc=kc, R=R, i=i, pst3=pst3: e.transpose(out=pst3[:, kc, :R], in_=xn[:R, i, kc * 128:(kc + 1) * 128], identity=identb[:R, :R]), [xn, identb], [P[7]])
                A(lambda e, R=R, col=col, pst3=pst3: e.activation(out=xsT[:, :, col:col + R], in_=pst3[:, :, :R], func=ACT.Copy), [P[7]], [xsT])
            col += R

    def front_compute(cx):
        front_norm(cx); front_T(cx)

    def proj_fm(cx, ps, M, W, c0, ntok):
        xsT = XST[cx.par]
        for kc in range(8):
            T(lambda e, kc=kc: e.matmul(ps[:M, :ntok], lhsT=W[:, kc, c0:c0 + M], rhs=xsT[:, kc, :ntok], start=(kc == 0), stop=(kc == 7)), [xsT, Wcur[0]], [ps])

    def proj_tm(cx, ps, Tn, col, W, c0, ncol):
        xsT = XST[cx.par]
        for kc in range(8):
            T(lambda e, kc=kc: e.matmul(ps[:Tn, :ncol], lhsT=xsT[:, kc, col:col + Tn], rhs=W[:, kc, c0:c0 + ncol], start=(kc == 0), stop=(kc == 7)), [xsT, Wcur[0]], [ps])

    Wcur = [WA]

    def epilogue_tile(cx, xb, i, R, col):
        if cx.final:
            A(lambda e: e.activation(out=xn[:R, i, :], in_=xb[:R, i, :], func=ACT.Square, accum_out=stat[:R, 56:57]), [xb], [xn, stat])
            A(lambda e: e.activation(out=stat[:R, 57:58], in_=stat[:R, 56:57], func=ACT.Ln, scale=1.0 / D, bias=EPS), [stat], [stat])
            A(lambda e: e.activation(out=stat[:R, 58:59], in_=stat[:R, 57:58], func=ACT.Exp, scale=-0.5), [stat], [stat])
            V(lambda e: e.scalar_tensor_tensor(out=xb[:R, i, :], in0=xb[:R, i, :], scalar=stat[:R, 58:59], in1=gfin[:R, :], op0=ALU.mult, op1=ALU.mult), [xb, stat, gfin], [xb])
            dma_out(cx.fdst[cx.r0 + col:cx.r0 + col + R, :], xb[:R, i, :], xb)
        else:
            fw.dma(cx.dst[cx.r0 + col:cx.r0 + col + R, :], xb[:R, i, :], reads=[xb], writes=[cx.reg])

    def out_proj_store(cx, Wo):
        xb = XB[cx.par]
        col = 0
        for i, R in enumerate(cx.tiles):
            for half in range(2):
                ps = P[half]
                for ec in range(8):
                    T(lambda e, ec=ec, R=R, col=col, half=half, ps=ps: e.matmul(ps[:R, :], lhsT=hsT[:, ec, col:col + R], rhs=Wo[:, ec, half * 512:(half + 1) * 512], start=(ec == 0), stop=(ec == 7)), [hT, WB], [ps])
                V(lambda e, i=i, R=R, half=half, ps=ps: e.tensor_tensor(out=xb[:R, i, half * 512:(half + 1) * 512], in0=ps[:R, :], in1=xb[:R, i, half * 512:(half + 1) * 512], op=ALU.add), [ps, xb], [xb])
            fw.dma(cx.dst[cx.r0 + col:cx.r0 + col + R, :], xb[:R, i, :], reads=[xb], writes=[cx.reg])
            col += R

    def hs_to_hsT(Tn, col):
        for g in range(2):
            ps = P[4 + g]
            ps3 = v3(ps[:, :], 4)
            for e4 in range(4):
                ec = g * 4 + e4
                T(lambda e, ec=ec, e4=e4, ps3=ps3: e.transpose(out=ps3[:, e4, :Tn], in_=hs[:Tn, ec * 128:(ec + 1) * 128], identity=ident[:Tn, :Tn]), [hs, ident], [ps])
            A(lambda e, g=g, ps3=ps3: e.activation(out=hsT[:, g * 4:(g + 1) * 4, col:col + Tn], in_=ps3[:, :, :Tn], func=ACT.Copy), [ps], [hT])

    def head_rmsnorm_gate(Tn, nh, dv, rden_ap, so_ap=None, so_buf=None):
        so_ap = so[:Tn, :] if so_ap is None else so_ap
        so_buf = so if so_buf is None else so_buf
        h3 = v3(hs[:Tn, :], nh)
        sqv = numS.t[:Tn, 0:512].bitcast(BF16)
        V(lambda e: e.tensor_tensor(out=sqv, in0=hs[:Tn, :], in1=hs[:Tn, :], op=ALU.mult), [hs], [numS])
        V(lambda e: e.tensor_reduce(out=stat[:Tn, 8:8 + nh], in_=v3(sqv, nh), axis=AX.X, op=ALU.add), [numS], [stat])
        if rden_ap is not None:
            V(lambda e: e.tensor_tensor(out=stat[:Tn, 16:16 + nh], in0=rden_ap, in1=rden_ap, op=ALU.mult), [stat], [stat])
            V(lambda e: e.tensor_tensor(out=stat[:Tn, 8:8 + nh], in0=stat[:Tn, 8:8 + nh], in1=stat[:Tn, 16:16 + nh], op=ALU.mult), [stat], [stat])
        A(lambda e: e.activation(out=stat[:Tn, 8:8 + nh], in_=stat[:Tn, 8:8 + nh], func=ACT.Ln, scale=1.0 / dv, bias=EPS), [stat], [stat])
        A(lambda e: e.activation(out=stat[:Tn, 8:8 + nh], in_=stat[:Tn, 8:8 + nh], func=ACT.Exp, scale=-0.5), [stat], [stat])
        if rden_ap is not None:
            V(lambda e: e.tensor_tensor(out=stat[:Tn, 8:8 + nh], in0=stat[:Tn, 8:8 + nh], in1=rden_ap, op=ALU.mult), [stat], [stat])
        V(lambda e: e.tensor_tensor(out=h3, in0=h3, in1=stat[:Tn, 8:8 + nh].unsqueeze(2).to_broadcast([Tn, nh, dv]), op=ALU.mult), [hs, stat], [hs])
        V(lambda e: e.tensor_tensor(out=hs[:Tn, :], in0=hs[:Tn, :], in1=so_ap, op=ALU.mult), [hs, so_buf], [hs])

    def flush(lst):
        while lst:
            lst.pop(0)()

    def ensure_items(cx):
        if getattr(cx, 'pend_fm', None) is not None:
            return
        W = v3(WA[:, 0:8 * 3088], 8)
        ntok, Tn = cx.ntok, cx.Tn
        q3 = v3(QTB[cx.par].t[:64, :].bitcast(BF16), 8); k3 = v3(KTB[cx.par].t[:64, :].bitcast(BF16), 8)
        qTc = QTB[cx.par]; kTc = KTB[cx.par]
        fm = []
        for h in range(16):
            def it(h=h):
                ps = P[h % 2]
                proj_fm(cx, ps, 64, W, h * 64, ntok)
                if h < 8:
                    A(lambda e: e.activation(out=q3[:, h, :ntok], in_=ps[:64, :ntok], func=ACT.Copy), [ps], [qTc])
                else:
                    V(lambda e: e.tensor_scalar(out=k3[:, h - 8, :ntok], in0=ps[:64, :ntok], scalar1=0.125, scalar2=None, op0=ALU.mult), [ps], [kTc])
            fm.append(it)
        cx.pend_fm = fm
        cx.pend_tm = []
        for ti in range(cx.ntile):
            tp = (cx.tbase + ti) % 2
            c0 = ti * Tn
            ktm_v = ktm.t[:Tn, tp * 256:(tp + 1) * 256].bitcast(BF16)
            veb = VE[tp]; sob = SOB[tp]
            ve = v3(veb.t[:Tn, 0:516].bitcast(BF16), 8)
            so_v = sob.t[:Tn, :].bitcast(BF16)
            lst = []

            def it_k(c0=c0, ktm_v=ktm_v):
                proj_tm(cx, P[0], Tn, c0, W, 512, 512)
                V(lambda e: e.tensor_scalar(out=ktm_v, in0=P[0][:Tn, :], scalar1=0.125, scalar2=None, op0=ALU.mult), [P[0]], [ktm])
            lst.append(it_k)
            for hf in range(2):
                def it_v(hf=hf, c0=c0, ve=ve, veb=veb):
                    if hf == 0:
                        G(lambda e: e.memset(ve[:, :, 128:129], 1.0), [], [veb])
                    proj_tm(cx, P[1], Tn, c0, W, 1024 + hf * 512, 512)
                    A(lambda e: e.activation(out=ve[:, hf * 4:(hf + 1) * 4, 0:128], in_=v3(P[1][:Tn, :], 4), func=ACT.Copy), [P[1]], [veb])
                lst.append(it_v)
            for hf in range(2):
                def it_o(hf=hf, c0=c0, so_v=so_v, sob=sob):
                    proj_tm(cx, P[hf], Tn, c0, W, 2048 + hf * 512, 512)
                    A(lambda e: e.activation(out=so_v[:, hf * 512:(hf + 1) * 512], in_=P[hf][:Tn, :], func=ACT.Sigmoid), [P[hf]], [sob])
                    G(lambda e: e.tensor_tensor(out=so_v[:, hf * 512:(hf + 1) * 512], in0=so_v[:, hf * 512:(hf + 1) * 512], in1=gnb[:Tn, hf * 512:(hf + 1) * 512], op=ALU.mult), [sob, gnb], [sob])
                lst.append(it_o)
            cx.pend_tm.append(lst)

    def mlstm_block(cx):
        j, r0, tiles, reg, ntok, Tn, ntile, sample = cx.j, cx.r0, cx.tiles, cx.reg, cx.ntok, cx.Tn, cx.ntile, cx.sample
        W = v3(WA[:, 0:8 * 3088], 8)
        Wo = v3(WB[:, 0:8 * 1024], 8)
        if CFG['stop'] <= 1: return
        ensure_items(cx)
        flush(cx.pend_fm)
        q3 = v3(QTB[cx.par].t[:64, :].bitcast(BF16), 8); k3 = v3(KTB[cx.par].t[:64, :].bitcast(BF16), 8)
        qTc = QTB[cx.par]; kTc = KTB[cx.par]
        if CFG['stop'] <= 2: return
        GW = BLK + NSEQ
        igc = grow[:8, 0:ntok]; lf = grow[:8, GW:GW + ntok]; Fc = grow[:8, 2 * GW:2 * GW + ntok]; Mt = grow[:8, 3 * GW:3 * GW + ntok]
        nseg = ntile if sample else 1
        seglen = ntok // nseg
        mext = v3(grow[:8, 4 * GW:4 * GW + nseg * (seglen + 1)], nseg)
        proj_fm(cx, P[0], 8, W, 3072, ntok)
        proj_fm(cx, P[1], 8, W, 3080, ntok)
        A(lambda e: e.activation(out=igc, in_=P[0][:8, :ntok], func=ACT.Tanh, scale=1.0 / 15, bias=sp[:8, 0:1]), [P[0], sp], [grow])
        A(lambda e: e.activation(out=lf, in_=P[1][:8, :ntok], func=ACT.Tanh, scale=1.0 / 15, bias=sp[:8, 1:2]), [P[1], sp], [grow])
        V(lambda e: e.tensor_scalar(out=igc, in0=igc, scalar1=15.0, scalar2=None, op0=ALU.mult), [grow], [grow])
        xg = Fc; ug = Mt
        V(lambda e: e.tensor_scalar(out=xg, in0=lf, scalar1=15.0, scalar2=None, op0=ALU.mult), [grow], [grow])
        V(lambda e: e.scalar_tensor_tensor(out=ug, in0=xg, scalar=-1.0, in1=xg, op0=ALU.mult, op1=ALU.max), [grow], [grow])
        A(lambda e: e.activation(out=ug, in_=ug, func=ACT.Exp, scale=-1.0), [grow], [grow])
        V(lambda e: e.tensor_scalar(out=lf, in0=ug, scalar1=2.0, scalar2=None, op0=ALU.add), [grow], [grow])
        V(lambda e: e.reciprocal(out=lf, in_=lf), [grow], [grow])
        V(lambda e: e.tensor_tensor(out=ug, in0=ug, in1=lf, op=ALU.mult), [grow], [grow])
        V(lambda e: e.tensor_tensor(out=lf, in0=ug, in1=ug, op=ALU.mult), [grow], [grow])
        zp = trow[:8, 0:ntok]
        V(lambda e: e.tensor_scalar(out=zp, in0=lf, scalar1=1.0 / 9, scalar2=None, op0=ALU.mult), [grow], [trow])
        for cc in (1.0 / 7, 1.0 / 5, 1.0 / 3):
            V(lambda e, cc=cc: e.scalar_tensor_tensor(out=zp, in0=zp, scalar=cc, in1=lf, op0=ALU.add, op1=ALU.mult), [trow, grow], [trow])
        V(lambda e: e.scalar_tensor_tensor(out=zp, in0=zp, scalar=1.0, in1=ug, op0=ALU.add, op1=ALU.mult), [trow, grow], [trow])
        V(lambda e: e.tensor_scalar(out=xg, in0=xg, scalar1=0.0, scalar2=None, op0=ALU.min), [grow], [grow])
        V(lambda e: e.scalar_tensor_tensor(out=lf, in0=zp, scalar=-2.0, in1=xg, op0=ALU.mult, op1=ALU.add), [trow, grow], [grow])
        if sample:
            fw.dma(mext[:, :, 0:1], si_m[j].unsqueeze(2), writes=[grow])
        for s in range(nseg):
            V(lambda e, s=s: e.tensor_tensor_scan(out=mext[:, s, 1:1 + seglen], data0=lf[:, s * seglen:(s + 1) * seglen], data1=igc[:, s * seglen:(s + 1) * seglen], initial=mext[:, s, 0:1], op0=ALU.add, op1=ALU.max), [grow], [grow])
        rm = rm_s if sample else rm_p
        V(lambda e: e.tensor_tensor_scan(out=Fc, data0=rm[:8, :ntok], data1=lf, initial=0.0, op0=ALU.mult, op1=ALU.add), [grow, rm], [grow])
        V(lambda e: e.tensor_tensor(out=igc, in0=igc, in1=Fc, op=ALU.subtract), [grow], [grow])
        for s in range(nseg):
            V(lambda e, s=s: e.tensor_tensor(out=Mt[:, s * seglen:(s + 1) * seglen], in0=mext[:, s, 1:1 + seglen], in1=Fc[:, s * seglen:(s + 1) * seglen], op=ALU.subtract), [grow], [grow])
        a_r = igc
        cx.mid1()
        if CFG['stop'] <= 3: return
        def tile(ti):
            c0 = ti * Tn
            if sample:
                st = Cst[1 + ti % 2]
                if ti == 0:
                    fw.dma(st[:, :, 0:128], v3(si_c[j, 0], 8), writes=[st])
                    fw.dma(st[:, :, 128:129], si_n[j, 0].unsqueeze(2), writes=[st])
                if ti + 1 < ntile:
                    stn = Cst[1 + (ti + 1) % 2]
                    fw.dma(stn[:, :, 0:128], v3(si_c[j, ti + 1], 8), writes=[stn])
                    fw.dma(stn[:, :, 128:129], si_n[j, ti + 1].unsqueeze(2), writes=[stn])
                car = mext[:, ti, 0:1]; mt = mext[:, ti, 1:1 + Tn]
            else:
                st = Cst[0]
                car = mext[:, 0, c0:c0 + 1]; mt = mext[:, 0, 1 + c0:1 + c0 + Tn]
            flush(cx.pend_tm[ti])
            tp = (cx.tbase + ti) % 2
            ktm_v = ktm.t[:Tn, tp * 256:(tp + 1) * 256].bitcast(BF16)
            veb = VE[tp]; sob = SOB[tp]
            ve = v3(veb.t[:Tn, 0:516].bitcast(BF16), 8)
            so_v = sob.t[:Tn, :].bitcast(BF16)
            stb = v3(stbuf.t[:64, 0:516].bitcast(BF16), 8)
            A(lambda e: e.activation(out=stb, in_=st[:, :, :], func=ACT.Copy), [st], [stbuf])
            if ti + 1 < ntile:
                srcs = [cx.pend_tm[ti + 1]]
            elif cx.next is not None:
                cx.mid()
                ensure_items(cx.next)
                srcs = [cx.next.pend_fm, cx.next.pend_tm[0]]
            else:
                srcs = []
            npts = [7]

            def fillpt():
                rem = sum(len(l_) for l_ in srcs)
                k = 1 if rem > 0 else 0
                for l_ in srcs:
                    while k > 0 and l_:
                        l_.pop(0)()
                        k -= 1
            if CFG['stop'] <= 4: return
            g_r = trow[:8, 0:Tn]; enm_r = trow[:8, 128:128 + Tn]; wl_r = trow[:8, 256:256 + Tn]; nml = trow[:8, 384:385]
            V(lambda e: e.tensor_scalar(out=nml, in0=Mt[:, c0 + Tn - 1:c0 + Tn], scalar1=-1.0, scalar2=None, op0=ALU.mult), [grow], [trow])
            A(lambda e: e.activation(out=g_r, in_=Mt[:, c0:c0 + Tn], func=ACT.Exp, scale=-1.0, bias=car), [grow], [trow])
            A(lambda e: e.activation(out=enm_r, in_=mt, func=ACT.Exp, scale=-1.0), [grow], [trow])
            A(lambda e: e.activation(out=wl_r, in_=a_r[:, c0:c0 + Tn], func=ACT.Exp, scale=1.0, bias=nml), [grow, trow], [trow])
            px = P[7]
            for qi, row in enumerate((a_r[:, c0:c0 + Tn], g_r, enm_r, wl_r)):
                T(lambda e, qi=qi, row=row: e.transpose(out=px[:Tn, qi * 8:(qi + 1) * 8], in_=row, identity=ident[:8, :8]), [grow, trow, ident], [px])
            V(lambda e: e.tensor_copy(out=cols[:Tn, 0:32], in_=px[:Tn, 0:32]), [px], [cols])
            a_c = cols[:Tn, 0:8]; g_c = cols[:Tn, 8:16]; enm_c = cols[:Tn, 16:24]; wl_c = cols[:Tn, 24:32]
            rb3 = v3(rbd[:8, 0:8 * Tn], 8)
            V(lambda e: e.tensor_tensor(out=rb3, in0=Mt[:, c0:c0 + Tn].unsqueeze(1).to_broadcast([8, 8, Tn]), in1=negI8[:, :].unsqueeze(2).to_broadcast([8, 8, Tn]), op=ALU.mult), [grow, negI8], [rbd])
            V(lambda e: e.tensor_scalar(out=trow[:8, 388:396], in0=negI8[:, :], scalar1=g_r[:, Tn - 1:Tn], scalar2=-1.0, op0=ALU.mult, op1=ALU.mult), [negI8, trow], [trow])
            T(lambda e: e.matmul(px[:64, 40:48], lhsT=ones8[:, 0:64], rhs=trow[:8, 388:396], start=True, stop=True), [ones8, trow], [px])
            V(lambda e: e.tensor_copy(out=glb[:64, 0:8], in_=px[:64, 40:48]), [px], [glb])
            if CFG['stop'] <= 5: return
            kw3 = v3(kw.t[:Tn, 0:256].bitcast(BF16), 8)
            V(lambda e: e.tensor_tensor(out=kw3, in0=v3(ktm_v, 8), in1=wl_c.unsqueeze(2).to_broadcast([Tn, 8, 64]), op=ALU.mult), [ktm, cols], [kw])
            fillpt()
            pD = P[7]
            def bufs(hh):
                if hh == 0:
                    return P[2], P[3], Wt
                return P[6], P[5], Wt2

            def ptbuf(hh):
                if hh == 0:
                    return kw, v3(kw.t[:Tn, 256:512].bitcast(BF16)[:, 0:4 * Tn], 4)
                return sqs, v3(sqs.t[:Tn, 0:256].bitcast(BF16)[:, 0:4 * Tn], 4)

            def halfA(hh):
                psB, psS, Wtb = bufs(hh)
                T(lambda e: e.matmul(psB[:Tn, 0:4 * Tn], lhsT=ones8[:, :Tn], rhs=rbd[:8, hh * 4 * Tn:(hh + 1) * 4 * Tn], start=True, stop=True), [ones8, rbd], [psB])
                for h4 in range(4):
                    h = hh * 4 + h4
                    T(lambda e, h4=h4, h=h: e.matmul(psS[:Tn, h4 * Tn:(h4 + 1) * Tn], lhsT=k3[:, h, c0:c0 + Tn], rhs=q3[:, h, c0:c0 + Tn], start=True, stop=True), [kTc, qTc], [psS])
                W3 = v3(Wtb[:Tn, 0:4 * Tn], 4)
                for h4 in range(4):
                    h = hh * 4 + h4
                    A(lambda e, h=h, h4=h4: e.activation(out=W3[:, h4, :], in_=psB[:Tn, h4 * Tn:(h4 + 1) * Tn], func=ACT.Exp, bias=a_c[:, h:h + 1], scale=1.0), [psB, cols], [Wtb])
                V(lambda e: e.tensor_tensor(out=W3, in0=W3, in1=maskc[:Tn, :Tn].unsqueeze(1).to_broadcast([Tn, 4, Tn]), op=ALU.mult), [Wtb, maskc], [Wtb])
                ptB, PT3 = ptbuf(hh)
                V(lambda e: e.tensor_tensor(out=PT3, in0=v3(psS[:Tn, 0:4 * Tn], 4), in1=W3, op=ALU.mult), [psS, Wtb], [ptB])

            def halfB(hh):
                psN = P[4]; psI = P[5]
                Wtb, W3 = ptbuf(hh)
                for h4 in range(4):
                    h = hh * 4 + h4
                    T(lambda e, h=h, h4=h4: e.matmul(psN[:Tn, h4 * 128:(h4 + 1) * 128], lhsT=W3[:, h4, :], rhs=ve[:, h, 0:128], start=True, stop=True), [Wtb, veb], [psN])
                    T(lambda e, h=h, h4=h4: e.matmul(pD[:Tn, 64 + h:65 + h], lhsT=W3[:, h4, :], rhs=ve[:, h, 128:129], start=True, stop=True), [Wtb, veb], [pD])
                    T(lambda e, h4=h4, h=h: e.matmul(psI[:Tn, h4 * 128:(h4 + 1) * 128], lhsT=q3[:, h, c0:c0 + Tn], rhs=stb[:, h, 0:128], start=True, stop=True), [qTc, stbuf], [psI])
                    T(lambda e, h=h: e.matmul(pD[:Tn, 80 + h:81 + h], lhsT=q3[:, h, c0:c0 + Tn], rhs=stb[:, h, 128:129], start=True, stop=True), [qTc, stbuf], [pD])
                A(lambda e: e.activation(out=numS[:Tn, :], in_=psN[:Tn, :], func=ACT.Copy), [psN], [numS])
                hsl = v3(hs[:Tn, hh * 512:(hh + 1) * 512], 4)
                V(lambda e: e.tensor_tensor(out=hsl, in0=v3(psI[:Tn, :], 4), in1=g_c[:, hh * 4:(hh + 1) * 4].unsqueeze(2).to_broadcast([Tn, 4, 128]), op=ALU.mult), [psI, cols], [hs])
                V(lambda e: e.tensor_tensor(out=hs[:Tn, hh * 512:(hh + 1) * 512], in0=hs[:Tn, hh * 512:(hh + 1) * 512], in1=numS[:Tn, :], op=ALU.add), [hs, numS], [hs])

            halfA(0); fillpt(); halfA(1); fillpt(); halfB(0); fillpt(); halfB(1); fillpt()
            if CFG['stop'] <= 6: return
            V(lambda e: e.tensor_tensor(out=stat[:Tn, 32:40], in0=pD[:Tn, 80:88], in1=g_c, op=ALU.mult), [pD, cols], [stat])
            V(lambda e: e.tensor_tensor(out=stat[:Tn, 24:32], in0=pD[:Tn, 64:72], in1=stat[:Tn, 32:40], op=ALU.add), [pD, stat], [stat])
            V(lambda e: e.scalar_tensor_tensor(out=stat[:Tn, 24:32], in0=stat[:Tn, 24:32], scalar=-1.0, in1=stat[:Tn, 24:32], op0=ALU.mult, op1=ALU.max), [stat], [stat])
            V(lambda e: e.tensor_tensor(out=stat[:Tn, 24:32], in0=stat[:Tn, 24:32], in1=enm_c, op=ALU.max), [stat, cols], [stat])
            V(lambda e: e.reciprocal(out=stat[:Tn, 24:32], in_=stat[:Tn, 24:32]), [stat], [stat])
            head_rmsnorm_gate(Tn, 8, 128, stat[:Tn, 24:32], so_v, sob)
            fillpt()
            hs_to_hsT(Tn, c0)
            if CFG['stop'] <= 7: return
            pUn = P[7]
            for h in range(8):
                psU = P[2 + h // 4]
                T(lambda e, h=h, psU=psU: e.matmul(psU[:64, (h % 4) * 128:(h % 4 + 1) * 128], lhsT=kw3[:, h, :], rhs=ve[:, h, 0:128], start=True, stop=True), [kw, veb], [psU])
                T(lambda e, h=h: e.matmul(pUn[:64, 48 + h:49 + h], lhsT=kw3[:, h, :], rhs=ve[:, h, 128:129], start=True, stop=True), [kw, veb], [pUn])
            fillpt()
            for l_ in srcs:
                flush(l_)
            for h in range(8):
                psU = P[2 + h // 4]
                V(lambda e, h=h, psU=psU: e.scalar_tensor_tensor(out=st[:, h, 0:128], in0=st[:, h, 0:128], scalar=glb[:64, h:h + 1], in1=psU[:64, (h % 4) * 128:(h % 4 + 1) * 128], op0=ALU.mult, op1=ALU.add), [st, glb, psU], [st])
            V(lambda e: e.tensor_tensor(out=st[:, :, 128:129], in0=st[:, :, 128:129], in1=glb[:64, 0:8].unsqueeze(2), op=ALU.mult), [st, glb], [st])
            V(lambda e: e.tensor_tensor(out=st[:, :, 128:129], in0=st[:, :, 128:129], in1=pUn[:64, 48:56].unsqueeze(2), op=ALU.add), [st, pUn], [st])
            if sample:
                dma_out(v3(so_c[j, ti], 8), st[:, :, 0:128], st)
                dma_out(so_n[j, ti].unsqueeze(2), st[:, :, 128:129], st)
        for ti in range(ntile):
            tile(ti)
        if sample:
            dma_out(so_m[j].unsqueeze(2), mext[:, :, Tn:Tn + 1], grow)
        else:
            V(lambda e: e.tensor_copy(out=mext[:, 0, 0:1], in_=mext[:, 0, ntok:ntok + 1]), [grow], [grow])
            if cx.last_block:
                dma_out(po_m[j], mext[:, 0, 0:1], grow)
        out_proj_store(cx, Wo)

    def gla_block(cx):
        j, r0, tiles, reg, ntok, Tn, ntile, sample = cx.j, cx.r0, cx.tiles, cx.reg, cx.ntok, cx.Tn, cx.ntile, cx.sample
        W = v3(WA[:, 0:8 * 3088], 8)
        Wo = v3(WB[:, 0:8 * 1024], 8)
        q3 = v3(qT.t[:, 0:2 * BLK].bitcast(BF16), 4); k3 = v3(kT.t[:, 0:2 * BLK].bitcast(BF16), 4); kl3 = v3(qT[:, 4 * BLK:8 * BLK], 4); lg3 = v3(lgT[:, :], 4)
        for c in range(8):
            ps = P[c % 2]
            proj_fm(cx, ps, 128, W, c * 128, ntok)
            if c < 4:
                V(lambda e, c=c, ps=ps: e.tensor_scalar(out=q3[:, c, :ntok], in0=ps[:, :ntok], scalar1=128.0 ** -0.5, scalar2=None, op0=ALU.mult), [ps], [qT])
            else:
                A(lambda e, c=c, ps=ps: e.activation(out=k3[:, c - 4, :ntok], in_=ps[:, :ntok], func=ACT.Copy), [ps], [kT])
        proj_fm(cx, P[0], 16, W, 3072, ntok)
        zT = grow[:16, 0:ntok]
        V(lambda e: e.tensor_copy(out=zT, in_=P[0][:16, :ntok]), [P[0]], [grow])
        rm = rm_s if sample else rm_p
        for h in range(4):
            ps = P[h % 2]
            T(lambda e, h=h, ps=ps: e.matmul(ps[:, :ntok], lhsT=sp[:16, 16 + h * 128:16 + (h + 1) * 128], rhs=zT, start=True, stop=True), [sp, grow], [ps])
            A(lambda e, h=h, ps=ps: e.activation(out=lg3[:, h, :ntok], in_=ps[:, :ntok], func=ACT.Exp, scale=-1.0, bias=sp[:, 8 + h:9 + h]), [ps, sp], [lgT])
            A(lambda e, h=h: e.activation(out=lg3[:, h, :ntok], in_=lg3[:, h, :ntok], func=ACT.Ln, bias=1.0, scale=1.0), [lgT], [lgT])
            V(lambda e, h=h: e.tensor_scalar(out=lg3[:, h, :ntok], in0=lg3[:, h, :ntok], scalar1=-1.0 / 16, scalar2=None, op0=ALU.mult), [lgT], [lgT])
            V(lambda e, h=h: e.tensor_copy(out=hs[:, h * BLK:h * BLK + ntok], in_=lg3[:, h, :ntok]), [lgT], [hs])
            V(lambda e, h=h: e.tensor_tensor_scan(out=lg3[:, h, :ntok], data0=rm[:, :ntok], data1=hs[:, h * BLK:h * BLK + ntok], initial=0.0, op0=ALU.mult, op1=ALU.add), [hs, rm], [lgT])
        def tile(ti):
            c0 = ti * Tn
            if sample:
                st = Sst[1 + ti % 2]
                if ti == 0:
                    fw.dma(st[:, :, :], v3(si_g[j, 0], 4), writes=[st])
                if ti + 1 < ntile:
                    stn = Sst[1 + (ti + 1) % 2]
                    fw.dma(stn[:, :, :], v3(si_g[j, ti + 1], 4), writes=[stn])
            else:
                st = Sst[0]
            bl = cols[:, 32:36]; ebl = cols[:, 36:40]
            V(lambda e: e.tensor_copy(out=bl.unsqueeze(2), in_=lg3[:, :, c0 + Tn - 1:c0 + Tn]), [lgT], [cols])
            A(lambda e: e.activation(out=ebl, in_=bl, func=ACT.Exp), [cols], [cols])
            for h in range(4):
                A(lambda e, h=h: e.activation(out=kl3[:, h, c0:c0 + Tn], in_=lg3[:, h, c0:c0 + Tn], func=ACT.Exp, scale=-1.0, bias=bl[:, h:h + 1]), [lgT, cols], [qT])
            V(lambda e: e.tensor_tensor(out=kl3[:, :, c0:c0 + Tn], in0=kl3[:, :, c0:c0 + Tn], in1=k3[:, :, c0:c0 + Tn], op=ALU.mult), [qT, kT], [qT])
            pk = P[6]
            for h in range(4):
                T(lambda e, h=h: e.transpose(out=pk[:Tn, h * 128:(h + 1) * 128], in_=kl3[:, h, c0:c0 + Tn], identity=ident[:, :]), [qT, ident], [pk])
            A(lambda e: e.activation(out=kw.t[:Tn, 0:256].bitcast(BF16), in_=pk[:Tn, :], func=ACT.Copy), [pk], [kw])
            kl_tm = v3(kw.t[:Tn, 0:256].bitcast(BF16), 4)
            A(lambda e: e.activation(out=v3(Wt[:, 0:4 * Tn], 4), in_=lg3[:, :, c0:c0 + Tn], func=ACT.Exp), [lgT], [Wt])
            V(lambda e: e.tensor_tensor(out=q3[:, :, c0:c0 + Tn], in0=q3[:, :, c0:c0 + Tn], in1=v3(Wt[:, 0:4 * Tn], 4), op=ALU.mult), [qT, Wt], [qT])
            A(lambda e: e.activation(out=v3(Wt[:, 0:4 * Tn], 4), in_=lg3[:, :, c0:c0 + Tn], func=ACT.Exp, scale=-1.0), [lgT], [Wt])
            V(lambda e: e.tensor_tensor(out=k3[:, :, c0:c0 + Tn], in0=k3[:, :, c0:c0 + Tn], in1=v3(Wt[:, 0:4 * Tn], 4), op=ALU.mult), [kT, Wt], [kT])
            vt = v3(vext.t[:Tn, 0:512].bitcast(BF16), 4)
            vtf = vext.t[:Tn, 0:512].bitcast(BF16)
            stb = v3(vext.t[:, 520:1032].bitcast(BF16), 4)
            G(lambda e: e.tensor_copy(out=stb, in_=st[:, :, :]), [st], [vext])
            for hf in range(2):
                proj_tm(cx, P[hf], Tn, c0, W, 1024 + hf * 512, 512)
                A(lambda e, hf=hf: e.activation(out=vtf[:, hf * 512:(hf + 1) * 512], in_=P[hf][:Tn, :], func=ACT.Copy), [P[hf]], [vext])
            for hf in range(2):
                proj_tm(cx, P[hf], Tn, c0, W, 2048 + hf * 512, 512)
                A(lambda e, hf=hf: e.activation(out=so[:Tn, hf * 512:(hf + 1) * 512], in_=P[hf][:Tn, :], func=ACT.Silu), [P[hf]], [so])
            G(lambda e: e.tensor_tensor(out=so[:Tn, :], in0=so[:Tn, :], in1=gnb[:Tn, :], op=ALU.mult), [so, gnb], [so])
            psS = P[3]
            for h in range(4):
                T(lambda e, h=h: e.matmul(psS[:Tn, h * Tn:(h + 1) * Tn], lhsT=k3[:, h, c0:c0 + Tn], rhs=q3[:, h, c0:c0 + Tn], start=True, stop=True), [kT, qT], [psS])
            PT3 = v3(numS.t[:Tn, 0:256].bitcast(BF16)[:, 0:4 * Tn], 4)
            V(lambda e: e.tensor_tensor(out=PT3, in0=v3(psS[:Tn, 0:4 * Tn], 4), in1=maskc[:Tn, :Tn].unsqueeze(1).to_broadcast([Tn, 4, Tn]), op=ALU.mult), [psS, maskc], [numS])
            for h in range(4):
                ps = P[4 + h // 2]
                o0 = (h % 2) * 256
                T(lambda e, h=h, ps=ps, o0=o0: e.matmul(ps[:Tn, o0:o0 + 256], lhsT=PT3[:, h, :], rhs=vt[:, h, :], start=True, stop=False), [numS, vext], [ps])
                T(lambda e, h=h, ps=ps, o0=o0: e.matmul(ps[:Tn, o0:o0 + 256], lhsT=q3[:, h, c0:c0 + Tn], rhs=stb[:, h, :], start=False, stop=True), [qT, vext], [ps])
            A(lambda e: e.activation(out=hs[:Tn, 0:512], in_=P[4][:Tn, :], func=ACT.Copy), [P[4]], [hs])
            V(lambda e: e.tensor_copy(out=hs[:Tn, 512:1024], in_=P[5][:Tn, :]), [P[5]], [hs])
            for h in range(4):
                ps = P[2]
                T(lambda e, h=h, ps=ps: e.matmul(ps[:, 0:256], lhsT=kl_tm[:, h, :], rhs=vt[:, h, :], start=True, stop=True), [kw, vext], [ps])
                V(lambda e, h=h, ps=ps: e.scalar_tensor_tensor(out=st[:, h, :], in0=st[:, h, :], scalar=ebl[:, h:h + 1], in1=ps[:, 0:256], op0=ALU.mult, op1=ALU.add), [st, cols, ps], [st])
            if sample:
                dma_out(v3(so_g[j, ti], 4), st[:, :, :], st)
            head_rmsnorm_gate(Tn, 4, 256, None)
            hs_to_hsT(Tn, c0)
        for ti in range(ntile):
            tile(ti)
            if ti == 0:
                cx.mid1()
        cx.mid()
        out_proj_store(cx, Wo)

    def swa_block(cx):
        j, r0, tiles, reg, ntok, Tn, ntile, sample, last_block = cx.j, cx.r0, cx.tiles, cx.reg, cx.ntok, cx.Tn, cx.ntile, cx.sample, cx.last_block
        W = v3(WA[:, 0:8 * 1536], 8)
        Wo = v3(WB[:, 0:8 * 1024], 8)
        q8 = v3(qT.t[:64, 0:1024].bitcast(BF16), 16); k2 = v3(kT.t[:64, 0:256].bitcast(BF16), 4)
        for h in range(20):
            ps = P[h % 2]
            proj_fm(cx, ps, 64, W, h * 64, ntok)
            if h < 16:
                A(lambda e, h=h, ps=ps: e.activation(out=q8[:, h, :ntok], in_=ps[:64, :ntok], func=ACT.Identity, bias=sp[:64, 528 + h:529 + h], scale=1.0), [ps, sp], [qT])
            else:
                A(lambda e, h=h, ps=ps: e.activation(out=k2[:, h - 16, :ntok], in_=ps[:64, :ntok], func=ACT.Identity, bias=sp[:64, 544 + h - 16:545 + h - 16], scale=1.0), [ps, sp], [kT])
        def tile(ti):
            c0 = ti * Tn
            proj_tm(cx, P[0], Tn, c0, W, 1024, 512)
            V(lambda e: e.tensor_tensor(out=ktm[:Tn, :], in0=P[0][:Tn, :], in1=gnb[:Tn, 0:512], op=ALU.add), [P[0], gnb], [ktm])
            ve = v3(vext.t[:Tn, 0:130].bitcast(BF16), 4)
            if sample:
                V(lambda e: e.memset(vext[:, 0:4 * 65], 0.0), [], [vext])
                V(lambda e: e.memset(numS[:, 0:256], 0.0), [], [numS])
                V(lambda e: e.memset(Wt2[:, 256:512], 0.0), [], [Wt2])
            G(lambda e: e.memset(ve[:, :, 64:65], 1.0), [], [vext])
            G(lambda e: e.tensor_copy(out=ve[:, :, 0:64], in_=v3(ktm[:Tn, 256:512], 4)), [ktm], [vext])
            vefull = v3(vext.t[:, 0:130].bitcast(BF16), 4)
            if sample:
                kp = kTprev[1 + ti % 2]; vp = vprev[1 + ti % 2]; raw = kvraw[ti % 2]
                if ti == 0:
                    fw.dma(raw[:, 0:256], si_k[j, 0], writes=[raw])
                    fw.dma(raw[:, 256:512], si_v[j, 0], writes=[raw])
                if ti + 1 < ntile:
                    rawn = kvraw[(ti + 1) % 2]
                    fw.dma(rawn[:, 0:256], si_k[j, ti + 1], writes=[rawn])
                    fw.dma(rawn[:, 256:512], si_v[j, ti + 1], writes=[rawn])
                pk = P[6]
                for c in range(4):
                    T(lambda e, c=c: e.transpose(out=pk[:64, c * 128:(c + 1) * 128], in_=raw[:, c * 64:(c + 1) * 64], identity=ident[:, :]), [raw, ident], [pk])
                A(lambda e: e.activation(out=kp[:, :, :], in_=v3(pk[:64, 0:512], 4), func=ACT.Copy), [pk], [kp])
                G(lambda e: e.memset(vp[:, :, 64:65], 1.0), [], [vp])
                G(lambda e: e.tensor_copy(out=vp[:, :, 0:64], in_=v3(raw[:, 256:512], 4)), [raw], [vp])
                has_prev = True
                dma_out(so_k[j, ti, 0:124, :], raw[4:128, 0:256], raw)
                dma_out(so_v[j, ti, 0:124, :], raw[4:128, 256:512], raw)
                dma_out(so_k[j, ti, 124:128, :], ktm[:Tn, 0:256], ktm)
                dma_out(so_v[j, ti, 124:128, :], ktm[:Tn, 256:512], ktm)
            else:
                kp = kTprev[0]; vp = vprev[0]
                has_prev = not (r0 == 0 and ti == 0)
                if last_block and ti == ntile - 1:
                    dma_out(po_k[j], ktm[:Tn, 0:256], ktm)
                    dma_out(po_v[j], ktm[:Tn, 256:512], ktm)
            blocks = ([(kp, vp, 128, maskp)] if has_prev else []) + [(None, None, Tn, maskc)]
            pD = P[7]
            def kvhead(kh):
                PTs = []
                for bi, (kpb, vpb, nk, msk) in enumerate(blocks):
                    psS = P[2 + bi] if kh % 2 == 0 else P[bi]
                    for g in range(4):
                        if kpb is None:
                            T(lambda e, g=g, psS=psS, kh=kh, nk=nk: e.matmul(psS[:nk, g * Tn:(g + 1) * Tn], lhsT=k2[:, kh, c0:c0 + nk], rhs=q8[:, kh * 4 + g, c0:c0 + Tn], start=True, stop=True), [kT, qT], [psS])
                        else:
                            T(lambda e, g=g, psS=psS, kpb=kpb, kh=kh, nk=nk: e.matmul(psS[:nk, g * Tn:(g + 1) * Tn], lhsT=kpb[:, kh, 0:nk], rhs=q8[:, kh * 4 + g, c0:c0 + Tn], start=True, stop=True), [kpb, qT], [psS])
                    if kh % 2 == 0:
                        PTb = Wt if bi == 0 else numS
                        PTv = PTb.t[:, 0:256].bitcast(BF16)
                    else:
                        PTb = Wt2
                        PTv = Wt2.t[:, bi * 256:(bi + 1) * 256].bitcast(BF16)
                    A(lambda e, psS=psS, PTb=PTb, PTv=PTv, nk=nk: e.activation(out=PTv[:nk, 0:4 * Tn], in_=psS[:nk, 0:4 * Tn], func=ACT.Exp, scale=0.125), [psS], [PTb])
                    V(lambda e, PTb=PTb, PTv=PTv, nk=nk, msk=msk: e.tensor_tensor(out=v3(PTv[:nk, 0:4 * Tn], 4), in0=v3(PTv[:nk, 0:4 * Tn], 4), in1=msk[:nk, :Tn].unsqueeze(1).to_broadcast([nk, 4, Tn]), op=ALU.mult), [PTb, msk], [PTb])
                    PTs.append((PTb, PTv, (128 if sample else nk), (vefull if sample else ve) if kpb is None else vpb, vext if kpb is None else vpb))
                for g in range(4):
                    hq = kh * 4 + g
                    psO = P[4 + hq // 8]
                    o0 = (hq % 8) * 64
                    for bi, (PTb, PTv, nk, vv, vbuf) in enumerate(PTs):
                        T(lambda e, g=g, PTb=PTb, PTv=PTv, nk=nk, vv=vv, psO=psO, o0=o0, bi=bi, kh=kh: e.matmul(psO[:Tn, o0:o0 + 64], lhsT=PTv[:nk, g * Tn:(g + 1) * Tn], rhs=vv[:nk, kh, 0:64], start=(bi == 0), stop=(bi == len(PTs) - 1)), [PTb, vbuf], [psO])
                    for bi, (PTb, PTv, nk, vv, vbuf) in enumerate(PTs):
                        T(lambda e, g=g, PTb=PTb, PTv=PTv, nk=nk, vv=vv, hq=hq, bi=bi, kh=kh: e.matmul(pD[:Tn, 96 + hq:97 + hq], lhsT=PTv[:nk, g * Tn:(g + 1) * Tn], rhs=vv[:nk, kh, 64:65], start=(bi == 0), stop=(bi == len(PTs) - 1)), [PTb, vbuf], [pD])
            for kh in range(4):
                kvhead(kh)
            V(lambda e: e.tensor_tensor(out=stat[:Tn, 40:56], in0=pD[:Tn, 96:112], in1=sp[:Tn, 552:568], op=ALU.add), [pD, sp], [stat])
            V(lambda e: e.reciprocal(out=stat[:Tn, 40:56], in_=stat[:Tn, 40:56]), [stat], [stat])
            for hf in range(2):
                V(lambda e, hf=hf: e.tensor_tensor(out=v3(hs[:Tn, hf * 512:(hf + 1) * 512], 8), in0=v3(P[4 + hf][:Tn, :], 8), in1=stat[:Tn, 40 + hf * 8:48 + hf * 8].unsqueeze(2).to_broadcast([Tn, 8, 64]), op=ALU.mult), [P[4 + hf], stat], [hs])
            hs_to_hsT(Tn, c0)
            if not sample:
                G(lambda e: e.tensor_copy(out=kTprev[0][:, :, :], in_=k2[:, :, c0:c0 + Tn]), [kT], [kTprev[0]])
                G(lambda e: e.tensor_copy(out=vprev[0][:, :, :], in_=ve), [vext], [vprev[0]])
        for ti in range(ntile):
            tile(ti)
            if ti == 0:
                cx.mid1()
        cx.mid()
        out_proj_store(cx, Wo)

    def ffn_block(cx):
        layer, r0, tiles, reg, ntok, nseq, halo = cx.layer, cx.r0, cx.tiles, cx.reg, cx.ntok, cx.nseq, cx.halo
        xb = XB[cx.par]; xsT = XST[cx.par]
        Wg = v3(WA[:, 0:8 * DFF], 8); Wu = v3(WB[:, 0:8 * DFF], 8); Wd = v3(WC[:, 0:NFC * D], NFC)
        Tq = ntok // nseq
        h4 = v4(halo[:, :], NFC, nseq)
        def stageA(c):
            psG = P[2 + (c % 2) * 2]; psU = P[3 + (c % 2) * 2]
            for kc in range(8):
                T(lambda e, kc=kc: e.matmul(psG[:, :ntok], lhsT=Wg[:, kc, c * 128:(c + 1) * 128], rhs=xsT[:, kc, :ntok], start=(kc == 0), stop=(kc == 7)), [xsT, WA], [psG])
            for kc in range(8):
                T(lambda e, kc=kc: e.matmul(psU[:, :ntok], lhsT=Wu[:, kc, c * 128:(c + 1) * 128], rhs=xsT[:, kc, :ntok], start=(kc == 0), stop=(kc == 7)), [xsT, WB], [psU])
            ge = gext[c % 2]; ca = cacc[c % 2]
            ge3 = v3(ge[:, 0:nseq * (Tq + 2)], nseq)
            ca3 = v3(ca[:, 0:ntok], nseq)
            G(lambda e: e.tensor_copy(out=ge3[:, :, 0:2], in_=h4[:, c, :, :]), [halo], [ge])
            A(lambda e: e.activation(out=ge3[:, :, 2:2 + Tq], in_=v3(psG[:, :ntok], nseq), func=ACT.Copy), [psG], [ge])
            G(lambda e: e.tensor_copy(out=h4[:, c, :, :], in_=ge3[:, :, Tq:Tq + 2]), [ge], [halo])
            G(lambda e: e.tensor_scalar(out=ca3, in0=ge3[:, :, 0:Tq], scalar1=cwb[:, c * 3:c * 3 + 1], scalar2=cbb[:, c:c + 1], op0=ALU.mult, op1=ALU.add), [ge, cwb, cbb], [ca])

        def stageB(c):
            psU = P[3 + (c % 2) * 2]
            ge = gext[c % 2]; ca = cacc[c % 2]
            ge3 = v3(ge[:, 0:nseq * (Tq + 2)], nseq)
            ca3 = v3(ca[:, 0:ntok], nseq)
            V(lambda e: e.scalar_tensor_tensor(out=ca3, in0=ge3[:, :, 1:1 + Tq], scalar=cwb[:, c * 3 + 1:c * 3 + 2], in1=ca3, op0=ALU.mult, op1=ALU.add), [ge, cwb, ca], [ca])
            V(lambda e: e.scalar_tensor_tensor(out=ca3, in0=ge3[:, :, 2:2 + Tq], scalar=cwb[:, c * 3 + 2:c * 3 + 3], in1=ca3, op0=ALU.mult, op1=ALU.add), [ge, cwb, ca], [ca])
            A(lambda e: e.activation(out=ca[:, 0:ntok], in_=ca[:, 0:ntok], func=ACT.Silu), [ca], [ca])
            V(lambda e: e.tensor_tensor(out=hT[:, c, :ntok], in0=psU[:, :ntok], in1=ca[:, 0:ntok], op=ALU.mult), [psU, ca], [hT])

        for c in range(NFC + 1):
            if c < NFC:
                stageA(c)
            if c >= 1:
                stageB(c - 1)
        cx.midn(0); cx.midn(1)
        col = 0
        for i, R in enumerate(tiles):
            for half in range(2):
                ps = P[half]
                for c in range(NFC):
                    T(lambda e, c=c, R=R, col=col, half=half, ps=ps: e.matmul(ps[:R, :], lhsT=hT[:, c, col:col + R], rhs=Wd[:, c, half * 512:(half + 1) * 512], start=(c == 0), stop=(c == NFC - 1)), [hT, WC], [ps])
                V(lambda e, i=i, R=R, half=half, ps=ps: e.tensor_tensor(out=xb[:R, i, half * 512:(half + 1) * 512], in0=ps[:R, :], in1=xb[:R, i, half * 512:(half + 1) * 512], op=ALU.add), [ps, xb], [xb])
            cx.midt(i)
            if i == len(tiles) - 1 and len(tiles) == 1:
                cx.midt(1)
            epilogue_tile(cx, xb, i, R, col)
            col += R

    NB = TP // BLK
    fw.dma(gfin[:, :], norm_fin[0].partition_broadcast(128), writes=[gfin])

    def run_sublayer(fn, blocks):
        tb = 0
        for i, cx in enumerate(blocks):
            cx.par = i % 2
            cx.tbase = tb
            tb += getattr(cx, 'ntile', 0)
            cx.next = blocks[i + 1] if i + 1 < len(blocks) else None
        if not blocks:
            return
        front_load(blocks[0]); front_compute(blocks[0])
        for i, cx in enumerate(blocks):
            nxt = blocks[i + 1] if i + 1 < len(blocks) else None
            if nxt is not None:
                front_load(nxt)
                cx.mid1 = (lambda nxt=nxt: front_norm(nxt))
                cx.mid = (lambda nxt=nxt: front_T(nxt))
                cx.midn = (lambda i, nxt=nxt: front_norm(nxt, i))
                cx.midt = (lambda i, nxt=nxt: front_T(nxt, i))
            else:
                cx.mid1 = (lambda: None)
                cx.mid = (lambda: None)
                cx.midn = (lambda i: None)
                cx.midt = (lambda i: None)
            fn(cx)

    regs_p = [fw.region("rp%d" % i) for i in range(NB)]
    reg_s = fw.region("rs")
    zero_done = False
    for layer in range(CFG['L']):
        kind = layer % 3; j = layer // 3
        handoff([WC, WB], msb)
        handoff(pst_bufs, pst_bufs)
        w_in = (a_w_in, b_w_in, c_w_in)[kind][j]; w_out = (a_w_out, b_w_out, c_w_out)[kind][j]
        ncol = 1536 if kind == 2 else 3088
        load_w(WA, ncol, w_in, 8)
        load_w(WB, 1024, w_out, 8)
        fw.dma(gb[:, :], norm_mix[layer].partition_broadcast(128), writes=[gb])
        if kind == 0:
            fw.dma(gnb[:, :], a_norm[j].partition_broadcast(128), writes=[gnb])
            fw.dma(sp[:8, 0:1], a_bi[j], writes=[sp]); fw.dma(sp[:8, 1:2], a_bf[j], writes=[sp])
            V(lambda e: e.tensor_scalar(out=sp[:8, 0:2], in0=sp[:8, 0:2], scalar1=1.0 / 15, scalar2=None, op0=ALU.mult), [sp], [sp])
            V(lambda e: e.memset(Cst[0][:, :, :], 0.0), [], [Cst[0]])
            V(lambda e: e.memset(grow[:8, 4 * (BLK + NSEQ):4 * (BLK + NSEQ) + 1], 0.0), [], [grow])
        elif kind == 1:
            fw.dma(gnb[:, :], b_norm[j].partition_broadcast(128), writes=[gnb])
            fw.dma(sp[:, 8:12], b_bg[j], writes=[sp])
            V(lambda e: e.tensor_scalar(out=sp[:, 8:12], in0=sp[:, 8:12], scalar1=-1.0, scalar2=None, op0=ALU.mult), [sp], [sp])
            fw.dma(sp[:16, 16:16 + 512], b_wgu[j], writes=[sp])
            V(lambda e: e.memset(Sst[0][:, :, :], 0.0), [], [Sst[0]])
        else:
            fw.dma(gnb[:, 0:512], c_bkv[j].partition_broadcast(128), writes=[gnb])
            fw.dma(sp[:64, 528:544], c_bq[j], writes=[sp]); fw.dma(sp[:64, 544:548], c_bk[j], writes=[sp])
            fw.dma(sp[:, 552:568], c_snk[j].partition_broadcast(128), writes=[sp])
            A(lambda e: e.activation(out=sp[:, 552:568], in_=sp[:, 552:568], func=ACT.Exp), [sp], [sp])
        Wcur[0] = WA
        blocks = []
        for which in (CFG['parts'] if CFG['mix'] else ''):
            if which == "p":
                src = xp if layer == 0 else res_p
                if kind == 2:
                    nbk = 2 * CFG['nb']
                    for b in range(nbk):
                        blocks.append(Cx(j=j, layer=layer, src=src, dst=res_p, r0=b * 128, tiles=[128], reg=regs_p[b // 2], ntok=128, Tn=128, ntile=1, sample=False, last_block=(b == nbk - 1), final=False))
                else:
                    for b in range(CFG['nb']):
                        blocks.append(Cx(j=j, layer=layer, src=src, dst=res_p, r0=b * BLK, tiles=[128, 128], reg=regs_p[b], ntok=BLK, Tn=128, ntile=2, sample=False, last_block=(b == CFG['nb'] - 1), final=False))
            else:
                src = xsm if layer == 0 else res_s
                blocks.append(Cx(j=j, layer=layer, src=src, dst=res_s, r0=0, tiles=[NS], reg=reg_s, ntok=NS, Tn=4, ntile=NSEQ, sample=True, last_block=True, final=False))
        run_sublayer((mlstm_block, gla_block, swa_block)[kind], blocks)
        if 'p' in CFG['parts'] and CFG['mix']:
            if kind == 0:
                dma_out(v3(po_c[j], 8), Cst[0][:, :, 0:128], Cst[0])
                dma_out(po_n[j].unsqueeze(2), Cst[0][:, :, 128:129], Cst[0])
            elif kind == 1:
                dma_out(v3(po_g[j], 4), Sst[0][:, :, :], Sst[0])
        if not CFG['ffn']:
            continue
        handoff(msb, [WC, WB])
        load_w(WA, DFF, f_w_up[layer][:, 0:DFF], 8)
        load_w(WB, DFF, f_w_up[layer][:, DFF:2 * DFF], 8)
        load_w(WC, D, f_w_down[layer], NFC)
        fw.dma(gb[:, :], norm_ffn[layer].partition_broadcast(128), writes=[gb])
        fw.dma(cwb[:, :], f_cw[layer], writes=[cwb]); fw.dma(cbb[:, :], f_cb[layer], writes=[cbb])
        V(lambda e: e.memset(halo_p[:, :], 0.0), [], [halo_p])
        fw.dma(halo_s[:, :], si_f[layer], writes=[halo_s])
        fin = (layer == CFG['L'] - 1)
        blocks = []
        for b in range(CFG['nb'] if 'p' in CFG['parts'] else 0):
            blocks.append(Cx(j=j, layer=layer, src=(xp if (layer == 0 and not CFG['mix']) else res_p), dst=res_p, fdst=yp, r0=b * BLK, tiles=[128, 128], reg=regs_p[b], ntok=BLK, nseq=1, halo=halo_p, final=fin, po=(b == CFG['nb'] - 1)))
        if 's' in CFG['parts']:
            blocks.append(Cx(j=j, layer=layer, src=(xsm if (layer == 0 and not CFG['mix']) else res_s), dst=res_s, fdst=ys, r0=0, tiles=[NS], reg=reg_s, ntok=NS, nseq=NSEQ, halo=halo_s, final=fin, po=False))
        run_sublayer(ffn_block, blocks)
        dma_out(po_f[layer], halo_p[:, :], halo_p)
        dma_out(so_f[layer], halo_s[:, :], halo_s)
    fw.final_wait(out_toks)
    fw.emit()
    print("instr counts", fw.stats())
    return nc


_NC = [None]


def _lay(x):
    return np.ascontiguousarray(x, dtype=np.float32)


def kernel(**inp):
    inp = {k: np.asarray(v) for k, v in inp.items()}
    if _NC[0] is None:
        _NC[0] = build()
    nc = _NC[0]
    f32 = np.float32
    ii = np.arange(128)
    consts = {
        "k_ident": np.eye(128, dtype=f32),
        "k_maskc": (ii[:, None] <= ii[None, :]).astype(f32),
        "k_maskp": (ii[:, None] >= ii[None, :]).astype(f32),
        "k_ones8": np.ones((8, 128), f32),
        "k_negI8": -np.eye(8, dtype=f32),
        "k_sel8": ((np.arange(8)[:, None] % 2) == (ii[None, :] // 64)).astype(f32),
        "k_selc": ((np.arange(8)[:, None] // 2) == np.arange(4)[None, :]).astype(f32),
        "k_rm_p": np.tile(((np.arange(BLK) % 128) != 0).astype(f32)[None], (128, 1)),
        "k_rm_s": np.tile(((np.arange(NS) % 4) != 0).astype(f32)[None], (128, 1)),
    }
    c_w_in = inp["c_w_in"]
    c_b = inp["c_b_in"]
    c_bq = c_b[:, 0:1024].reshape(1, 16, 64).transpose(0, 2, 1)
    c_bk = c_b[:, 1024:1280].reshape(1, 4, 64).transpose(0, 2, 1)
    shared = {
        "a_w_in": inp["a_w_in"], "a_w_out": inp["a_w_out"], "b_w_in": inp["b_w_in"], "b_w_out": inp["b_w_out"],
        "c_w_in": c_w_in, "c_w_out": inp["c_w_out"], "f_w_up": inp["f_w_up"], "f_w_down": inp["f_w_down"],
        "b_wgu": inp["b_w_gate_up"],
        "norm_mix": inp["norm_mix_g"], "norm_ffn": inp["norm_ffn_g"], "norm_fin": inp["norm_final_g"].reshape(1, D),
        "a_norm": inp["a_norm_g"], "b_norm": inp["b_norm_g"],
        "a_bi": inp["a_b_i"].reshape(2, 8, 1), "a_bf": inp["a_b_f"].reshape(2, 8, 1),
        "b_bg": inp["b_b_gate"].reshape(1, 4, 128).transpose(0, 2, 1),
        "c_bq": c_bq, "c_bk": c_bk, "c_bkv": c_b[:, 1024:1536], "c_snk": inp["c_sinks"],
        "f_cw": inp["f_conv_w"].reshape(4, 3, NFC, 128).transpose(0, 3, 2, 1).reshape(4, 128, NFC * 3),
        "f_cb": inp["f_conv_b"].reshape(4, NFC, 128).transpose(0, 2, 1),
    }
    shared.update(consts)
    shared = {k: _lay(v) for k, v in shared.items()}
    in_maps = []
    for c in range(8):
        sl = slice(c * NSEQ, (c + 1) * NSEQ)
        m = dict(shared)
        m["xp"] = _lay(inp["x_prompt"][c % 4])
        m["xsm"] = _lay(inp["x_sample"][sl].reshape(NS, D))
        C = inp["state_mlstm_c"][:, sl]
        m["si_c"] = _lay(C.transpose(0, 1, 3, 2, 4).reshape(2, NSEQ, 64, 1024))
        n = inp["state_mlstm_n"][:, sl]
        m["si_n"] = _lay(n.transpose(0, 1, 3, 2))
        m["si_m"] = _lay(inp["state_mlstm_m"][:, sl].transpose(0, 2, 1))
        m["si_g"] = _lay(inp["state_gla"][:, sl].transpose(0, 1, 3, 2, 4).reshape(1, NSEQ, 128, 1024))
        m["si_k"] = _lay(inp["cache_swa_k"][:, sl].reshape(1, NSEQ, 128, 256))
        m["si_v"] = _lay(inp["cache_swa_v"][:, sl].reshape(1, NSEQ, 128, 256))
        f = inp["state_ffn_conv"][:, sl]
        m["si_f"] = _lay(f.reshape(4, NSEQ, 2, NFC, 128).transpose(0, 4, 3, 1, 2).reshape(4, 128, NFC * NSEQ * 2))
        in_maps.append(m)
    ncr = CFG.get('cores', 8)
    if os.environ.get('KN_TRACE'):
        res = run_bass_kernel_spmd(nc, in_maps[:ncr], core_ids=list(range(ncr)), trace=True)
        print('EXEC_TIME_NS', res.exec_time_ns, flush=True)
    else:
        res = run_bass_kernel_spmd(nc, in_maps[:ncr], core_ids=list(range(ncr)))
    R = list(res.results)
    while len(R) < 8:
        R.append(R[0])
    B = 4
    y_prompt = np.stack([R[b]["yp"] for b in range(B)])
    y_sample = np.concatenate([R[c]["ys"].reshape(NSEQ, 4, D) for c in range(8)], 0)

    def unC(x):
        return x.reshape(2, 64, 8, 128).transpose(0, 2, 1, 3)

    def unN(x):
        return x.transpose(0, 2, 1)

    p_c = np.stack([unC(R[b]["po_c"]) for b in range(B)], 1)
    p_n = np.stack([unN(R[b]["po_n"]) for b in range(B)], 1)
    p_m = np.stack([R[b]["po_m"].reshape(2, 8) for b in range(B)], 1)
    p_g = np.stack([R[b]["po_g"].reshape(1, 128, 4, 256).transpose(0, 2, 1, 3) for b in range(B)], 1)
    p_k = np.stack([R[b]["po_k"].reshape(1, 128, 4, 64) for b in range(B)], 1)
    p_v = np.stack([R[b]["po_v"].reshape(1, 128, 4, 64) for b in range(B)], 1)
    p_f = np.stack([R[b]["po_f"].reshape(4, 128, NFC, 2).transpose(0, 3, 2, 1).reshape(4, 2, DFF) for b in range(B)], 1)
    s_c = np.concatenate([R[c]["so_c"].reshape(2, NSEQ, 64, 8, 128).transpose(0, 1, 3, 2, 4) for c in range(8)], 1)
    s_n = np.concatenate([R[c]["so_n"].transpose(0, 1, 3, 2) for c in range(8)], 1)
    s_m = np.concatenate([R[c]["so_m"].transpose(0, 2, 1) for c in range(8)], 1)
    s_g = np.concatenate([R[c]["so_g"].reshape(1, NSEQ, 128, 4, 256).transpose(0, 1, 3, 2, 4) for c in range(8)], 1)
    s_k = np.concatenate([R[c]["so_k"].reshape(1, NSEQ, 128, 4, 64) for c in range(8)], 1)
    s_v = np.concatenate([R[c]["so_v"].reshape(1, NSEQ, 128, 4, 64) for c in range(8)], 1)
    s_f = np.concatenate([R[c]["so_f"].reshape(4, 128, NFC, NSEQ, 2).transpose(0, 3, 4, 2, 1).reshape(4, NSEQ, 2, DFF) for c in range(8)], 1)
    outs = (y_prompt, y_sample, p_c, p_n, p_m, p_g, p_k, p_v, p_f, s_c, s_n, s_m, s_g, s_k, s_v, s_f)
    return tuple(np.ascontiguousarray(o, dtype=np.float32) for o in outs)
```

```python
import numpy as np
from contextlib import ExitStack
import concourse.bass as bass
import concourse.mybir as mybir

F32 = mybir.dt.float32
BF16 = mybir.dt.bfloat16
ACT = mybir.ActivationFunctionType
ALU = mybir.AluOpType
AX = mybir.AxisListType

SEM_LIMIT = 30000


class Sem:
    __slots__ = ("h", "issued", "is_dma")

    def __init__(self, h, is_dma):
        self.h = h
        self.issued = 0
        self.is_dma = is_dma


class Buf:
    def __init__(self, t, name, space="sb"):
        self.t = t
        self.name = name
        self.space = space
        self.w = None
        self.r = {}
        self.chan = None
        self.schan = None

    def __getitem__(self, k):
        return self.t[k]


class Eng:
    def __init__(self, name, obj):
        self.name = name
        self.obj = obj
        self.sem = None
        self.prog = []
        self.seen = {}


class FW:
    def __init__(self, nc):
        self.nc = nc
        self.es = ExitStack()
        self.eng = {}
        for n in ("tensor", "vector", "scalar", "gpsimd", "sync"):
            self.eng[n] = Eng(n, getattr(nc, n))
        self.nsem = 0
        self.nbuf = 0
        self.out_tokens = []

    def new_sem(self, is_dma):
        self.nsem += 1
        h = self.es.enter_context(self.nc.semaphore("s%d" % self.nsem))
        return Sem(h, is_dma)

    def sbuf(self, shape, dtype=F32, name=None):
        self.nbuf += 1
        name = "%s_%d" % (name or "sb", self.nbuf)
        t = self.es.enter_context(self.nc.sbuf_tensor(name, list(shape), dtype))
        return Buf(t, name)

    def psum(self, shape, dtype=F32, name=None):
        self.nbuf += 1
        name = "%s_%d" % (name or "ps", self.nbuf)
        t = self.es.enter_context(self.nc.psum_tensor(name, list(shape), dtype))
        return Buf(t, name, "ps")

    def dram(self, name, shape, dtype=F32, kind="Internal"):
        t = self.nc.dram_tensor(name, list(shape), dtype, kind=kind)
        return Buf(t.ap(), name, "dram")

    def region(self, name):
        return Buf(None, name, "dram")

    def _needs(self, E, reads, writes):
        needs = {}

        def need(tok):
            if tok is None:
                return
            s, v = tok
            if s.is_dma:
                v = s.issued
            if needs.get(s, 0) < v:
                needs[s] = v

        for b in reads:
            need(b.w)
        for b in writes:
            need(b.w)
            for s, v in b.r.items():
                need((s, v))
        waits = []
        for s, v in needs.items():
            if E.name == "tensor" and s is E.sem:
                continue
            if E.seen.get(s, 0) >= v:
                continue
            E.seen[s] = v
            waits.append((s.h, v))
        return waits

    def op(self, eng, fn, reads=(), writes=()):
        E = self.eng[eng]
        if E.sem is None or E.sem.issued >= SEM_LIMIT:
            E.sem = self.new_sem(False)
        waits = self._needs(E, reads, writes)
        E.sem.issued += 1
        tok = (E.sem, E.sem.issued)
        E.prog.append((waits, fn, (E.sem.h, 1)))
        for b in reads:
            b.r[tok[0]] = tok[1]
        for b in writes:
            b.w = tok
            b.r = {}
        return tok

    def dma(self, out_ap, in_ap, reads=(), writes=(), q="sync", chan=None, **kw):
        E = self.eng[q]
        waits = self._needs(E, reads, writes)
        if chan is None:
            b = writes[0] if (writes and writes[0].space != "dram") else None
            if b is not None:
                if b.chan is None:
                    b.chan = self.new_sem(True)
                chan = b.chan
            else:
                b = reads[0]
                if b.schan is None:
                    b.schan = self.new_sem(True)
                chan = b.schan
        chan.issued += 16
        tok = (chan, chan.issued)

        def fn(e, out_ap=out_ap, in_ap=in_ap, kw=kw):
            kw2 = dict(kw); kw2.setdefault("allow_slow_non_contiguous", True); return e.dma_start(out=out_ap, in_=in_ap, **kw2)

        E.prog.append((waits, fn, (chan.h, 16)))
        for b in reads:
            b.r[tok[0]] = tok[1]
        for b in writes:
            b.w = tok
            b.r = {}
        return tok

    def final_wait(self, toks, eng="sync"):
        E = self.eng[eng]
        waits = []
        seen = {}
        for s, v in toks:
            if s.is_dma:
                v = s.issued
            if seen.get(s, 0) < v:
                seen[s] = v
        for s, v in seen.items():
            waits.append((s.h, v))
        E.prog.append((waits, None, None))

    def emit(self):
        nc = self.nc
        with nc.Block() as block:
            def run(E):
                def body(e):
                    for waits, fn, inc in E.prog:
                        for h, v in waits:
                            e.wait_ge(h, v)
                        if fn is not None:
                            ins = fn(e)
                            ins.then_inc(inc[0], inc[1])
                return body
            block.tensor(run(self.eng["tensor"]))
            block.vector(run(self.eng["vector"]))
            block.scalar(run(self.eng["scalar"]))
            block.gpsimd(run(self.eng["gpsimd"]))
            block.sync(run(self.eng["sync"]))

    def close(self):
        self.es.close()

    def stats(self):
        return {n: len(E.prog) for n, E in self.eng.items()}, self.nsem

from concourse.bass_utils import run_bass_kernel_spmd

D = 1024
TP = 4096
NS = 64
NSEQ = 16
DFF = 2816
NFC = 22
BLK = 256
EPS = 1e-6


def v3(ap, a):
    return ap.rearrange("p (a b) -> p a b", a=a)


def v4(ap, a, b):
    return ap.rearrange("p (a b c) -> p a b c", a=a, b=b)


class Arena:
    def __init__(self, fw, nbytes, name):
        self.fw = fw
        self.raw = fw.sbuf([128, nbytes // 4], F32, name)
        self.off = 0
        self.n = nbytes // 4

    def take(self, nfree, dtype=F32, name="ar"):
        words = nfree if dtype == F32 else (nfree + 1) // 2
        assert self.off + words <= self.n, (name, self.off, words, self.n)
        ap = self.raw.t[:, self.off:self.off + words]
        self.off += words
        if dtype != F32:
            ap = ap.bitcast(dtype)
        return Buf(ap, name, "sb")


def handoff(frm, to):
    toks = {}
    for b in frm:
        if b.w is not None:
            s, v = b.w
            toks[s] = max(toks.get(s, 0), v)
        for s, v in b.r.items():
            toks[s] = max(toks.get(s, 0), v)
    for b in to:
        for s, v in toks.items():
            b.r[s] = max(b.r.get(s, 0), v)


import os
CFG = {}

def build():
    CFG['L'] = int(os.environ.get('KN_LAYERS', '4')); CFG['parts'] = os.environ.get('KN_PARTS', 'ps'); CFG['ffn'] = int(os.environ.get('KN_FFN', '1')); CFG['nb'] = int(os.environ.get('KN_NB', '16')); CFG['mix'] = int(os.environ.get('KN_MIX', '1')); CFG['stop'] = float(os.environ.get('KN_STOP', '99')); CFG['cores'] = int(os.environ.get('KN_CORES', '8'))
    nc = bass.Bass("TRN2", target_bir_lowering=False)
    fw = FW(nc)

    def din(name, shape):
        return nc.dram_tensor(name, list(shape), F32, kind="ExternalInput").ap()

    def dout(name, shape):
        return nc.dram_tensor(name, list(shape), F32, kind="ExternalOutput").ap()

    xp = din("xp", [TP, D]); xsm = din("xsm", [NS, D])
    a_w_in = din("a_w_in", [2, D, 3088]); a_w_out = din("a_w_out", [2, D, D])
    b_w_in = din("b_w_in", [1, D, 3088]); b_w_out = din("b_w_out", [1, D, D])
    c_w_in = din("c_w_in", [1, D, 1536]); c_w_out = din("c_w_out", [1, D, D])
    f_w_up = din("f_w_up", [4, D, 2 * DFF]); f_w_down = din("f_w_down", [4, DFF, D])
    b_wgu = din("b_wgu", [1, 16, 512])
    norm_mix = din("norm_mix", [4, D]); norm_ffn = din("norm_ffn", [4, D]); norm_fin = din("norm_fin", [1, D])
    a_norm = din("a_norm", [2, D]); b_norm = din("b_norm", [1, D])
    a_bi = din("a_bi", [2, 8, 1]); a_bf = din("a_bf", [2, 8, 1])
    b_bg = din("b_bg", [1, 128, 4])
    c_bq = din("c_bq", [1, 64, 16]); c_bk = din("c_bk", [1, 64, 4]); c_bkv = din("c_bkv", [1, 512])
    c_snk = din("c_snk", [1, 16])
    f_cw = din("f_cw", [4, 128, NFC * 3]); f_cb = din("f_cb", [4, 128, NFC])
    k_ident = din("k_ident", [128, 128]); k_maskc = din("k_maskc", [128, 128]); k_maskp = din("k_maskp", [128, 128])
    k_ones8 = din("k_ones8", [8, 128]); k_negI8 = din("k_negI8", [8, 8]); k_sel8 = din("k_sel8", [8, 128]); k_selc = din("k_selc", [8, 4])
    k_rm_p = din("k_rm_p", [128, BLK]); k_rm_s = din("k_rm_s", [128, NS])
    si_c = din("si_c", [2, NSEQ, 64, 8 * 128]); si_n = din("si_n", [2, NSEQ, 64, 8]); si_m = din("si_m", [2, 8, NSEQ])
    si_g = din("si_g", [1, NSEQ, 128, 4 * 256]); si_k = din("si_k", [1, NSEQ, 128, 256]); si_v = din("si_v", [1, NSEQ, 128, 256])
    si_f = din("si_f", [4, 128, NFC * NSEQ * 2])
    yp = dout("yp", [TP, D]); ys = dout("ys", [NS, D])
    po_c = dout("po_c", [2, 64, 8 * 128]); po_n = dout("po_n", [2, 64, 8]); po_m = dout("po_m", [2, 8, 1])
    po_g = dout("po_g", [1, 128, 4 * 256]); po_k = dout("po_k", [1, 128, 256]); po_v = dout("po_v", [1, 128, 256])
    po_f = dout("po_f", [4, 128, NFC * 2])
    so_c = dout("so_c", [2, NSEQ, 64, 8 * 128]); so_n = dout("so_n", [2, NSEQ, 64, 8]); so_m = dout("so_m", [2, 8, NSEQ])
    so_g = dout("so_g", [1, NSEQ, 128, 4 * 256]); so_k = dout("so_k", [1, NSEQ, 128, 256]); so_v = dout("so_v", [1, NSEQ, 128, 256])
    so_f = dout("so_f", [4, 128, NFC * NSEQ * 2])
    res_p = nc.dram_tensor("res_p", [TP, D], F32, kind="Internal").ap()
    res_s = nc.dram_tensor("res_s", [NS, D], F32, kind="Internal").ap()

    OUT = fw.region("outputs")
    out_toks = []

    def dma_out(dst, src_ap, srcbuf, q="sync"):
        tok = fw.dma(dst, src_ap, reads=[srcbuf], writes=[], q=q)
        out_toks.append(tok)

    WA = fw.sbuf([128, 24704], BF16, "WA")
    WBa = Arena(fw, 22528 * 2, "WB")
    WB = Buf(WBa.raw.t[:, :].bitcast(BF16), "WBw", "sb")
    WBa.off = 4096
    WCa = Arena(fw, 22528 * 2, "WC")
    WC = Buf(WCa.raw.t[:, :].bitcast(BF16), "WCw", "sb")
    ident = fw.sbuf([128, 128], F32, "ident"); identb = fw.sbuf([128, 128], BF16, "identb")
    maskc = fw.sbuf([128, 128], F32, "maskc"); maskp = fw.sbuf([128, 128], F32, "maskp")
    ones8 = fw.sbuf([8, 128], F32, "ones8"); negI8 = fw.sbuf([8, 8], F32, "negI8")
    sel8 = fw.sbuf([8, 128], F32, "sel8"); selc = fw.sbuf([8, 4], F32, "selc")
    rm_p = fw.sbuf([128, BLK], F32, "rm_p"); rm_s = fw.sbuf([128, NS], F32, "rm_s")
    gb = fw.sbuf([128, D], F32, "gb")
    sp = fw.sbuf([128, 576], F32, "sp")
    XB = [fw.sbuf([128, 2, D], F32, "xb%d" % i) for i in range(2)]
    gfin = fw.sbuf([128, D], F32, "gfin")
    xn = fw.sbuf([128, 2, D], BF16, "xn")
    XST = [fw.sbuf([128, 8, BLK], BF16, "xsT%d" % i) for i in range(2)]
    stat = fw.sbuf([128, 64], F32, "stat")
    mhalf = fw.sbuf([128, 2], F32, "mhalf")
    hT = fw.sbuf([128, NFC, BLK], BF16, "hT")
    hsT = Buf(v3(hT.t[:, 0:8, :].rearrange("p a b -> p (a b)"), 8), "hsT", "sb")
    gext = [fw.sbuf([128, BLK + 2 * NSEQ], F32, "gext%d" % i) for i in range(2)]
    cacc = [fw.sbuf([128, BLK], F32, "cacc%d" % i) for i in range(2)]
    halo_p = fw.sbuf([128, NFC * 2], F32, "halo_p")
    halo_s = fw.sbuf([128, NFC * NSEQ * 2], F32, "halo_s")
    cwb = fw.sbuf([128, NFC * 3], F32, "cwb"); cbb = fw.sbuf([128, NFC], F32, "cbb")
    PST = fw.sbuf([128, 1032], F32, "PST")
    Cst = [Buf(v3(PST.t[:64, :], 8), "Cst0", "sb"), None, None]
    Sst = [Buf(v3(PST.t[:, 0:1024], 4), "Sst0", "sb"), None, None]
    kTprev = [Buf(v3(PST.t[:64, 0:256].bitcast(BF16), 4), "kTprev0", "sb"), None, None]
    vprev = [Buf(v3(PST.t[:, 512:642].bitcast(BF16), 4), "vprev0", "sb"), None, None]
    pst_bufs = [Cst[0], Sst[0], kTprev[0], vprev[0]]
    kvraw = [None, None]
    MS = {}
    def ms(ar, name, nfree, dtype=F32):
        MS[name] = ar.take(nfree, dtype, name)
        return MS[name]
    qT = ms(WCa, "qT", 8 * BLK); kT = ms(WCa, "kT", 8 * BLK)
    ktm = ms(WCa, "ktm", 512); vext = ms(WCa, "vext", 8 * 129 + 8)
    Wt = ms(WCa, "Wt", 512); numS = ms(WCa, "numS", 512); hs = ms(WCa, "hs", 1024)
    kw = ms(WCa, "kw", 512); sqs = ms(WCa, "sqs", 256)
    Wt2 = ms(WCa, "Wt2", 512)
    grow = ms(WCa, "grow", 5 * (BLK + NSEQ)); trow = ms(WCa, "trow", 3 * 128 + 16)
    cols = ms(WCa, "cols", 64); glb = ms(WCa, "glb", 16)
    print("WC scratch words", WCa.off, "of", WCa.n)
    gnb = ms(WBa, "gnb", 1024)
    so = ms(WBa, "so", 1024)
    rbd = ms(WBa, "rbd", 1024)
    o_save = WBa.off; WBa.off -= 1024
    lgT = ms(WBa, "lgT", 4 * BLK)
    WBa.off = o_save
    u0 = WBa.off
    for i in (1, 2):
        b_ = ms(WBa, "Cst%d" % i, 1032); Cst[i] = Buf(v3(b_.t[:64, :], 8), b_.name, "sb"); MS[b_.name] = Cst[i]
    u1 = WBa.off
    WBa.off = u0
    for i in (1, 2):
        b_ = ms(WBa, "Sst%d" % i, 1024); Sst[i] = Buf(v3(b_.t, 4), b_.name, "sb"); MS[b_.name] = Sst[i]
    u1 = max(u1, WBa.off)
    WBa.off = u0
    for i in (1, 2):
        b_ = ms(WBa, "kTprev%d" % i, 512); kTprev[i] = Buf(v3(b_.t[:64, 0:256].bitcast(BF16), 4), b_.name, "sb"); MS[b_.name] = kTprev[i]
        b_ = ms(WBa, "vprev%d" % i, 260); vprev[i] = Buf(v3(b_.t[:, 0:130].bitcast(BF16), 4), b_.name, "sb"); MS[b_.name] = vprev[i]
        kvraw[i - 1] = ms(WBa, "kvraw%d" % i, 512)
    WBa.off = max(WBa.off, u1)
    print("WB scratch words", WBa.off, "of", WBa.n)
    ve2 = ms(WBa, "ve2", 516); stbuf = ms(WBa, "stbuf", 516)
    print("WB scratch words (after filler bufs)", WBa.off, "of", WBa.n)
    VE = [Buf(vext.t[:, 0:516], "ve0", "sb"), ve2]
    SOB = [Buf(so.t[:, 0:512], "so0", "sb"), Buf(so.t[:, 512:1024], "so1", "sb")]
    QTB = [Buf(qT.t[:, 0:1024], "qT0", "sb"), Buf(qT.t[:, 1024:2048], "qT1", "sb")]
    KTB = [Buf(kT.t[:, 0:1024], "kT0", "sb"), Buf(kT.t[:, 1024:2048], "kT1", "sb")]
    for b_ in VE[:1] + SOB + QTB + KTB:
        MS[b_.name] = b_
    msb = list(MS.values())

    P = [fw.psum([128, 512], F32, "P%d" % i) for i in range(8)]

    for (b, src) in ((ident, k_ident), (maskc, k_maskc), (maskp, k_maskp), (ones8, k_ones8), (negI8, k_negI8),
                     (sel8, k_sel8), (selc, k_selc), (rm_p, k_rm_p), (rm_s, k_rm_s)):
        fw.dma(b[:, :], src, writes=[b])
    fw.op("vector", lambda e: e.tensor_copy(out=identb[:, :], in_=ident[:, :]), [ident], [identb])
    fw.op("vector", lambda e: e.memset(mhalf[:, :], -0.5), [], [mhalf])

    V = lambda fn, r, w: fw.op("vector", fn, r, w)
    A = lambda fn, r, w: fw.op("scalar", fn, r, w)
    G = lambda fn, r, w: fw.op("gpsimd", fn, r, w)
    T = lambda fn, r, w: fw.op("tensor", fn, r, w)

    def load_w(dst, ncol, src2d, nk, q="gpsimd"):
        view = v3(dst[:, 0:nk * ncol], nk)
        src = src2d.rearrange("(k p) e -> p k e", p=128)
        step = max(1, nk // 8) if nk > 8 else 1
        for k0 in range(0, nk, 2 if nk <= 8 else 4):
            k1 = min(nk, k0 + (2 if nk <= 8 else 4))
            fw.dma(view[:, k0:k1, :], src[:, k0:k1, :], writes=[dst], q=q)
        return view

    class Cx:
        def __init__(self, **kw):
            self.__dict__.update(kw)

    def front_load(cx):
        xb = XB[cx.par]
        col = 0
        for i, R in enumerate(cx.tiles):
            fw.dma(xb[:R, i, :], cx.src[cx.r0 + col:cx.r0 + col + R, :], reads=[cx.reg], writes=[xb])
            col += R

    def front_norm(cx, only=None):
        xb = XB[cx.par]
        for i, R in enumerate(cx.tiles):
            if only is not None and i != only:
                continue
            A(lambda e, i=i, R=R: e.activation(out=xn[:R, i, :], in_=xb[:R, i, :], func=ACT.Square, accum_out=stat[:R, 3 * i:3 * i + 1]), [xb], [xn, stat])
            V(lambda e, i=i, R=R: e.tensor_scalar(out=stat[:R, 3 * i + 1:3 * i + 2], in0=stat[:R, 3 * i:3 * i + 1], scalar1=1.0 / D, scalar2=EPS, op0=ALU.mult, op1=ALU.add), [stat], [stat])
            G(lambda e, i=i, R=R: e.tensor_tensor(out=stat[:R, 3 * i + 2:3 * i + 3], in0=stat[:R, 3 * i + 1:3 * i + 2], in1=mhalf[:R, 0:1], op=ALU.pow), [stat, mhalf], [stat])
            V(lambda e, i=i, R=R: e.scalar_tensor_tensor(out=xn[:R, i, :], in0=xb[:R, i, :], scalar=stat[:R, 3 * i + 2:3 * i + 3], in1=gb[:R, :], op0=ALU.mult, op1=ALU.mult), [xb, stat, gb], [xn])

    def front_T(cx, only=None):
        xsT = XST[cx.par]
        col = 0
        for i, R in enumerate(cx.tiles):
            if only is None or i == only:
                pst = P[7][:, :].bitcast(BF16)
                pst3 = v3(pst, 8)
                for kc in range(8):
                    T(lambda e, kc=kc, R=R, i=i, pst3=pst3: e.transpose(out=pst3[:, kc, :R], in_=xn[:R, i, kc * 128:(kc + 1) * 128], identity=identb[:R, :R]), [xn, identb], [P[7]])
                A(lambda e, R=R, col=col, pst3=pst3: e.activation(out=xsT[:, :, col:col + R], in_=pst3[:, :, :R], func=ACT.Copy), [P[7]], [xsT])
            col += R

    def front_compute(cx):
        front_norm(cx); front_T(cx)

    def proj_fm(cx, ps, M, W, c0, ntok):
        xsT = XST[cx.par]
        for kc in range(8):
            T(lambda e, kc=kc: e.matmul(ps[:M, :ntok], lhsT=W[:, kc, c0:c0 + M], rhs=xsT[:, kc, :ntok], start=(kc == 0), stop=(kc == 7)), [xsT, Wcur[0]], [ps])

    def proj_tm(cx, ps, Tn, col, W, c0, ncol):
        xsT = XST[cx.par]
        for kc in range(8):
            T(lambda e, kc=kc: e.matmul(ps[:Tn, :ncol], lhsT=xsT[:, kc, col:col + Tn], rhs=W[:, kc, c0:c0 + ncol], start=(kc == 0), stop=(kc == 7)), [xsT, Wcur[0]], [ps])

    Wcur = [WA]

    def epilogue_tile(cx, xb, i, R, col):
        if cx.final:
            A(lambda e: e.activation(out=xn[:R, i, :], in_=xb[:R, i, :], func=ACT.Square, accum_out=stat[:R, 56:57]), [xb], [xn, stat])
            A(lambda e: e.activation(out=stat[:R, 57:58], in_=stat[:R, 56:57], func=ACT.Ln, scale=1.0 / D, bias=EPS), [stat], [stat])
            A(lambda e: e.activation(out=stat[:R, 58:59], in_=stat[:R, 57:58], func=ACT.Exp, scale=-0.5), [stat], [stat])
            V(lambda e: e.scalar_tensor_tensor(out=xb[:R, i, :], in0=xb[:R, i, :], scalar=stat[:R, 58:59], in1=gfin[:R, :], op0=ALU.mult, op1=ALU.mult), [xb, stat, gfin], [xb])
            dma_out(cx.fdst[cx.r0 + col:cx.r0 + col + R, :], xb[:R, i, :], xb)
        else:
            fw.dma(cx.dst[cx.r0 + col:cx.r0 + col + R, :], xb[:R, i, :], reads=[xb], writes=[cx.reg])

    def out_proj_store(cx, Wo):
        xb = XB[cx.par]
        col = 0
        for i, R in enumerate(cx.tiles):
            for half in range(2):
                ps = P[half]
                for ec in range(8):
                    T(lambda e, ec=ec, R=R, col=col, half=half, ps=ps: e.matmul(ps[:R, :], lhsT=hsT[:, ec, col:col + R], rhs=Wo[:, ec, half * 512:(half + 1) * 512], start=(ec == 0), stop=(ec == 7)), [hT, WB], [ps])
                V(lambda e, i=i, R=R, half=half, ps=ps: e.tensor_tensor(out=xb[:R, i, half * 512:(half + 1) * 512], in0=ps[:R, :], in1=xb[:R, i, half * 512:(half + 1) * 512], op=ALU.add), [ps, xb], [xb])
            fw.dma(cx.dst[cx.r0 + col:cx.r0 + col + R, :], xb[:R, i, :], reads=[xb], writes=[cx.reg])
            col += R

    def hs_to_hsT(Tn, col):
        for g in range(2):
            ps = P[4 + g]
            ps3 = v3(ps[:, :], 4)
            for e4 in range(4):
                ec = g * 4 + e4
                T(lambda e, ec=ec, e4=e4, ps3=ps3: e.transpose(out=ps3[:, e4, :Tn], in_=hs[:Tn, ec * 128:(ec + 1) * 128], identity=ident[:Tn, :Tn]), [hs, ident], [ps])
            A(lambda e, g=g, ps3=ps3: e.activation(out=hsT[:, g * 4:(g + 1) * 4, col:col + Tn], in_=ps3[:, :, :Tn], func=ACT.Copy), [ps], [hT])

    def head_rmsnorm_gate(Tn, nh, dv, rden_ap, so_ap=None, so_buf=None):
        so_ap = so[:Tn, :] if so_ap is None else so_ap
        so_buf = so if so_buf is None else so_buf
        h3 = v3(hs[:Tn, :], nh)
        if nh <= 4:
            for h_ in range(nh):
                A(lambda e, h_=h_: e.activation(out=sqs[:Tn, 0:dv], in_=h3[:, h_, :], func=ACT.Square, accum_out=stat[:Tn, 8 + h_:9 + h_]), [hs], [sqs, stat])
        else:
            sqv = numS.t[:Tn, 0:512].bitcast(BF16)
            V(lambda e: e.tensor_tensor(out=sqv, in0=hs[:Tn, :], in1=hs[:Tn, :], op=ALU.mult), [hs], [numS])
            V(lambda e: e.tensor_reduce(out=stat[:Tn, 8:8 + nh], in_=v3(sqv, nh), axis=AX.X, op=ALU.add), [numS], [stat])
        if rden_ap is not None:
            V(lambda e: e.tensor_tensor(out=stat[:Tn, 16:16 + nh], in0=rden_ap, in1=rden_ap, op=ALU.mult), [stat], [stat])
            V(lambda e: e.tensor_tensor(out=stat[:Tn, 8:8 + nh], in0=stat[:Tn, 8:8 + nh], in1=stat[:Tn, 16:16 + nh], op=ALU.mult), [stat], [stat])
        A(lambda e: e.activation(out=stat[:Tn, 8:8 + nh], in_=stat[:Tn, 8:8 + nh], func=ACT.Ln, scale=1.0 / dv, bias=EPS), [stat], [stat])
        A(lambda e: e.activation(out=stat[:Tn, 8:8 + nh], in_=stat[:Tn, 8:8 + nh], func=ACT.Exp, scale=-0.5), [stat], [stat])
        if rden_ap is not None:
            V(lambda e: e.tensor_tensor(out=stat[:Tn, 8:8 + nh], in0=stat[:Tn, 8:8 + nh], in1=rden_ap, op=ALU.mult), [stat], [stat])
        V(lambda e: e.tensor_tensor(out=h3, in0=h3, in1=stat[:Tn, 8:8 + nh].unsqueeze(2).to_broadcast([Tn, nh, dv]), op=ALU.mult), [hs, stat], [hs])
        V(lambda e: e.tensor_tensor(out=hs[:Tn, :], in0=hs[:Tn, :], in1=so_ap, op=ALU.mult), [hs, so_buf], [hs])

    def flush(lst):
        while lst:
            lst.pop(0)()

    def ensure_items(cx):
        if getattr(cx, 'pend_fm', None) is not None:
            return
        W = v3(WA[:, 0:8 * 3088], 8)
        ntok, Tn = cx.ntok, cx.Tn
        q3 = v3(QTB[cx.par].t[:64, :].bitcast(BF16), 8); k3 = v3(KTB[cx.par].t[:64, :].bitcast(BF16), 8)
        qTc = QTB[cx.par]; kTc = KTB[cx.par]
        fm = []
        for h in range(16):
            def it(h=h):
                ps = P[h % 2]
                proj_fm(cx, ps, 64, W, h * 64, ntok)
                if h < 8:
                    A(lambda e: e.activation(out=q3[:, h, :ntok], in_=ps[:64, :ntok], func=ACT.Copy), [ps], [qTc])
                else:
                    V(lambda e: e.tensor_scalar(out=k3[:, h - 8, :ntok], in0=ps[:64, :ntok], scalar1=0.125, scalar2=None, op0=ALU.mult), [ps], [kTc])
            fm.append(it)
        cx.pend_fm = fm
        cx.pend_tm = []
        for ti in range(cx.ntile):
            tp = (cx.tbase + ti) % 2
            c0 = ti * Tn
            ktm_v = ktm.t[:Tn, tp * 256:(tp + 1) * 256].bitcast(BF16)
            veb = VE[tp]; sob = SOB[tp]
            ve = v3(veb.t[:Tn, 0:516].bitcast(BF16), 8)
            so_v = sob.t[:Tn, :].bitcast(BF16)
            lst = []

            def it_k(c0=c0, ktm_v=ktm_v):
                proj_tm(cx, P[0], Tn, c0, W, 512, 512)
                V(lambda e: e.tensor_scalar(out=ktm_v, in0=P[0][:Tn, :], scalar1=0.125, scalar2=None, op0=ALU.mult), [P[0]], [ktm])
            lst.append(it_k)
            for hf in range(2):
                def it_v(hf=hf, c0=c0, ve=ve, veb=veb):
                    if hf == 0:
                        G(lambda e: e.memset(ve[:, :, 128:129], 1.0), [], [veb])
                    proj_tm(cx, P[1], Tn, c0, W, 1024 + hf * 512, 512)
                    A(lambda e: e.activation(out=ve[:, hf * 4:(hf + 1) * 4, 0:128], in_=v3(P[1][:Tn, :], 4), func=ACT.Copy), [P[1]], [veb])
                lst.append(it_v)
            for hf in range(2):
                def it_o(hf=hf, c0=c0, so_v=so_v, sob=sob):
                    proj_tm(cx, P[hf], Tn, c0, W, 2048 + hf * 512, 512)
                    A(lambda e: e.activation(out=so_v[:, hf * 512:(hf + 1) * 512], in_=P[hf][:Tn, :], func=ACT.Sigmoid), [P[hf]], [sob])
                    G(lambda e: e.tensor_tensor(out=so_v[:, hf * 512:(hf + 1) * 512], in0=so_v[:, hf * 512:(hf + 1) * 512], in1=gnb[:Tn, hf * 512:(hf + 1) * 512], op=ALU.mult), [sob, gnb], [sob])
                lst.append(it_o)
            cx.pend_tm.append(lst)

    def mlstm_block(cx):
        j, r0, tiles, reg, ntok, Tn, ntile, sample = cx.j, cx.r0, cx.tiles, cx.reg, cx.ntok, cx.Tn, cx.ntile, cx.sample
        W = v3(WA[:, 0:8 * 3088], 8)
        Wo = v3(WB[:, 0:8 * 1024], 8)
        if CFG['stop'] <= 1: return
        ensure_items(cx)
        flush(cx.pend_fm)
        q3 = v3(QTB[cx.par].t[:64, :].bitcast(BF16), 8); k3 = v3(KTB[cx.par].t[:64, :].bitcast(BF16), 8)
        qTc = QTB[cx.par]; kTc = KTB[cx.par]
        if CFG['stop'] <= 2: return
        GW = BLK + NSEQ
        igc = grow[:8, 0:ntok]; lf = grow[:8, GW:GW + ntok]; Fc = grow[:8, 2 * GW:2 * GW + ntok]; Mt = grow[:8, 3 * GW:3 * GW + ntok]
        nseg = ntile if sample else 1
        seglen = ntok // nseg
        mext = v3(grow[:8, 4 * GW:4 * GW + nseg * (seglen + 1)], nseg)
        proj_fm(cx, P[0], 8, W, 3072, ntok)
        proj_fm(cx, P[1], 8, W, 3080, ntok)
        A(lambda e: e.activation(out=igc, in_=P[0][:8, :ntok], func=ACT.Tanh, scale=1.0 / 15, bias=sp[:8, 0:1]), [P[0], sp], [grow])
        A(lambda e: e.activation(out=lf, in_=P[1][:8, :ntok], func=ACT.Tanh, scale=1.0 / 15, bias=sp[:8, 1:2]), [P[1], sp], [grow])
        V(lambda e: e.tensor_scalar(out=igc, in0=igc, scalar1=15.0, scalar2=None, op0=ALU.mult), [grow], [grow])
        xg = Fc; ug = Mt
        V(lambda e: e.tensor_scalar(out=xg, in0=lf, scalar1=15.0, scalar2=None, op0=ALU.mult), [grow], [grow])
        V(lambda e: e.scalar_tensor_tensor(out=ug, in0=xg, scalar=-1.0, in1=xg, op0=ALU.mult, op1=ALU.max), [grow], [grow])
        A(lambda e: e.activation(out=ug, in_=ug, func=ACT.Exp, scale=-1.0), [grow], [grow])
        V(lambda e: e.tensor_scalar(out=lf, in0=ug, scalar1=2.0, scalar2=None, op0=ALU.add), [grow], [grow])
        V(lambda e: e.reciprocal(out=lf, in_=lf), [grow], [grow])
        V(lambda e: e.tensor_tensor(out=ug, in0=ug, in1=lf, op=ALU.mult), [grow], [grow])
        V(lambda e: e.tensor_tensor(out=lf, in0=ug, in1=ug, op=ALU.mult), [grow], [grow])
        zp = trow[:8, 0:ntok]
        V(lambda e: e.tensor_scalar(out=zp, in0=lf, scalar1=1.0 / 9, scalar2=None, op0=ALU.mult), [grow], [trow])
        for cc in (1.0 / 7, 1.0 / 5, 1.0 / 3):
            V(lambda e, cc=cc: e.scalar_tensor_tensor(out=zp, in0=zp, scalar=cc, in1=lf, op0=ALU.add, op1=ALU.mult), [trow, grow], [trow])
        V(lambda e: e.scalar_tensor_tensor(out=zp, in0=zp, scalar=1.0, in1=ug, op0=ALU.add, op1=ALU.mult), [trow, grow], [trow])
        V(lambda e: e.tensor_scalar(out=xg, in0=xg, scalar1=0.0, scalar2=None, op0=ALU.min), [grow], [grow])
        V(lambda e: e.scalar_tensor_tensor(out=lf, in0=zp, scalar=-2.0, in1=xg, op0=ALU.mult, op1=ALU.add), [trow, grow], [grow])
        if sample:
            fw.dma(mext[:, :, 0:1], si_m[j].unsqueeze(2), writes=[grow])
        for s in range(nseg):
            V(lambda e, s=s: e.tensor_tensor_scan(out=mext[:, s, 1:1 + seglen], data0=lf[:, s * seglen:(s + 1) * seglen], data1=igc[:, s * seglen:(s + 1) * seglen], initial=mext[:, s, 0:1], op0=ALU.add, op1=ALU.max), [grow], [grow])
        rm = rm_s if sample else rm_p
        V(lambda e: e.tensor_tensor_scan(out=Fc, data0=rm[:8, :ntok], data1=lf, initial=0.0, op0=ALU.mult, op1=ALU.add), [grow, rm], [grow])
        V(lambda e: e.tensor_tensor(out=igc, in0=igc, in1=Fc, op=ALU.subtract), [grow], [grow])
        for s in range(nseg):
            V(lambda e, s=s: e.tensor_tensor(out=Mt[:, s * seglen:(s + 1) * seglen], in0=mext[:, s, 1:1 + seglen], in1=Fc[:, s * seglen:(s + 1) * seglen], op=ALU.subtract), [grow], [grow])
        a_r = igc
        cx.mid1()
        if CFG['stop'] <= 3: return
        def tile(ti):
            c0 = ti * Tn
            if sample:
                st = Cst[1 + ti % 2]
                if ti == 0:
                    fw.dma(st[:, :, 0:128], v3(si_c[j, 0], 8), writes=[st])
                    fw.dma(st[:, :, 128:129], si_n[j, 0].unsqueeze(2), writes=[st])
                if ti + 1 < ntile:
                    stn = Cst[1 + (ti + 1) % 2]
                    fw.dma(stn[:, :, 0:128], v3(si_c[j, ti + 1], 8), writes=[stn])
                    fw.dma(stn[:, :, 128:129], si_n[j, ti + 1].unsqueeze(2), writes=[stn])
                car = mext[:, ti, 0:1]; mt = mext[:, ti, 1:1 + Tn]
            else:
                st = Cst[0]
                car = mext[:, 0, c0:c0 + 1]; mt = mext[:, 0, 1 + c0:1 + c0 + Tn]
            flush(cx.pend_tm[ti])
            tp = (cx.tbase + ti) % 2
            ktm_v = ktm.t[:Tn, tp * 256:(tp + 1) * 256].bitcast(BF16)
            veb = VE[tp]; sob = SOB[tp]
            ve = v3(veb.t[:Tn, 0:516].bitcast(BF16), 8)
            so_v = sob.t[:Tn, :].bitcast(BF16)
            stb = v3(stbuf.t[:64, 0:516].bitcast(BF16), 8)
            A(lambda e: e.activation(out=stb, in_=st[:, :, :], func=ACT.Copy), [st], [stbuf])
            if ti + 1 < ntile:
                srcs = [cx.pend_tm[ti + 1]]
            elif cx.next is not None:
                cx.mid()
                ensure_items(cx.next)
                srcs = [cx.next.pend_fm, cx.next.pend_tm[0]]
            else:
                srcs = []
            npts = [7]

            def fillpt():
                rem = sum(len(l_) for l_ in srcs)
                k = 1 if rem > 0 else 0
                for l_ in srcs:
                    while k > 0 and l_:
                        l_.pop(0)()
                        k -= 1
            if CFG['stop'] <= 4: return
            g_r = trow[:8, 0:Tn]; enm_r = trow[:8, 128:128 + Tn]; wl_r = trow[:8, 256:256 + Tn]; nml = trow[:8, 384:385]
            V(lambda e: e.tensor_scalar(out=nml, in0=Mt[:, c0 + Tn - 1:c0 + Tn], scalar1=-1.0, scalar2=None, op0=ALU.mult), [grow], [trow])
            A(lambda e: e.activation(out=g_r, in_=Mt[:, c0:c0 + Tn], func=ACT.Exp, scale=-1.0, bias=car), [grow], [trow])
            A(lambda e: e.activation(out=enm_r, in_=mt, func=ACT.Exp, scale=-1.0), [grow], [trow])
            A(lambda e: e.activation(out=wl_r, in_=a_r[:, c0:c0 + Tn], func=ACT.Exp, scale=1.0, bias=nml), [grow, trow], [trow])
            px = P[7]
            for qi, row in enumerate((a_r[:, c0:c0 + Tn], g_r, enm_r, wl_r)):
                T(lambda e, qi=qi, row=row: e.transpose(out=px[:Tn, qi * 8:(qi + 1) * 8], in_=row, identity=ident[:8, :8]), [grow, trow, ident], [px])
            V(lambda e: e.tensor_copy(out=cols[:Tn, 0:32], in_=px[:Tn, 0:32]), [px], [cols])
            a_c = cols[:Tn, 0:8]; g_c = cols[:Tn, 8:16]; enm_c = cols[:Tn, 16:24]; wl_c = cols[:Tn, 24:32]
            rb3 = v3(rbd[:8, 0:8 * Tn], 8)
            V(lambda e: e.tensor_tensor(out=rb3, in0=Mt[:, c0:c0 + Tn].unsqueeze(1).to_broadcast([8, 8, Tn]), in1=negI8[:, :].unsqueeze(2).to_broadcast([8, 8, Tn]), op=ALU.mult), [grow, negI8], [rbd])
            V(lambda e: e.tensor_scalar(out=trow[:8, 388:396], in0=negI8[:, :], scalar1=g_r[:, Tn - 1:Tn], scalar2=-1.0, op0=ALU.mult, op1=ALU.mult), [negI8, trow], [trow])
            T(lambda e: e.matmul(px[:64, 40:48], lhsT=ones8[:, 0:64], rhs=trow[:8, 388:396], start=True, stop=True), [ones8, trow], [px])
            V(lambda e: e.tensor_copy(out=glb[:64, 0:8], in_=px[:64, 40:48]), [px], [glb])
            if CFG['stop'] <= 5: return
            kw3 = v3(kw.t[:Tn, 0:256].bitcast(BF16), 8)
            V(lambda e: e.tensor_tensor(out=kw3, in0=v3(ktm_v, 8), in1=wl_c.unsqueeze(2).to_broadcast([Tn, 8, 64]), op=ALU.mult), [ktm, cols], [kw])
            fillpt()
            pD = P[7]
            def bufs(hh):
                if hh == 0:
                    return P[2], P[3], Wt
                return P[6], P[5], Wt2

            def ptbuf(hh):
                if hh == 0:
                    return kw, v3(kw.t[:Tn, 256:512].bitcast(BF16)[:, 0:4 * Tn], 4)
                return sqs, v3(sqs.t[:Tn, 0:256].bitcast(BF16)[:, 0:4 * Tn], 4)

            def halfA(hh):
                psB, psS, Wtb = bufs(hh)
                T(lambda e: e.matmul(psB[:Tn, 0:4 * Tn], lhsT=ones8[:, :Tn], rhs=rbd[:8, hh * 4 * Tn:(hh + 1) * 4 * Tn], start=True, stop=True), [ones8, rbd], [psB])
                for h4 in range(4):
                    h = hh * 4 + h4
                    T(lambda e, h4=h4, h=h: e.matmul(psS[:Tn, h4 * Tn:(h4 + 1) * Tn], lhsT=k3[:, h, c0:c0 + Tn], rhs=q3[:, h, c0:c0 + Tn], start=True, stop=True), [kTc, qTc], [psS])
                W3 = v3(Wtb[:Tn, 0:4 * Tn], 4)
                for h4 in range(4):
                    h = hh * 4 + h4
                    A(lambda e, h=h, h4=h4: e.activation(out=W3[:, h4, :], in_=psB[:Tn, h4 * Tn:(h4 + 1) * Tn], func=ACT.Exp, bias=a_c[:, h:h + 1], scale=1.0), [psB, cols], [Wtb])
                V(lambda e: e.tensor_tensor(out=W3, in0=W3, in1=maskc[:Tn, :Tn].unsqueeze(1).to_broadcast([Tn, 4, Tn]), op=ALU.mult), [Wtb, maskc], [Wtb])
                ptB, PT3 = ptbuf(hh)
                V(lambda e: e.tensor_tensor(out=PT3, in0=v3(psS[:Tn, 0:4 * Tn], 4), in1=W3, op=ALU.mult), [psS, Wtb], [ptB])

            def halfB(hh):
                psN = P[4]; psI = P[5]
                Wtb, W3 = ptbuf(hh)
                for h4 in range(4):
                    h = hh * 4 + h4
                    T(lambda e, h=h, h4=h4: e.matmul(psN[:Tn, h4 * 128:(h4 + 1) * 128], lhsT=W3[:, h4, :], rhs=ve[:, h, 0:128], start=True, stop=True), [Wtb, veb], [psN])
                    T(lambda e, h=h, h4=h4: e.matmul(pD[:Tn, 64 + h:65 + h], lhsT=W3[:, h4, :], rhs=ve[:, h, 128:129], start=True, stop=True), [Wtb, veb], [pD])
                    T(lambda e, h4=h4, h=h: e.matmul(psI[:Tn, h4 * 128:(h4 + 1) * 128], lhsT=q3[:, h, c0:c0 + Tn], rhs=stb[:, h, 0:128], start=True, stop=True), [qTc, stbuf], [psI])
                    T(lambda e, h=h: e.matmul(pD[:Tn, 80 + h:81 + h], lhsT=q3[:, h, c0:c0 + Tn], rhs=stb[:, h, 128:129], start=True, stop=True), [qTc, stbuf], [pD])
                A(lambda e: e.activation(out=numS[:Tn, :], in_=psN[:Tn, :], func=ACT.Copy), [psN], [numS])
                hsl = v3(hs[:Tn, hh * 512:(hh + 1) * 512], 4)
                V(lambda e: e.tensor_tensor(out=hsl, in0=v3(psI[:Tn, :], 4), in1=g_c[:, hh * 4:(hh + 1) * 4].unsqueeze(2).to_broadcast([Tn, 4, 128]), op=ALU.mult), [psI, cols], [hs])
                V(lambda e: e.tensor_tensor(out=hs[:Tn, hh * 512:(hh + 1) * 512], in0=hs[:Tn, hh * 512:(hh + 1) * 512], in1=numS[:Tn, :], op=ALU.add), [hs, numS], [hs])

            halfA(0); fillpt(); halfA(1); fillpt(); halfB(0); fillpt(); halfB(1); fillpt()
            if CFG['stop'] <= 6: return
            V(lambda e: e.tensor_tensor(out=stat[:Tn, 32:40], in0=pD[:Tn, 80:88], in1=g_c, op=ALU.mult), [pD, cols], [stat])
            V(lambda e: e.tensor_tensor(out=stat[:Tn, 24:32], in0=pD[:Tn, 64:72], in1=stat[:Tn, 32:40], op=ALU.add), [pD, stat], [stat])
            V(lambda e: e.scalar_tensor_tensor(out=stat[:Tn, 24:32], in0=stat[:Tn, 24:32], scalar=-1.0, in1=stat[:Tn, 24:32], op0=ALU.mult, op1=ALU.max), [stat], [stat])
            V(lambda e: e.tensor_tensor(out=stat[:Tn, 24:32], in0=stat[:Tn, 24:32], in1=enm_c, op=ALU.max), [stat, cols], [stat])
            V(lambda e: e.reciprocal(out=stat[:Tn, 24:32], in_=stat[:Tn, 24:32]), [stat], [stat])
            head_rmsnorm_gate(Tn, 8, 128, stat[:Tn, 24:32], so_v, sob)
            fillpt()
            hs_to_hsT(Tn, c0)
            if CFG['stop'] <= 7: return
            pUn = P[7]
            for h in range(8):
                psU = P[2 + h // 4]
                T(lambda e, h=h, psU=psU: e.matmul(psU[:64, (h % 4) * 128:(h % 4 + 1) * 128], lhsT=kw3[:, h, :], rhs=ve[:, h, 0:128], start=True, stop=True), [kw, veb], [psU])
                T(lambda e, h=h: e.matmul(pUn[:64, 48 + h:49 + h], lhsT=kw3[:, h, :], rhs=ve[:, h, 128:129], start=True, stop=True), [kw, veb], [pUn])
            fillpt()
            for l_ in srcs:
                flush(l_)
            for h in range(8):
                psU = P[2 + h // 4]
                V(lambda e, h=h, psU=psU: e.scalar_tensor_tensor(out=st[:, h, 0:128], in0=st[:, h, 0:128], scalar=glb[:64, h:h + 1], in1=psU[:64, (h % 4) * 128:(h % 4 + 1) * 128], op0=ALU.mult, op1=ALU.add), [st, glb, psU], [st])
            V(lambda e: e.tensor_tensor(out=st[:, :, 128:129], in0=st[:, :, 128:129], in1=glb[:64, 0:8].unsqueeze(2), op=ALU.mult), [st, glb], [st])
            V(lambda e: e.tensor_tensor(out=st[:, :, 128:129], in0=st[:, :, 128:129], in1=pUn[:64, 48:56].unsqueeze(2), op=ALU.add), [st, pUn], [st])
            if sample:
                dma_out(v3(so_c[j, ti], 8), st[:, :, 0:128], st)
                dma_out(so_n[j, ti].unsqueeze(2), st[:, :, 128:129], st)
        for ti in range(ntile):
            tile(ti)
        if sample:
            dma_out(so_m[j].unsqueeze(2), mext[:, :, Tn:Tn + 1], grow)
        else:
            V(lambda e: e.tensor_copy(out=mext[:, 0, 0:1], in_=mext[:, 0, ntok:ntok + 1]), [grow], [grow])
            if cx.last_block:
                dma_out(po_m[j], mext[:, 0, 0:1], grow)
        out_proj_store(cx, Wo)

    def gla_block(cx):
        j, r0, tiles, reg, ntok, Tn, ntile, sample = cx.j, cx.r0, cx.tiles, cx.reg, cx.ntok, cx.Tn, cx.ntile, cx.sample
        W = v3(WA[:, 0:8 * 3088], 8)
        Wo = v3(WB[:, 0:8 * 1024], 8)
        q3 = v3(qT.t[:, 0:2 * BLK].bitcast(BF16), 4); k3 = v3(kT.t[:, 0:2 * BLK].bitcast(BF16), 4); kl3 = v3(qT[:, 4 * BLK:8 * BLK], 4); lg3 = v3(lgT[:, :], 4)
        for c in range(8):
            ps = P[c % 2]
            proj_fm(cx, ps, 128, W, c * 128, ntok)
            if c < 4:
                V(lambda e, c=c, ps=ps: e.tensor_scalar(out=q3[:, c, :ntok], in0=ps[:, :ntok], scalar1=128.0 ** -0.5, scalar2=None, op0=ALU.mult), [ps], [qT])
            else:
                A(lambda e, c=c, ps=ps: e.activation(out=k3[:, c - 4, :ntok], in_=ps[:, :ntok], func=ACT.Copy), [ps], [kT])
        proj_fm(cx, P[0], 16, W, 3072, ntok)
        zT = grow[:16, 0:ntok]
        V(lambda e: e.tensor_copy(out=zT, in_=P[0][:16, :ntok]), [P[0]], [grow])
        rm = rm_s if sample else rm_p
        for h in range(4):
            ps = P[h % 2]
            T(lambda e, h=h, ps=ps: e.matmul(ps[:, :ntok], lhsT=sp[:16, 16 + h * 128:16 + (h + 1) * 128], rhs=zT, start=True, stop=True), [sp, grow], [ps])
            A(lambda e, h=h, ps=ps: e.activation(out=lg3[:, h, :ntok], in_=ps[:, :ntok], func=ACT.Exp, scale=-1.0, bias=sp[:, 8 + h:9 + h]), [ps, sp], [lgT])
            A(lambda e, h=h: e.activation(out=lg3[:, h, :ntok], in_=lg3[:, h, :ntok], func=ACT.Ln, bias=1.0, scale=1.0), [lgT], [lgT])
            V(lambda e, h=h: e.tensor_scalar(out=lg3[:, h, :ntok], in0=lg3[:, h, :ntok], scalar1=-1.0 / 16, scalar2=None, op0=ALU.mult), [lgT], [lgT])
            V(lambda e, h=h: e.tensor_copy(out=hs[:, h * BLK:h * BLK + ntok], in_=lg3[:, h, :ntok]), [lgT], [hs])
            V(lambda e, h=h: e.tensor_tensor_scan(out=lg3[:, h, :ntok], data0=rm[:, :ntok], data1=hs[:, h * BLK:h * BLK + ntok], initial=0.0, op0=ALU.mult, op1=ALU.add), [hs, rm], [lgT])
        def tile(ti):
            c0 = ti * Tn
            if sample:
                st = Sst[1 + ti % 2]
                if ti == 0:
                    fw.dma(st[:, :, :], v3(si_g[j, 0], 4), writes=[st])
                if ti + 1 < ntile:
                    stn = Sst[1 + (ti + 1) % 2]
                    fw.dma(stn[:, :, :], v3(si_g[j, ti + 1], 4), writes=[stn])
            else:
                st = Sst[0]
            bl = cols[:, 32:36]; ebl = cols[:, 36:40]
            V(lambda e: e.tensor_copy(out=bl.unsqueeze(2), in_=lg3[:, :, c0 + Tn - 1:c0 + Tn]), [lgT], [cols])
            A(lambda e: e.activation(out=ebl, in_=bl, func=ACT.Exp), [cols], [cols])
            for h in range(4):
                A(lambda e, h=h: e.activation(out=kl3[:, h, c0:c0 + Tn], in_=lg3[:, h, c0:c0 + Tn], func=ACT.Exp, scale=-1.0, bias=bl[:, h:h + 1]), [lgT, cols], [qT])
            G(lambda e: e.tensor_tensor(out=kl3[:, :, c0:c0 + Tn], in0=kl3[:, :, c0:c0 + Tn], in1=k3[:, :, c0:c0 + Tn], op=ALU.mult), [qT, kT], [qT])
            pk = P[6]
            for h in range(4):
                T(lambda e, h=h: e.transpose(out=pk[:Tn, h * 128:(h + 1) * 128], in_=kl3[:, h, c0:c0 + Tn], identity=ident[:, :]), [qT, ident], [pk])
            A(lambda e: e.activation(out=kw.t[:Tn, 0:256].bitcast(BF16), in_=pk[:Tn, :], func=ACT.Copy), [pk], [kw])
            kl_tm = v3(kw.t[:Tn, 0:256].bitcast(BF16), 4)
            A(lambda e: e.activation(out=v3(Wt[:, 0:4 * Tn], 4), in_=lg3[:, :, c0:c0 + Tn], func=ACT.Exp), [lgT], [Wt])
            V(lambda e: e.tensor_tensor(out=q3[:, :, c0:c0 + Tn], in0=q3[:, :, c0:c0 + Tn], in1=v3(Wt[:, 0:4 * Tn], 4), op=ALU.mult), [qT, Wt], [qT])
            A(lambda e: e.activation(out=v3(Wt[:, 0:4 * Tn], 4), in_=lg3[:, :, c0:c0 + Tn], func=ACT.Exp, scale=-1.0), [lgT], [Wt])
            V(lambda e: e.tensor_tensor(out=k3[:, :, c0:c0 + Tn], in0=k3[:, :, c0:c0 + Tn], in1=v3(Wt[:, 0:4 * Tn], 4), op=ALU.mult), [kT, Wt], [kT])
            vt = v3(vext.t[:Tn, 0:512].bitcast(BF16), 4)
            vtf = vext.t[:Tn, 0:512].bitcast(BF16)
            stb = v3(vext.t[:, 520:1032].bitcast(BF16), 4)
            G(lambda e: e.tensor_copy(out=stb, in_=st[:, :, :]), [st], [vext])
            for hf in range(2):
                proj_tm(cx, P[hf], Tn, c0, W, 1024 + hf * 512, 512)
                A(lambda e, hf=hf: e.activation(out=vtf[:, hf * 512:(hf + 1) * 512], in_=P[hf][:Tn, :], func=ACT.Copy), [P[hf]], [vext])
            for hf in range(2):
                proj_tm(cx, P[hf], Tn, c0, W, 2048 + hf * 512, 512)
                A(lambda e, hf=hf: e.activation(out=so[:Tn, hf * 512:(hf + 1) * 512], in_=P[hf][:Tn, :], func=ACT.Silu), [P[hf]], [so])
            G(lambda e: e.tensor_tensor(out=so[:Tn, :], in0=so[:Tn, :], in1=gnb[:Tn, :], op=ALU.mult), [so, gnb], [so])
            psS = P[3]
            for h in range(4):
                T(lambda e, h=h: e.matmul(psS[:Tn, h * Tn:(h + 1) * Tn], lhsT=k3[:, h, c0:c0 + Tn], rhs=q3[:, h, c0:c0 + Tn], start=True, stop=True), [kT, qT], [psS])
            PT3 = v3(numS.t[:Tn, 0:256].bitcast(BF16)[:, 0:4 * Tn], 4)
            V(lambda e: e.tensor_tensor(out=PT3, in0=v3(psS[:Tn, 0:4 * Tn], 4), in1=maskc[:Tn, :Tn].unsqueeze(1).to_broadcast([Tn, 4, Tn]), op=ALU.mult), [psS, maskc], [numS])
            for h in range(4):
                ps = P[4 + h // 2]
                o0 = (h % 2) * 256
                T(lambda e, h=h, ps=ps, o0=o0: e.matmul(ps[:Tn, o0:o0 + 256], lhsT=PT3[:, h, :], rhs=vt[:, h, :], start=True, stop=False), [numS, vext], [ps])
                T(lambda e, h=h, ps=ps, o0=o0: e.matmul(ps[:Tn, o0:o0 + 256], lhsT=q3[:, h, c0:c0 + Tn], rhs=stb[:, h, :], start=False, stop=True), [qT, vext], [ps])
            A(lambda e: e.activation(out=hs[:Tn, 0:512], in_=P[4][:Tn, :], func=ACT.Copy), [P[4]], [hs])
            V(lambda e: e.tensor_copy(out=hs[:Tn, 512:1024], in_=P[5][:Tn, :]), [P[5]], [hs])
            for h in range(4):
                ps = P[2]
                T(lambda e, h=h, ps=ps: e.matmul(ps[:, 0:256], lhsT=kl_tm[:, h, :], rhs=vt[:, h, :], start=True, stop=True), [kw, vext], [ps])
                V(lambda e, h=h, ps=ps: e.scalar_tensor_tensor(out=st[:, h, :], in0=st[:, h, :], scalar=ebl[:, h:h + 1], in1=ps[:, 0:256], op0=ALU.mult, op1=ALU.add), [st, cols, ps], [st])
            if sample:
                dma_out(v3(so_g[j, ti], 4), st[:, :, :], st)
            head_rmsnorm_gate(Tn, 4, 256, None)
            hs_to_hsT(Tn, c0)
        for ti in range(ntile):
            tile(ti)
            if ti == 0:
                cx.mid1()
        cx.mid()
        out_proj_store(cx, Wo)

    def swa_block(cx):
        j, r0, tiles, reg, ntok, Tn, ntile, sample, last_block = cx.j, cx.r0, cx.tiles, cx.reg, cx.ntok, cx.Tn, cx.ntile, cx.sample, cx.last_block
        W = v3(WA[:, 0:8 * 1536], 8)
        Wo = v3(WB[:, 0:8 * 1024], 8)
        q8 = v3(qT.t[:64, 0:1024].bitcast(BF16), 16); k2 = v3(kT.t[:64, 0:256].bitcast(BF16), 4)
        for h in range(20):
            ps = P[h % 2]
            proj_fm(cx, ps, 64, W, h * 64, ntok)
            if h < 16:
                A(lambda e, h=h, ps=ps: e.activation(out=q8[:, h, :ntok], in_=ps[:64, :ntok], func=ACT.Identity, bias=sp[:64, 528 + h:529 + h], scale=1.0), [ps, sp], [qT])
            else:
                A(lambda e, h=h, ps=ps: e.activation(out=k2[:, h - 16, :ntok], in_=ps[:64, :ntok], func=ACT.Identity, bias=sp[:64, 544 + h - 16:545 + h - 16], scale=1.0), [ps, sp], [kT])
        def tile(ti):
            c0 = ti * Tn
            proj_tm(cx, P[0], Tn, c0, W, 1024, 512)
            V(lambda e: e.tensor_tensor(out=ktm[:Tn, :], in0=P[0][:Tn, :], in1=gnb[:Tn, 0:512], op=ALU.add), [P[0], gnb], [ktm])
            ve = v3(vext.t[:Tn, 0:130].bitcast(BF16), 4)
            if sample:
                V(lambda e: e.memset(vext[:, 0:4 * 65], 0.0), [], [vext])
                V(lambda e: e.memset(numS[:, 0:256], 0.0), [], [numS])
                V(lambda e: e.memset(Wt2[:, 256:512], 0.0), [], [Wt2])
            G(lambda e: e.memset(ve[:, :, 64:65], 1.0), [], [vext])
            G(lambda e: e.tensor_copy(out=ve[:, :, 0:64], in_=v3(ktm[:Tn, 256:512], 4)), [ktm], [vext])
            vefull = v3(vext.t[:, 0:130].bitcast(BF16), 4)
            if sample:
                kp = kTprev[1 + ti % 2]; vp = vprev[1 + ti % 2]; raw = kvraw[ti % 2]
                if ti == 0:
                    fw.dma(raw[:, 0:256], si_k[j, 0], writes=[raw])
                    fw.dma(raw[:, 256:512], si_v[j, 0], writes=[raw])
                if ti + 1 < ntile:
                    rawn = kvraw[(ti + 1) % 2]
                    fw.dma(rawn[:, 0:256], si_k[j, ti + 1], writes=[rawn])
                    fw.dma(rawn[:, 256:512], si_v[j, ti + 1], writes=[rawn])
                pk = P[6]
                for c in range(4):
                    T(lambda e, c=c: e.transpose(out=pk[:64, c * 128:(c + 1) * 128], in_=raw[:, c * 64:(c + 1) * 64], identity=ident[:, :]), [raw, ident], [pk])
                A(lambda e: e.activation(out=kp[:, :, :], in_=v3(pk[:64, 0:512], 4), func=ACT.Copy), [pk], [kp])
                G(lambda e: e.memset(vp[:, :, 64:65], 1.0), [], [vp])
                G(lambda e: e.tensor_copy(out=vp[:, :, 0:64], in_=v3(raw[:, 256:512], 4)), [raw], [vp])
                has_prev = True
                dma_out(so_k[j, ti, 0:124, :], raw[4:128, 0:256], raw)
                dma_out(so_v[j, ti, 0:124, :], raw[4:128, 256:512], raw)
                dma_out(so_k[j, ti, 124:128, :], ktm[:Tn, 0:256], ktm)
                dma_out(so_v[j, ti, 124:128, :], ktm[:Tn, 256:512], ktm)
            else:
                kp = kTprev[0]; vp = vprev[0]
                has_prev = not (r0 == 0 and ti == 0)
                if last_block and ti == ntile - 1:
                    dma_out(po_k[j], ktm[:Tn, 0:256], ktm)
                    dma_out(po_v[j], ktm[:Tn, 256:512], ktm)
            blocks = ([(kp, vp, 128, maskp)] if has_prev else []) + [(None, None, Tn, maskc)]
            pD = P[7]
            def kvhead(kh):
                PTs = []
                for bi, (kpb, vpb, nk, msk) in enumerate(blocks):
                    psS = P[2 + bi] if kh % 2 == 0 else P[bi]
                    for g in range(4):
                        if kpb is None:
                            T(lambda e, g=g, psS=psS, kh=kh, nk=nk: e.matmul(psS[:nk, g * Tn:(g + 1) * Tn], lhsT=k2[:, kh, c0:c0 + nk], rhs=q8[:, kh * 4 + g, c0:c0 + Tn], start=True, stop=True), [kT, qT], [psS])
                        else:
                            T(lambda e, g=g, psS=psS, kpb=kpb, kh=kh, nk=nk: e.matmul(psS[:nk, g * Tn:(g + 1) * Tn], lhsT=kpb[:, kh, 0:nk], rhs=q8[:, kh * 4 + g, c0:c0 + Tn], start=True, stop=True), [kpb, qT], [psS])
                    if kh % 2 == 0:
                        PTb = Wt if bi == 0 else numS
                        PTv = PTb.t[:, 0:256].bitcast(BF16)
                    else:
                        PTb = Wt2
                        PTv = Wt2.t[:, bi * 256:(bi + 1) * 256].bitcast(BF16)
                    A(lambda e, psS=psS, PTb=PTb, PTv=PTv, nk=nk: e.activation(out=PTv[:nk, 0:4 * Tn], in_=psS[:nk, 0:4 * Tn], func=ACT.Exp, scale=0.125), [psS], [PTb])
                    V(lambda e, PTb=PTb, PTv=PTv, nk=nk, msk=msk: e.tensor_tensor(out=v3(PTv[:nk, 0:4 * Tn], 4), in0=v3(PTv[:nk, 0:4 * Tn], 4), in1=msk[:nk, :Tn].unsqueeze(1).to_broadcast([nk, 4, Tn]), op=ALU.mult), [PTb, msk], [PTb])
                    PTs.append((PTb, PTv, (128 if sample else nk), (vefull if sample else ve) if kpb is None else vpb, vext if kpb is None else vpb))
                for g in range(4):
                    hq = kh * 4 + g
                    psO = P[4 + hq // 8]
                    o0 = (hq % 8) * 64
                    for bi, (PTb, PTv, nk, vv, vbuf) in enumerate(PTs):
                        T(lambda e, g=g, PTb=PTb, PTv=PTv, nk=nk, vv=vv, psO=psO, o0=o0, bi=bi, kh=kh: e.matmul(psO[:Tn, o0:o0 + 64], lhsT=PTv[:nk, g * Tn:(g + 1) * Tn], rhs=vv[:nk, kh, 0:64], start=(bi == 0), stop=(bi == len(PTs) - 1)), [PTb, vbuf], [psO])
                    for bi, (PTb, PTv, nk, vv, vbuf) in enumerate(PTs):
                        T(lambda e, g=g, PTb=PTb, PTv=PTv, nk=nk, vv=vv, hq=hq, bi=bi, kh=kh: e.matmul(pD[:Tn, 96 + hq:97 + hq], lhsT=PTv[:nk, g * Tn:(g + 1) * Tn], rhs=vv[:nk, kh, 64:65], start=(bi == 0), stop=(bi == len(PTs) - 1)), [PTb, vbuf], [pD])
            for kh in range(4):
                kvhead(kh)
            V(lambda e: e.tensor_tensor(out=stat[:Tn, 40:56], in0=pD[:Tn, 96:112], in1=sp[:Tn, 552:568], op=ALU.add), [pD, sp], [stat])
            V(lambda e: e.reciprocal(out=stat[:Tn, 40:56], in_=stat[:Tn, 40:56]), [stat], [stat])
            for hf in range(2):
                V(lambda e, hf=hf: e.tensor_tensor(out=v3(hs[:Tn, hf * 512:(hf + 1) * 512], 8), in0=v3(P[4 + hf][:Tn, :], 8), in1=stat[:Tn, 40 + hf * 8:48 + hf * 8].unsqueeze(2).to_broadcast([Tn, 8, 64]), op=ALU.mult), [P[4 + hf], stat], [hs])
            hs_to_hsT(Tn, c0)
            if not sample:
                G(lambda e: e.tensor_copy(out=kTprev[0][:, :, :], in_=k2[:, :, c0:c0 + Tn]), [kT], [kTprev[0]])
                G(lambda e: e.tensor_copy(out=vprev[0][:, :, :], in_=ve), [vext], [vprev[0]])
        for ti in range(ntile):
            tile(ti)
            if ti == 0:
                cx.mid1()
        cx.mid()
        out_proj_store(cx, Wo)

    def ffn_block(cx):
        layer, r0, tiles, reg, ntok, nseq, halo = cx.layer, cx.r0, cx.tiles, cx.reg, cx.ntok, cx.nseq, cx.halo
        xb = XB[cx.par]; xsT = XST[cx.par]
        Wg = v3(WA[:, 0:8 * DFF], 8); Wu = v3(WB[:, 0:8 * DFF], 8); Wd = v3(WC[:, 0:NFC * D], NFC)
        Tq = ntok // nseq
        h4 = v4(halo[:, :], NFC, nseq)
        def stageA(c):
            psG = P[2 + (c % 2) * 2]; psU = P[3 + (c % 2) * 2]
            for kc in range(8):
                T(lambda e, kc=kc: e.matmul(psG[:, :ntok], lhsT=Wg[:, kc, c * 128:(c + 1) * 128], rhs=xsT[:, kc, :ntok], start=(kc == 0), stop=(kc == 7)), [xsT, WA], [psG])
            for kc in range(8):
                T(lambda e, kc=kc: e.matmul(psU[:, :ntok], lhsT=Wu[:, kc, c * 128:(c + 1) * 128], rhs=xsT[:, kc, :ntok], start=(kc == 0), stop=(kc == 7)), [xsT, WB], [psU])
            ge = gext[c % 2]; ca = cacc[c % 2]
            ge3 = v3(ge[:, 0:nseq * (Tq + 2)], nseq)
            ca3 = v3(ca[:, 0:ntok], nseq)
            G(lambda e: e.tensor_copy(out=ge3[:, :, 0:2], in_=h4[:, c, :, :]), [halo], [ge])
            A(lambda e: e.activation(out=ge3[:, :, 2:2 + Tq], in_=v3(psG[:, :ntok], nseq), func=ACT.Copy), [psG], [ge])
            G(lambda e: e.tensor_copy(out=h4[:, c, :, :], in_=ge3[:, :, Tq:Tq + 2]), [ge], [halo])
            G(lambda e: e.tensor_scalar(out=ca3, in0=ge3[:, :, 0:Tq], scalar1=cwb[:, c * 3:c * 3 + 1], scalar2=cbb[:, c:c + 1], op0=ALU.mult, op1=ALU.add), [ge, cwb, cbb], [ca])

        def stageB(c):
            psU = P[3 + (c % 2) * 2]
            ge = gext[c % 2]; ca = cacc[c % 2]
            ge3 = v3(ge[:, 0:nseq * (Tq + 2)], nseq)
            ca3 = v3(ca[:, 0:ntok], nseq)
            V(lambda e: e.scalar_tensor_tensor(out=ca3, in0=ge3[:, :, 1:1 + Tq], scalar=cwb[:, c * 3 + 1:c * 3 + 2], in1=ca3, op0=ALU.mult, op1=ALU.add), [ge, cwb, ca], [ca])
            V(lambda e: e.scalar_tensor_tensor(out=ca3, in0=ge3[:, :, 2:2 + Tq], scalar=cwb[:, c * 3 + 2:c * 3 + 3], in1=ca3, op0=ALU.mult, op1=ALU.add), [ge, cwb, ca], [ca])
            A(lambda e: e.activation(out=ca[:, 0:ntok], in_=ca[:, 0:ntok], func=ACT.Silu), [ca], [ca])
            V(lambda e: e.tensor_tensor(out=hT[:, c, :ntok], in0=psU[:, :ntok], in1=ca[:, 0:ntok], op=ALU.mult), [psU, ca], [hT])

        for c in range(NFC + 1):
            if c < NFC:
                stageA(c)
            if c >= 1:
                stageB(c - 1)
        cx.midn(0); cx.midn(1)
        col = 0
        for i, R in enumerate(tiles):
            for half in range(2):
                ps = P[half]
                for c in range(NFC):
                    T(lambda e, c=c, R=R, col=col, half=half, ps=ps: e.matmul(ps[:R, :], lhsT=hT[:, c, col:col + R], rhs=Wd[:, c, half * 512:(half + 1) * 512], start=(c == 0), stop=(c == NFC - 1)), [hT, WC], [ps])
                V(lambda e, i=i, R=R, half=half, ps=ps: e.tensor_tensor(out=xb[:R, i, half * 512:(half + 1) * 512], in0=ps[:R, :], in1=xb[:R, i, half * 512:(half + 1) * 512], op=ALU.add), [ps, xb], [xb])
            cx.midt(i)
            if i == len(tiles) - 1 and len(tiles) == 1:
                cx.midt(1)
            epilogue_tile(cx, xb, i, R, col)
            col += R

    NB = TP // BLK
    fw.dma(gfin[:, :], norm_fin[0].partition_broadcast(128), writes=[gfin])

    def run_sublayer(fn, blocks):
        tb = 0
        for i, cx in enumerate(blocks):
            cx.par = i % 2
            cx.tbase = tb
            tb += getattr(cx, 'ntile', 0)
            cx.next = blocks[i + 1] if i + 1 < len(blocks) else None
        if not blocks:
            return
        front_load(blocks[0]); front_compute(blocks[0])
        for i, cx in enumerate(blocks):
            nxt = blocks[i + 1] if i + 1 < len(blocks) else None
            if nxt is not None:
                front_load(nxt)
                cx.mid1 = (lambda nxt=nxt: front_norm(nxt))
                cx.mid = (lambda nxt=nxt: front_T(nxt))
                cx.midn = (lambda i, nxt=nxt: front_norm(nxt, i))
                cx.midt = (lambda i, nxt=nxt: front_T(nxt, i))
            else:
                cx.mid1 = (lambda: None)
                cx.mid = (lambda: None)
                cx.midn = (lambda i: None)
                cx.midt = (lambda i: None)
            fn(cx)

    regs_p = [fw.region("rp%d" % i) for i in range(NB)]
    reg_s = fw.region("rs")
    zero_done = False
    for layer in range(CFG['L']):
        kind = layer % 3; j = layer // 3
        handoff([WC, WB], msb)
        handoff(pst_bufs, pst_bufs)
        w_in = (a_w_in, b_w_in, c_w_in)[kind][j]; w_out = (a_w_out, b_w_out, c_w_out)[kind][j]
        ncol = 1536 if kind == 2 else 3088
        load_w(WA, ncol, w_in, 8)
        load_w(WB, 1024, w_out, 8)
        fw.dma(gb[:, :], norm_mix[layer].partition_broadcast(128), writes=[gb])
        if kind == 0:
            fw.dma(gnb[:, :], a_norm[j].partition_broadcast(128), writes=[gnb])
            fw.dma(sp[:8, 0:1], a_bi[j], writes=[sp]); fw.dma(sp[:8, 1:2], a_bf[j], writes=[sp])
            V(lambda e: e.tensor_scalar(out=sp[:8, 0:2], in0=sp[:8, 0:2], scalar1=1.0 / 15, scalar2=None, op0=ALU.mult), [sp], [sp])
            V(lambda e: e.memset(Cst[0][:, :, :], 0.0), [], [Cst[0]])
            V(lambda e: e.memset(grow[:8, 4 * (BLK + NSEQ):4 * (BLK + NSEQ) + 1], 0.0), [], [grow])
        elif kind == 1:
            fw.dma(gnb[:, :], b_norm[j].partition_broadcast(128), writes=[gnb])
            fw.dma(sp[:, 8:12], b_bg[j], writes=[sp])
            V(lambda e: e.tensor_scalar(out=sp[:, 8:12], in0=sp[:, 8:12], scalar1=-1.0, scalar2=None, op0=ALU.mult), [sp], [sp])
            fw.dma(sp[:16, 16:16 + 512], b_wgu[j], writes=[sp])
            V(lambda e: e.memset(Sst[0][:, :, :], 0.0), [], [Sst[0]])
        else:
            fw.dma(gnb[:, 0:512], c_bkv[j].partition_broadcast(128), writes=[gnb])
            fw.dma(sp[:64, 528:544], c_bq[j], writes=[sp]); fw.dma(sp[:64, 544:548], c_bk[j], writes=[sp])
            fw.dma(sp[:, 552:568], c_snk[j].partition_broadcast(128), writes=[sp])
            A(lambda e: e.activation(out=sp[:, 552:568], in_=sp[:, 552:568], func=ACT.Exp), [sp], [sp])
        Wcur[0] = WA
        blocks = []
        for which in (CFG['parts'] if CFG['mix'] else ''):
            if which == "p":
                src = xp if layer == 0 else res_p
                if kind == 2:
                    nbk = 2 * CFG['nb']
                    for b in range(nbk):
                        blocks.append(Cx(j=j, layer=layer, src=src, dst=res_p, r0=b * 128, tiles=[128], reg=regs_p[b // 2], ntok=128, Tn=128, ntile=1, sample=False, last_block=(b == nbk - 1), final=False))
                else:
                    for b in range(CFG['nb']):
                        blocks.append(Cx(j=j, layer=layer, src=src, dst=res_p, r0=b * BLK, tiles=[128, 128], reg=regs_p[b], ntok=BLK, Tn=128, ntile=2, sample=False, last_block=(b == CFG['nb'] - 1), final=False))
            else:
                src = xsm if layer == 0 else res_s
                blocks.append(Cx(j=j, layer=layer, src=src, dst=res_s, r0=0, tiles=[NS], reg=reg_s, ntok=NS, Tn=4, ntile=NSEQ, sample=True, last_block=True, final=False))
        run_sublayer((mlstm_block, gla_block, swa_block)[kind], blocks)
        if 'p' in CFG['parts'] and CFG['mix']:
            if kind == 0:
                dma_out(v3(po_c[j], 8), Cst[0][:, :, 0:128], Cst[0])
                dma_out(po_n[j].unsqueeze(2), Cst[0][:, :, 128:129], Cst[0])
            elif kind == 1:
                dma_out(v3(po_g[j], 4), Sst[0][:, :, :], Sst[0])
        if not CFG['ffn']:
            continue
        handoff(msb, [WC, WB])
        load_w(WA, DFF, f_w_up[layer][:, 0:DFF], 8)
        load_w(WB, DFF, f_w_up[layer][:, DFF:2 * DFF], 8)
        load_w(WC, D, f_w_down[layer], NFC)
        fw.dma(gb[:, :], norm_ffn[layer].partition_broadcast(128), writes=[gb])
        fw.dma(cwb[:, :], f_cw[layer], writes=[cwb]); fw.dma(cbb[:, :], f_cb[layer], writes=[cbb])
        V(lambda e: e.memset(halo_p[:, :], 0.0), [], [halo_p])
        fw.dma(halo_s[:, :], si_f[layer], writes=[halo_s])
        fin = (layer == CFG['L'] - 1)
        blocks = []
        for b in range(CFG['nb'] if 'p' in CFG['parts'] else 0):
            blocks.append(Cx(j=j, layer=layer, src=(xp if (layer == 0 and not CFG['mix']) else res_p), dst=res_p, fdst=yp, r0=b * BLK, tiles=[128, 128], reg=regs_p[b], ntok=BLK, nseq=1, halo=halo_p, final=fin, po=(b == CFG['nb'] - 1)))
        if 's' in CFG['parts']:
            blocks.append(Cx(j=j, layer=layer, src=(xsm if (layer == 0 and not CFG['mix']) else res_s), dst=res_s, fdst=ys, r0=0, tiles=[NS], reg=reg_s, ntok=NS, nseq=NSEQ, halo=halo_s, final=fin, po=False))
        run_sublayer(ffn_block, blocks)
        dma_out(po_f[layer], halo_p[:, :], halo_p)
        dma_out(so_f[layer], halo_s[:, :], halo_s)
    fw.final_wait(out_toks)
    fw.emit()
    print("instr counts", fw.stats())
    return nc


_NC = [None]


def _lay(x):
    return np.ascontiguousarray(x, dtype=np.float32)


def kernel(**inp):
    inp = {k: np.asarray(v) for k, v in inp.items()}
    if _NC[0] is None:
        _NC[0] = build()
    nc = _NC[0]
    f32 = np.float32
    ii = np.arange(128)
    consts = {
        "k_ident": np.eye(128, dtype=f32),
        "k_maskc": (ii[:, None] <= ii[None, :]).astype(f32),
        "k_maskp": (ii[:, None] >= ii[None, :]).astype(f32),
        "k_ones8": np.ones((8, 128), f32),
        "k_negI8": -np.eye(8, dtype=f32),
        "k_sel8": ((np.arange(8)[:, None] % 2) == (ii[None, :] // 64)).astype(f32),
        "k_selc": ((np.arange(8)[:, None] // 2) == np.arange(4)[None, :]).astype(f32),
        "k_rm_p": np.tile(((np.arange(BLK) % 128) != 0).astype(f32)[None], (128, 1)),
        "k_rm_s": np.tile(((np.arange(NS) % 4) != 0).astype(f32)[None], (128, 1)),
    }
    c_w_in = inp["c_w_in"]
    c_b = inp["c_b_in"]
    c_bq = c_b[:, 0:1024].reshape(1, 16, 64).transpose(0, 2, 1)
    c_bk = c_b[:, 1024:1280].reshape(1, 4, 64).transpose(0, 2, 1)
    shared = {
        "a_w_in": inp["a_w_in"], "a_w_out": inp["a_w_out"], "b_w_in": inp["b_w_in"], "b_w_out": inp["b_w_out"],
        "c_w_in": c_w_in, "c_w_out": inp["c_w_out"], "f_w_up": inp["f_w_up"], "f_w_down": inp["f_w_down"],
        "b_wgu": inp["b_w_gate_up"],
        "norm_mix": inp["norm_mix_g"], "norm_ffn": inp["norm_ffn_g"], "norm_fin": inp["norm_final_g"].reshape(1, D),
        "a_norm": inp["a_norm_g"], "b_norm": inp["b_norm_g"],
        "a_bi": inp["a_b_i"].reshape(2, 8, 1), "a_bf": inp["a_b_f"].reshape(2, 8, 1),
        "b_bg": inp["b_b_gate"].reshape(1, 4, 128).transpose(0, 2, 1),
        "c_bq": c_bq, "c_bk": c_bk, "c_bkv": c_b[:, 1024:1536], "c_snk": inp["c_sinks"],
        "f_cw": inp["f_conv_w"].reshape(4, 3, NFC, 128).transpose(0, 3, 2, 1).reshape(4, 128, NFC * 3),
        "f_cb": inp["f_conv_b"].reshape(4, NFC, 128).transpose(0, 2, 1),
    }
    shared.update(consts)
    shared = {k: _lay(v) for k, v in shared.items()}
    in_maps = []
    for c in range(8):
        sl = slice(c * NSEQ, (c + 1) * NSEQ)
        m = dict(shared)
        m["xp"] = _lay(inp["x_prompt"][c % 4])
        m["xsm"] = _lay(inp["x_sample"][sl].reshape(NS, D))
        C = inp["state_mlstm_c"][:, sl]
        m["si_c"] = _lay(C.transpose(0, 1, 3, 2, 4).reshape(2, NSEQ, 64, 1024))
        n = inp["state_mlstm_n"][:, sl]
        m["si_n"] = _lay(n.transpose(0, 1, 3, 2))
        m["si_m"] = _lay(inp["state_mlstm_m"][:, sl].transpose(0, 2, 1))
        m["si_g"] = _lay(inp["state_gla"][:, sl].transpose(0, 1, 3, 2, 4).reshape(1, NSEQ, 128, 1024))
        m["si_k"] = _lay(inp["cache_swa_k"][:, sl].reshape(1, NSEQ, 128, 256))
        m["si_v"] = _lay(inp["cache_swa_v"][:, sl].reshape(1, NSEQ, 128, 256))
        f = inp["state_ffn_conv"][:, sl]
        m["si_f"] = _lay(f.reshape(4, NSEQ, 2, NFC, 128).transpose(0, 4, 3, 1, 2).reshape(4, 128, NFC * NSEQ * 2))
        in_maps.append(m)
    ncr = CFG.get('cores', 8)
    if os.environ.get('KN_TRACE'):
        res = run_bass_kernel_spmd(nc, in_maps[:ncr], core_ids=list(range(ncr)), trace=True)
        print('EXEC_TIME_NS', res.exec_time_ns, flush=True)
    else:
        res = run_bass_kernel_spmd(nc, in_maps[:ncr], core_ids=list(range(ncr)))
    R = list(res.results)
    while len(R) < 8:
        R.append(R[0])
    B = 4
    y_prompt = np.stack([R[b]["yp"] for b in range(B)])
    y_sample = np.concatenate([R[c]["ys"].reshape(NSEQ, 4, D) for c in range(8)], 0)

    def unC(x):
        return x.reshape(2, 64, 8, 128).transpose(0, 2, 1, 3)

    def unN(x):
        return x.transpose(0, 2, 1)

    p_c = np.stack([unC(R[b]["po_c"]) for b in range(B)], 1)
    p_n = np.stack([unN(R[b]["po_n"]) for b in range(B)], 1)
    p_m = np.stack([R[b]["po_m"].reshape(2, 8) for b in range(B)], 1)
    p_g = np.stack([R[b]["po_g"].reshape(1, 128, 4, 256).transpose(0, 2, 1, 3) for b in range(B)], 1)
    p_k = np.stack([R[b]["po_k"].reshape(1, 128, 4, 64) for b in range(B)], 1)
    p_v = np.stack([R[b]["po_v"].reshape(1, 128, 4, 64) for b in range(B)], 1)
    p_f = np.stack([R[b]["po_f"].reshape(4, 128, NFC, 2).transpose(0, 3, 2, 1).reshape(4, 2, DFF) for b in range(B)], 1)
    s_c = np.concatenate([R[c]["so_c"].reshape(2, NSEQ, 64, 8, 128).transpose(0, 1, 3, 2, 4) for c in range(8)], 1)
    s_n = np.concatenate([R[c]["so_n"].transpose(0, 1, 3, 2) for c in range(8)], 1)
    s_m = np.concatenate([R[c]["so_m"].transpose(0, 2, 1) for c in range(8)], 1)
    s_g = np.concatenate([R[c]["so_g"].reshape(1, NSEQ, 128, 4, 256).transpose(0, 1, 3, 2, 4) for c in range(8)], 1)
    s_k = np.concatenate([R[c]["so_k"].reshape(1, NSEQ, 128, 4, 64) for c in range(8)], 1)
    s_v = np.concatenate([R[c]["so_v"].reshape(1, NSEQ, 128, 4, 64) for c in range(8)], 1)
    s_f = np.concatenate([R[c]["so_f"].reshape(4, 128, NFC, NSEQ, 2).transpose(0, 3, 4, 2, 1).reshape(4, NSEQ, 2, DFF) for c in range(8)], 1)
    outs = (y_prompt, y_sample, p_c, p_n, p_m, p_g, p_k, p_v, p_f, s_c, s_n, s_m, s_g, s_k, s_v, s_f)
    return tuple(np.ascontiguousarray(o, dtype=np.float32) for o in outs)
```
